# Optimizing a Trainium2 kernel written in Bass

```python
import math
import jax, jax.numpy as jnp
from jax import lax
import numpy as np

D_MODEL = 1024
BATCH = 16
SEQ = 2048
DEPTH = 1

HEAD_DIM = 64
MIX_WIDTH = D_MODEL
ATT_HEADS = 8
GDN_HEADS = 8
ATT_WIDTH = ATT_HEADS * HEAD_DIM
GDN_WIDTH = GDN_HEADS * HEAD_DIM
MOBA_BLOCK = 256
MOBA_TOPK = 3
MOBA_QCHUNK = 32
GDN_CHUNK = 64
CONV_WIDTH = 4
D_FF = 4 * D_MODEL
REL_BUCKETS = 32
REL_MAX_EXACT = 16
REL_MAX_DIST = 128
EPS = 1e-6
IN_SIZES = [ATT_WIDTH, ATT_WIDTH, ATT_WIDTH, 3 * GDN_WIDTH, GDN_WIDTH, GDN_HEADS, GDN_HEADS]
IN_COLS = sum(IN_SIZES)

kernel_name = "hymba_moba_gdn_sandwich_layer"


def rms_norm(x, w):
    xf = x.astype(jnp.float32)
    y = xf * lax.rsqrt(jnp.mean(xf * xf, axis=-1, keepdims=True) + EPS)
    return (y * w.astype(jnp.float32)).astype(x.dtype)


def l2_normalize(x):
    xf = x.astype(jnp.float32)
    return xf * lax.rsqrt(jnp.sum(xf * xf, axis=-1, keepdims=True) + EPS)


def rel_bucket(dist):
    d = jnp.maximum(dist, 0)
    large = REL_MAX_EXACT + (
        jnp.log(jnp.maximum(d, 1).astype(jnp.float32) / REL_MAX_EXACT)
        / math.log(REL_MAX_DIST / REL_MAX_EXACT)
        * (REL_BUCKETS - REL_MAX_EXACT)
    ).astype(jnp.int32)
    large = jnp.minimum(large, REL_BUCKETS - 1)
    return jnp.where(d < REL_MAX_EXACT, d, large)


def moba_attention(q, k, v, rel_bias):
    B, S, H, dh = q.shape
    nb = -(-S // MOBA_BLOCK)
    s_pad = nb * MOBA_BLOCK
    qh = q.transpose(0, 2, 1, 3)
    pad = ((0, 0), (0, 0), (0, s_pad - S), (0, 0))
    kh = jnp.pad(k.transpose(0, 2, 1, 3), pad)
    vh = jnp.pad(v.transpose(0, 2, 1, 3), pad)
    kb = kh.reshape(B, H, nb, MOBA_BLOCK, dh)
    vb = vh.reshape(B, H, nb, MOBA_BLOCK, dh)
    kmean = jnp.mean(kb.astype(jnp.float32), axis=3)
    rel_bias = rel_bias.astype(jnp.float32)

    gate = jnp.einsum('bhsd,bhnd->bhsn', qh.astype(jnp.float32), kmean)
    pos = jnp.arange(S)
    n_past = pos // MOBA_BLOCK
    past = jnp.arange(nb)[None, :] < n_past[:, None]
    gate = jnp.where(past, gate, -jnp.inf)
    kk = min(MOBA_TOPK, nb)
    _, sel = lax.top_k(gate, kk)
    sel_valid = jnp.arange(kk)[None, :] < jnp.minimum(n_past, MOBA_TOPK)[:, None]

    nc = S // MOBA_QCHUNK
    q_c = qh.reshape(B, H, nc, MOBA_QCHUNK, dh).transpose(2, 0, 1, 3, 4)
    sel_c = sel.reshape(B, H, nc, MOBA_QCHUNK, kk).transpose(2, 0, 1, 3, 4)
    valid_c = sel_valid.reshape(nc, MOBA_QCHUNK, kk)
    scale = HEAD_DIM ** -0.5
    off = jnp.arange(MOBA_BLOCK)
    head_idx = jnp.arange(H)[:, None, None, None]
    gather_blocks = jax.vmap(jax.vmap(lambda blocks, idx: blocks[idx]))

    def chunk_fn(args):
        c, qc, sc, vc = args
        q0 = c * MOBA_QCHUNK
        qpos = q0 + jnp.arange(MOBA_QCHUNK)
        own = q0 // MOBA_BLOCK
        k_own = lax.dynamic_slice_in_dim(kb, own, 1, axis=2)[:, :, 0]
        v_own = lax.dynamic_slice_in_dim(vb, own, 1, axis=2)[:, :, 0]
        d_own = qpos[:, None] - (own * MOBA_BLOCK + off)[None, :]
        l_own = jnp.einsum('bhqd,bhkd->bhqk', qc, k_own).astype(jnp.float32) * scale
        l_own = l_own + rel_bias[:, rel_bucket(d_own)]
        l_own = jnp.where(d_own >= 0, l_own, -jnp.inf)
        k_sel = gather_blocks(kb, sc)
        v_sel = gather_blocks(vb, sc)
        d_sel = qpos[:, None, None] - (sc[..., None] * MOBA_BLOCK + off)
        l_sel = jnp.einsum('bhqd,bhqnkd->bhqnk', qc, k_sel).astype(jnp.float32) * scale
        l_sel = l_sel + rel_bias[head_idx, rel_bucket(d_sel)]
        l_sel = jnp.where(vc[:, :, None], l_sel, -jnp.inf)
        logits = jnp.concatenate(
            [l_own, l_sel.reshape(B, H, MOBA_QCHUNK, kk * MOBA_BLOCK)], axis=-1)
        p = jax.nn.softmax(logits, axis=-1)
        p_own = p[..., :MOBA_BLOCK].astype(v.dtype)
        p_sel = p[..., MOBA_BLOCK:].reshape(B, H, MOBA_QCHUNK, kk, MOBA_BLOCK).astype(v.dtype)
        return (jnp.einsum('bhqk,bhkd->bhqd', p_own, v_own)
                + jnp.einsum('bhqnk,bhqnkd->bhqd', p_sel, v_sel))

    o = lax.map(chunk_fn, (jnp.arange(nc), q_c, sel_c, valid_c))
    return o.transpose(1, 0, 3, 2, 4).reshape(B, S, H * dh)


def causal_depthwise_conv_silu(x, w):
    C = x.shape[-1]
    y = lax.conv_general_dilated(
        x, w[:, None, :].astype(x.dtype), window_strides=(1,),
        padding=[(CONV_WIDTH - 1, 0)], dimension_numbers=('NWC', 'WIO', 'NWC'),
        feature_group_count=C)
    return jax.nn.silu(y)


def gated_deltanet(qkv, z, a, b, conv_w, A_log, dt_bias, norm_w):
    B, S, _ = qkv.shape
    H, dh, C = GDN_HEADS, HEAD_DIM, GDN_CHUNK
    nc = S // C
    qkv = causal_depthwise_conv_silu(qkv, conv_w)
    q, k, v = jnp.split(qkv, 3, axis=-1)
    q = l2_normalize(q.reshape(B, S, H, dh)) * (dh ** -0.5)
    k = l2_normalize(k.reshape(B, S, H, dh))
    v = v.reshape(B, S, H, dh).astype(jnp.float32)
    beta = jax.nn.sigmoid(b.astype(jnp.float32))
    g = -jnp.exp(A_log.astype(jnp.float32)) * jax.nn.softplus(
        a.astype(jnp.float32) + dt_bias.astype(jnp.float32))

    def to_chunks(t):
        t = jnp.moveaxis(t, 2, 1)
        return t.reshape((B, H, nc, C) + t.shape[3:])

    qc, kc, vc = to_chunks(q), to_chunks(k), to_chunks(v)
    bc = to_chunks(beta)
    g_cum = jnp.cumsum(to_chunks(g), axis=-1)
    causal = jnp.tril(jnp.ones((C, C), dtype=bool))
    strict = jnp.tril(jnp.ones((C, C), dtype=bool), k=-1)
    decay = jnp.exp(jnp.where(causal, g_cum[..., :, None] - g_cum[..., None, :], -jnp.inf))
    k_beta = kc * bc[..., None]
    m = jnp.where(strict, jnp.einsum('bhnid,bhnjd->bhnij', k_beta, kc) * decay, 0.0)
    a_mat = m + jnp.eye(C, dtype=jnp.float32)
    rhs = jnp.concatenate([vc * bc[..., None], k_beta * jnp.exp(g_cum)[..., None]], axis=-1)
    uw = lax.linalg.triangular_solve(a_mat, rhs, left_side=True, lower=True, unit_diagonal=True)
    u, w = uw[..., :dh], uw[..., dh:]
    attn_intra = jnp.einsum('bhnid,bhnjd->bhnij', qc, kc) * decay
    q_dec = qc * jnp.exp(g_cum)[..., None]
    k_dec = kc * jnp.exp(g_cum[..., -1:] - g_cum)[..., None]
    g_last = jnp.exp(g_cum[..., -1])

    def step(state, xs):
        u_c, w_c, q_c, k_c, a_c, gl = xs
        v_new = u_c - jnp.einsum('bhck,bhkv->bhcv', w_c, state)
        o = jnp.einsum('bhck,bhkv->bhcv', q_c, state) + jnp.einsum('bhcs,bhsv->bhcv', a_c, v_new)
        state = state * gl[..., None, None] + jnp.einsum('bhck,bhcv->bhkv', k_c, v_new)
        return state, o

    xs = tuple(jnp.moveaxis(t, 2, 0) for t in (u, w, q_dec, k_dec, attn_intra, g_last))
    state0 = jnp.zeros((B, H, dh, dh), dtype=jnp.float32)
    _, o = lax.scan(step, state0, xs)
    o = jnp.moveaxis(o, 0, 2).reshape(B, H, S, dh).transpose(0, 2, 1, 3)
    o = o * lax.rsqrt(jnp.mean(o * o, axis=-1, keepdims=True) + EPS) * norm_w.astype(jnp.float32)
    o = o * jax.nn.silu(z.reshape(B, S, H, dh).astype(jnp.float32))
    return o.reshape(B, S, GDN_WIDTH).astype(qkv.dtype)


def setup_inputs(seed: int = 0) -> dict:
    key = jax.random.key(seed)
    ks = jax.random.split(key, 16)
    f32 = jnp.float32
    x = jax.random.normal(ks[0], (BATCH, SEQ, D_MODEL), f32)
    w_in = jax.random.normal(ks[1], (DEPTH, D_MODEL, IN_COLS), f32) * D_MODEL ** -0.5
    w_out = jax.random.normal(ks[2], (DEPTH, MIX_WIDTH, D_MODEL), f32) * MIX_WIDTH ** -0.5
    conv_w = jax.random.normal(ks[3], (DEPTH, CONV_WIDTH, 3 * GDN_WIDTH), f32) * CONV_WIDTH ** -0.5
    A_log = jnp.log(jax.random.uniform(ks[4], (DEPTH, GDN_HEADS), f32, 1.0, 16.0))
    dt = jnp.exp(jax.random.uniform(ks[5], (DEPTH, GDN_HEADS), f32, math.log(1e-3), math.log(1e-1)))
    dt_bias = dt + jnp.log(-jnp.expm1(-dt))
    gdn_norm_w = 1.0 + 0.05 * jax.random.normal(ks[6], (DEPTH, HEAD_DIM), f32)
    rel_bias = 0.5 * jax.random.normal(ks[7], (ATT_HEADS, REL_BUCKETS), f32)
    pre_mix_norm = 1.0 + 0.05 * jax.random.normal(ks[8], (DEPTH, D_MODEL), f32)
    post_mix_norm = 1.0 + 0.05 * jax.random.normal(ks[9], (DEPTH, D_MODEL), f32)
    pre_mlp_norm = 1.0 + 0.05 * jax.random.normal(ks[10], (DEPTH, D_MODEL), f32)
    post_mlp_norm = 1.0 + 0.05 * jax.random.normal(ks[11], (DEPTH, D_MODEL), f32)
    w_up = jax.random.normal(ks[12], (DEPTH, D_MODEL, D_FF), f32) * D_MODEL ** -0.5
    w_down = jax.random.normal(ks[13], (DEPTH, D_FF, D_MODEL), f32) * D_FF ** -0.5
    return {"x": x, "w_in": w_in, "w_out": w_out, "conv_w": conv_w, "A_log": A_log,
            "dt_bias": dt_bias, "gdn_norm_w": gdn_norm_w, "rel_bias": rel_bias,
            "pre_mix_norm": pre_mix_norm, "post_mix_norm": post_mix_norm,
            "pre_mlp_norm": pre_mlp_norm, "post_mlp_norm": post_mlp_norm,
            "w_up": w_up, "w_down": w_down}


def reference(x, w_in, w_out, conv_w, A_log, dt_bias, gdn_norm_w, rel_bias,
              pre_mix_norm, post_mix_norm, pre_mlp_norm, post_mlp_norm, w_up, w_down):
    B, S, _ = x.shape
    split_points = np.cumsum(IN_SIZES)[:-1].tolist()
    for l in range(DEPTH):
        h = rms_norm(x, pre_mix_norm[l])
        proj = jnp.einsum('bsd,dc->bsc', h, w_in[l])
        att_q, att_k, att_v, gdn_qkv, gdn_z, gdn_a, gdn_b = jnp.split(proj, split_points, axis=-1)
        o_att = moba_attention(att_q.reshape(B, S, ATT_HEADS, HEAD_DIM),
                               att_k.reshape(B, S, ATT_HEADS, HEAD_DIM),
                               att_v.reshape(B, S, ATT_HEADS, HEAD_DIM), rel_bias)
        o_gdn = gated_deltanet(gdn_qkv, gdn_z, gdn_a, gdn_b, conv_w[l], A_log[l],
                               dt_bias[l], gdn_norm_w[l])
        mix = jnp.einsum('bsc,cd->bsd', jnp.concatenate([o_att, o_gdn], axis=-1), w_out[l])
        x = x + rms_norm(mix, post_mix_norm[l])
        h = rms_norm(x, pre_mlp_norm[l])
        m = jnp.einsum('bsf,fd->bsd', jnp.square(jax.nn.relu(jnp.einsum('bsd,df->bsf', h, w_up[l]))), w_down[l])
        x = x + rms_norm(m, post_mlp_norm[l])
    return x
```

```python
import math
from contextlib import ExitStack

import numpy as np
import concourse.bass as bass
import concourse.mybir as mybir
from concourse.bass_utils import run_bass_kernel_spmd

F32 = mybir.dt.float32
BF16 = mybir.dt.bfloat16
AF = mybir.ActivationFunctionType
ALU = mybir.AluOpType
AX = mybir.AxisListType

NDMA = 8
SEQ = 2048
DM = 1024
NT = SEQ // 128
DFF = 4096
INC = 3600
EPS = 1e-6
NEG = -30000.0


class Sched:
    ENGS = ("pe", "act", "dve", "pool", "sp")

    def __init__(self, nc):
        self.nc = nc
        self.q = {e: [] for e in self.ENGS}
        self.cnt = {e: 0 for e in self.ENGS}
        self.pending = {e: False for e in self.ENGS}
        self.seen = {e: {} for e in self.ENGS}
        self.lastw = {}
        self.readers = {}
        self.dma_i = 0
        self.dma_uses = [0] * NDMA
        self.bar = {}
        self.n_ins = {e: 0 for e in self.ENGS}

    def _deps(self, eng, reads, writes):
        deps = dict(self.bar)

        def add(tok):
            k, v = tok
            if deps.get(k, 0) < v:
                deps[k] = v

        for r in reads:
            if r in self.lastw:
                add(self.lastw[r])
            if isinstance(r, tuple) and r[0] == "ps":
                for k, v in self.readers.get(r, {}).items():
                    if k != eng:
                        add((k, v))
        for w in writes:
            if w in self.lastw:
                add(self.lastw[w])
            for k, v in self.readers.get(w, {}).items():
                add((k, v))
        waits = []
        for k, v in deps.items():
            if k == eng and eng == "pe":
                continue
            if self.seen[eng].get(k, 0) >= v:
                continue
            self.seen[eng][k] = v
            waits.append((k, v))
        return waits

    def _record(self, tok, reads, writes):
        k, v = tok
        for r in reads:
            d = self.readers.setdefault(r, {})
            if d.get(k, 0) < v:
                d[k] = v
        for w in writes:
            self.lastw[w] = tok
            self.readers[w] = {}

    def emit(self, eng, fn, reads=(), writes=(), signal=True):
        waits = self._deps(eng, reads, writes)
        if signal:
            self.cnt[eng] += 1
            self.pending[eng] = False
            tok = (eng, self.cnt[eng])
        else:
            self.pending[eng] = True
            tok = (eng, self.cnt[eng] + 1)
        self._record(tok, reads, writes)
        self.n_ins[eng] += 1

        def run(E, sems, waits=waits, fn=fn, signal=signal, eng=eng):
            for k, v in waits:
                E.wait_ge(sems[k], v)
            ins = fn(E)
            if signal:
                ins.then_inc(sems[eng], 1)

        self.q[eng].append(run)
        return tok

    def dma(self, out, in_, reads=(), writes=(), q="sp", **kw):
        slot = self.dma_i % NDMA
        self.dma_i += 1
        key = ("dma", slot)
        waits = self._deps(q, reads, writes)
        prev = 16 * self.dma_uses[slot]
        if prev > 0 and self.seen[q].get(key, 0) < prev:
            self.seen[q][key] = prev
            waits.append((key, prev))
        self.dma_uses[slot] += 1
        tok = (key, 16 * self.dma_uses[slot])
        self._record(tok, reads, writes)
        self.n_ins[q] += 1

        def run(E, sems, waits=waits, out=out, in_=in_, key=key, kw=kw):
            for k, v in waits:
                E.wait_ge(sems[k], v)
            E.dma_start(out=out, in_=in_, **kw).then_inc(sems[key], 16)

        self.q[q].append(run)
        return tok

    def barrier(self):
        for e in self.ENGS:
            assert not self.pending[e]
            if self.cnt[e] > 0:
                self.bar[e] = self.cnt[e]
        for s in range(NDMA):
            if self.dma_uses[s] > 0:
                self.bar[("dma", s)] = 16 * self.dma_uses[s]

    def finish(self):
        waits = []
        for slot in range(NDMA):
            v = 16 * self.dma_uses[slot]
            key = ("dma", slot)
            if v > 0 and self.seen["sp"].get(key, 0) < v:
                self.seen["sp"][key] = v
                waits.append((key, v))

        def run(E, sems, waits=waits):
            for k, v in waits:
                E.wait_ge(sems[k], v)

        self.q["sp"].append(run)
        for e in self.ENGS:
            assert not self.pending[e], f"engine {e} has unsignaled trailing instruction"

    def build(self, stack):
        nc = self.nc
        sems = {}
        for e in self.ENGS:
            sems[e] = stack.enter_context(nc.semaphore("s_" + e))
        for s in range(NDMA):
            sems[("dma", s)] = stack.enter_context(nc.semaphore("s_dma%d" % s))
        block = stack.enter_context(nc.Block())
        q = self.q

        @block.tensor
        def _(E):
            for f in q["pe"]:
                f(E, sems)

        @block.scalar
        def _(E):
            for f in q["act"]:
                f(E, sems)

        @block.vector
        def _(E):
            for f in q["dve"]:
                f(E, sems)

        @block.gpsimd
        def _(E):
            for f in q["pool"]:
                f(E, sems)

        @block.sync
        def _(E):
            for f in q["sp"]:
                f(E, sems)


def rel_bucket_np(d):
    d = np.maximum(d, 0)
    large = 16 + (np.log(np.maximum(d, 1).astype(np.float32) / 16) / math.log(128 / 16) * 16).astype(np.int32)
    large = np.minimum(large, 31)
    return np.where(d < 16, d, large)


class Prog:
    def __init__(self, nseq, stages=("A", "B1", "C1", "C2", "D"), dbg=()):
        self.nseq = nseq
        self.stages = stages
        self.dbg = dbg
        nc = bass.Bass("TRN2", target_bir_lowering=False, dynamic_dma_scratch_size=256)
        self.nc = nc
        self.S = Sched(nc)
        d = {}

        def din(name, shape):
            d[name] = nc.dram_tensor(name, list(shape), F32, kind="ExternalInput").ap()

        din("x", [nseq, SEQ, DM])
        din("w_in", [DM, INC])
        din("w_out", [DM, DM])
        din("w_up", [DM, DFF])
        din("w_down", [DFF, DM])
        din("premix_pk", [128, 8])
        din("premlp_pk", [128, 8])
        din("postmix", [1, DM])
        din("postmlp", [1, DM])
        din("ttab", [128, 8 * 2 * 128])
        din("rb31", [1, 8])
        din("convw_pk", [128, 12 * 4])
        din("alog", [1, 8])
        din("dtb", [1, 8])
        din("gnw", [1, 64])
        self.out = nc.dram_tensor("out", [nseq, SEQ, DM], F32, kind="ExternalOutput").ap()
        self.dbg_out = {}
        for name, shape in dbg:
            self.dbg_out[name] = nc.dram_tensor("dbg_" + name, list(shape), F32, kind="ExternalOutput").ap()
        self.d = d
        self._rr = 0
        with ExitStack() as st:
            self.build(st)
            self.S.finish()
            self.S.build(st)

    def sb(self, st, name, shape, dt):
        self._uid = getattr(self, "_uid", 0) + 1
        return st.enter_context(self.nc.sbuf_tensor("%s_u%d" % (name, self._uid), list(shape), dt))

    def evac(self, out, in_, reads, writes, eng=None):
        if eng is None:
            eng = ("act", "dve")[self._rr % 2]
            self._rr += 1
        if eng == "act":
            self.S.emit("act", lambda E: E.activation(out=out, in_=in_, func=AF.Identity), reads=reads, writes=writes)
        else:
            self.S.emit("dve", lambda E: E.tensor_copy(out=out, in_=in_), reads=reads, writes=writes)

    def build(self, st):
        nc, S, d = self.nc, self.S, self.d
        sb = self.sb
        self.PS = [st.enter_context(nc.psum_tensor("ps%d" % i, [128, 512], F32)) for i in range(8)]
        P = {}
        self.P = P
        P["ident"] = sb(st, "ident", [128, 128], BF16)
        P["postmix_b"] = sb(st, "postmix_b", [128, DM], F32)
        P["postmlp_b"] = sb(st, "postmlp_b", [128, DM], F32)
        P["premix"] = sb(st, "premix", [128, 8], F32)
        P["premlp"] = sb(st, "premlp", [128, 8], F32)
        P["epsc"] = sb(st, "epsc", [128, 1], F32)
        P["oT"] = sb(st, "oT", [128, 8, SEQ], BF16)
        ident = P["ident"]
        S.emit("pool", lambda E: E.memset(ident[:], 0.0), writes=["ident"])
        S.emit("pool", lambda E: E.affine_select(out=ident[:], in_=ident[:], pattern=[[-1, 128]],
                                                  compare_op=ALU.not_equal, fill=1.0, base=0, channel_multiplier=1),
               reads=["ident"], writes=["ident"])
        S.emit("pool", lambda E: E.memset(P["epsc"][:], EPS), writes=["epsc"])
        S.dma(P["postmix_b"][:], d["postmix"].partition_broadcast(128), writes=["postmix_b"])
        S.dma(P["postmlp_b"][:], d["postmlp"].partition_broadcast(128), writes=["postmlp_b"])
        S.dma(P["premix"][:], d["premix_pk"], writes=["premix"])
        S.dma(P["premlp"][:], d["premlp_pk"], writes=["premlp"])
        if "C2" not in self.stages:
            oT = P["oT"]
            S.emit("pool", lambda E: E.memset(oT[:, 4:8, :], 0.0), writes=[("oT", c) for c in range(4, 8)])

        for b in range(self.nseq):
          S.barrier()
          with ExitStack() as s_seq:
            hT_keep = sb(s_seq, "hT", [128, 8, SEQ], BF16)
            with ExitStack() as s_att:
                A = {}
                A["qT"] = sb(s_att, "qT", [128, 4, SEQ], BF16)
                A["kT"] = sb(s_att, "kT", [128, 4, SEQ], BF16)
                A["vaug"] = sb(s_att, "vaug", [128, NT, 4, 3, 64], BF16)
                A["maskT"] = sb(s_att, "maskT", [128, SEQ], BF16)
                A["kmT"] = sb(s_att, "kmT", [128, 4, 8], BF16)
                with ExitStack() as s_pa:
                    wA = self.prep_B1(s_pa) if "B1" in self.stages else None
                    hT = self.phase_A(s_pa, b, hT=hT_keep)
                    if "B1" in self.stages:
                        self.phase_B1(s_pa, b, hT, A, wA)
                S.barrier()
                if "C1" in self.stages:
                    with ExitStack() as s_c1:
                        self.phase_C1(s_c1, b, A)
                S.barrier()
            S.barrier()
            if "C2" in self.stages or "B2" in self.stages:
                with ExitStack() as s_g:
                    G = {}
                    G["qT"] = sb(s_g, "gqT", [128, 4, SEQ], BF16)
                    G["kT"] = sb(s_g, "gkT", [128, 4, SEQ], BF16)
                    G["vT"] = sb(s_g, "gvT", [128, 4, SEQ], BF16)
                    G["zs"] = sb(s_g, "gzs", [128, NT, 512], BF16)
                    G["gab"] = sb(s_g, "gab", [128, NT, 16], F32)
                    G["g"] = sb(s_g, "gg", [128, NT, 8], F32)
                    G["beta"] = sb(s_g, "gbeta", [128, NT, 8], F32)
                    G["nbeta"] = sb(s_g, "gnbeta", [128, NT, 8], F32)
                    with ExitStack() as s_pb:
                        self.phase_B2(s_pb, b, hT_keep, G)
                    S.barrier()
                    with ExitStack() as s_c2:
                        if "C2" in self.stages:
                            self.phase_C2(s_c2, b, G)
                    S.barrier()
                    if "oTg" in self.dbg_out:
                        with ExitStack() as s_dbg:
                            oT = P["oT"]
                            otf = sb(s_dbg, "otfg", [128, 4, SEQ], F32)
                            S.emit("dve", lambda E: E.tensor_copy(out=otf[:], in_=oT[:, 4:8, :]),
                                   reads=[("oT", c) for c in range(4, 8)], writes=["otfg"])
                            S.dma(self.dbg_out["oTg"][b].rearrange("c p s -> p c s"), otf[:], reads=["otfg"],
                                  writes=["dbg_oTg"])
                        S.barrier()
          S.barrier()
          if "D" in self.stages:
              with ExitStack() as s_d:
                  self.phase_D(s_d, b)
          S.barrier()

    def phase_A(self, st, b, nbuf=2, hT=None):
        nc, S, d, P, PS = self.nc, self.S, self.d, self.P, self.PS
        if hT is None:
            hT = self.sb(st, "hT", [128, 8, SEQ], BF16)
        xt = [self.sb(st, "xt%d" % i, [128, DM], F32) for i in range(nbuf)] * (2 // nbuf)
        hb = [self.sb(st, "hb%d" % i, [128, DM], BF16) for i in range(nbuf)] * (2 // nbuf)
        ss = self.sb(st, "ssA", [128, NT], F32)
        rs = self.sb(st, "rsA", [128, NT], F32)
        rstd = self.sb(st, "rstdA", [128, NT], F32)
        ident = P["ident"]
        S.emit("dve", lambda E: E.memset(ss[:], 0.0), writes=["ssA"])
        for t in range(NT):
            i = t % nbuf
            S.dma(xt[i][:], d["x"][b, t * 128:(t + 1) * 128, :], writes=[("xt", i)])
            S.emit("act", lambda E, i=i, t=t: E.activation(out=hb[i][:], in_=xt[i][:], func=AF.Square,
                                                           accum_out=ss[:, t:t + 1]),
                   reads=[("xt", i), "ssA"], writes=[("hb", i), "ssA"])
            S.emit("act", lambda E, t=t: E.activation(out=rs[:, t:t + 1], in_=ss[:, t:t + 1], func=AF.Sqrt,
                                                      scale=1.0 / DM, bias=P["epsc"][:]),
                   reads=["ssA", "epsc"], writes=["rsA"])
            S.emit("dve", lambda E, t=t: E.reciprocal(out=rstd[:, t:t + 1], in_=rs[:, t:t + 1]),
                   reads=["rsA"], writes=["rstdA"])
            S.emit("act", lambda E, i=i, t=t: E.activation(out=hb[i][:], in_=xt[i][:], func=AF.Identity,
                                                           scale=rstd[:, t:t + 1]),
                   reads=[("xt", i), "rstdA"], writes=[("hb", i)])
            bank = t % 2
            psb = PS[bank][:].bitcast(BF16)
            for k in range(8):
                S.emit("pe", lambda E, k=k, i=i, psb=psb: E.transpose(out=psb[:, k * 128:(k + 1) * 128],
                                                                      in_=hb[i][:, k * 128:(k + 1) * 128],
                                                                      identity=ident[:]),
                       reads=[("hb", i), "ident"], writes=[("ps", bank)], signal=(k == 7))
            S.emit("dve", lambda E, t=t, psb=psb: E.tensor_copy(out=hT[:, :, t * 128:(t + 1) * 128],
                                                                in_=psb.rearrange("p (k c) -> p k c", k=8)),
                   reads=[("ps", bank)], writes=[("hT", t)])
        return hT

    def prep_B1(self, st):
        nc, S, d, P, PS = self.nc, self.S, self.d, self.P, self.PS
        wA = self.sb(st, "wA", [128, 8, 1536], BF16)
        stg = [self.sb(st, "stgA%d" % i, [128, 1536], F32) for i in range(3)]
        premix = P["premix"]
        for k in range(8):
            i = k % 3
            S.dma(stg[i][:], d["w_in"][k * 128:(k + 1) * 128, 0:1536], writes=[("stgA", i)])
            S.emit("pool", lambda E, k=k, i=i: E.tensor_scalar(out=wA[:, k, 0:512], in0=stg[i][:, 0:512],
                                                               scalar1=premix[:, k:k + 1], scalar2=0.125,
                                                               op0=ALU.mult, op1=ALU.mult),
                   reads=[("stgA", i), "premix"], writes=[("wA", k)])
            S.emit("pool", lambda E, k=k, i=i: E.tensor_scalar(out=wA[:, k, 512:1536], in0=stg[i][:, 512:1536],
                                                               scalar1=premix[:, k:k + 1], scalar2=None,
                                                               op0=ALU.mult),
                   reads=[("stgA", i), "premix"], writes=[("wA", k)])
        return wA

    def phase_B1(self, st, b, hT, A, wA):
        nc, S, d, P, PS = self.nc, self.S, self.d, self.P, self.PS
        qT, kT, vaug = A["qT"], A["kT"], A["vaug"]
        S.emit("pool", lambda E: E.memset(vaug[:, :, :, 1, :], 1.0), writes=["vones"])
        wA_all = [("wA", k) for k in range(8)]
        nb = 0
        for which, dst, name in ((0, qT, "qT"), (1, kT, "kT")):
            for pr in range(4):
                col0 = which * 512 + pr * 128
                for tc in range(4):
                    bank = 2 + (nb % 4)
                    nb += 1
                    for k in range(8):
                        S.emit("pe", lambda E, k=k, col0=col0, tc=tc, bank=bank: E.matmul(
                            PS[bank][:, :], lhsT=wA[:, k, col0:col0 + 128], rhs=hT[:, k, tc * 512:(tc + 1) * 512],
                            start=(k == 0), stop=(k == 7)),
                               reads=wA_all + [("hT", 4 * tc + j) for j in range(4)], writes=[("ps", bank)],
                               signal=(k == 7))
                    self.evac(dst[:, pr, tc * 512:(tc + 1) * 512], PS[bank][:, :], reads=[("ps", bank)],
                              writes=[(name, pr, tc)])
        for t in range(NT):
            bank = 2 + (nb % 4)
            nb += 1
            for k in range(8):
                S.emit("pe", lambda E, k=k, t=t, bank=bank: E.matmul(
                    PS[bank][:, :], lhsT=hT[:, k, t * 128:(t + 1) * 128], rhs=wA[:, k, 1024:1536],
                    start=(k == 0), stop=(k == 7)),
                       reads=wA_all + [("hT", t)], writes=[("ps", bank)], signal=(k == 7))
            self.evac(vaug[:, t, :, 0:3:2, :], PS[bank][:, :].rearrange("p (a b c) -> p a b c", a=4, b=2),
                      reads=[("ps", bank)], writes=[("vaug", t)])
        kmf = self.sb(st, "kmf", [128, 4, 8], F32)
        for pr in range(4):
            S.emit("dve", lambda E, pr=pr: E.tensor_reduce(out=kmf[:, pr, :],
                                                           in_=kT[:, pr, :].rearrange("p (n c) -> p n c", n=8),
                                                           axis=AX.X, op=ALU.add),
                   reads=[("kT", pr, tc) for tc in range(4)], writes=["kmf"])
        S.emit("dve", lambda E: E.tensor_scalar(out=A["kmT"][:], in0=kmf[:], scalar1=1.0 / 256, scalar2=None,
                                                op0=ALU.mult),
               reads=["kmf"], writes=["kmT"])

    def phase_C1(self, st, b, A):
        nc, S, d, P, PS = self.nc, self.S, self.d, self.P, self.PS
        sb = self.sb
        qT, kT, vaug, maskT, kmT = A["qT"], A["kT"], A["vaug"], A["maskT"], A["kmT"]
        ident = P["ident"]
        oT = P["oT"]
        IND = sb(st, "IND", [128, 64, 128], BF16)
        TT = sb(st, "TT", [128, 8, 2, 128], F32)
        rb31 = sb(st, "rb31", [128, 8], F32)
        PAST = sb(st, "PAST", [128, 8, 8, 8], F32)
        OWN = sb(st, "OWN", [128, 8, 8, 8], F32)
        PT = [[sb(st, "PT%d%d" % (h, i), [128, 512], BF16) for i in range(2)] for h in range(2)]
        rden = [sb(st, "rden%d" % h, [128, 512], F32) for h in range(2)]
        gsb = sb(st, "gsb", [128, 2, 8, 8], F32)
        g2 = sb(st, "g2", [128, 2, 8, 8], F32)
        eq = sb(st, "eq", [128, 2, 8, 8], F32)
        mx = sb(st, "mx", [128, 16], F32)
        mtok = sb(st, "mtok", [128, 128], BF16)

        S.emit("pool", lambda E: E.memset(IND[:], 0.0), writes=["IND"])
        S.emit("pool", lambda E: E.affine_select(out=IND[0:64], in_=IND[0:64], pattern=[[-1, 64], [0, 128]],
                                                  compare_op=ALU.not_equal, fill=1.0, base=0, channel_multiplier=1),
               reads=["IND"], writes=["IND"])
        S.emit("dve", lambda E: E.tensor_copy(out=IND[64:128], in_=IND[0:64]), reads=["IND"], writes=["IND"])
        S.emit("pool", lambda E: E.memset(PAST[:], 0.0), writes=["PAST"])
        S.emit("pool", lambda E: E.affine_select(out=PAST[:], in_=PAST[:], pattern=[[1, 8], [0, 8], [-1, 8]],
                                                  compare_op=ALU.is_gt, fill=-1e30, base=0, channel_multiplier=0),
               reads=["PAST"], writes=["PAST"])
        S.emit("pool", lambda E: E.memset(OWN[:], 0.0), writes=["OWN"])
        S.emit("pool", lambda E: E.affine_select(out=OWN[:], in_=OWN[:], pattern=[[1, 8], [0, 8], [-1, 8]],
                                                  compare_op=ALU.not_equal, fill=1.0, base=0, channel_multiplier=0),
               reads=["OWN"], writes=["OWN"])
        S.dma(TT[:].rearrange("p h a c -> p (h a c)"), d["ttab"], writes=["TT"])
        S.dma(rb31[:], d["rb31"].partition_broadcast(128), writes=["rb31"])
        S.emit("dve", lambda E: E.tensor_tensor(out=TT[:].rearrange("p h a c -> p h (a c)"),
                                                in0=TT[:].rearrange("p h a c -> p h (a c)"),
                                                in1=rb31[:].unsqueeze(2).to_broadcast([128, 8, 256]),
                                                op=ALU.subtract),
               reads=["TT", "rb31"], writes=["TT"])

        GB, TB = 6, 7
        import os
        FL = os.environ.get("C1FLAGS", "mask,main,toep,pv,norm,maskmm").split(",")
        if "mask" not in FL:
            S.emit("pool", lambda E: E.memset(maskT[:], 0.0), writes=[("maskT", qt) for qt in range(NT)])
        GBK = (6, 7)
        TB = 6

        def mask_pass(blk):
                for j in range(2):
                    qt = 2 * blk + j
                    for h in range(8):
                        pr, hh = h // 2, h % 2
                        S.emit("pe", lambda E, h=h, pr=pr, hh=hh, qt=qt, j=j: E.matmul(
                            PS[GBK[hh]][:, j * 32 + pr * 8:j * 32 + (pr + 1) * 8],
                            lhsT=qT[hh * 64:(hh + 1) * 64, pr, qt * 128:(qt + 1) * 128],
                            rhs=kmT[hh * 64:(hh + 1) * 64, pr, :], start=True, stop=True),
                               reads=[("qT", pr, qt // 4), "kmT"], writes=[("ps", GBK[hh])], signal=(j == 1 and h >= 6))
                for hh in range(2):
                    g3v = PS[GBK[hh]][:, 0:64].rearrange("p (j h n) -> p j h n", j=2, h=4)
                    S.emit("dve", lambda E, blk=blk, g3v=g3v, hh=hh: E.tensor_tensor(
                        out=gsb[:, :, hh:8:2, :], in0=g3v,
                        in1=PAST[:, blk, hh:8:2, :].unsqueeze(1).to_broadcast([128, 2, 4, 8]), op=ALU.add),
                           reads=[("ps", GBK[hh]), "PAST"], writes=["gsb"])
                cur = gsb
                mxb = mx[:].rearrange("p (j h) -> p j h", j=2).unsqueeze(3).to_broadcast([128, 2, 8, 8])
                for it in range(2):
                    S.emit("dve", lambda E, cur=cur: E.tensor_reduce(out=mx[:], in_=cur[:].rearrange("p j h n -> p (j h) n"),
                                                                     axis=AX.X, op=ALU.max),
                           reads=["gsb", "g2"], writes=["mx"])
                    S.emit("dve", lambda E, cur=cur: E.tensor_tensor(out=eq[:], in0=cur[:], in1=mxb, op=ALU.is_equal),
                           reads=["gsb", "g2", "mx"], writes=["eq"])
                    S.emit("dve", lambda E, cur=cur: E.scalar_tensor_tensor(out=g2[:], in0=eq[:], scalar=-1e30,
                                                                            in1=cur[:], op0=ALU.mult, op1=ALU.add),
                           reads=["eq", "gsb", "g2"], writes=["g2"])
                    cur = g2
                S.emit("dve", lambda E: E.tensor_reduce(out=mx[:], in_=g2[:].rearrange("p j h n -> p (j h) n"),
                                                        axis=AX.X, op=ALU.max),
                       reads=["g2"], writes=["mx"])
                S.emit("dve", lambda E: E.tensor_scalar(out=mx[:], in0=mx[:], scalar1=-1e29, scalar2=None, op0=ALU.max),
                       reads=["mx"], writes=["mx"])
                S.emit("dve", lambda E: E.tensor_tensor(out=eq[:], in0=gsb[:], in1=mxb, op=ALU.is_ge),
                       reads=["gsb", "mx"], writes=["eq"])
                S.emit("dve", lambda E, blk=blk: E.tensor_tensor(
                    out=eq[:], in0=eq[:], in1=OWN[:, blk, :, :].unsqueeze(1).to_broadcast([128, 2, 8, 8]), op=ALU.add),
                       reads=["eq", "OWN"], writes=["eq"])
                S.emit("dve", lambda E: E.tensor_scalar(out=mtok[:], in0=eq[:].rearrange("p j h n -> p (j h n)"),
                                                        scalar1=-1.0, scalar2=-NEG, op0=ALU.add, op1=ALU.mult),
                       reads=["eq"], writes=["mtok"])
                tb = PS[TB][:].bitcast(BF16)
                for j in range(2):
                    S.emit("pe", lambda E, tb=tb, j=j: E.transpose(out=tb[0:64, j * 128:(j + 1) * 128],
                                                                   in_=mtok[:, j * 64:(j + 1) * 64], identity=ident[:]),
                           reads=["mtok", "ident"], writes=[("ps", TB)], signal=(j == 1))
                S.emit("act", lambda E, tb=tb, blk=blk: E.activation(out=maskT[0:64, blk * 256:(blk + 1) * 256],
                                                                     in_=tb[0:64, 0:256], func=AF.Identity),
                       reads=[("ps", TB)], writes=[("maskT", 2 * blk), ("maskT", 2 * blk + 1)])
                S.emit("act", lambda E, tb=tb, blk=blk: E.activation(out=maskT[64:128, blk * 256:(blk + 1) * 256],
                                                                     in_=tb[0:64, 0:256], func=AF.Identity),
                       reads=[("ps", TB)], writes=[("maskT", 2 * blk), ("maskT", 2 * blk + 1)])

        vflat = vaug[:].rearrange("p t a b c -> p t a (b c)")

        def qk(pr, qc, kt):
            qs = max(0, kt * 128 - qc * 512)
            N = 512 - qs
            q0 = qc * 512 + qs
            for hh in range(2):
                h = 2 * pr + hh
                bank = hh * 2 + (kt % 2)
                rows = slice(hh * 64, (hh + 1) * 64)
                mm = "maskmm" in FL
                S.emit("pe", lambda E, bank=bank, rows=rows, N=N, q0=q0, mm=mm: E.matmul(
                    PS[bank][:, 0:N], lhsT=kT[rows, pr, kt * 128:(kt + 1) * 128], rhs=qT[rows, pr, q0:q0 + N],
                    start=True, stop=not mm),
                       reads=[("kT", pr, kt // 4), ("qT", pr, qc)], writes=[("ps", bank)], signal=not mm)
                if mm:
                    S.emit("pe", lambda E, bank=bank, h=h, N=N, q0=q0, rows=rows: E.matmul(
                        PS[bank][:, 0:N], lhsT=IND[rows, h * 8 + kt // 2, :], rhs=maskT[rows, q0:q0 + N],
                        start=False, stop=True),
                           reads=["IND"] + [("maskT", 4 * qc + j) for j in range(4)], writes=[("ps", bank)])
                for dq in range(2 if "toep" in FL else 0):
                    qt = kt + dq
                    if qt * 128 < q0 or qt >= (qc + 1) * 4:
                        continue
                    off = qt * 128 - q0
                    S.emit("dve", lambda E, bank=bank, off=off, h=h, dq=dq: E.tensor_tensor(
                        out=PS[bank][:, off:off + 128], in0=PS[bank][:, off:off + 128], in1=TT[:, h, dq, :],
                        op=ALU.add),
                           reads=[("ps", bank), "TT"], writes=[("ps", bank)])
                S.emit("act", lambda E, bank=bank, hh=hh, N=N: E.activation(
                    out=PT[hh][kt % 2][:, 0:N], in_=PS[bank][:, 0:N], func=AF.Exp),
                       reads=[("ps", bank)], writes=[("PT", hh, kt % 2)])

        def pv(pr, qc, kt, nkt):
            qs = max(0, kt * 128 - qc * 512)
            N = 512 - qs
            for hh in range(2):
                bank = 4 + hh
                S.emit("pe", lambda E, bank=bank, hh=hh, qs=qs, N=N: E.matmul(
                    PS[bank][:, qs:512], lhsT=vflat[:, kt, pr, hh * 64:hh * 64 + 128], rhs=PT[hh][kt % 2][:, 0:N],
                    start=(kt == 0), stop=(kt == nkt - 1)),
                       reads=[("PT", hh, kt % 2), ("vaug", kt), "vones"], writes=[("ps", bank)],
                       signal=(kt == nkt - 1))

        for qc in range(4 if "main" in FL else 0):
            if "mask" in FL:
                mask_pass(2 * qc)
                mask_pass(2 * qc + 1)
            for pr in range(4):
                nkt = 4 * (qc + 1)
                for kt in range(nkt):
                    qk(pr, qc, kt)
                    if kt > 0 and "pv" in FL:
                        pv(pr, qc, kt - 1, nkt)
                if "pv" in FL:
                    pv(pr, qc, nkt - 1, nkt)
                for hh in range(2 if "norm" in FL else 0):
                    bank = 4 + hh
                    orows = slice(hh * 64, (hh + 1) * 64)
                    drows = slice((1 - hh) * 64, (2 - hh) * 64)
                    S.emit("dve", lambda E, bank=bank, hh=hh, orows=orows, drows=drows: E.reciprocal(
                        out=rden[hh][orows, :], in_=PS[bank][drows, :]),
                           reads=[("ps", bank)], writes=[("rden", hh)])
                    S.emit("dve", lambda E, bank=bank, hh=hh, orows=orows, pr=pr, qc=qc: E.tensor_tensor(
                        out=oT[orows, pr, qc * 512:(qc + 1) * 512], in0=PS[bank][orows, :], in1=rden[hh][orows, :],
                        op=ALU.mult),
                           reads=[("ps", bank), ("rden", hh)], writes=[("oT", pr)])

        if "oT" in self.dbg_out:
            otf = sb(st, "otf", [128, 4, SEQ], F32)
            S.emit("dve", lambda E: E.tensor_copy(out=otf[:], in_=oT[:, 0:4, :]),
                   reads=[("oT", c) for c in range(4)], writes=["otf"])
            S.dma(self.dbg_out["oT"][b].rearrange("c p s -> p c s"), otf[:], reads=["otf"], writes=["dbg_oT"])

    def phase_B2(self, st, b, hT, G):
        nc, S, d, P, PS = self.nc, self.S, self.d, self.P, self.PS
        sb = self.sb
        premix = P["premix"]
        stgw = [sb(st, "stgw%d" % i, [128, 8, 128], F32) for i in range(2)]
        wc = [sb(st, "wc%d" % i, [128, 8, 128], BF16) for i in range(2)]
        wz = sb(st, "wz", [128, 8, 512], BF16)
        wab = sb(st, "wab", [128, 8, 16], BF16)
        pres = [sb(st, "pre%d" % i, [128, 3 + SEQ], F32) for i in range(2)]
        accs = [sb(st, "acc%d" % i, [128, SEQ], F32) for i in range(2)]
        sq = sb(st, "sqg", [128, SEQ], BF16)
        srs = [sb(st, "srg%d" % i, [128, 512], F32) for i in range(2)]
        cw = sb(st, "cw", [128, 12, 4], F32)
        BLK = sb(st, "BLK", [128, 128], BF16)
        dtb = sb(st, "dtb_b", [128, 8], F32)
        alog = sb(st, "alog_b", [128, 8], F32)
        S.dma(cw[:].rearrange("p c t -> p (c t)"), d["convw_pk"], writes=["cw"])
        S.dma(dtb[:], d["dtb"].partition_broadcast(128), writes=["dtb"])
        S.dma(alog[:], d["alog"].partition_broadcast(128), writes=["alog"])
        S.emit("pool", lambda E: E.memset(BLK[:], 0.0), writes=["BLK"])
        S.emit("pool", lambda E: E.memset(BLK[0:64, 0:64], 1.0), reads=["BLK"], writes=["BLK"])
        S.emit("pool", lambda E: E.memset(BLK[64:128, 64:128], 1.0), reads=["BLK"], writes=["BLK"])
        for i in range(2):
            S.emit("pool", lambda E, i=i: E.memset(pres[i][:, 0:3], 0.0), writes=[("pre0", i)])
        hT_all = [("hT", t) for t in range(NT)]
        self._nw = 0

        def load_w(col0, ncol, dst, dname):
            i = self._nw % 2
            self._nw += 1
            S.dma(stgw[i][:, :, 0:ncol], d["w_in"][:, col0:col0 + ncol].rearrange("(k p) c -> p k c", p=128),
                  writes=[("stgw", i)])
            S.emit("pool", lambda E, i=i, ncol=ncol, dst=dst: E.tensor_tensor(
                out=dst, in0=stgw[i][:, :, 0:ncol], in1=premix[:].unsqueeze(2).to_broadcast([128, 8, ncol]),
                op=ALU.mult),
                   reads=[("stgw", i), "premix"], writes=[dname])

        nb = 0
        dsts = (G["qT"], G["kT"], G["vT"])
        for c in range(12):
            wi = c % 2
            pre, acc = pres[wi], accs[wi]
            PRE, ACC, PRE0 = ("pre", wi), ("acc", wi), ("pre0", wi)
            load_w(1536 + c * 128, 128, wc[wi][:], ("wc", wi))
            for tc in range(4):
                bank = 4 + (nb % 4)
                nb += 1
                for k in range(8):
                    S.emit("pe", lambda E, k=k, tc=tc, bank=bank, wi=wi: E.matmul(
                        PS[bank][:, :], lhsT=wc[wi][:, k, :], rhs=hT[:, k, tc * 512:(tc + 1) * 512],
                        start=(k == 0), stop=(k == 7)),
                           reads=[("wc", wi)] + [("hT", 4 * tc + j) for j in range(4)], writes=[("ps", bank)],
                           signal=(k == 7))
                S.emit("act", lambda E, tc=tc, bank=bank, pre=pre: E.activation(
                    out=pre[:, 3 + tc * 512:3 + (tc + 1) * 512], in_=PS[bank][:, :], func=AF.Identity),
                       reads=[("ps", bank)], writes=[PRE])
            ce = "dve"
            S.emit(ce, lambda E, c=c, pre=pre, acc=acc: E.tensor_scalar(out=acc[:], in0=pre[:, 0:SEQ],
                                                                        scalar1=cw[:, c, 0:1], scalar2=None, op0=ALU.mult),
                   reads=[PRE, PRE0, "cw"], writes=[ACC])
            for tp in range(1, 4):
                S.emit(ce, lambda E, c=c, tp=tp, pre=pre, acc=acc: E.scalar_tensor_tensor(
                    out=acc[:], in0=pre[:, tp:tp + SEQ], scalar=cw[:, c, tp:tp + 1], in1=acc[:],
                    op0=ALU.mult, op1=ALU.add),
                       reads=[PRE, PRE0, "cw", ACC], writes=[ACC])
            dst = dsts[c // 4]
            dn = ("gq", "gk", "gv")[c // 4]
            S.emit("act", lambda E, dst=dst, c=c, acc=acc: E.activation(out=dst[:, c % 4, :], in_=acc[:], func=AF.Silu),
                   reads=[ACC], writes=[(dn, c % 4)])
        for c in range(8):
            dst = dsts[c // 4]
            dn = ("gq", "gk")[c // 4]
            pr = c % 4
            S.emit("act", lambda E, dst=dst, pr=pr: E.activation(out=sq[:], in_=dst[:, pr, :], func=AF.Square),
                   reads=[(dn, pr)], writes=["sqg"])
            for tc in range(4):
                bank = 4 + (nb % 4)
                nb += 1
                cs = slice(tc * 512, (tc + 1) * 512)
                S.emit("pe", lambda E, bank=bank, cs=cs: E.matmul(PS[bank][:, :], lhsT=BLK[:], rhs=sq[:, cs],
                                                                  start=True, stop=True),
                       reads=["sqg", "BLK"], writes=[("ps", bank)])
                sr = srs[tc % 2]
                SR = ("srg", tc % 2)
                S.emit("act", lambda E, bank=bank, sr=sr: E.activation(out=sr[:], in_=PS[bank][:, :], func=AF.Sqrt,
                                                                       bias=P["epsc"][:], scale=1.0),
                       reads=[("ps", bank), "epsc"], writes=[SR])
                S.emit("dve", lambda E, sr=sr: E.reciprocal(out=sr[:], in_=sr[:]), reads=[SR], writes=[SR])
                scl = 0.125 if c < 4 else 1.0
                S.emit("dve", lambda E, dst=dst, pr=pr, cs=cs, scl=scl, sr=sr: E.scalar_tensor_tensor(
                    out=dst[:, pr, cs], in0=dst[:, pr, cs], scalar=scl, in1=sr[:], op0=ALU.mult, op1=ALU.mult),
                       reads=[(dn, pr), SR], writes=[(dn, pr)])
        for j in range(4):
            load_w(3072 + j * 128, 128, wz[:, :, j * 128:(j + 1) * 128], "wz")
        load_w(3584, 16, wab[:], "wab")
        zs, gab = G["zs"], G["gab"]
        for t in range(NT):
            bank = 4 + (nb % 4)
            nb += 1
            tk = slice(t * 128, (t + 1) * 128)
            for k in range(8):
                S.emit("pe", lambda E, k=k, tk=tk, bank=bank: E.matmul(PS[bank][:, :], lhsT=hT[:, k, tk],
                                                                       rhs=wz[:, k, :], start=(k == 0), stop=(k == 7)),
                       reads=["wz", ("hT", t)], writes=[("ps", bank)], signal=(k == 7))
            S.emit("act", lambda E, t=t, bank=bank: E.activation(out=zs[:, t, :], in_=PS[bank][:, :], func=AF.Silu),
                   reads=[("ps", bank)], writes=[("gzs", t)])
            bank = 4 + (nb % 4)
            nb += 1
            for k in range(8):
                S.emit("pe", lambda E, k=k, tk=tk, bank=bank: E.matmul(PS[bank][:, 0:16], lhsT=hT[:, k, tk],
                                                                       rhs=wab[:, k, :], start=(k == 0), stop=(k == 7)),
                       reads=["wab", ("hT", t)], writes=[("ps", bank)], signal=(k == 7))
            S.emit("dve", lambda E, t=t, bank=bank: E.tensor_copy(out=gab[:, t, :], in_=PS[bank][:, 0:16]),
                   reads=[("ps", bank)], writes=["gab"])
        g, beta, nbeta = G["g"], G["beta"], G["nbeta"]
        S.emit("dve", lambda E: E.tensor_tensor(out=g[:], in0=gab[:, :, 0:8],
                                                in1=dtb[:].unsqueeze(1).to_broadcast([128, NT, 8]), op=ALU.add),
               reads=["gab", "dtb"], writes=["gg"])
        S.emit("act", lambda E: E.activation(out=g[:], in_=g[:], func=AF.Exp), reads=["gg"], writes=["gg"])
        S.emit("act", lambda E: E.activation(out=g[:], in_=g[:], func=AF.Ln, bias=1.0), reads=["gg"], writes=["gg"])
        S.emit("act", lambda E: E.activation(out=alog[:], in_=alog[:], func=AF.Exp), reads=["alog"], writes=["alog"])
        S.emit("dve", lambda E: E.scalar_tensor_tensor(out=g[:], in0=g[:], scalar=-1.0,
                                                       in1=alog[:].unsqueeze(1).to_broadcast([128, NT, 8]),
                                                       op0=ALU.mult, op1=ALU.mult),
               reads=["gg", "alog"], writes=["gg"])
        S.emit("act", lambda E: E.activation(out=beta[:], in_=gab[:, :, 8:16], func=AF.Sigmoid),
               reads=["gab"], writes=["gbeta"])
        S.emit("dve", lambda E: E.tensor_scalar(out=nbeta[:], in0=beta[:], scalar1=-1.0, scalar2=None, op0=ALU.mult),
               reads=["gbeta"], writes=["gnbeta"])

    def phase_C2(self, st, b, G):
        nc, S, d, P, PS = self.nc, self.S, self.d, self.P, self.PS
        sb = self.sb
        ident = P["ident"]
        oT = P["oT"]
        qTg, kTg, vTg, zs = G["qT"], G["kT"], G["vT"], G["zs"]
        g, beta, nbeta = G["g"], G["beta"], G["nbeta"]
        BIG = 3.0e38
        TRI = sb(st, "TRI", [128, 128], F32)
        BLKS = sb(st, "BLKS", [128, 128], F32)
        MASKU = sb(st, "MASKU", [128, 8, 128], F32)
        STRICT = sb(st, "STRICT", [128, 8, 128], F32)
        HEADM = sb(st, "HEADM", [8, 8, 1], F32)
        SEL = sb(st, "SEL", [8, 4, 128], F32)
        ONES8 = sb(st, "ONES8", [8, 128], F32)
        gnw = sb(st, "gnw_b", [128, 64], F32)
        S.dma(gnw[:], d["gnw"].partition_broadcast(128), writes=["gnw"])

        def tri_like(T, val, strict, name):
            nd = len(T.shape)
            pat = [[0, 8], [1, 128]] if nd == 3 else [[1, 128]]
            pat2 = [[0, 8], [-1, 128]] if nd == 3 else [[-1, 128]]
            S.emit("pool", lambda E: E.memset(T[:], val), writes=[name])
            S.emit("pool", lambda E: E.affine_select(out=T[:], in_=T[:], pattern=pat,
                                                      compare_op=(ALU.is_gt if strict else ALU.is_ge), fill=0.0,
                                                      base=0, channel_multiplier=-1),
                   reads=[name], writes=[name])
            S.emit("pool", lambda E: E.affine_select(out=T[0:64], in_=T[0:64], pattern=pat2,
                                                      compare_op=ALU.is_ge, fill=0.0, base=63, channel_multiplier=0),
                   reads=[name], writes=[name])

        tri_like(TRI, 1.0, False, "TRI")
        tri_like(MASKU, BIG, False, "MASKU")
        tri_like(STRICT, 1.0, True, "STRICT")
        S.emit("pool", lambda E: E.memset(BLKS[:], 0.0), writes=["BLKS"])
        S.emit("pool", lambda E: E.memset(BLKS[0:64, 0:64], 1.0), reads=["BLKS"], writes=["BLKS"])
        S.emit("pool", lambda E: E.memset(BLKS[64:128, 64:128], 1.0), reads=["BLKS"], writes=["BLKS"])
        S.emit("pool", lambda E: E.memset(HEADM[:], 0.0), writes=["HEADM"])
        S.emit("pool", lambda E: E.affine_select(out=HEADM[:], in_=HEADM[:], pattern=[[-1, 8], [0, 1]],
                                                  compare_op=ALU.not_equal, fill=1.0, base=0, channel_multiplier=1),
               reads=["HEADM"], writes=["HEADM"])
        S.emit("pool", lambda E: E.memset(SEL[:], 0.0), writes=["SEL"])
        for half in range(2):
            S.emit("pool", lambda E, half=half: E.affine_select(
                out=SEL[:, :, half * 64:(half + 1) * 64], in_=SEL[:, :, half * 64:(half + 1) * 64],
                pattern=[[-2, 4], [0, 64]], compare_op=ALU.not_equal, fill=1.0, base=-half, channel_multiplier=1),
                   reads=["SEL"], writes=["SEL"])
        S.emit("pool", lambda E: E.memset(ONES8[:], 1.0), writes=["ONES8"])

        import os
        NPS = int(os.environ.get("C2NPS", "1"))
        NSLOT = NPS + 1
        rhsBDs = [sb(st, "rhsBD%d" % i, [8, 8, 128], F32) for i in range(NPS)]
        gcTs = [sb(st, "gcT%d" % i, [8, 128], F32) for i in range(NPS)]
        gcts = [sb(st, "gct%d" % i, [128, 24], F32) for i in range(NPS)]
        egts = [sb(st, "egt%d" % i, [128, 16], F32) for i in range(NPS)]
        EAs = [sb(st, "EA%d" % i, [128, 8, 128], F32) for i in range(NPS)]
        EAsbs = [sb(st, "EAsb%d" % i, [128, 8, 128], F32) for i in range(NPS)]
        ktoks = [sb(st, "ktok%d" % i, [128, 8, 64], BF16) for i in range(NPS)]
        Bms = [[sb(st, "Bm%d%d" % (p, i), [128, 8, 128], BF16) for i in range(2)] for p in range(NPS)]
        Nms = [[sb(st, "Nm%d%d" % (p, i), [128, 8, 128], BF16) for i in range(2)] for p in range(NPS)]
        qzs = [sb(st, "qz%d" % i, [128, 2, 4, 128], BF16) for i in range(NPS)]
        kzs = [sb(st, "kz%d" % i, [128, 2, 4, 128], BF16) for i in range(NPS)]
        X0s = [sb(st, "X0_%d" % i, [128, 2, 8, 64], BF16) for i in range(NPS)]
        Bp0s = [sb(st, "Bp0_%d" % i, [128, 8, 128], BF16) for i in range(NPS)]
        EGs = [sb(st, "EG%d" % i, [128, 4, 128], F32) for i in range(NSLOT)]
        qdTs = [sb(st, "qdT%d" % i, [128, 4, 128], BF16) for i in range(NSLOT)]
        kdecs = [[sb(st, "kdec%d%d" % (p, i), [128, 8, 64], BF16) for i in range(2)] for p in range(NSLOT)]
        X1s = [sb(st, "X1_%d" % i, [128, 2, 8, 64], BF16) for i in range(NSLOT)]
        attnTs = [sb(st, "attnT%d" % i, [128, 8, 128], BF16) for i in range(NSLOT)]
        Bp1s = [sb(st, "Bp1_%d" % i, [128, 8, 128], BF16) for i in range(NSLOT)]
        nwTs = [sb(st, "nwT%d" % i, [128, 4, 128], BF16) for i in range(NSLOT)]
        identb = ident[:].unsqueeze(1)
        S32 = sb(st, "S32", [128, 4, 64], F32)
        tmpS = sb(st, "tmpS", [128, 4, 64], F32)
        Sbs = [sb(st, "Sb%d" % i, [128, 4, 2, 64], BF16) for i in range(2)]
        vnew = sb(st, "vnew", [128, 8, 64], BF16)
        osb = sb(st, "osb", [128, 8, 64], F32)
        osq = sb(st, "osq", [128, 8, 64], F32)
        oss = sb(st, "oss", [128, 16], F32)
        og = sb(st, "og", [128, 512], BF16)
        for i in range(NPS):
            S.emit("pool", lambda E, i=i: E.memset(qzs[i][:], 0.0), writes=[("qz", i)])
            S.emit("pool", lambda E, i=i: E.memset(kzs[i][:], 0.0), writes=[("kz", i)])
        S.emit("dve", lambda E: E.memset(S32[:], 0.0), writes=["S32"])
        for i in range(2):
            S.emit("dve", lambda E, i=i: E.memset(Sbs[i][:], 0.0), writes=[("Sb", i)])
        S.emit("dve", lambda E: E.memset(vnew[:], 0.0), writes=["vnew"])
        for p in range(NSLOT):
            for i in range(2):
                S.emit("pool", lambda E, p=p, i=i: E.memset(kdecs[p][i][:], 0.0), writes=[("kdec", p, i)])
        self._bp = 0
        self._bs = 0
        PBANKS = (4, 5, 1, 2)
        SBANKS = (6, 7)

        def pbank():
            self._bp += 1
            return PBANKS[self._bp % 4]

        def sbank():
            self._bs += 1
            return SBANKS[self._bs % 2]

        def gen_P(t):
            par = t % NSLOT
            ps = t % NPS
            tk = slice(t * 128, (t + 1) * 128)
            gt = g[:, t, :]
            EG, qdT, kdec, attnT, nwT = EGs[par], qdTs[par], kdecs[par], attnTs[par], nwTs[par]
            X = (X0s[ps], X1s[par])
            Bp = (Bp0s[ps], Bp1s[par])
            XN = (("X0", ps), ("X", par, 1))
            BPN = (("Bp0", ps), ("Bp", par, 1))
            rhsBD, gcT, gct, egt, EA, EAsb, ktok = rhsBDs[ps], gcTs[ps], gcts[ps], egts[ps], EAs[ps], EAsbs[ps], ktoks[ps]
            Bm, Nm, qz, kz = Bms[ps], Nms[ps], qzs[ps], kzs[ps]
            RB, GC, GT_, EGT, EAn, EASn, KT = ("rhsBD", ps), ("gcT", ps), ("gct", ps), ("egt", ps), ("EA", ps), ("EAsb", ps), ("ktok", ps)
            QZ, KZ = ("qz", ps), ("kz", ps)
            MYB = (0, 1, 2) if ps == 0 else (3, 4, 5)
            ROT = MYB if NPS == 2 else (3, 4, 5, 1, 2)
            B0, B1_, B2_ = MYB
            rot = [0]

            def pbank():
                rot[0] += 1
                return ROT[rot[0] % len(ROT)]
            if NPS == 1 and os.environ.get("C2REORD", "1") == "1":
                bk, bv, RBK, kq, kk = 2, 3, 1, (4, 5), (2, 3)
                S.emit("pe", lambda E: E.matmul(PS[B0][0:8, 0:128], lhsT=gt, rhs=TRI[:], start=True, stop=True),
                       reads=["gg", "TRI"], writes=[("ps", B0)], signal=False)
                S.emit("pe", lambda E: E.matmul(PS[B0][:, 128:136], lhsT=TRI[:], rhs=gt, start=True, stop=True),
                       reads=["gg", "TRI"], writes=[("ps", B0)], signal=False)
                S.emit("pe", lambda E: E.matmul(PS[B0][:, 136:144], lhsT=BLKS[:], rhs=gt, start=True, stop=True),
                       reads=["gg", "BLKS"], writes=[("ps", B0)])
                tbk = PS[bk][:].bitcast(BF16)
                for pr in range(4):
                    S.emit("pe", lambda E, pr=pr, tbk=tbk: E.transpose(out=tbk[:, pr * 128:(pr + 1) * 128],
                                                                       in_=kTg[:, pr, tk], identity=ident[:]),
                           reads=[("gk", pr), "ident"], writes=[("ps", bk)], signal=(pr == 3))
                tbv = PS[bv][:].bitcast(BF16)
                for pr in range(4):
                    S.emit("pe", lambda E, pr=pr, tbv=tbv: E.transpose(out=tbv[:, pr * 128:(pr + 1) * 128],
                                                                       in_=vTg[:, pr, tk], identity=ident[:]),
                           reads=[("gv", pr), "ident"], writes=[("ps", bv)], signal=(pr == 3))
                yield
                S.emit("act", lambda E: E.activation(out=gcT[:], in_=PS[B0][0:8, 0:128], func=AF.Identity),
                       reads=[("ps", B0)], writes=[GC])
                S.emit("dve", lambda E: E.tensor_copy(out=gct[:, 0:16], in_=PS[B0][:, 128:144]),
                       reads=[("ps", B0)], writes=[GT_])
                S.emit("dve", lambda E: E.tensor_tensor(out=gct[:, 16:24], in0=gct[:, 8:16], in1=gct[:, 0:8],
                                                        op=ALU.subtract), reads=[GT_], writes=[GT_])
                for hh in range(2):
                    rows = slice(hh * 64, (hh + 1) * 64)
                    S.emit("pool", lambda E, hh=hh, rows=rows: E.tensor_copy(out=kz[rows, hh, :, :], in_=kTg[rows, :, tk]),
                           reads=[("gk", pr) for pr in range(4)], writes=[KZ])
                S.emit("dve", lambda E: E.tensor_tensor(out=rhsBD[:], in0=HEADM[:].to_broadcast([8, 8, 128]),
                                                        in1=gcT[:].unsqueeze(1).to_broadcast([8, 8, 128]), op=ALU.mult),
                       reads=[GC, "HEADM"], writes=[RB])
                yield
                S.emit("act", lambda E, tbk=tbk: E.activation(out=ktok[:].rearrange("p h d -> p (h d)"),
                                                              in_=tbk[:, 0:512], func=AF.Identity),
                       reads=[("ps", bk)], writes=[KT])
                S.emit("act", lambda E: E.activation(out=egt[:, 0:8], in_=gct[:, 0:8], func=AF.Exp),
                       reads=[GT_], writes=[EGT])
                S.emit("act", lambda E: E.activation(out=egt[:, 8:16], in_=gct[:, 16:24], func=AF.Exp),
                       reads=[GT_, EGT], writes=[EGT])
                X0 = X[0]
                S.emit("dve", lambda E, tbv=tbv: E.tensor_copy(out=X0[:, 0, :, :],
                                                               in_=tbv[:, 0:512].rearrange("p (h d) -> p h d", h=8)),
                       reads=[("ps", bv)], writes=[XN[0] + (0,), XN[0] + (1,)])
                yield
                for hf in range(2):
                    S.emit("pe", lambda E, hf=hf: E.matmul(
                        PS[RBK][:, :], lhsT=ONES8[:], rhs=rhsBD[:, 4 * hf:4 * hf + 4, :].rearrange("p h i -> p (h i)"),
                        start=True, stop=True),
                           reads=[RB, "ONES8"], writes=[("ps", RBK)])
                    S.emit("dve", lambda E, hf=hf: E.tensor_tensor(
                        out=EA[:, 4 * hf:4 * hf + 4, :], in0=PS[RBK][:, :].rearrange("p (h i) -> p h i", h=4),
                        in1=gct[:, 4 * hf:4 * hf + 4].unsqueeze(2).to_broadcast([128, 4, 128]), op=ALU.subtract),
                           reads=[("ps", RBK), GT_], writes=[EAn])
                    for h in range(4 * hf, 4 * hf + 4):
                        pr, hh = h // 2, h % 2
                        S.emit("pe", lambda E, pr=pr, hh=hh: E.matmul(
                            PS[kq[hh]][:, pr * 128:(pr + 1) * 128], lhsT=kz[:, hh, pr, :], rhs=qTg[:, pr, tk],
                            start=True, stop=True),
                               reads=[("gq", pr), KZ], writes=[("ps", kq[hh])], signal=(h >= 6))
                    yield
                S.emit("act", lambda E: E.activation(out=EA[:], in_=EA[:], func=AF.Exp), reads=[EAn], writes=[EAn])
                for pr in range(4):
                    S.emit("pe", lambda E, pr=pr: E.matmul(PS[B0][:, pr * 128:(pr + 1) * 128], lhsT=SEL[:, pr, :],
                                                           rhs=gcT[:], start=True, stop=True),
                           reads=[GC, "SEL"], writes=[("ps", B0)], signal=(pr == 3))
                for h in range(8):
                    pr, hh = h // 2, h % 2
                    S.emit("pe", lambda E, pr=pr, hh=hh: E.matmul(
                        PS[kk[hh]][:, pr * 128:(pr + 1) * 128], lhsT=kz[:, hh, pr, :], rhs=kTg[:, pr, tk],
                        start=True, stop=True),
                           reads=[("gk", pr), KZ], writes=[("ps", kk[hh])], signal=(h >= 6))
                yield
                S.emit("dve", lambda E: E.tensor_tensor(out=EA[:], in0=EA[:], in1=MASKU[:], op=ALU.min),
                       reads=[EAn, "MASKU"], writes=[EAn])
                S.emit("act", lambda E: E.activation(out=EG[:].rearrange("p a i -> p (a i)"), in_=PS[B0][:, :], func=AF.Exp),
                       reads=[("ps", B0)], writes=[("EG", par)])
                S.emit("pool", lambda E: E.tensor_tensor(out=EAsb[:], in0=EA[:], in1=STRICT[:], op=ALU.mult),
                       reads=[EAn, "STRICT"], writes=[EASn])
                S.emit("pool", lambda E: E.tensor_tensor(out=EAsb[:], in0=EAsb[:],
                                                         in1=nbeta[:, t, :].unsqueeze(2).to_broadcast([128, 8, 128]),
                                                         op=ALU.mult),
                       reads=[EASn, "gnbeta"], writes=[EASn])
                yield
                for hh in range(2):
                    S.emit("dve", lambda E, hh=hh: E.tensor_tensor(
                        out=attnT[:, hh:8:2, :], in0=PS[kq[hh]][:, :].rearrange("p (a i) -> p a i", a=4),
                        in1=EA[:, hh:8:2, :], op=ALU.mult),
                           reads=[("ps", kq[hh]), EAn], writes=[("attnT", par)])
                S.emit("dve", lambda E: E.tensor_tensor(out=X0[:, 1, :, :], in0=ktok[:],
                                                        in1=egt[:, 0:8].unsqueeze(2).to_broadcast([128, 8, 64]),
                                                        op=ALU.mult),
                       reads=[KT, EGT, XN[0] + (0,), XN[0] + (1,)], writes=[XN[0] + (0,), XN[0] + (1,)])
                for hh in range(2):
                    S.emit("dve", lambda E, hh=hh: E.tensor_tensor(
                        out=Bm[0][:, hh:8:2, :], in0=PS[kk[hh]][:, :].rearrange("p (a i) -> p a i", a=4),
                        in1=EAsb[:, hh:8:2, :], op=ALU.mult),
                           reads=[("ps", kk[hh]), EASn], writes=[("Bm", ps, 0, 0), ("Bm", ps, 0, 1)])
                S.emit("pool", lambda E: E.tensor_tensor(out=qdT[:], in0=qTg[:, :, tk], in1=EG[:], op=ALU.mult),
                       reads=[("EG", par)] + [("gq", pr) for pr in range(4)], writes=[("qdT", par)])
                for hf in range(2):
                    rows = slice(hf * 64, (hf + 1) * 64)
                    S.emit("pool", lambda E, hf=hf, rows=rows: E.tensor_tensor(
                        out=kdec[hf][rows], in0=ktok[rows],
                        in1=egt[rows, 8:16].unsqueeze(2).to_broadcast([64, 8, 64]), op=ALU.mult),
                           reads=[KT, EGT], writes=[("kdec", par, hf)])
                S.emit("pool", lambda E: E.tensor_tensor(out=Bp[0][:], in0=Bm[0][:],
                                                         in1=identb.to_broadcast([128, 8, 128]), op=ALU.add),
                       reads=[("Bm", ps, 0, 0), ("Bm", ps, 0, 1), "ident"], writes=[BPN[0] + (0,), BPN[0] + (1,)])
                yield
            else:
                S.emit("pe", lambda E: E.matmul(PS[B0][0:8, 0:128], lhsT=gt, rhs=TRI[:], start=True, stop=True),
                       reads=["gg", "TRI"], writes=[("ps", B0)], signal=False)
                S.emit("pe", lambda E: E.matmul(PS[B0][:, 128:136], lhsT=TRI[:], rhs=gt, start=True, stop=True),
                       reads=["gg", "TRI"], writes=[("ps", B0)], signal=False)
                S.emit("pe", lambda E: E.matmul(PS[B0][:, 136:144], lhsT=BLKS[:], rhs=gt, start=True, stop=True),
                       reads=["gg", "BLKS"], writes=[("ps", B0)])
                yield
                S.emit("act", lambda E: E.activation(out=gcT[:], in_=PS[B0][0:8, 0:128], func=AF.Identity),
                       reads=[("ps", B0)], writes=[GC])
                S.emit("dve", lambda E: E.tensor_copy(out=gct[:, 0:16], in_=PS[B0][:, 128:144]),
                       reads=[("ps", B0)], writes=[GT_])
                S.emit("dve", lambda E: E.tensor_tensor(out=gct[:, 16:24], in0=gct[:, 8:16], in1=gct[:, 0:8],
                                                        op=ALU.subtract), reads=[GT_], writes=[GT_])
                S.emit("act", lambda E: E.activation(out=egt[:, 0:8], in_=gct[:, 0:8], func=AF.Exp),
                       reads=[GT_], writes=[EGT])
                S.emit("act", lambda E: E.activation(out=egt[:, 8:16], in_=gct[:, 16:24], func=AF.Exp),
                       reads=[GT_, EGT], writes=[EGT])
                yield
                S.emit("dve", lambda E: E.tensor_tensor(out=rhsBD[:], in0=HEADM[:].to_broadcast([8, 8, 128]),
                                                        in1=gcT[:].unsqueeze(1).to_broadcast([8, 8, 128]), op=ALU.mult),
                       reads=[GC, "HEADM"], writes=[RB])
                for hf in range(2):
                    S.emit("pe", lambda E, hf=hf: E.matmul(
                        PS[MYB[1 + hf]][:, :], lhsT=ONES8[:], rhs=rhsBD[:, 4 * hf:4 * hf + 4, :].rearrange("p h i -> p (h i)"),
                        start=True, stop=True),
                           reads=[RB, "ONES8"], writes=[("ps", MYB[1 + hf])])
                yield
                for hf in range(2):
                    S.emit("dve", lambda E, hf=hf: E.tensor_tensor(
                        out=EA[:, 4 * hf:4 * hf + 4, :], in0=PS[MYB[1 + hf]][:, :].rearrange("p (h i) -> p h i", h=4),
                        in1=gct[:, 4 * hf:4 * hf + 4].unsqueeze(2).to_broadcast([128, 4, 128]), op=ALU.subtract),
                           reads=[("ps", MYB[1 + hf]), GT_], writes=[EAn])
                S.emit("act", lambda E: E.activation(out=EA[:], in_=EA[:], func=AF.Exp), reads=[EAn], writes=[EAn])
                yield
                S.emit("dve", lambda E: E.tensor_tensor(out=EA[:], in0=EA[:], in1=MASKU[:], op=ALU.min),
                       reads=[EAn, "MASKU"], writes=[EAn])
                S.emit("pool", lambda E: E.tensor_tensor(out=EAsb[:], in0=EA[:], in1=STRICT[:], op=ALU.mult),
                       reads=[EAn, "STRICT"], writes=[EASn])
                S.emit("pool", lambda E: E.tensor_tensor(out=EAsb[:], in0=EAsb[:],
                                                         in1=nbeta[:, t, :].unsqueeze(2).to_broadcast([128, 8, 128]),
                                                         op=ALU.mult),
                       reads=[EASn, "gnbeta"], writes=[EASn])
                yield
                for pr in range(4):
                    S.emit("pe", lambda E, pr=pr: E.matmul(PS[B0][:, pr * 128:(pr + 1) * 128], lhsT=SEL[:, pr, :],
                                                           rhs=gcT[:], start=True, stop=True),
                           reads=[GC, "SEL"], writes=[("ps", B0)], signal=(pr == 3))
                S.emit("act", lambda E: E.activation(out=EG[:].rearrange("p a i -> p (a i)"), in_=PS[B0][:, :], func=AF.Exp),
                       reads=[("ps", B0)], writes=[("EG", par)])
                S.emit("pool", lambda E: E.tensor_tensor(out=qdT[:], in0=qTg[:, :, tk], in1=EG[:], op=ALU.mult),
                       reads=[("EG", par)] + [("gq", pr) for pr in range(4)], writes=[("qdT", par)])
                yield
                bk = pbank()
                tbk = PS[bk][:].bitcast(BF16)
                for pr in range(4):
                    S.emit("pe", lambda E, pr=pr, tbk=tbk: E.transpose(out=tbk[:, pr * 128:(pr + 1) * 128],
                                                                       in_=kTg[:, pr, tk], identity=ident[:]),
                           reads=[("gk", pr), "ident"], writes=[("ps", bk)], signal=(pr == 3))
                S.emit("act", lambda E, tbk=tbk: E.activation(out=ktok[:].rearrange("p h d -> p (h d)"),
                                                              in_=tbk[:, 0:512], func=AF.Identity),
                       reads=[("ps", bk)], writes=[KT])
                bv = pbank()
                tbv = PS[bv][:].bitcast(BF16)
                for pr in range(4):
                    S.emit("pe", lambda E, pr=pr, tbv=tbv: E.transpose(out=tbv[:, pr * 128:(pr + 1) * 128],
                                                                       in_=vTg[:, pr, tk], identity=ident[:]),
                           reads=[("gv", pr), "ident"], writes=[("ps", bv)], signal=(pr == 3))
                yield
                X0 = X[0]
                S.emit("dve", lambda E, tbv=tbv: E.tensor_copy(out=X0[:, 0, :, :],
                                                               in_=tbv[:, 0:512].rearrange("p (h d) -> p h d", h=8)),
                       reads=[("ps", bv)], writes=[XN[0] + (0,), XN[0] + (1,)])
                S.emit("dve", lambda E: E.tensor_tensor(out=X0[:, 1, :, :], in0=ktok[:],
                                                        in1=egt[:, 0:8].unsqueeze(2).to_broadcast([128, 8, 64]),
                                                        op=ALU.mult),
                       reads=[KT, EGT, XN[0] + (0,), XN[0] + (1,)], writes=[XN[0] + (0,), XN[0] + (1,)])
                for hf in range(2):
                    rows = slice(hf * 64, (hf + 1) * 64)
                    S.emit("pool", lambda E, hf=hf, rows=rows: E.tensor_tensor(
                        out=kdec[hf][rows], in0=ktok[rows],
                        in1=egt[rows, 8:16].unsqueeze(2).to_broadcast([64, 8, 64]), op=ALU.mult),
                           reads=[KT, EGT], writes=[("kdec", par, hf)])
                yield
                for hh in range(2):
                    rows = slice(hh * 64, (hh + 1) * 64)
                    S.emit("pool", lambda E, hh=hh, rows=rows: E.tensor_copy(out=qz[rows, hh, :, :], in_=qTg[rows, :, tk]),
                           reads=[("gq", pr) for pr in range(4)], writes=[QZ])
                    S.emit("pool", lambda E, hh=hh, rows=rows: E.tensor_copy(out=kz[rows, hh, :, :], in_=kTg[rows, :, tk]),
                           reads=[("gk", pr) for pr in range(4)], writes=[KZ])
                kq = (pbank(), pbank())
                for h in range(8):
                    pr, hh = h // 2, h % 2
                    S.emit("pe", lambda E, pr=pr, hh=hh: E.matmul(
                        PS[kq[hh]][:, pr * 128:(pr + 1) * 128], lhsT=kTg[:, pr, tk], rhs=qz[:, hh, pr, :],
                        start=True, stop=True),
                           reads=[("gk", pr), QZ], writes=[("ps", kq[hh])], signal=(h >= 6))
                for hh in range(2):
                    S.emit("dve", lambda E, hh=hh: E.tensor_tensor(
                        out=attnT[:, hh:8:2, :], in0=PS[kq[hh]][:, :].rearrange("p (a i) -> p a i", a=4),
                        in1=EA[:, hh:8:2, :], op=ALU.mult),
                           reads=[("ps", kq[hh]), EAn], writes=[("attnT", par)])
                yield
                kk = (pbank(), pbank())
                for h in range(8):
                    pr, hh = h // 2, h % 2
                    S.emit("pe", lambda E, pr=pr, hh=hh: E.matmul(
                        PS[kk[hh]][:, pr * 128:(pr + 1) * 128], lhsT=kTg[:, pr, tk], rhs=kz[:, hh, pr, :],
                        start=True, stop=True),
                           reads=[("gk", pr), KZ], writes=[("ps", kk[hh])], signal=(h >= 6))
                for hh in range(2):
                    S.emit("dve", lambda E, hh=hh: E.tensor_tensor(
                        out=Bm[0][:, hh:8:2, :], in0=PS[kk[hh]][:, :].rearrange("p (a i) -> p a i", a=4),
                        in1=EAsb[:, hh:8:2, :], op=ALU.mult),
                           reads=[("ps", kk[hh]), EASn], writes=[("Bm", ps, 0, 0), ("Bm", ps, 0, 1)])
                S.emit("pool", lambda E: E.tensor_tensor(out=Bp[0][:], in0=Bm[0][:], in1=identb.to_broadcast([128, 8, 128]),
                                                         op=ALU.add),
                       reads=[("Bm", ps, 0, 0), ("Bm", ps, 0, 1), "ident"], writes=[BPN[0] + (0,), BPN[0] + (1,)])
                yield
            for a in range(2):
                bn = pbank()
                for h4 in range(4):
                    h = 4 * a + h4
                    S.emit("pe", lambda E, h=h, h4=h4, bn=bn: E.matmul(
                        PS[bn][:, h4 * 128:(h4 + 1) * 128], lhsT=Bm[0][:, h, :], rhs=ident[:], start=True, stop=True),
                           reads=[("Bm", ps, 0, a), "ident"], writes=[("ps", bn)], signal=(h4 == 3))
                self.evac(Nm[0][:, 4 * a:4 * a + 4, :].rearrange("p h i -> p (h i)"), PS[bn][:, :],
                          reads=[("ps", bn)], writes=[("Nm", ps, 0, a)])
                yield
            for lv in range(5):
                ci, ni = lv % 2, (lv + 1) % 2
                for a in range(2):
                    if lv < 4:
                        bnn = pbank()
                        for h4 in range(4):
                            h = 4 * a + h4
                            S.emit("pe", lambda E, h=h, h4=h4, bnn=bnn, ci=ci: E.matmul(
                                PS[bnn][:, h4 * 128:(h4 + 1) * 128], lhsT=Bm[ci][:, h, :], rhs=Nm[ci][:, h, :],
                                start=True, stop=True),
                                   reads=[("Nm", ps, ci, a), ("Bm", ps, ci, a)], writes=[("ps", bnn)], signal=(h4 == 3))
                        self.evac(Nm[ni][:, 4 * a:4 * a + 4, :].rearrange("p h i -> p (h i)"), PS[bnn][:, :],
                                  reads=[("ps", bnn)], writes=[("Nm", ps, ni, a)])
                        yield
                    bbb = pbank()
                    for h4 in range(4):
                        h = 4 * a + h4
                        S.emit("pe", lambda E, h=h, h4=h4, bbb=bbb, ci=ci: E.matmul(
                            PS[bbb][:, h4 * 128:(h4 + 1) * 128], lhsT=Nm[ci][:, h, :], rhs=Bm[ci][:, h, :],
                            start=True, stop=True),
                               reads=[("Nm", ps, ci, a), ("Bm", ps, ci, a)], writes=[("ps", bbb)], signal=(h4 == 3))
                    self.evac(Bm[ni][:, 4 * a:4 * a + 4, :].rearrange("p h i -> p (h i)"), PS[bbb][:, :],
                              reads=[("ps", bbb)], writes=[("Bm", ps, ni, a)])
                    S.emit("pool", lambda E, a=a, ni=ni: E.tensor_tensor(
                        out=Bp[ni][:, 4 * a:4 * a + 4, :], in0=Bm[ni][:, 4 * a:4 * a + 4, :],
                        in1=identb.to_broadcast([128, 4, 128]), op=ALU.add),
                           reads=[("Bm", ps, ni, a), "ident"], writes=[BPN[ni] + (a,)])
                    yield
                for a in range(2):
                    bx = pbank()
                    for h4 in range(4):
                        h = 4 * a + h4
                        S.emit("pe", lambda E, h=h, h4=h4, bx=bx, ci=ci: E.matmul(
                            PS[bx][:, h4 * 128:(h4 + 1) * 128], lhsT=Bp[ci][:, h, :], rhs=X[ci][:, :, h, :],
                            start=True, stop=True),
                               reads=[XN[ci] + (a,), BPN[ci] + (a,)], writes=[("ps", bx)], signal=(h4 == 3))
                    self.evac(X[ni][:, :, 4 * a:4 * a + 4, :].rearrange("p s h d -> p h s d"),
                              PS[bx][:, :].rearrange("p (h s d) -> p h s d", h=4, s=2),
                              reads=[("ps", bx)], writes=[XN[ni] + (a,)])
                    yield
            X5, B5 = X[1], Bp[1]

        def gen_S(t):
            par = t % NSLOT
            tk = slice(t * 128, (t + 1) * 128)
            EG, qdT, kdec, attnT, nwT = EGs[par], qdTs[par], kdecs[par], attnTs[par], nwTs[par]
            X5, B5 = X1s[par], Bp1s[par]
            for a in range(2):
                bw = sbank()
                for h4 in range(4):
                    h = 4 * a + h4
                    pr = h // 2
                    lw = X5[:, 1, 2 * pr:2 * pr + 2, :].rearrange("p h d -> p (h d)")
                    S.emit("pe", lambda E, h=h, h4=h4, bw=bw, lw=lw: E.matmul(
                        PS[bw][:, h4 * 128:(h4 + 1) * 128], lhsT=lw, rhs=B5[:, h, :], start=True, stop=True),
                           reads=[("X", par, 1, a), ("Bp", par, 1, a)], writes=[("ps", bw)], signal=(h4 == 3))
                for h4 in range(4):
                    h = 4 * a + h4
                    pr, hh = h // 2, h % 2
                    rows = slice(hh * 64, (hh + 1) * 64)
                    S.emit("dve", lambda E, h4=h4, bw=bw, pr=pr, rows=rows: E.tensor_scalar(
                        out=nwT[rows, pr, :], in0=PS[bw][rows, h4 * 128:(h4 + 1) * 128], scalar1=-1.0, scalar2=None,
                        op0=ALU.mult),
                           reads=[("ps", bw)], writes=[("nwT", par)])
                yield
            for hf in range(2):
                rows = slice(hf * 64, (hf + 1) * 64)
                ci_ = (2 * t + hf) % 2
                Sb, Sbn = Sbs[ci_], Sbs[1 - ci_]
                SBC, SBN = ("Sb", ci_), ("Sb", 1 - ci_)
                bvn = sbank()
                for h in range(8):
                    pr, hh = h // 2, h % 2
                    cs = slice(h * 64, (h + 1) * 64)
                    S.emit("pe", lambda E, h=h, cs=cs, bvn=bvn: E.matmul(
                        PS[bvn][:, cs], lhsT=B5[:, h, :], rhs=X5[:, 0, h, :], start=True, stop=False),
                           reads=[("X", par, 1, 0), ("X", par, 1, 1), ("Bp", par, 1, 0), ("Bp", par, 1, 1)], writes=[("ps", bvn)], signal=False)
                    S.emit("pe", lambda E, pr=pr, hh=hh, cs=cs, bvn=bvn, Sb=Sb: E.matmul(
                        PS[bvn][:, cs], lhsT=nwT[:, pr, :], rhs=Sb[:, pr, hh, :], start=False, stop=True),
                           reads=[("nwT", par), SBC], writes=[("ps", bvn)], signal=(h == 7))
                yield
                S.emit("dve", lambda E, rows=rows, bvn=bvn: E.tensor_tensor(
                    out=vnew[rows], in0=PS[bvn][rows, :].rearrange("p (h d) -> p h d", h=8),
                    in1=beta[rows, t, :].unsqueeze(2).to_broadcast([64, 8, 64]), op=ALU.mult),
                       reads=[("ps", bvn), "gbeta"], writes=["vnew"])
                yield
                bs = sbank()
                for h in range(8):
                    pr, hh = h // 2, h % 2
                    S.emit("pe", lambda E, h=h, pr=pr, hh=hh, bs=bs, hf=hf: E.matmul(
                        PS[bs][:, (pr * 2 + hh) * 64:(pr * 2 + hh + 1) * 64],
                        lhsT=kdec[hf][:, 2 * pr:2 * pr + 2, :].rearrange("p h d -> p (h d)"), rhs=vnew[:, h, :],
                        start=True, stop=True),
                           reads=[("kdec", par, hf), "vnew"], writes=[("ps", bs)], signal=(h == 7))
                yield
                gl = EG[:, :, hf * 64 + 63:hf * 64 + 64]
                S.emit("dve", lambda E, gl=gl: E.tensor_tensor(out=tmpS[:], in0=S32[:],
                                                               in1=gl.to_broadcast([128, 4, 64]), op=ALU.mult),
                       reads=["S32", ("EG", par)], writes=["tmpS"])
                dS = PS[bs][:, :].rearrange("p (a b d) -> p a b d", a=4, b=2)
                for hh in range(2):
                    r2 = slice(hh * 64, (hh + 1) * 64)
                    S.emit("dve", lambda E, hh=hh, r2=r2, dS=dS: E.tensor_tensor(
                        out=S32[r2], in0=tmpS[r2], in1=dS[r2, :, hh, :], op=ALU.add),
                           reads=["tmpS", ("ps", bs)], writes=["S32"])
                    S.emit("act", lambda E, hh=hh, r2=r2, Sbn=Sbn: E.activation(out=Sbn[r2, :, hh, :], in_=S32[r2],
                                                                                func=AF.Identity),
                           reads=["S32"], writes=[SBN])
                yield
                bo = sbank()
                for h in range(8):
                    pr, hh = h // 2, h % 2
                    cs = slice(h * 64, (h + 1) * 64)
                    S.emit("pe", lambda E, pr=pr, hh=hh, cs=cs, bo=bo, Sb=Sb: E.matmul(
                        PS[bo][:, cs], lhsT=qdT[:, pr, :], rhs=Sb[:, pr, hh, :], start=True, stop=False),
                           reads=[("qdT", par), SBC], writes=[("ps", bo)], signal=False)
                    S.emit("pe", lambda E, h=h, cs=cs, bo=bo: E.matmul(
                        PS[bo][:, cs], lhsT=attnT[:, h, :], rhs=vnew[:, h, :], start=False, stop=True),
                           reads=[("attnT", par), "vnew"], writes=[("ps", bo)], signal=(h == 7))
                yield
                S.emit("act", lambda E, rows=rows, bo=bo: E.activation(
                    out=osb[rows].rearrange("p h d -> p (h d)"), in_=PS[bo][rows, :], func=AF.Identity),
                       reads=[("ps", bo)], writes=["osb"])
                yield
            S.emit("pool", lambda E: E.tensor_tensor(out=osq[:], in0=osb[:], in1=osb[:], op=ALU.mult),
                   reads=["osb"], writes=["osq"])
            S.emit("dve", lambda E: E.tensor_reduce(out=oss[:, 0:8], in_=osq[:], axis=AX.X, op=ALU.add),
                   reads=["osq"], writes=["oss"])
            yield
            S.emit("act", lambda E: E.activation(out=oss[:, 8:16], in_=oss[:, 0:8], func=AF.Sqrt, scale=1.0 / 64,
                                                 bias=P["epsc"][:]),
                   reads=["oss", "epsc"], writes=["oss"])
            S.emit("dve", lambda E: E.reciprocal(out=oss[:, 0:8], in_=oss[:, 8:16]), reads=["oss"], writes=["oss"])
            S.emit("dve", lambda E: E.tensor_tensor(out=osq[:], in0=osb[:],
                                                    in1=oss[:, 0:8].unsqueeze(2).to_broadcast([128, 8, 64]),
                                                    op=ALU.mult),
                   reads=["osb", "oss", "osq"], writes=["osq"])
            yield
            S.emit("pool", lambda E: E.tensor_tensor(out=osq[:], in0=osq[:],
                                                     in1=gnw[:].unsqueeze(1).to_broadcast([128, 8, 64]),
                                                     op=ALU.mult),
                   reads=["osq", "gnw"], writes=["osq"])
            S.emit("dve", lambda E: E.tensor_tensor(out=og[:], in0=osq[:].rearrange("p h d -> p (h d)"),
                                                    in1=zs[:, t, :], op=ALU.mult),
                   reads=["osq", ("gzs", t)], writes=["og"])
            yield
            bt = sbank()
            tbo = PS[bt][:].bitcast(BF16)
            for pr in range(4):
                S.emit("pe", lambda E, pr=pr, tbo=tbo: E.transpose(out=tbo[:, pr * 128:(pr + 1) * 128],
                                                                   in_=og[:, pr * 128:(pr + 1) * 128],
                                                                   identity=ident[:]),
                       reads=["og", "ident"], writes=[("ps", bt)], signal=(pr == 3))
            S.emit("act", lambda E, tbo=tbo: E.activation(out=oT[:, 4:8, tk],
                                                          in_=tbo[:, 0:512].rearrange("p (a i) -> p a i", a=4),
                                                          func=AF.Identity),
                   reads=[("ps", bt)], writes=[("oT", 4 + pr) for pr in range(4)])
            yield

        def drain(gen):
            for _ in gen:
                pass

        def step(gen):
            try:
                next(gen)
                return True
            except StopIteration:
                return False

        active_p = []
        next_p = 0
        p_done = set()
        scan_t = 0
        scan_gen = None
        while scan_t < NT:
            while next_p < NT and len(active_p) < NPS and next_p < scan_t + NSLOT:
                active_p.append([next_p, gen_P(next_p)])
                next_p += 1
            for ent in list(active_p):
                for _ in range(int(os.environ.get("C2RATIO", "2"))):
                    if not step(ent[1]):
                        p_done.add(ent[0])
                        active_p.remove(ent)
                        break
            if scan_gen is None and scan_t in p_done:
                scan_gen = gen_S(scan_t)
            if scan_gen is not None:
                if not step(scan_gen):
                    scan_gen = None
                    scan_t += 1

    def phase_D(self, st, b):
        nc, S, d, P, PS = self.nc, self.S, self.d, self.P, self.PS
        sb = self.sb
        oT = P["oT"]
        ident = P["ident"]
        wo = sb(st, "wo", [128, 8, DM], BF16)
        wu = sb(st, "wu", [128, 8, DFF], BF16)
        wd = sb(st, "wd", [128, 32, DM], BF16)
        premlp = P["premlp"]
        with ExitStack() as s_stg:
            NSTG = 5
            stg = [sb(s_stg, "stgD%d" % i, [128, DM], F32) for i in range(NSTG)]
            self._ns = 0

            def load_cast(src, dst, dname, scal=None):
                i = self._ns % NSTG
                eng = ("pool", "act", "dve")[self._ns % 3]
                self._ns += 1
                S.dma(stg[i][:], src, writes=[("stgD", i)])
                rd = [("stgD", i)] + (["premlp"] if scal is not None else [])
                if eng == "act":
                    if scal is None:
                        S.emit("act", lambda E, i=i: E.activation(out=dst, in_=stg[i][:], func=AF.Identity),
                               reads=rd, writes=[dname])
                    else:
                        S.emit("act", lambda E, i=i: E.activation(out=dst, in_=stg[i][:], func=AF.Identity,
                                                                  scale=scal), reads=rd, writes=[dname])
                else:
                    if scal is None:
                        S.emit(eng, lambda E, i=i: E.tensor_copy(out=dst, in_=stg[i][:]), reads=rd, writes=[dname])
                    else:
                        S.emit(eng, lambda E, i=i: E.tensor_scalar(out=dst, in0=stg[i][:], scalar1=scal, scalar2=None,
                                                                   op0=ALU.mult), reads=rd, writes=[dname])

            for k in range(8):
                load_cast(d["w_out"][k * 128:(k + 1) * 128, :], wo[:, k, :], ("wo", k))
            for k in range(8):
                for qd in range(4):
                    load_cast(d["w_up"][k * 128:(k + 1) * 128, qd * 1024:(qd + 1) * 1024],
                              wu[:, k, qd * 1024:(qd + 1) * 1024], ("wu", k, qd), scal=premlp[:, k:k + 1])
            for k in range(32):
                load_cast(d["w_down"][k * 128:(k + 1) * 128, :], wd[:, k, :], ("wd", k))
        S.barrier()

        GT = 2
        xt = [sb(st, "xtD%d" % i, [128, DM], F32) for i in range(GT)]
        tmp = sb(st, "tmpD", [128, DM], F32)
        h2 = sb(st, "h2D", [128, DM], BF16)
        h2T = sb(st, "h2T", [128, 8, GT * 128], BF16)
        uT = sb(st, "uT", [128, 32, GT * 128], BF16)
        rl = [sb(st, "rlD%d" % i, [128, 512], F32) for i in range(2)]
        sm = sb(st, "smD", [128, NT, 12], F32)
        S.emit("dve", lambda E: E.memset(sm[:], 0.0), writes=["smD"])
        postmix_b, postmlp_b, epsc = P["postmix_b"], P["postmlp_b"], P["epsc"]
        oT_all = [("oT", c) for c in range(8)]

        def rms_scale(src_banks, t, col):
            for hf in range(2):
                S.emit("act", lambda E, hf=hf: E.activation(out=tmp[:, hf * 512:(hf + 1) * 512],
                                                            in_=PS[src_banks[hf]][:, :], func=AF.Square,
                                                            accum_out=sm[:, t, col + hf:col + hf + 1]),
                       reads=[("ps", src_banks[hf]), "smD"], writes=["tmpD", "smD"])
            S.emit("dve", lambda E: E.tensor_tensor(out=sm[:, t, col:col + 1], in0=sm[:, t, col:col + 1],
                                                    in1=sm[:, t, col + 1:col + 2], op=ALU.add),
                   reads=["smD"], writes=["smD"])
            S.emit("act", lambda E: E.activation(out=sm[:, t, col + 1:col + 2], in_=sm[:, t, col:col + 1],
                                                 func=AF.Sqrt, scale=1.0 / DM, bias=epsc[:]),
                   reads=["smD", "epsc"], writes=["smD"])
            S.emit("dve", lambda E: E.reciprocal(out=sm[:, t, col + 2:col + 3], in_=sm[:, t, col + 1:col + 2]),
                   reads=["smD"], writes=["smD"])

        def stage1(t, j):
            tok = slice(t * 128, (t + 1) * 128)
            S.dma(xt[j][:], d["x"][b, tok, :], writes=[("xtD", j)])
            for hf in range(2):
                for c in range(8):
                    S.emit("pe", lambda E, hf=hf, c=c: E.matmul(PS[hf][:, :], lhsT=oT[:, c, tok],
                                                                rhs=wo[:, c, hf * 512:(hf + 1) * 512],
                                                                start=(c == 0), stop=(c == 7)),
                           reads=oT_all + ["wo"], writes=[("ps", hf)], signal=(c == 7))
            rms_scale((0, 1), t, 0)
            for hf in range(2):
                cs = slice(hf * 512, (hf + 1) * 512)
                S.emit("dve", lambda E, hf=hf, cs=cs: E.scalar_tensor_tensor(
                    out=tmp[:, cs], in0=PS[hf][:, :], scalar=sm[:, t, 2:3], in1=postmix_b[:, cs],
                    op0=ALU.mult, op1=ALU.mult),
                       reads=[("ps", hf), "smD", "postmix_b", "tmpD"], writes=["tmpD"])
            S.emit("pool", lambda E: E.tensor_tensor(out=xt[j][:], in0=tmp[:], in1=xt[j][:], op=ALU.add),
                   reads=["tmpD", ("xtD", j)], writes=[("xtD", j)])
            if "x1" in self.dbg_out:
                S.dma(self.dbg_out["x1"][b, tok, :], xt[j][:], reads=[("xtD", j)], writes=[("dbgx1", t)])
            S.emit("act", lambda E: E.activation(out=h2[:], in_=xt[j][:], func=AF.Square, accum_out=sm[:, t, 3:4]),
                   reads=[("xtD", j), "smD"], writes=["h2D", "smD"])
            S.emit("act", lambda E: E.activation(out=sm[:, t, 4:5], in_=sm[:, t, 3:4], func=AF.Sqrt,
                                                 scale=1.0 / DM, bias=epsc[:]),
                   reads=["smD", "epsc"], writes=["smD"])
            S.emit("dve", lambda E: E.reciprocal(out=sm[:, t, 5:6], in_=sm[:, t, 4:5]), reads=["smD"], writes=["smD"])
            S.emit("act", lambda E: E.activation(out=h2[:], in_=xt[j][:], func=AF.Identity, scale=sm[:, t, 5:6]),
                   reads=[("xtD", j), "smD"], writes=["h2D"])
            tb = PS[2][:].bitcast(BF16)
            for k in range(8):
                S.emit("pe", lambda E, k=k: E.transpose(out=tb[:, k * 128:(k + 1) * 128],
                                                        in_=h2[:, k * 128:(k + 1) * 128], identity=ident[:]),
                       reads=["h2D", "ident"], writes=[("ps", 2)], signal=(k == 7))
            S.emit("dve", lambda E: E.tensor_copy(out=h2T[:, :, j * 128:(j + 1) * 128],
                                                  in_=tb.rearrange("p (k c) -> p k c", k=8)),
                   reads=[("ps", 2)], writes=[("h2T", j)])

        def stage2():
            W = GT * 128
            nf = 512 // W
            for g in range(32 // nf):
                bank = 3 + (g % 3)
                for f in range(nf):
                    fc = g * nf + f
                    for k in range(8):
                        S.emit("pe", lambda E, bank=bank, f=f, fc=fc, k=k: E.matmul(
                            PS[bank][:, f * W:(f + 1) * W], lhsT=wu[:, k, fc * 128:(fc + 1) * 128],
                            rhs=h2T[:, k, :], start=(k == 0), stop=(k == 7)),
                               reads=["wu"] + [("h2T", j) for j in range(GT)], writes=[("ps", bank)],
                               signal=(k == 7 and f == nf - 1))
                uv = uT[:, g * nf:(g + 1) * nf, :].rearrange("p a c -> p (a c)")
                ri = g % 2
                S.emit("act", lambda E, bank=bank, ri=ri: E.activation(out=rl[ri][:], in_=PS[bank][:, :], func=AF.Relu),
                       reads=[("ps", bank)], writes=[("rlD", ri)])
                S.emit("pool", lambda E, uv=uv, ri=ri: E.tensor_tensor(out=uv, in0=rl[ri][:], in1=rl[ri][:], op=ALU.mult),
                       reads=[("rlD", ri)], writes=["uT"])

        def stage3(t, j):
            tok = slice(t * 128, (t + 1) * 128)
            for hf in range(2):
                for fc in range(32):
                    S.emit("pe", lambda E, hf=hf, fc=fc: E.matmul(PS[6 + hf][:, :], lhsT=uT[:, fc, j * 128:(j + 1) * 128],
                                                                  rhs=wd[:, fc, hf * 512:(hf + 1) * 512],
                                                                  start=(fc == 0), stop=(fc == 31)),
                           reads=["uT", "wd"], writes=[("ps", 6 + hf)], signal=(fc == 31))
            rms_scale((6, 7), t, 8)
            for hf in range(2):
                cs = slice(hf * 512, (hf + 1) * 512)
                S.emit("dve", lambda E, hf=hf, cs=cs: E.scalar_tensor_tensor(
                    out=tmp[:, cs], in0=PS[6 + hf][:, :], scalar=sm[:, t, 10:11], in1=postmlp_b[:, cs],
                    op0=ALU.mult, op1=ALU.mult),
                       reads=[("ps", 6 + hf), "smD", "postmlp_b", "tmpD"], writes=["tmpD"])
            S.emit("pool", lambda E: E.tensor_tensor(out=xt[j][:], in0=tmp[:], in1=xt[j][:], op=ALU.add),
                   reads=["tmpD", ("xtD", j)], writes=[("xtD", j)])
            S.dma(self.out[b, tok, :], xt[j][:], reads=[("xtD", j)], writes=[("out", b, t)])

        for gi in range(NT // GT):
            for j in range(GT):
                stage1(gi * GT + j, j)
            stage2()
            for j in range(GT):
                stage3(gi * GT + j, j)


def host_inputs(inputs, core, nseq):
    f = lambda a: np.ascontiguousarray(np.asarray(a, dtype=np.float32))
    m = {}
    m["x"] = f(inputs["x"][core * nseq:(core + 1) * nseq])
    m["w_in"] = f(inputs["w_in"][0])
    m["w_out"] = f(inputs["w_out"][0])
    m["w_up"] = f(inputs["w_up"][0])
    m["w_down"] = f(inputs["w_down"][0])
    m["premix_pk"] = f(np.asarray(inputs["pre_mix_norm"][0]).reshape(8, 128).T)
    m["premlp_pk"] = f(np.asarray(inputs["pre_mlp_norm"][0]).reshape(8, 128).T)
    m["postmix"] = f(np.asarray(inputs["post_mix_norm"][0]).reshape(1, DM))
    m["postmlp"] = f(np.asarray(inputs["post_mlp_norm"][0]).reshape(1, DM))
    rb = np.asarray(inputs["rel_bias"], dtype=np.float32)
    tab = np.concatenate([rb, np.full((8, 1), NEG, np.float32)], axis=1)
    j = np.arange(128)[:, None]
    i = np.arange(128)[None, :]
    idx0 = np.where(i - j >= 0, rel_bucket_np(i - j), 32)
    idx1 = rel_bucket_np(128 + i - j)
    idx = np.stack([idx0, idx1], axis=0)
    tt = tab[:, idx]
    m["ttab"] = f(tt.transpose(2, 0, 1, 3).reshape(128, 8 * 2 * 128))
    m["rb31"] = f(rb[:, 31].reshape(1, 8))
    cw = np.asarray(inputs["conv_w"][0], dtype=np.float32)
    m["convw_pk"] = f(cw.T.reshape(12, 128, 4).transpose(1, 0, 2).reshape(128, 48))
    m["alog"] = f(np.asarray(inputs["A_log"][0]).reshape(1, 8))
    m["dtb"] = f(np.asarray(inputs["dt_bias"][0]).reshape(1, 8))
    m["gnw"] = f(np.asarray(inputs["gdn_norm_w"][0]).reshape(1, 64))
    return m


_PROG = {}


def kernel(**inputs):
    ncores = 8
    nseq = 16 // ncores
    if "p" not in _PROG:
        _PROG["p"] = Prog(nseq)
    prog = _PROG["p"]
    in_maps = [host_inputs(inputs, c, nseq) for c in range(ncores)]
    res = run_bass_kernel_spmd(prog.nc, in_maps, core_ids=list(range(ncores)))
    out = np.concatenate([r["out"] for r in res.results], axis=0)
    return out.astype(np.float32)
```

```python
import math
from contextlib import ExitStack

import numpy as np
import concourse.bass as bass
import concourse.mybir as mybir
from concourse.bass_utils import run_bass_kernel_spmd

F32 = mybir.dt.float32
BF16 = mybir.dt.bfloat16
AF = mybir.ActivationFunctionType
ALU = mybir.AluOpType
AX = mybir.AxisListType

NDMA = 8
SEQ = 2048
DM = 1024
NT = SEQ // 128
DFF = 4096
INC = 3600
EPS = 1e-6
NEG = -30000.0


class Sched:
    ENGS = ("pe", "act", "dve", "pool", "sp")

    def __init__(self, nc):
        self.nc = nc
        self.q = {e: [] for e in self.ENGS}
        self.cnt = {e: 0 for e in self.ENGS}
        self.pending = {e: False for e in self.ENGS}
        self.seen = {e: {} for e in self.ENGS}
        self.lastw = {}
        self.readers = {}
        self.dma_i = 0
        self.dma_uses = [0] * NDMA
        self.bar = {}
        self.n_ins = {e: 0 for e in self.ENGS}

    def _deps(self, eng, reads, writes):
        deps = dict(self.bar)

        def add(tok):
            k, v = tok
            if deps.get(k, 0) < v:
                deps[k] = v

        for r in reads:
            if r in self.lastw:
                add(self.lastw[r])
            if isinstance(r, tuple) and r[0] == "ps":
                for k, v in self.readers.get(r, {}).items():
                    if k != eng:
                        add((k, v))
        for w in writes:
            if w in self.lastw:
                add(self.lastw[w])
            for k, v in self.readers.get(w, {}).items():
                add((k, v))
        waits = []
        for k, v in deps.items():
            if k == eng and eng == "pe":
                continue
            if self.seen[eng].get(k, 0) >= v:
                continue
            self.seen[eng][k] = v
            waits.append((k, v))
        return waits

    def _record(self, tok, reads, writes):
        k, v = tok
        for r in reads:
            d = self.readers.setdefault(r, {})
            if d.get(k, 0) < v:
                d[k] = v
        for w in writes:
            self.lastw[w] = tok
            self.readers[w] = {}

    def emit(self, eng, fn, reads=(), writes=(), signal=True):
        waits = self._deps(eng, reads, writes)
        if signal:
            self.cnt[eng] += 1
            self.pending[eng] = False
            tok = (eng, self.cnt[eng])
        else:
            self.pending[eng] = True
            tok = (eng, self.cnt[eng] + 1)
        self._record(tok, reads, writes)
        self.n_ins[eng] += 1

        def run(E, sems, waits=waits, fn=fn, signal=signal, eng=eng):
            for k, v in waits:
                E.wait_ge(sems[k], v)
            ins = fn(E)
            if signal:
                ins.then_inc(sems[eng], 1)

        self.q[eng].append(run)
        return tok

    def dma(self, out, in_, reads=(), writes=(), q="sp", **kw):
        slot = self.dma_i % NDMA
        self.dma_i += 1
        key = ("dma", slot)
        waits = self._deps(q, reads, writes)
        prev = 16 * self.dma_uses[slot]
        if prev > 0 and self.seen[q].get(key, 0) < prev:
            self.seen[q][key] = prev
            waits.append((key, prev))
        self.dma_uses[slot] += 1
        tok = (key, 16 * self.dma_uses[slot])
        self._record(tok, reads, writes)
        self.n_ins[q] += 1

        def run(E, sems, waits=waits, out=out, in_=in_, key=key, kw=kw):
            for k, v in waits:
                E.wait_ge(sems[k], v)
            E.dma_start(out=out, in_=in_, **kw).then_inc(sems[key], 16)

        self.q[q].append(run)
        return tok

    def barrier(self):
        for e in self.ENGS:
            assert not self.pending[e]
            if self.cnt[e] > 0:
                self.bar[e] = self.cnt[e]
        for s in range(NDMA):
            if self.dma_uses[s] > 0:
                self.bar[("dma", s)] = 16 * self.dma_uses[s]

    def finish(self):
        waits = []
        for slot in range(NDMA):
            v = 16 * self.dma_uses[slot]
            key = ("dma", slot)
            if v > 0 and self.seen["sp"].get(key, 0) < v:
                self.seen["sp"][key] = v
                waits.append((key, v))

        def run(E, sems, waits=waits):
            for k, v in waits:
                E.wait_ge(sems[k], v)

        self.q["sp"].append(run)
        for e in self.ENGS:
            assert not self.pending[e], f"engine {e} has unsignaled trailing instruction"

    def build(self, stack):
        nc = self.nc
        sems = {}
        for e in self.ENGS:
            sems[e] = stack.enter_context(nc.semaphore("s_" + e))
        for s in range(NDMA):
            sems[("dma", s)] = stack.enter_context(nc.semaphore("s_dma%d" % s))
        block = stack.enter_context(nc.Block())
        q = self.q

        @block.tensor
        def _(E):
            for f in q["pe"]:
                f(E, sems)

        @block.scalar
        def _(E):
            for f in q["act"]:
                f(E, sems)

        @block.vector
        def _(E):
            for f in q["dve"]:
                f(E, sems)

        @block.gpsimd
        def _(E):
            for f in q["pool"]:
                f(E, sems)

        @block.sync
        def _(E):
            for f in q["sp"]:
                f(E, sems)


def rel_bucket_np(d):
    d = np.maximum(d, 0)
    large = 16 + (np.log(np.maximum(d, 1).astype(np.float32) / 16) / math.log(128 / 16) * 16).astype(np.int32)
    large = np.minimum(large, 31)
    return np.where(d < 16, d, large)


class Prog:
    def __init__(self, nseq, stages=("A", "B1", "C1", "C2", "D"), dbg=()):
        self.nseq = nseq
        self.stages = stages
        self.dbg = dbg
        nc = bass.Bass("TRN2", target_bir_lowering=False, dynamic_dma_scratch_size=256)
        self.nc = nc
        self.S = Sched(nc)
        d = {}

        def din(name, shape):
            d[name] = nc.dram_tensor(name, list(shape), F32, kind="ExternalInput").ap()

        din("x", [nseq, SEQ, DM])
        din("w_in", [DM, INC])
        din("w_out", [DM, DM])
        din("w_up", [DM, DFF])
        din("w_down", [DFF, DM])
        din("premix_pk", [128, 8])
        din("premlp_pk", [128, 8])
        din("postmix", [1, DM])
        din("postmlp", [1, DM])
        din("ttab", [128, 8 * 2 * 128])
        din("rb31", [1, 8])
        din("convw_pk", [128, 12 * 4])
        din("alog", [1, 8])
        din("dtb", [1, 8])
        din("gnw", [1, 64])
        self.out = nc.dram_tensor("out", [nseq, SEQ, DM], F32, kind="ExternalOutput").ap()
        self.dbg_out = {}
        for name, shape in dbg:
            self.dbg_out[name] = nc.dram_tensor("dbg_" + name, list(shape), F32, kind="ExternalOutput").ap()
        self.d = d
        self._rr = 0
        with ExitStack() as st:
            self.build(st)
            self.S.finish()
            self.S.build(st)

    def sb(self, st, name, shape, dt):
        self._uid = getattr(self, "_uid", 0) + 1
        return st.enter_context(self.nc.sbuf_tensor("%s_u%d" % (name, self._uid), list(shape), dt))

    def evac(self, out, in_, reads, writes, eng=None):
        if eng is None:
            eng = ("act", "dve")[self._rr % 2]
            self._rr += 1
        if eng == "act":
            self.S.emit("act", lambda E: E.activation(out=out, in_=in_, func=AF.Identity), reads=reads, writes=writes)
        else:
            self.S.emit("dve", lambda E: E.tensor_copy(out=out, in_=in_), reads=reads, writes=writes)

    def build(self, st):
        nc, S, d = self.nc, self.S, self.d
        sb = self.sb
        self.PS = [st.enter_context(nc.psum_tensor("ps%d" % i, [128, 512], F32)) for i in range(8)]
        P = {}
        self.P = P
        P["ident"] = sb(st, "ident", [128, 128], BF16)
        P["postmix_b"] = sb(st, "postmix_b", [128, DM], F32)
        P["postmlp_b"] = sb(st, "postmlp_b", [128, DM], F32)
        P["premix"] = sb(st, "premix", [128, 8], F32)
        P["premlp"] = sb(st, "premlp", [128, 8], F32)
        P["epsc"] = sb(st, "epsc", [128, 1], F32)
        P["oT"] = sb(st, "oT", [128, 8, SEQ], BF16)
        ident = P["ident"]
        S.emit("pool", lambda E: E.memset(ident[:], 0.0), writes=["ident"])
        S.emit("pool", lambda E: E.affine_select(out=ident[:], in_=ident[:], pattern=[[-1, 128]],
                                                  compare_op=ALU.not_equal, fill=1.0, base=0, channel_multiplier=1),
               reads=["ident"], writes=["ident"])
        S.emit("pool", lambda E: E.memset(P["epsc"][:], EPS), writes=["epsc"])
        S.dma(P["postmix_b"][:], d["postmix"].partition_broadcast(128), writes=["postmix_b"])
        S.dma(P["postmlp_b"][:], d["postmlp"].partition_broadcast(128), writes=["postmlp_b"])
        S.dma(P["premix"][:], d["premix_pk"], writes=["premix"])
        S.dma(P["premlp"][:], d["premlp_pk"], writes=["premlp"])
        if "C2" not in self.stages:
            oT = P["oT"]
            S.emit("pool", lambda E: E.memset(oT[:, 4:8, :], 0.0), writes=[("oT", c) for c in range(4, 8)])

        for b in range(self.nseq):
          S.barrier()
          with ExitStack() as s_seq:
            hT_keep = sb(s_seq, "hT", [128, 8, SEQ], BF16)
            with ExitStack() as s_att:
                A = {}
                A["qT"] = sb(s_att, "qT", [128, 4, SEQ], BF16)
                A["kT"] = sb(s_att, "kT", [128, 4, SEQ], BF16)
                A["vaug"] = sb(s_att, "vaug", [128, NT, 4, 3, 64], BF16)
                A["maskT"] = sb(s_att, "maskT", [128, SEQ], BF16)
                A["kmT"] = sb(s_att, "kmT", [128, 4, 8], BF16)
                with ExitStack() as s_pa:
                    wA = self.prep_B1(s_pa) if "B1" in self.stages else None
                    hT = self.phase_A(s_pa, b, hT=hT_keep)
                    if "B1" in self.stages:
                        self.phase_B1(s_pa, b, hT, A, wA)
                S.barrier()
                if "C1" in self.stages:
                    with ExitStack() as s_c1:
                        self.phase_C1(s_c1, b, A)
                S.barrier()
            S.barrier()
            if "C2" in self.stages or "B2" in self.stages:
                with ExitStack() as s_g:
                    G = {}
                    G["qT"] = sb(s_g, "gqT", [128, 4, SEQ], BF16)
                    G["kT"] = sb(s_g, "gkT", [128, 4, SEQ], BF16)
                    G["vT"] = sb(s_g, "gvT", [128, 4, SEQ], BF16)
                    G["zs"] = sb(s_g, "gzs", [128, NT, 512], BF16)
                    G["gab"] = sb(s_g, "gab", [128, NT, 16], F32)
                    G["g"] = sb(s_g, "gg", [128, NT, 8], F32)
                    G["beta"] = sb(s_g, "gbeta", [128, NT, 8], F32)
                    G["nbeta"] = sb(s_g, "gnbeta", [128, NT, 8], F32)
                    with ExitStack() as s_pb:
                        self.phase_B2(s_pb, b, hT_keep, G)
                    S.barrier()
                    with ExitStack() as s_c2:
                        if "C2" in self.stages:
                            self.phase_C2(s_c2, b, G)
                    S.barrier()
                    if "oTg" in self.dbg_out:
                        with ExitStack() as s_dbg:
                            oT = P["oT"]
                            otf = sb(s_dbg, "otfg", [128, 4, SEQ], F32)
                            S.emit("dve", lambda E: E.tensor_copy(out=otf[:], in_=oT[:, 4:8, :]),
                                   reads=[("oT", c) for c in range(4, 8)], writes=["otfg"])
                            S.dma(self.dbg_out["oTg"][b].rearrange("c p s -> p c s"), otf[:], reads=["otfg"],
                                  writes=["dbg_oTg"])
                        S.barrier()
          S.barrier()
          if "D" in self.stages:
              with ExitStack() as s_d:
                  self.phase_D(s_d, b)
          S.barrier()

    def phase_A(self, st, b, nbuf=2, hT=None):
        nc, S, d, P, PS = self.nc, self.S, self.d, self.P, self.PS
        if hT is None:
            hT = self.sb(st, "hT", [128, 8, SEQ], BF16)
        xt = [self.sb(st, "xt%d" % i, [128, DM], F32) for i in range(nbuf)] * (2 // nbuf)
        hb = [self.sb(st, "hb%d" % i, [128, DM], BF16) for i in range(nbuf)] * (2 // nbuf)
        ss = self.sb(st, "ssA", [128, NT], F32)
        rs = self.sb(st, "rsA", [128, NT], F32)
        rstd = self.sb(st, "rstdA", [128, NT], F32)
        ident = P["ident"]
        S.emit("dve", lambda E: E.memset(ss[:], 0.0), writes=["ssA"])
        for t in range(NT):
            i = t % nbuf
            S.dma(xt[i][:], d["x"][b, t * 128:(t + 1) * 128, :], writes=[("xt", i)])
            S.emit("act", lambda E, i=i, t=t: E.activation(out=hb[i][:], in_=xt[i][:], func=AF.Square,
                                                           accum_out=ss[:, t:t + 1]),
                   reads=[("xt", i), "ssA"], writes=[("hb", i), "ssA"])
            S.emit("act", lambda E, t=t: E.activation(out=rs[:, t:t + 1], in_=ss[:, t:t + 1], func=AF.Sqrt,
                                                      scale=1.0 / DM, bias=P["epsc"][:]),
                   reads=["ssA", "epsc"], writes=["rsA"])
            S.emit("dve", lambda E, t=t: E.reciprocal(out=rstd[:, t:t + 1], in_=rs[:, t:t + 1]),
                   reads=["rsA"], writes=["rstdA"])
            S.emit("act", lambda E, i=i, t=t: E.activation(out=hb[i][:], in_=xt[i][:], func=AF.Identity,
                                                           scale=rstd[:, t:t + 1]),
                   reads=[("xt", i), "rstdA"], writes=[("hb", i)])
            bank = t % 2
            psb = PS[bank][:].bitcast(BF16)
            for k in range(8):
                S.emit("pe", lambda E, k=k, i=i, psb=psb: E.transpose(out=psb[:, k * 128:(k + 1) * 128],
                                                                      in_=hb[i][:, k * 128:(k + 1) * 128],
                                                                      identity=ident[:]),
                       reads=[("hb", i), "ident"], writes=[("ps", bank)], signal=(k == 7))
            S.emit("dve", lambda E, t=t, psb=psb: E.tensor_copy(out=hT[:, :, t * 128:(t + 1) * 128],
                                                                in_=psb.rearrange("p (k c) -> p k c", k=8)),
                   reads=[("ps", bank)], writes=[("hT", t)])
        return hT

    def prep_B1(self, st):
        nc, S, d, P, PS = self.nc, self.S, self.d, self.P, self.PS
        wA = self.sb(st, "wA", [128, 8, 1536], BF16)
        stg = [self.sb(st, "stgA%d" % i, [128, 1536], F32) for i in range(3)]
        premix = P["premix"]
        for k in range(8):
            i = k % 3
            S.dma(stg[i][:], d["w_in"][k * 128:(k + 1) * 128, 0:1536], writes=[("stgA", i)])
            S.emit("pool", lambda E, k=k, i=i: E.tensor_scalar(out=wA[:, k, 0:512], in0=stg[i][:, 0:512],
                                                               scalar1=premix[:, k:k + 1], scalar2=0.125,
                                                               op0=ALU.mult, op1=ALU.mult),
                   reads=[("stgA", i), "premix"], writes=[("wA", k)])
            S.emit("pool", lambda E, k=k, i=i: E.tensor_scalar(out=wA[:, k, 512:1536], in0=stg[i][:, 512:1536],
                                                               scalar1=premix[:, k:k + 1], scalar2=None,
                                                               op0=ALU.mult),
                   reads=[("stgA", i), "premix"], writes=[("wA", k)])
        return wA

    def phase_B1(self, st, b, hT, A, wA):
        nc, S, d, P, PS = self.nc, self.S, self.d, self.P, self.PS
        qT, kT, vaug = A["qT"], A["kT"], A["vaug"]
        S.emit("pool", lambda E: E.memset(vaug[:, :, :, 1, :], 1.0), writes=["vones"])
        wA_all = [("wA", k) for k in range(8)]
        nb = 0
        for which, dst, name in ((0, qT, "qT"), (1, kT, "kT")):
            for pr in range(4):
                col0 = which * 512 + pr * 128
                for tc in range(4):
                    bank = 2 + (nb % 4)
                    nb += 1
                    for k in range(8):
                        S.emit("pe", lambda E, k=k, col0=col0, tc=tc, bank=bank: E.matmul(
                            PS[bank][:, :], lhsT=wA[:, k, col0:col0 + 128], rhs=hT[:, k, tc * 512:(tc + 1) * 512],
                            start=(k == 0), stop=(k == 7)),
                               reads=wA_all + [("hT", 4 * tc + j) for j in range(4)], writes=[("ps", bank)],
                               signal=(k == 7))
                    self.evac(dst[:, pr, tc * 512:(tc + 1) * 512], PS[bank][:, :], reads=[("ps", bank)],
                              writes=[(name, pr, tc)])
        for t in range(NT):
            bank = 2 + (nb % 4)
            nb += 1
            for k in range(8):
                S.emit("pe", lambda E, k=k, t=t, bank=bank: E.matmul(
                    PS[bank][:, :], lhsT=hT[:, k, t * 128:(t + 1) * 128], rhs=wA[:, k, 1024:1536],
                    start=(k == 0), stop=(k == 7)),
                       reads=wA_all + [("hT", t)], writes=[("ps", bank)], signal=(k == 7))
            self.evac(vaug[:, t, :, 0:3:2, :], PS[bank][:, :].rearrange("p (a b c) -> p a b c", a=4, b=2),
                      reads=[("ps", bank)], writes=[("vaug", t)])
        kmf = self.sb(st, "kmf", [128, 4, 8], F32)
        for pr in range(4):
            S.emit("dve", lambda E, pr=pr: E.tensor_reduce(out=kmf[:, pr, :],
                                                           in_=kT[:, pr, :].rearrange("p (n c) -> p n c", n=8),
                                                           axis=AX.X, op=ALU.add),
                   reads=[("kT", pr, tc) for tc in range(4)], writes=["kmf"])
        S.emit("dve", lambda E: E.tensor_scalar(out=A["kmT"][:], in0=kmf[:], scalar1=1.0 / 256, scalar2=None,
                                                op0=ALU.mult),
               reads=["kmf"], writes=["kmT"])

    def phase_C1(self, st, b, A):
        nc, S, d, P, PS = self.nc, self.S, self.d, self.P, self.PS
        sb = self.sb
        qT, kT, vaug, maskT, kmT = A["qT"], A["kT"], A["vaug"], A["maskT"], A["kmT"]
        ident = P["ident"]
        oT = P["oT"]
        IND = sb(st, "IND", [128, 64, 128], BF16)
        TT = sb(st, "TT", [128, 8, 2, 128], F32)
        rb31 = sb(st, "rb31", [128, 8], F32)
        PAST = sb(st, "PAST", [128, 8, 8, 8], F32)
        OWN = sb(st, "OWN", [128, 8, 8, 8], F32)
        PT = [[sb(st, "PT%d%d" % (h, i), [128, 512], BF16) for i in range(2)] for h in range(2)]
        rden = [sb(st, "rden%d" % h, [128, 512], F32) for h in range(2)]
        gsb = sb(st, "gsb", [128, 2, 8, 8], F32)
        g2 = sb(st, "g2", [128, 2, 8, 8], F32)
        eq = sb(st, "eq", [128, 2, 8, 8], F32)
        mx = sb(st, "mx", [128, 16], F32)
        mtok = sb(st, "mtok", [128, 128], BF16)

        S.emit("pool", lambda E: E.memset(IND[:], 0.0), writes=["IND"])
        S.emit("pool", lambda E: E.affine_select(out=IND[0:64], in_=IND[0:64], pattern=[[-1, 64], [0, 128]],
                                                  compare_op=ALU.not_equal, fill=1.0, base=0, channel_multiplier=1),
               reads=["IND"], writes=["IND"])
        S.emit("dve", lambda E: E.tensor_copy(out=IND[64:128], in_=IND[0:64]), reads=["IND"], writes=["IND"])
        S.emit("pool", lambda E: E.memset(PAST[:], 0.0), writes=["PAST"])
        S.emit("pool", lambda E: E.affine_select(out=PAST[:], in_=PAST[:], pattern=[[1, 8], [0, 8], [-1, 8]],
                                                  compare_op=ALU.is_gt, fill=-1e30, base=0, channel_multiplier=0),
               reads=["PAST"], writes=["PAST"])
        S.emit("pool", lambda E: E.memset(OWN[:], 0.0), writes=["OWN"])
        S.emit("pool", lambda E: E.affine_select(out=OWN[:], in_=OWN[:], pattern=[[1, 8], [0, 8], [-1, 8]],
                                                  compare_op=ALU.not_equal, fill=1.0, base=0, channel_multiplier=0),
               reads=["OWN"], writes=["OWN"])
        S.dma(TT[:].rearrange("p h a c -> p (h a c)"), d["ttab"], writes=["TT"])
        S.dma(rb31[:], d["rb31"].partition_broadcast(128), writes=["rb31"])
        S.emit("dve", lambda E: E.tensor_tensor(out=TT[:].rearrange("p h a c -> p h (a c)"),
                                                in0=TT[:].rearrange("p h a c -> p h (a c)"),
                                                in1=rb31[:].unsqueeze(2).to_broadcast([128, 8, 256]),
                                                op=ALU.subtract),
               reads=["TT", "rb31"], writes=["TT"])

        GB, TB = 6, 7
        import os
        FL = os.environ.get("C1FLAGS", "mask,main,toep,pv,norm,maskmm").split(",")
        if "mask" not in FL:
            S.emit("pool", lambda E: E.memset(maskT[:], 0.0), writes=[("maskT", qt) for qt in range(NT)])
        GBK = (6, 7)
        TB = 6

        def mask_pass(blk):
                for j in range(2):
                    qt = 2 * blk + j
                    for h in range(8):
                        pr, hh = h // 2, h % 2
                        S.emit("pe", lambda E, h=h, pr=pr, hh=hh, qt=qt, j=j: E.matmul(
                            PS[GBK[hh]][:, j * 32 + pr * 8:j * 32 + (pr + 1) * 8],
                            lhsT=qT[hh * 64:(hh + 1) * 64, pr, qt * 128:(qt + 1) * 128],
                            rhs=kmT[hh * 64:(hh + 1) * 64, pr, :], start=True, stop=True),
                               reads=[("qT", pr, qt // 4), "kmT"], writes=[("ps", GBK[hh])], signal=(j == 1 and h >= 6))
                for hh in range(2):
                    g3v = PS[GBK[hh]][:, 0:64].rearrange("p (j h n) -> p j h n", j=2, h=4)
                    S.emit("dve", lambda E, blk=blk, g3v=g3v, hh=hh: E.tensor_tensor(
                        out=gsb[:, :, hh:8:2, :], in0=g3v,
                        in1=PAST[:, blk, hh:8:2, :].unsqueeze(1).to_broadcast([128, 2, 4, 8]), op=ALU.add),
                           reads=[("ps", GBK[hh]), "PAST"], writes=["gsb"])
                cur = gsb
                mxb = mx[:].rearrange("p (j h) -> p j h", j=2).unsqueeze(3).to_broadcast([128, 2, 8, 8])
                for it in range(2):
                    S.emit("dve", lambda E, cur=cur: E.tensor_reduce(out=mx[:], in_=cur[:].rearrange("p j h n -> p (j h) n"),
                                                                     axis=AX.X, op=ALU.max),
                           reads=["gsb", "g2"], writes=["mx"])
                    S.emit("dve", lambda E, cur=cur: E.tensor_tensor(out=eq[:], in0=cur[:], in1=mxb, op=ALU.is_equal),
                           reads=["gsb", "g2", "mx"], writes=["eq"])
                    S.emit("dve", lambda E, cur=cur: E.scalar_tensor_tensor(out=g2[:], in0=eq[:], scalar=-1e30,
                                                                            in1=cur[:], op0=ALU.mult, op1=ALU.add),
                           reads=["eq", "gsb", "g2"], writes=["g2"])
                    cur = g2
                S.emit("dve", lambda E: E.tensor_reduce(out=mx[:], in_=g2[:].rearrange("p j h n -> p (j h) n"),
                                                        axis=AX.X, op=ALU.max),
                       reads=["g2"], writes=["mx"])
                S.emit("dve", lambda E: E.tensor_scalar(out=mx[:], in0=mx[:], scalar1=-1e29, scalar2=None, op0=ALU.max),
                       reads=["mx"], writes=["mx"])
                S.emit("dve", lambda E: E.tensor_tensor(out=eq[:], in0=gsb[:], in1=mxb, op=ALU.is_ge),
                       reads=["gsb", "mx"], writes=["eq"])
                S.emit("dve", lambda E, blk=blk: E.tensor_tensor(
                    out=eq[:], in0=eq[:], in1=OWN[:, blk, :, :].unsqueeze(1).to_broadcast([128, 2, 8, 8]), op=ALU.add),
                       reads=["eq", "OWN"], writes=["eq"])
                S.emit("dve", lambda E: E.tensor_scalar(out=mtok[:], in0=eq[:].rearrange("p j h n -> p (j h n)"),
                                                        scalar1=-1.0, scalar2=-NEG, op0=ALU.add, op1=ALU.mult),
                       reads=["eq"], writes=["mtok"])
                tb = PS[TB][:].bitcast(BF16)
                for j in range(2):
                    S.emit("pe", lambda E, tb=tb, j=j: E.transpose(out=tb[0:64, j * 128:(j + 1) * 128],
                                                                   in_=mtok[:, j * 64:(j + 1) * 64], identity=ident[:]),
                           reads=["mtok", "ident"], writes=[("ps", TB)], signal=(j == 1))
                S.emit("act", lambda E, tb=tb, blk=blk: E.activation(out=maskT[0:64, blk * 256:(blk + 1) * 256],
                                                                     in_=tb[0:64, 0:256], func=AF.Identity),
                       reads=[("ps", TB)], writes=[("maskT", 2 * blk), ("maskT", 2 * blk + 1)])
                S.emit("act", lambda E, tb=tb, blk=blk: E.activation(out=maskT[64:128, blk * 256:(blk + 1) * 256],
                                                                     in_=tb[0:64, 0:256], func=AF.Identity),
                       reads=[("ps", TB)], writes=[("maskT", 2 * blk), ("maskT", 2 * blk + 1)])

        vflat = vaug[:].rearrange("p t a b c -> p t a (b c)")

        def qk(pr, qc, kt):
            qs = max(0, kt * 128 - qc * 512)
            N = 512 - qs
            q0 = qc * 512 + qs
            for hh in range(2):
                h = 2 * pr + hh
                bank = hh * 2 + (kt % 2)
                rows = slice(hh * 64, (hh + 1) * 64)
                mm = "maskmm" in FL
                S.emit("pe", lambda E, bank=bank, rows=rows, N=N, q0=q0, mm=mm: E.matmul(
                    PS[bank][:, 0:N], lhsT=kT[rows, pr, kt * 128:(kt + 1) * 128], rhs=qT[rows, pr, q0:q0 + N],
                    start=True, stop=not mm),
                       reads=[("kT", pr, kt // 4), ("qT", pr, qc)], writes=[("ps", bank)], signal=not mm)
                if mm:
                    S.emit("pe", lambda E, bank=bank, h=h, N=N, q0=q0, rows=rows: E.matmul(
                        PS[bank][:, 0:N], lhsT=IND[rows, h * 8 + kt // 2, :], rhs=maskT[rows, q0:q0 + N],
                        start=False, stop=True),
                           reads=["IND"] + [("maskT", 4 * qc + j) for j in range(4)], writes=[("ps", bank)])
                for dq in range(2 if "toep" in FL else 0):
                    qt = kt + dq
                    if qt * 128 < q0 or qt >= (qc + 1) * 4:
                        continue
                    off = qt * 128 - q0
                    S.emit("dve", lambda E, bank=bank, off=off, h=h, dq=dq: E.tensor_tensor(
                        out=PS[bank][:, off:off + 128], in0=PS[bank][:, off:off + 128], in1=TT[:, h, dq, :],
                        op=ALU.add),
                           reads=[("ps", bank), "TT"], writes=[("ps", bank)])
                S.emit("act", lambda E, bank=bank, hh=hh, N=N: E.activation(
                    out=PT[hh][kt % 2][:, 0:N], in_=PS[bank][:, 0:N], func=AF.Exp),
                       reads=[("ps", bank)], writes=[("PT", hh, kt % 2)])

        def pv(pr, qc, kt, nkt):
            qs = max(0, kt * 128 - qc * 512)
            N = 512 - qs
            for hh in range(2):
                bank = 4 + hh
                S.emit("pe", lambda E, bank=bank, hh=hh, qs=qs, N=N: E.matmul(
                    PS[bank][:, qs:512], lhsT=vflat[:, kt, pr, hh * 64:hh * 64 + 128], rhs=PT[hh][kt % 2][:, 0:N],
                    start=(kt == 0), stop=(kt == nkt - 1)),
                       reads=[("PT", hh, kt % 2), ("vaug", kt), "vones"], writes=[("ps", bank)],
                       signal=(kt == nkt - 1))

        for qc in range(4 if "main" in FL else 0):
            if "mask" in FL:
                mask_pass(2 * qc)
                mask_pass(2 * qc + 1)
            for pr in range(4):
                nkt = 4 * (qc + 1)
                for kt in range(nkt):
                    qk(pr, qc, kt)
                    if kt > 0 and "pv" in FL:
                        pv(pr, qc, kt - 1, nkt)
                if "pv" in FL:
                    pv(pr, qc, nkt - 1, nkt)
                for hh in range(2 if "norm" in FL else 0):
                    bank = 4 + hh
                    orows = slice(hh * 64, (hh + 1) * 64)
                    drows = slice((1 - hh) * 64, (2 - hh) * 64)
                    S.emit("dve", lambda E, bank=bank, hh=hh, orows=orows, drows=drows: E.reciprocal(
                        out=rden[hh][orows, :], in_=PS[bank][drows, :]),
                           reads=[("ps", bank)], writes=[("rden", hh)])
                    S.emit("dve", lambda E, bank=bank, hh=hh, orows=orows, pr=pr, qc=qc: E.tensor_tensor(
                        out=oT[orows, pr, qc * 512:(qc + 1) * 512], in0=PS[bank][orows, :], in1=rden[hh][orows, :],
                        op=ALU.mult),
                           reads=[("ps", bank), ("rden", hh)], writes=[("oT", pr)])

        if "oT" in self.dbg_out:
            otf = sb(st, "otf", [128, 4, SEQ], F32)
            S.emit("dve", lambda E: E.tensor_copy(out=otf[:], in_=oT[:, 0:4, :]),
                   reads=[("oT", c) for c in range(4)], writes=["otf"])
            S.dma(self.dbg_out["oT"][b].rearrange("c p s -> p c s"), otf[:], reads=["otf"], writes=["dbg_oT"])

    def phase_B2(self, st, b, hT, G):
        nc, S, d, P, PS = self.nc, self.S, self.d, self.P, self.PS
        sb = self.sb
        premix = P["premix"]
        stgw = [sb(st, "stgw%d" % i, [128, 8, 128], F32) for i in range(2)]
        wc = [sb(st, "wc%d" % i, [128, 8, 128], BF16) for i in range(2)]
        wz = sb(st, "wz", [128, 8, 512], BF16)
        wab = sb(st, "wab", [128, 8, 16], BF16)
        pres = [sb(st, "pre%d" % i, [128, 3 + SEQ], F32) for i in range(2)]
        accs = [sb(st, "acc%d" % i, [128, SEQ], F32) for i in range(2)]
        sq = sb(st, "sqg", [128, SEQ], BF16)
        srs = [sb(st, "srg%d" % i, [128, 512], F32) for i in range(2)]
        cw = sb(st, "cw", [128, 12, 4], F32)
        BLK = sb(st, "BLK", [128, 128], BF16)
        dtb = sb(st, "dtb_b", [128, 8], F32)
        alog = sb(st, "alog_b", [128, 8], F32)
        S.dma(cw[:].rearrange("p c t -> p (c t)"), d["convw_pk"], writes=["cw"])
        S.dma(dtb[:], d["dtb"].partition_broadcast(128), writes=["dtb"])
        S.dma(alog[:], d["alog"].partition_broadcast(128), writes=["alog"])
        S.emit("pool", lambda E: E.memset(BLK[:], 0.0), writes=["BLK"])
        S.emit("pool", lambda E: E.memset(BLK[0:64, 0:64], 1.0), reads=["BLK"], writes=["BLK"])
        S.emit("pool", lambda E: E.memset(BLK[64:128, 64:128], 1.0), reads=["BLK"], writes=["BLK"])
        for i in range(2):
            S.emit("pool", lambda E, i=i: E.memset(pres[i][:, 0:3], 0.0), writes=[("pre0", i)])
        hT_all = [("hT", t) for t in range(NT)]
        self._nw = 0

        def load_w(col0, ncol, dst, dname):
            i = self._nw % 2
            self._nw += 1
            S.dma(stgw[i][:, :, 0:ncol], d["w_in"][:, col0:col0 + ncol].rearrange("(k p) c -> p k c", p=128),
                  writes=[("stgw", i)])
            S.emit("pool", lambda E, i=i, ncol=ncol, dst=dst: E.tensor_tensor(
                out=dst, in0=stgw[i][:, :, 0:ncol], in1=premix[:].unsqueeze(2).to_broadcast([128, 8, ncol]),
                op=ALU.mult),
                   reads=[("stgw", i), "premix"], writes=[dname])

        nb = 0
        dsts = (G["qT"], G["kT"], G["vT"])
        for c in range(12):
            wi = c % 2
            pre, acc = pres[wi], accs[wi]
            PRE, ACC, PRE0 = ("pre", wi), ("acc", wi), ("pre0", wi)
            load_w(1536 + c * 128, 128, wc[wi][:], ("wc", wi))
            for tc in range(4):
                bank = 4 + (nb % 4)
                nb += 1
                for k in range(8):
                    S.emit("pe", lambda E, k=k, tc=tc, bank=bank, wi=wi: E.matmul(
                        PS[bank][:, :], lhsT=wc[wi][:, k, :], rhs=hT[:, k, tc * 512:(tc + 1) * 512],
                        start=(k == 0), stop=(k == 7)),
                           reads=[("wc", wi)] + [("hT", 4 * tc + j) for j in range(4)], writes=[("ps", bank)],
                           signal=(k == 7))
                S.emit("act", lambda E, tc=tc, bank=bank, pre=pre: E.activation(
                    out=pre[:, 3 + tc * 512:3 + (tc + 1) * 512], in_=PS[bank][:, :], func=AF.Identity),
                       reads=[("ps", bank)], writes=[PRE])
            ce = "dve"
            S.emit(ce, lambda E, c=c, pre=pre, acc=acc: E.tensor_scalar(out=acc[:], in0=pre[:, 0:SEQ],
                                                                        scalar1=cw[:, c, 0:1], scalar2=None, op0=ALU.mult),
                   reads=[PRE, PRE0, "cw"], writes=[ACC])
            for tp in range(1, 4):
                S.emit(ce, lambda E, c=c, tp=tp, pre=pre, acc=acc: E.scalar_tensor_tensor(
                    out=acc[:], in0=pre[:, tp:tp + SEQ], scalar=cw[:, c, tp:tp + 1], in1=acc[:],
                    op0=ALU.mult, op1=ALU.add),
                       reads=[PRE, PRE0, "cw", ACC], writes=[ACC])
            dst = dsts[c // 4]
            dn = ("gq", "gk", "gv")[c // 4]
            S.emit("act", lambda E, dst=dst, c=c, acc=acc: E.activation(out=dst[:, c % 4, :], in_=acc[:], func=AF.Silu),
                   reads=[ACC], writes=[(dn, c % 4)])
        for c in range(8):
            dst = dsts[c // 4]
            dn = ("gq", "gk")[c // 4]
            pr = c % 4
            S.emit("act", lambda E, dst=dst, pr=pr: E.activation(out=sq[:], in_=dst[:, pr, :], func=AF.Square),
                   reads=[(dn, pr)], writes=["sqg"])
            for tc in range(4):
                bank = 4 + (nb % 4)
                nb += 1
                cs = slice(tc * 512, (tc + 1) * 512)
                S.emit("pe", lambda E, bank=bank, cs=cs: E.matmul(PS[bank][:, :], lhsT=BLK[:], rhs=sq[:, cs],
                                                                  start=True, stop=True),
                       reads=["sqg", "BLK"], writes=[("ps", bank)])
                sr = srs[tc % 2]
                SR = ("srg", tc % 2)
                S.emit("act", lambda E, bank=bank, sr=sr: E.activation(out=sr[:], in_=PS[bank][:, :], func=AF.Sqrt,
                                                                       bias=P["epsc"][:], scale=1.0),
                       reads=[("ps", bank), "epsc"], writes=[SR])
                S.emit("dve", lambda E, sr=sr: E.reciprocal(out=sr[:], in_=sr[:]), reads=[SR], writes=[SR])
                scl = 0.125 if c < 4 else 1.0
                S.emit("dve", lambda E, dst=dst, pr=pr, cs=cs, scl=scl, sr=sr: E.scalar_tensor_tensor(
                    out=dst[:, pr, cs], in0=dst[:, pr, cs], scalar=scl, in1=sr[:], op0=ALU.mult, op1=ALU.mult),
                       reads=[(dn, pr), SR], writes=[(dn, pr)])
        for j in range(4):
            load_w(3072 + j * 128, 128, wz[:, :, j * 128:(j + 1) * 128], "wz")
        load_w(3584, 16, wab[:], "wab")
        zs, gab = G["zs"], G["gab"]
        for t in range(NT):
            bank = 4 + (nb % 4)
            nb += 1
            tk = slice(t * 128, (t + 1) * 128)
            for k in range(8):
                S.emit("pe", lambda E, k=k, tk=tk, bank=bank: E.matmul(PS[bank][:, :], lhsT=hT[:, k, tk],
                                                                       rhs=wz[:, k, :], start=(k == 0), stop=(k == 7)),
                       reads=["wz", ("hT", t)], writes=[("ps", bank)], signal=(k == 7))
            S.emit("act", lambda E, t=t, bank=bank: E.activation(out=zs[:, t, :], in_=PS[bank][:, :], func=AF.Silu),
                   reads=[("ps", bank)], writes=[("gzs", t)])
            bank = 4 + (nb % 4)
            nb += 1
            for k in range(8):
                S.emit("pe", lambda E, k=k, tk=tk, bank=bank: E.matmul(PS[bank][:, 0:16], lhsT=hT[:, k, tk],
                                                                       rhs=wab[:, k, :], start=(k == 0), stop=(k == 7)),
                       reads=["wab", ("hT", t)], writes=[("ps", bank)], signal=(k == 7))
            S.emit("dve", lambda E, t=t, bank=bank: E.tensor_copy(out=gab[:, t, :], in_=PS[bank][:, 0:16]),
                   reads=[("ps", bank)], writes=["gab"])
        g, beta, nbeta = G["g"], G["beta"], G["nbeta"]
        S.emit("dve", lambda E: E.tensor_tensor(out=g[:], in0=gab[:, :, 0:8],
                                                in1=dtb[:].unsqueeze(1).to_broadcast([128, NT, 8]), op=ALU.add),
               reads=["gab", "dtb"], writes=["gg"])
        S.emit("act", lambda E: E.activation(out=g[:], in_=g[:], func=AF.Exp), reads=["gg"], writes=["gg"])
        S.emit("act", lambda E: E.activation(out=g[:], in_=g[:], func=AF.Ln, bias=1.0), reads=["gg"], writes=["gg"])
        S.emit("act", lambda E: E.activation(out=alog[:], in_=alog[:], func=AF.Exp), reads=["alog"], writes=["alog"])
        S.emit("dve", lambda E: E.scalar_tensor_tensor(out=g[:], in0=g[:], scalar=-1.0,
                                                       in1=alog[:].unsqueeze(1).to_broadcast([128, NT, 8]),
                                                       op0=ALU.mult, op1=ALU.mult),
               reads=["gg", "alog"], writes=["gg"])
        S.emit("act", lambda E: E.activation(out=beta[:], in_=gab[:, :, 8:16], func=AF.Sigmoid),
               reads=["gab"], writes=["gbeta"])
        S.emit("dve", lambda E: E.tensor_scalar(out=nbeta[:], in0=beta[:], scalar1=-1.0, scalar2=None, op0=ALU.mult),
               reads=["gbeta"], writes=["gnbeta"])

    def phase_C2(self, st, b, G):
        nc, S, d, P, PS = self.nc, self.S, self.d, self.P, self.PS
        sb = self.sb
        ident = P["ident"]
        oT = P["oT"]
        qTg, kTg, vTg, zs = G["qT"], G["kT"], G["vT"], G["zs"]
        g, beta, nbeta = G["g"], G["beta"], G["nbeta"]
        BIG = 3.0e38
        TRI = sb(st, "TRI", [128, 128], F32)
        BLKS = sb(st, "BLKS", [128, 128], F32)
        MASKU = sb(st, "MASKU", [128, 8, 128], F32)
        STRICT = sb(st, "STRICT", [128, 8, 128], F32)
        HEADM = sb(st, "HEADM", [8, 8, 1], F32)
        SEL = sb(st, "SEL", [8, 4, 128], F32)
        ONES8 = sb(st, "ONES8", [8, 128], F32)
        gnw = sb(st, "gnw_b", [128, 64], F32)
        S.dma(gnw[:], d["gnw"].partition_broadcast(128), writes=["gnw"])

        def tri_like(T, val, strict, name):
            nd = len(T.shape)
            pat = [[0, 8], [1, 128]] if nd == 3 else [[1, 128]]
            pat2 = [[0, 8], [-1, 128]] if nd == 3 else [[-1, 128]]
            S.emit("pool", lambda E: E.memset(T[:], val), writes=[name])
            S.emit("pool", lambda E: E.affine_select(out=T[:], in_=T[:], pattern=pat,
                                                      compare_op=(ALU.is_gt if strict else ALU.is_ge), fill=0.0,
                                                      base=0, channel_multiplier=-1),
                   reads=[name], writes=[name])
            S.emit("pool", lambda E: E.affine_select(out=T[0:64], in_=T[0:64], pattern=pat2,
                                                      compare_op=ALU.is_ge, fill=0.0, base=63, channel_multiplier=0),
                   reads=[name], writes=[name])

        tri_like(TRI, 1.0, False, "TRI")
        tri_like(MASKU, BIG, False, "MASKU")
        tri_like(STRICT, 1.0, True, "STRICT")
        S.emit("pool", lambda E: E.memset(BLKS[:], 0.0), writes=["BLKS"])
        S.emit("pool", lambda E: E.memset(BLKS[0:64, 0:64], 1.0), reads=["BLKS"], writes=["BLKS"])
        S.emit("pool", lambda E: E.memset(BLKS[64:128, 64:128], 1.0), reads=["BLKS"], writes=["BLKS"])
        S.emit("pool", lambda E: E.memset(HEADM[:], 0.0), writes=["HEADM"])
        S.emit("pool", lambda E: E.affine_select(out=HEADM[:], in_=HEADM[:], pattern=[[-1, 8], [0, 1]],
                                                  compare_op=ALU.not_equal, fill=1.0, base=0, channel_multiplier=1),
               reads=["HEADM"], writes=["HEADM"])
        S.emit("pool", lambda E: E.memset(SEL[:], 0.0), writes=["SEL"])
        for half in range(2):
            S.emit("pool", lambda E, half=half: E.affine_select(
                out=SEL[:, :, half * 64:(half + 1) * 64], in_=SEL[:, :, half * 64:(half + 1) * 64],
                pattern=[[-2, 4], [0, 64]], compare_op=ALU.not_equal, fill=1.0, base=-half, channel_multiplier=1),
                   reads=["SEL"], writes=["SEL"])
        S.emit("pool", lambda E: E.memset(ONES8[:], 1.0), writes=["ONES8"])

        import os
        NPS = int(os.environ.get("C2NPS", "1"))
        NSLOT = NPS + 1
        rhsBDs = [sb(st, "rhsBD%d" % i, [8, 8, 128], F32) for i in range(NPS)]
        gcTs = [sb(st, "gcT%d" % i, [8, 128], F32) for i in range(NPS)]
        gcts = [sb(st, "gct%d" % i, [128, 24], F32) for i in range(NPS)]
        egts = [sb(st, "egt%d" % i, [128, 16], F32) for i in range(NPS)]
        EAs = [sb(st, "EA%d" % i, [128, 8, 128], F32) for i in range(NPS)]
        EAsbs = [sb(st, "EAsb%d" % i, [128, 8, 128], F32) for i in range(NPS)]
        ktoks = [sb(st, "ktok%d" % i, [128, 8, 64], BF16) for i in range(NPS)]
        Bms = [[sb(st, "Bm%d%d" % (p, i), [128, 8, 128], BF16) for i in range(2)] for p in range(NPS)]
        Nms = [[sb(st, "Nm%d%d" % (p, i), [128, 8, 128], BF16) for i in range(2)] for p in range(NPS)]
        qzs = [sb(st, "qz%d" % i, [128, 2, 4, 128], BF16) for i in range(NPS)]
        kzs = [sb(st, "kz%d" % i, [128, 2, 4, 128], BF16) for i in range(NPS)]
        X0s = [sb(st, "X0_%d" % i, [128, 2, 8, 64], BF16) for i in range(NPS)]
        Bp0s = [sb(st, "Bp0_%d" % i, [128, 8, 128], BF16) for i in range(NPS)]
        EGs = [sb(st, "EG%d" % i, [128, 4, 128], F32) for i in range(NSLOT)]
        qdTs = [sb(st, "qdT%d" % i, [128, 4, 128], BF16) for i in range(NSLOT)]
        kdecs = [[sb(st, "kdec%d%d" % (p, i), [128, 8, 64], BF16) for i in range(2)] for p in range(NSLOT)]
        X1s = [sb(st, "X1_%d" % i, [128, 2, 8, 64], BF16) for i in range(NSLOT)]
        attnTs = [sb(st, "attnT%d" % i, [128, 8, 128], BF16) for i in range(NSLOT)]
        Bp1s = [sb(st, "Bp1_%d" % i, [128, 8, 128], BF16) for i in range(NSLOT)]
        nwTs = [sb(st, "nwT%d" % i, [128, 4, 128], BF16) for i in range(NSLOT)]
        identb = ident[:].unsqueeze(1)
        S32 = sb(st, "S32", [128, 4, 64], F32)
        tmpS = sb(st, "tmpS", [128, 4, 64], F32)
        Sb = sb(st, "Sb", [128, 4, 2, 64], BF16)
        vnew = sb(st, "vnew", [128, 8, 64], BF16)
        osb = sb(st, "osb", [128, 8, 64], F32)
        osq = sb(st, "osq", [128, 8, 64], F32)
        oss = sb(st, "oss", [128, 16], F32)
        og = sb(st, "og", [128, 512], BF16)
        for i in range(NPS):
            S.emit("pool", lambda E, i=i: E.memset(qzs[i][:], 0.0), writes=[("qz", i)])
            S.emit("pool", lambda E, i=i: E.memset(kzs[i][:], 0.0), writes=[("kz", i)])
        S.emit("dve", lambda E: E.memset(S32[:], 0.0), writes=["S32"])
        S.emit("dve", lambda E: E.memset(Sb[:], 0.0), writes=["Sb"])
        S.emit("dve", lambda E: E.memset(vnew[:], 0.0), writes=["vnew"])
        for p in range(NSLOT):
            for i in range(2):
                S.emit("pool", lambda E, p=p, i=i: E.memset(kdecs[p][i][:], 0.0), writes=[("kdec", p, i)])
        self._bp = 0
        self._bs = 0
        PBANKS = (4, 5, 1, 2)
        SBANKS = (6, 7)

        def pbank():
            self._bp += 1
            return PBANKS[self._bp % 4]

        def sbank():
            self._bs += 1
            return SBANKS[self._bs % 2]

        def gen_P(t):
            par = t % NSLOT
            ps = t % NPS
            tk = slice(t * 128, (t + 1) * 128)
            gt = g[:, t, :]
            EG, qdT, kdec, attnT, nwT = EGs[par], qdTs[par], kdecs[par], attnTs[par], nwTs[par]
            X = (X0s[ps], X1s[par])
            Bp = (Bp0s[ps], Bp1s[par])
            XN = (("X0", ps), ("X", par, 1))
            BPN = (("Bp0", ps), ("Bp", par, 1))
            rhsBD, gcT, gct, egt, EA, EAsb, ktok = rhsBDs[ps], gcTs[ps], gcts[ps], egts[ps], EAs[ps], EAsbs[ps], ktoks[ps]
            Bm, Nm, qz, kz = Bms[ps], Nms[ps], qzs[ps], kzs[ps]
            RB, GC, GT_, EGT, EAn, EASn, KT = ("rhsBD", ps), ("gcT", ps), ("gct", ps), ("egt", ps), ("EA", ps), ("EAsb", ps), ("ktok", ps)
            QZ, KZ = ("qz", ps), ("kz", ps)
            MYB = (0, 1, 2) if ps == 0 else (3, 4, 5)
            ROT = MYB if NPS == 2 else (3, 4, 5, 1, 2)
            B0, B1_, B2_ = MYB
            rot = [0]

            def pbank():
                rot[0] += 1
                return ROT[rot[0] % len(ROT)]
            if NPS == 1 and os.environ.get("C2REORD", "1") == "1":
                bk, bv, RBK, kq, kk = 2, 3, 1, (4, 5), (2, 3)
                S.emit("pe", lambda E: E.matmul(PS[B0][0:8, 0:128], lhsT=gt, rhs=TRI[:], start=True, stop=True),
                       reads=["gg", "TRI"], writes=[("ps", B0)], signal=False)
                S.emit("pe", lambda E: E.matmul(PS[B0][:, 128:136], lhsT=TRI[:], rhs=gt, start=True, stop=True),
                       reads=["gg", "TRI"], writes=[("ps", B0)], signal=False)
                S.emit("pe", lambda E: E.matmul(PS[B0][:, 136:144], lhsT=BLKS[:], rhs=gt, start=True, stop=True),
                       reads=["gg", "BLKS"], writes=[("ps", B0)])
                tbk = PS[bk][:].bitcast(BF16)
                for pr in range(4):
                    S.emit("pe", lambda E, pr=pr, tbk=tbk: E.transpose(out=tbk[:, pr * 128:(pr + 1) * 128],
                                                                       in_=kTg[:, pr, tk], identity=ident[:]),
                           reads=[("gk", pr), "ident"], writes=[("ps", bk)], signal=(pr == 3))
                tbv = PS[bv][:].bitcast(BF16)
                for pr in range(4):
                    S.emit("pe", lambda E, pr=pr, tbv=tbv: E.transpose(out=tbv[:, pr * 128:(pr + 1) * 128],
                                                                       in_=vTg[:, pr, tk], identity=ident[:]),
                           reads=[("gv", pr), "ident"], writes=[("ps", bv)], signal=(pr == 3))
                yield
                S.emit("act", lambda E: E.activation(out=gcT[:], in_=PS[B0][0:8, 0:128], func=AF.Identity),
                       reads=[("ps", B0)], writes=[GC])
                S.emit("dve", lambda E: E.tensor_copy(out=gct[:, 0:16], in_=PS[B0][:, 128:144]),
                       reads=[("ps", B0)], writes=[GT_])
                S.emit("dve", lambda E: E.tensor_tensor(out=gct[:, 16:24], in0=gct[:, 8:16], in1=gct[:, 0:8],
                                                        op=ALU.subtract), reads=[GT_], writes=[GT_])
                for hh in range(2):
                    rows = slice(hh * 64, (hh + 1) * 64)
                    S.emit("pool", lambda E, hh=hh, rows=rows: E.tensor_copy(out=kz[rows, hh, :, :], in_=kTg[rows, :, tk]),
                           reads=[("gk", pr) for pr in range(4)], writes=[KZ])
                S.emit("dve", lambda E: E.tensor_tensor(out=rhsBD[:], in0=HEADM[:].to_broadcast([8, 8, 128]),
                                                        in1=gcT[:].unsqueeze(1).to_broadcast([8, 8, 128]), op=ALU.mult),
                       reads=[GC, "HEADM"], writes=[RB])
                yield
                S.emit("act", lambda E, tbk=tbk: E.activation(out=ktok[:].rearrange("p h d -> p (h d)"),
                                                              in_=tbk[:, 0:512], func=AF.Identity),
                       reads=[("ps", bk)], writes=[KT])
                S.emit("act", lambda E: E.activation(out=egt[:, 0:8], in_=gct[:, 0:8], func=AF.Exp),
                       reads=[GT_], writes=[EGT])
                S.emit("act", lambda E: E.activation(out=egt[:, 8:16], in_=gct[:, 16:24], func=AF.Exp),
                       reads=[GT_, EGT], writes=[EGT])
                X0 = X[0]
                S.emit("dve", lambda E, tbv=tbv: E.tensor_copy(out=X0[:, 0, :, :],
                                                               in_=tbv[:, 0:512].rearrange("p (h d) -> p h d", h=8)),
                       reads=[("ps", bv)], writes=[XN[0] + (0,), XN[0] + (1,)])
                yield
                for hf in range(2):
                    S.emit("pe", lambda E, hf=hf: E.matmul(
                        PS[RBK][:, :], lhsT=ONES8[:], rhs=rhsBD[:, 4 * hf:4 * hf + 4, :].rearrange("p h i -> p (h i)"),
                        start=True, stop=True),
                           reads=[RB, "ONES8"], writes=[("ps", RBK)])
                    S.emit("dve", lambda E, hf=hf: E.tensor_tensor(
                        out=EA[:, 4 * hf:4 * hf + 4, :], in0=PS[RBK][:, :].rearrange("p (h i) -> p h i", h=4),
                        in1=gct[:, 4 * hf:4 * hf + 4].unsqueeze(2).to_broadcast([128, 4, 128]), op=ALU.subtract),
                           reads=[("ps", RBK), GT_], writes=[EAn])
                    for h in range(4 * hf, 4 * hf + 4):
                        pr, hh = h // 2, h % 2
                        S.emit("pe", lambda E, pr=pr, hh=hh: E.matmul(
                            PS[kq[hh]][:, pr * 128:(pr + 1) * 128], lhsT=kz[:, hh, pr, :], rhs=qTg[:, pr, tk],
                            start=True, stop=True),
                               reads=[("gq", pr), KZ], writes=[("ps", kq[hh])], signal=(h >= 6))
                    yield
                S.emit("act", lambda E: E.activation(out=EA[:], in_=EA[:], func=AF.Exp), reads=[EAn], writes=[EAn])
                for pr in range(4):
                    S.emit("pe", lambda E, pr=pr: E.matmul(PS[B0][:, pr * 128:(pr + 1) * 128], lhsT=SEL[:, pr, :],
                                                           rhs=gcT[:], start=True, stop=True),
                           reads=[GC, "SEL"], writes=[("ps", B0)], signal=(pr == 3))
                for h in range(8):
                    pr, hh = h // 2, h % 2
                    S.emit("pe", lambda E, pr=pr, hh=hh: E.matmul(
                        PS[kk[hh]][:, pr * 128:(pr + 1) * 128], lhsT=kz[:, hh, pr, :], rhs=kTg[:, pr, tk],
                        start=True, stop=True),
                           reads=[("gk", pr), KZ], writes=[("ps", kk[hh])], signal=(h >= 6))
                yield
                S.emit("dve", lambda E: E.tensor_tensor(out=EA[:], in0=EA[:], in1=MASKU[:], op=ALU.min),
                       reads=[EAn, "MASKU"], writes=[EAn])
                S.emit("act", lambda E: E.activation(out=EG[:].rearrange("p a i -> p (a i)"), in_=PS[B0][:, :], func=AF.Exp),
                       reads=[("ps", B0)], writes=[("EG", par)])
                S.emit("pool", lambda E: E.tensor_tensor(out=EAsb[:], in0=EA[:], in1=STRICT[:], op=ALU.mult),
                       reads=[EAn, "STRICT"], writes=[EASn])
                S.emit("pool", lambda E: E.tensor_tensor(out=EAsb[:], in0=EAsb[:],
                                                         in1=nbeta[:, t, :].unsqueeze(2).to_broadcast([128, 8, 128]),
                                                         op=ALU.mult),
                       reads=[EASn, "gnbeta"], writes=[EASn])
                yield
                for hh in range(2):
                    S.emit("dve", lambda E, hh=hh: E.tensor_tensor(
                        out=attnT[:, hh:8:2, :], in0=PS[kq[hh]][:, :].rearrange("p (a i) -> p a i", a=4),
                        in1=EA[:, hh:8:2, :], op=ALU.mult),
                           reads=[("ps", kq[hh]), EAn], writes=[("attnT", par)])
                S.emit("dve", lambda E: E.tensor_tensor(out=X0[:, 1, :, :], in0=ktok[:],
                                                        in1=egt[:, 0:8].unsqueeze(2).to_broadcast([128, 8, 64]),
                                                        op=ALU.mult),
                       reads=[KT, EGT, XN[0] + (0,), XN[0] + (1,)], writes=[XN[0] + (0,), XN[0] + (1,)])
                for hh in range(2):
                    S.emit("dve", lambda E, hh=hh: E.tensor_tensor(
                        out=Bm[0][:, hh:8:2, :], in0=PS[kk[hh]][:, :].rearrange("p (a i) -> p a i", a=4),
                        in1=EAsb[:, hh:8:2, :], op=ALU.mult),
                           reads=[("ps", kk[hh]), EASn], writes=[("Bm", ps, 0, 0), ("Bm", ps, 0, 1)])
                S.emit("pool", lambda E: E.tensor_tensor(out=qdT[:], in0=qTg[:, :, tk], in1=EG[:], op=ALU.mult),
                       reads=[("EG", par)] + [("gq", pr) for pr in range(4)], writes=[("qdT", par)])
                for hf in range(2):
                    rows = slice(hf * 64, (hf + 1) * 64)
                    S.emit("pool", lambda E, hf=hf, rows=rows: E.tensor_tensor(
                        out=kdec[hf][rows], in0=ktok[rows],
                        in1=egt[rows, 8:16].unsqueeze(2).to_broadcast([64, 8, 64]), op=ALU.mult),
                           reads=[KT, EGT], writes=[("kdec", par, hf)])
                S.emit("pool", lambda E: E.tensor_tensor(out=Bp[0][:], in0=Bm[0][:],
                                                         in1=identb.to_broadcast([128, 8, 128]), op=ALU.add),
                       reads=[("Bm", ps, 0, 0), ("Bm", ps, 0, 1), "ident"], writes=[BPN[0] + (0,), BPN[0] + (1,)])
                yield
            else:
                S.emit("pe", lambda E: E.matmul(PS[B0][0:8, 0:128], lhsT=gt, rhs=TRI[:], start=True, stop=True),
                       reads=["gg", "TRI"], writes=[("ps", B0)], signal=False)
                S.emit("pe", lambda E: E.matmul(PS[B0][:, 128:136], lhsT=TRI[:], rhs=gt, start=True, stop=True),
                       reads=["gg", "TRI"], writes=[("ps", B0)], signal=False)
                S.emit("pe", lambda E: E.matmul(PS[B0][:, 136:144], lhsT=BLKS[:], rhs=gt, start=True, stop=True),
                       reads=["gg", "BLKS"], writes=[("ps", B0)])
                yield
                S.emit("act", lambda E: E.activation(out=gcT[:], in_=PS[B0][0:8, 0:128], func=AF.Identity),
                       reads=[("ps", B0)], writes=[GC])
                S.emit("dve", lambda E: E.tensor_copy(out=gct[:, 0:16], in_=PS[B0][:, 128:144]),
                       reads=[("ps", B0)], writes=[GT_])
                S.emit("dve", lambda E: E.tensor_tensor(out=gct[:, 16:24], in0=gct[:, 8:16], in1=gct[:, 0:8],
                                                        op=ALU.subtract), reads=[GT_], writes=[GT_])
                S.emit("act", lambda E: E.activation(out=egt[:, 0:8], in_=gct[:, 0:8], func=AF.Exp),
                       reads=[GT_], writes=[EGT])
                S.emit("act", lambda E: E.activation(out=egt[:, 8:16], in_=gct[:, 16:24], func=AF.Exp),
                       reads=[GT_, EGT], writes=[EGT])
                yield
                S.emit("dve", lambda E: E.tensor_tensor(out=rhsBD[:], in0=HEADM[:].to_broadcast([8, 8, 128]),
                                                        in1=gcT[:].unsqueeze(1).to_broadcast([8, 8, 128]), op=ALU.mult),
                       reads=[GC, "HEADM"], writes=[RB])
                for hf in range(2):
                    S.emit("pe", lambda E, hf=hf: E.matmul(
                        PS[MYB[1 + hf]][:, :], lhsT=ONES8[:], rhs=rhsBD[:, 4 * hf:4 * hf + 4, :].rearrange("p h i -> p (h i)"),
                        start=True, stop=True),
                           reads=[RB, "ONES8"], writes=[("ps", MYB[1 + hf])])
                yield
                for hf in range(2):
                    S.emit("dve", lambda E, hf=hf: E.tensor_tensor(
                        out=EA[:, 4 * hf:4 * hf + 4, :], in0=PS[MYB[1 + hf]][:, :].rearrange("p (h i) -> p h i", h=4),
                        in1=gct[:, 4 * hf:4 * hf + 4].unsqueeze(2).to_broadcast([128, 4, 128]), op=ALU.subtract),
                           reads=[("ps", MYB[1 + hf]), GT_], writes=[EAn])
                S.emit("act", lambda E: E.activation(out=EA[:], in_=EA[:], func=AF.Exp), reads=[EAn], writes=[EAn])
                yield
                S.emit("dve", lambda E: E.tensor_tensor(out=EA[:], in0=EA[:], in1=MASKU[:], op=ALU.min),
                       reads=[EAn, "MASKU"], writes=[EAn])
                S.emit("pool", lambda E: E.tensor_tensor(out=EAsb[:], in0=EA[:], in1=STRICT[:], op=ALU.mult),
                       reads=[EAn, "STRICT"], writes=[EASn])
                S.emit("pool", lambda E: E.tensor_tensor(out=EAsb[:], in0=EAsb[:],
                                                         in1=nbeta[:, t, :].unsqueeze(2).to_broadcast([128, 8, 128]),
                                                         op=ALU.mult),
                       reads=[EASn, "gnbeta"], writes=[EASn])
                yield
                for pr in range(4):
                    S.emit("pe", lambda E, pr=pr: E.matmul(PS[B0][:, pr * 128:(pr + 1) * 128], lhsT=SEL[:, pr, :],
                                                           rhs=gcT[:], start=True, stop=True),
                           reads=[GC, "SEL"], writes=[("ps", B0)], signal=(pr == 3))
                S.emit("act", lambda E: E.activation(out=EG[:].rearrange("p a i -> p (a i)"), in_=PS[B0][:, :], func=AF.Exp),
                       reads=[("ps", B0)], writes=[("EG", par)])
                S.emit("pool", lambda E: E.tensor_tensor(out=qdT[:], in0=qTg[:, :, tk], in1=EG[:], op=ALU.mult),
                       reads=[("EG", par)] + [("gq", pr) for pr in range(4)], writes=[("qdT", par)])
                yield
                bk = pbank()
                tbk = PS[bk][:].bitcast(BF16)
                for pr in range(4):
                    S.emit("pe", lambda E, pr=pr, tbk=tbk: E.transpose(out=tbk[:, pr * 128:(pr + 1) * 128],
                                                                       in_=kTg[:, pr, tk], identity=ident[:]),
                           reads=[("gk", pr), "ident"], writes=[("ps", bk)], signal=(pr == 3))
                S.emit("act", lambda E, tbk=tbk: E.activation(out=ktok[:].rearrange("p h d -> p (h d)"),
                                                              in_=tbk[:, 0:512], func=AF.Identity),
                       reads=[("ps", bk)], writes=[KT])
                bv = pbank()
                tbv = PS[bv][:].bitcast(BF16)
                for pr in range(4):
                    S.emit("pe", lambda E, pr=pr, tbv=tbv: E.transpose(out=tbv[:, pr * 128:(pr + 1) * 128],
                                                                       in_=vTg[:, pr, tk], identity=ident[:]),
                           reads=[("gv", pr), "ident"], writes=[("ps", bv)], signal=(pr == 3))
                yield
                X0 = X[0]
                S.emit("dve", lambda E, tbv=tbv: E.tensor_copy(out=X0[:, 0, :, :],
                                                               in_=tbv[:, 0:512].rearrange("p (h d) -> p h d", h=8)),
                       reads=[("ps", bv)], writes=[XN[0] + (0,), XN[0] + (1,)])
                S.emit("dve", lambda E: E.tensor_tensor(out=X0[:, 1, :, :], in0=ktok[:],
                                                        in1=egt[:, 0:8].unsqueeze(2).to_broadcast([128, 8, 64]),
                                                        op=ALU.mult),
                       reads=[KT, EGT, XN[0] + (0,), XN[0] + (1,)], writes=[XN[0] + (0,), XN[0] + (1,)])
                for hf in range(2):
                    rows = slice(hf * 64, (hf + 1) * 64)
                    S.emit("pool", lambda E, hf=hf, rows=rows: E.tensor_tensor(
                        out=kdec[hf][rows], in0=ktok[rows],
                        in1=egt[rows, 8:16].unsqueeze(2).to_broadcast([64, 8, 64]), op=ALU.mult),
                           reads=[KT, EGT], writes=[("kdec", par, hf)])
                yield
                for hh in range(2):
                    rows = slice(hh * 64, (hh + 1) * 64)
                    S.emit("pool", lambda E, hh=hh, rows=rows: E.tensor_copy(out=qz[rows, hh, :, :], in_=qTg[rows, :, tk]),
                           reads=[("gq", pr) for pr in range(4)], writes=[QZ])
                    S.emit("pool", lambda E, hh=hh, rows=rows: E.tensor_copy(out=kz[rows, hh, :, :], in_=kTg[rows, :, tk]),
                           reads=[("gk", pr) for pr in range(4)], writes=[KZ])
                kq = (pbank(), pbank())
                for h in range(8):
                    pr, hh = h // 2, h % 2
                    S.emit("pe", lambda E, pr=pr, hh=hh: E.matmul(
                        PS[kq[hh]][:, pr * 128:(pr + 1) * 128], lhsT=kTg[:, pr, tk], rhs=qz[:, hh, pr, :],
                        start=True, stop=True),
                           reads=[("gk", pr), QZ], writes=[("ps", kq[hh])], signal=(h >= 6))
                for hh in range(2):
                    S.emit("dve", lambda E, hh=hh: E.tensor_tensor(
                        out=attnT[:, hh:8:2, :], in0=PS[kq[hh]][:, :].rearrange("p (a i) -> p a i", a=4),
                        in1=EA[:, hh:8:2, :], op=ALU.mult),
                           reads=[("ps", kq[hh]), EAn], writes=[("attnT", par)])
                yield
                kk = (pbank(), pbank())
                for h in range(8):
                    pr, hh = h // 2, h % 2
                    S.emit("pe", lambda E, pr=pr, hh=hh: E.matmul(
                        PS[kk[hh]][:, pr * 128:(pr + 1) * 128], lhsT=kTg[:, pr, tk], rhs=kz[:, hh, pr, :],
                        start=True, stop=True),
                           reads=[("gk", pr), KZ], writes=[("ps", kk[hh])], signal=(h >= 6))
                for hh in range(2):
                    S.emit("dve", lambda E, hh=hh: E.tensor_tensor(
                        out=Bm[0][:, hh:8:2, :], in0=PS[kk[hh]][:, :].rearrange("p (a i) -> p a i", a=4),
                        in1=EAsb[:, hh:8:2, :], op=ALU.mult),
                           reads=[("ps", kk[hh]), EASn], writes=[("Bm", ps, 0, 0), ("Bm", ps, 0, 1)])
                S.emit("pool", lambda E: E.tensor_tensor(out=Bp[0][:], in0=Bm[0][:], in1=identb.to_broadcast([128, 8, 128]),
                                                         op=ALU.add),
                       reads=[("Bm", ps, 0, 0), ("Bm", ps, 0, 1), "ident"], writes=[BPN[0] + (0,), BPN[0] + (1,)])
                yield
            for a in range(2):
                bn = pbank()
                for h4 in range(4):
                    h = 4 * a + h4
                    S.emit("pe", lambda E, h=h, h4=h4, bn=bn: E.matmul(
                        PS[bn][:, h4 * 128:(h4 + 1) * 128], lhsT=Bm[0][:, h, :], rhs=ident[:], start=True, stop=True),
                           reads=[("Bm", ps, 0, a), "ident"], writes=[("ps", bn)], signal=(h4 == 3))
                self.evac(Nm[0][:, 4 * a:4 * a + 4, :].rearrange("p h i -> p (h i)"), PS[bn][:, :],
                          reads=[("ps", bn)], writes=[("Nm", ps, 0, a)])
                yield
            for lv in range(5):
                ci, ni = lv % 2, (lv + 1) % 2
                for a in range(2):
                    if lv < 4:
                        bnn = pbank()
                        for h4 in range(4):
                            h = 4 * a + h4
                            S.emit("pe", lambda E, h=h, h4=h4, bnn=bnn, ci=ci: E.matmul(
                                PS[bnn][:, h4 * 128:(h4 + 1) * 128], lhsT=Bm[ci][:, h, :], rhs=Nm[ci][:, h, :],
                                start=True, stop=True),
                                   reads=[("Nm", ps, ci, a), ("Bm", ps, ci, a)], writes=[("ps", bnn)], signal=(h4 == 3))
                        self.evac(Nm[ni][:, 4 * a:4 * a + 4, :].rearrange("p h i -> p (h i)"), PS[bnn][:, :],
                                  reads=[("ps", bnn)], writes=[("Nm", ps, ni, a)])
                        yield
                    bbb = pbank()
                    for h4 in range(4):
                        h = 4 * a + h4
                        S.emit("pe", lambda E, h=h, h4=h4, bbb=bbb, ci=ci: E.matmul(
                            PS[bbb][:, h4 * 128:(h4 + 1) * 128], lhsT=Nm[ci][:, h, :], rhs=Bm[ci][:, h, :],
                            start=True, stop=True),
                               reads=[("Nm", ps, ci, a), ("Bm", ps, ci, a)], writes=[("ps", bbb)], signal=(h4 == 3))
                    self.evac(Bm[ni][:, 4 * a:4 * a + 4, :].rearrange("p h i -> p (h i)"), PS[bbb][:, :],
                              reads=[("ps", bbb)], writes=[("Bm", ps, ni, a)])
                    S.emit("pool", lambda E, a=a, ni=ni: E.tensor_tensor(
                        out=Bp[ni][:, 4 * a:4 * a + 4, :], in0=Bm[ni][:, 4 * a:4 * a + 4, :],
                        in1=identb.to_broadcast([128, 4, 128]), op=ALU.add),
                           reads=[("Bm", ps, ni, a), "ident"], writes=[BPN[ni] + (a,)])
                    yield
                for a in range(2):
                    bx = pbank()
                    for h4 in range(4):
                        h = 4 * a + h4
                        S.emit("pe", lambda E, h=h, h4=h4, bx=bx, ci=ci: E.matmul(
                            PS[bx][:, h4 * 128:(h4 + 1) * 128], lhsT=Bp[ci][:, h, :], rhs=X[ci][:, :, h, :],
                            start=True, stop=True),
                               reads=[XN[ci] + (a,), BPN[ci] + (a,)], writes=[("ps", bx)], signal=(h4 == 3))
                    self.evac(X[ni][:, :, 4 * a:4 * a + 4, :].rearrange("p s h d -> p h s d"),
                              PS[bx][:, :].rearrange("p (h s d) -> p h s d", h=4, s=2),
                              reads=[("ps", bx)], writes=[XN[ni] + (a,)])
                    yield
            X5, B5 = X[1], Bp[1]

        def gen_S(t):
            par = t % NSLOT
            tk = slice(t * 128, (t + 1) * 128)
            EG, qdT, kdec, attnT, nwT = EGs[par], qdTs[par], kdecs[par], attnTs[par], nwTs[par]
            X5, B5 = X1s[par], Bp1s[par]
            for a in range(2):
                bw = sbank()
                for h4 in range(4):
                    h = 4 * a + h4
                    pr = h // 2
                    lw = X5[:, 1, 2 * pr:2 * pr + 2, :].rearrange("p h d -> p (h d)")
                    S.emit("pe", lambda E, h=h, h4=h4, bw=bw, lw=lw: E.matmul(
                        PS[bw][:, h4 * 128:(h4 + 1) * 128], lhsT=lw, rhs=B5[:, h, :], start=True, stop=True),
                           reads=[("X", par, 1, a), ("Bp", par, 1, a)], writes=[("ps", bw)], signal=(h4 == 3))
                for h4 in range(4):
                    h = 4 * a + h4
                    pr, hh = h // 2, h % 2
                    rows = slice(hh * 64, (hh + 1) * 64)
                    S.emit("dve", lambda E, h4=h4, bw=bw, pr=pr, rows=rows: E.tensor_scalar(
                        out=nwT[rows, pr, :], in0=PS[bw][rows, h4 * 128:(h4 + 1) * 128], scalar1=-1.0, scalar2=None,
                        op0=ALU.mult),
                           reads=[("ps", bw)], writes=[("nwT", par)])
                yield
            for hf in range(2):
                rows = slice(hf * 64, (hf + 1) * 64)
                bvn = sbank()
                for h in range(8):
                    pr, hh = h // 2, h % 2
                    cs = slice(h * 64, (h + 1) * 64)
                    S.emit("pe", lambda E, h=h, cs=cs, bvn=bvn: E.matmul(
                        PS[bvn][:, cs], lhsT=B5[:, h, :], rhs=X5[:, 0, h, :], start=True, stop=False),
                           reads=[("X", par, 1, 0), ("X", par, 1, 1), ("Bp", par, 1, 0), ("Bp", par, 1, 1)], writes=[("ps", bvn)], signal=False)
                    S.emit("pe", lambda E, pr=pr, hh=hh, cs=cs, bvn=bvn: E.matmul(
                        PS[bvn][:, cs], lhsT=nwT[:, pr, :], rhs=Sb[:, pr, hh, :], start=False, stop=True),
                           reads=[("nwT", par), "Sb"], writes=[("ps", bvn)], signal=(h == 7))
                yield
                S.emit("dve", lambda E, rows=rows, bvn=bvn: E.tensor_tensor(
                    out=vnew[rows], in0=PS[bvn][rows, :].rearrange("p (h d) -> p h d", h=8),
                    in1=beta[rows, t, :].unsqueeze(2).to_broadcast([64, 8, 64]), op=ALU.mult),
                       reads=[("ps", bvn), "gbeta"], writes=["vnew"])
                yield
                bo = sbank()
                for h in range(8):
                    pr, hh = h // 2, h % 2
                    cs = slice(h * 64, (h + 1) * 64)
                    S.emit("pe", lambda E, pr=pr, hh=hh, cs=cs, bo=bo: E.matmul(
                        PS[bo][:, cs], lhsT=qdT[:, pr, :], rhs=Sb[:, pr, hh, :], start=True, stop=False),
                           reads=[("qdT", par), "Sb"], writes=[("ps", bo)], signal=False)
                    S.emit("pe", lambda E, h=h, cs=cs, bo=bo: E.matmul(
                        PS[bo][:, cs], lhsT=attnT[:, h, :], rhs=vnew[:, h, :], start=False, stop=True),
                           reads=[("attnT", par), "vnew"], writes=[("ps", bo)], signal=(h == 7))
                yield
                S.emit("act", lambda E, rows=rows, bo=bo: E.activation(
                    out=osb[rows].rearrange("p h d -> p (h d)"), in_=PS[bo][rows, :], func=AF.Identity),
                       reads=[("ps", bo)], writes=["osb"])
                bs = sbank()
                for h in range(8):
                    pr, hh = h // 2, h % 2
                    S.emit("pe", lambda E, h=h, pr=pr, hh=hh, bs=bs, hf=hf: E.matmul(
                        PS[bs][:, (pr * 2 + hh) * 64:(pr * 2 + hh + 1) * 64],
                        lhsT=kdec[hf][:, 2 * pr:2 * pr + 2, :].rearrange("p h d -> p (h d)"), rhs=vnew[:, h, :],
                        start=True, stop=True),
                           reads=[("kdec", par, hf), "vnew"], writes=[("ps", bs)], signal=(h == 7))
                yield
                gl = EG[:, :, hf * 64 + 63:hf * 64 + 64]
                S.emit("dve", lambda E, gl=gl: E.tensor_tensor(out=tmpS[:], in0=S32[:],
                                                               in1=gl.to_broadcast([128, 4, 64]), op=ALU.mult),
                       reads=["S32", ("EG", par)], writes=["tmpS"])
                dS = PS[bs][:, :].rearrange("p (a b d) -> p a b d", a=4, b=2)
                for hh in range(2):
                    r2 = slice(hh * 64, (hh + 1) * 64)
                    S.emit("dve", lambda E, hh=hh, r2=r2, dS=dS: E.tensor_tensor(
                        out=S32[r2], in0=tmpS[r2], in1=dS[r2, :, hh, :], op=ALU.add),
                           reads=["tmpS", ("ps", bs)], writes=["S32"])
                    S.emit("act", lambda E, hh=hh, r2=r2: E.activation(out=Sb[r2, :, hh, :], in_=S32[r2],
                                                                       func=AF.Identity),
                           reads=["S32"], writes=["Sb"])
                yield
            S.emit("pool", lambda E: E.tensor_tensor(out=osq[:], in0=osb[:], in1=osb[:], op=ALU.mult),
                   reads=["osb"], writes=["osq"])
            S.emit("dve", lambda E: E.tensor_reduce(out=oss[:, 0:8], in_=osq[:], axis=AX.X, op=ALU.add),
                   reads=["osq"], writes=["oss"])
            yield
            S.emit("act", lambda E: E.activation(out=oss[:, 8:16], in_=oss[:, 0:8], func=AF.Sqrt, scale=1.0 / 64,
                                                 bias=P["epsc"][:]),
                   reads=["oss", "epsc"], writes=["oss"])
            S.emit("dve", lambda E: E.reciprocal(out=oss[:, 0:8], in_=oss[:, 8:16]), reads=["oss"], writes=["oss"])
            S.emit("dve", lambda E: E.tensor_tensor(out=osq[:], in0=osb[:],
                                                    in1=oss[:, 0:8].unsqueeze(2).to_broadcast([128, 8, 64]),
                                                    op=ALU.mult),
                   reads=["osb", "oss", "osq"], writes=["osq"])
            yield
            S.emit("pool", lambda E: E.tensor_tensor(out=osq[:], in0=osq[:],
                                                     in1=gnw[:].unsqueeze(1).to_broadcast([128, 8, 64]),
                                                     op=ALU.mult),
                   reads=["osq", "gnw"], writes=["osq"])
            S.emit("dve", lambda E: E.tensor_tensor(out=og[:], in0=osq[:].rearrange("p h d -> p (h d)"),
                                                    in1=zs[:, t, :], op=ALU.mult),
                   reads=["osq", ("gzs", t)], writes=["og"])
            yield
            bt = sbank()
            tbo = PS[bt][:].bitcast(BF16)
            for pr in range(4):
                S.emit("pe", lambda E, pr=pr, tbo=tbo: E.transpose(out=tbo[:, pr * 128:(pr + 1) * 128],
                                                                   in_=og[:, pr * 128:(pr + 1) * 128],
                                                                   identity=ident[:]),
                       reads=["og", "ident"], writes=[("ps", bt)], signal=(pr == 3))
            S.emit("act", lambda E, tbo=tbo: E.activation(out=oT[:, 4:8, tk],
                                                          in_=tbo[:, 0:512].rearrange("p (a i) -> p a i", a=4),
                                                          func=AF.Identity),
                   reads=[("ps", bt)], writes=[("oT", 4 + pr) for pr in range(4)])
            yield

        def drain(gen):
            for _ in gen:
                pass

        def step(gen):
            try:
                next(gen)
                return True
            except StopIteration:
                return False

        active_p = []
        next_p = 0
        p_done = set()
        scan_t = 0
        scan_gen = None
        while scan_t < NT:
            while next_p < NT and len(active_p) < NPS and next_p < scan_t + NSLOT:
                active_p.append([next_p, gen_P(next_p)])
                next_p += 1
            for ent in list(active_p):
                for _ in range(int(os.environ.get("C2RATIO", "2"))):
                    if not step(ent[1]):
                        p_done.add(ent[0])
                        active_p.remove(ent)
                        break
            if scan_gen is None and scan_t in p_done:
                scan_gen = gen_S(scan_t)
            if scan_gen is not None:
                if not step(scan_gen):
                    scan_gen = None
                    scan_t += 1

    def phase_D(self, st, b):
        nc, S, d, P, PS = self.nc, self.S, self.d, self.P, self.PS
        sb = self.sb
        oT = P["oT"]
        ident = P["ident"]
        wo = sb(st, "wo", [128, 8, DM], BF16)
        wu = sb(st, "wu", [128, 8, DFF], BF16)
        wd = sb(st, "wd", [128, 32, DM], BF16)
        premlp = P["premlp"]
        with ExitStack() as s_stg:
            NSTG = 5
            stg = [sb(s_stg, "stgD%d" % i, [128, DM], F32) for i in range(NSTG)]
            self._ns = 0

            def load_cast(src, dst, dname, scal=None):
                i = self._ns % NSTG
                eng = ("pool", "act", "dve")[self._ns % 3]
                self._ns += 1
                S.dma(stg[i][:], src, writes=[("stgD", i)])
                rd = [("stgD", i)] + (["premlp"] if scal is not None else [])
                if eng == "act":
                    if scal is None:
                        S.emit("act", lambda E, i=i: E.activation(out=dst, in_=stg[i][:], func=AF.Identity),
                               reads=rd, writes=[dname])
                    else:
                        S.emit("act", lambda E, i=i: E.activation(out=dst, in_=stg[i][:], func=AF.Identity,
                                                                  scale=scal), reads=rd, writes=[dname])
                else:
                    if scal is None:
                        S.emit(eng, lambda E, i=i: E.tensor_copy(out=dst, in_=stg[i][:]), reads=rd, writes=[dname])
                    else:
                        S.emit(eng, lambda E, i=i: E.tensor_scalar(out=dst, in0=stg[i][:], scalar1=scal, scalar2=None,
                                                                   op0=ALU.mult), reads=rd, writes=[dname])

            for k in range(8):
                load_cast(d["w_out"][k * 128:(k + 1) * 128, :], wo[:, k, :], ("wo", k))
            for k in range(8):
                for qd in range(4):
                    load_cast(d["w_up"][k * 128:(k + 1) * 128, qd * 1024:(qd + 1) * 1024],
                              wu[:, k, qd * 1024:(qd + 1) * 1024], ("wu", k, qd), scal=premlp[:, k:k + 1])
            for k in range(32):
                load_cast(d["w_down"][k * 128:(k + 1) * 128, :], wd[:, k, :], ("wd", k))
        S.barrier()

        GT = 2
        xt = [sb(st, "xtD%d" % i, [128, DM], F32) for i in range(GT)]
        tmp = sb(st, "tmpD", [128, DM], F32)
        h2 = sb(st, "h2D", [128, DM], BF16)
        h2T = sb(st, "h2T", [128, 8, GT * 128], BF16)
        uT = sb(st, "uT", [128, 32, GT * 128], BF16)
        rl = [sb(st, "rlD%d" % i, [128, 512], F32) for i in range(2)]
        sm = sb(st, "smD", [128, NT, 12], F32)
        S.emit("dve", lambda E: E.memset(sm[:], 0.0), writes=["smD"])
        postmix_b, postmlp_b, epsc = P["postmix_b"], P["postmlp_b"], P["epsc"]
        oT_all = [("oT", c) for c in range(8)]

        def rms_scale(src_banks, t, col):
            for hf in range(2):
                S.emit("act", lambda E, hf=hf: E.activation(out=tmp[:, hf * 512:(hf + 1) * 512],
                                                            in_=PS[src_banks[hf]][:, :], func=AF.Square,
                                                            accum_out=sm[:, t, col + hf:col + hf + 1]),
                       reads=[("ps", src_banks[hf]), "smD"], writes=["tmpD", "smD"])
            S.emit("dve", lambda E: E.tensor_tensor(out=sm[:, t, col:col + 1], in0=sm[:, t, col:col + 1],
                                                    in1=sm[:, t, col + 1:col + 2], op=ALU.add),
                   reads=["smD"], writes=["smD"])
            S.emit("act", lambda E: E.activation(out=sm[:, t, col + 1:col + 2], in_=sm[:, t, col:col + 1],
                                                 func=AF.Sqrt, scale=1.0 / DM, bias=epsc[:]),
                   reads=["smD", "epsc"], writes=["smD"])
            S.emit("dve", lambda E: E.reciprocal(out=sm[:, t, col + 2:col + 3], in_=sm[:, t, col + 1:col + 2]),
                   reads=["smD"], writes=["smD"])

        def stage1a(t, j, bk):
            tok = slice(t * 128, (t + 1) * 128)
            S.dma(xt[j][:], d["x"][b, tok, :], writes=[("xtD", j)])
            for hf in range(2):
                for c in range(8):
                    S.emit("pe", lambda E, hf=hf, c=c: E.matmul(PS[bk[hf]][:, :], lhsT=oT[:, c, tok],
                                                                rhs=wo[:, c, hf * 512:(hf + 1) * 512],
                                                                start=(c == 0), stop=(c == 7)),
                           reads=oT_all + ["wo"], writes=[("ps", bk[hf])], signal=(c == 7))

        def stage1b(t, j, bk):
            tok = slice(t * 128, (t + 1) * 128)
            rms_scale(bk, t, 0)
            for hf in range(2):
                cs = slice(hf * 512, (hf + 1) * 512)
                S.emit("dve", lambda E, hf=hf, cs=cs: E.scalar_tensor_tensor(
                    out=tmp[:, cs], in0=PS[bk[hf]][:, :], scalar=sm[:, t, 2:3], in1=postmix_b[:, cs],
                    op0=ALU.mult, op1=ALU.mult),
                       reads=[("ps", bk[hf]), "smD", "postmix_b", "tmpD"], writes=["tmpD"])
            S.emit("dve", lambda E: E.tensor_tensor(out=xt[j][:], in0=tmp[:], in1=xt[j][:], op=ALU.add),
                   reads=["tmpD", ("xtD", j)], writes=[("xtD", j)])
            if "x1" in self.dbg_out:
                S.dma(self.dbg_out["x1"][b, tok, :], xt[j][:], reads=[("xtD", j)], writes=[("dbgx1", t)])
            S.emit("act", lambda E: E.activation(out=h2[:], in_=xt[j][:], func=AF.Square, accum_out=sm[:, t, 3:4]),
                   reads=[("xtD", j), "smD"], writes=["h2D", "smD"])
            S.emit("act", lambda E: E.activation(out=sm[:, t, 4:5], in_=sm[:, t, 3:4], func=AF.Sqrt,
                                                 scale=1.0 / DM, bias=epsc[:]),
                   reads=["smD", "epsc"], writes=["smD"])
            S.emit("dve", lambda E: E.reciprocal(out=sm[:, t, 5:6], in_=sm[:, t, 4:5]), reads=["smD"], writes=["smD"])
            S.emit("act", lambda E: E.activation(out=h2[:], in_=xt[j][:], func=AF.Identity, scale=sm[:, t, 5:6]),
                   reads=[("xtD", j), "smD"], writes=["h2D"])
            tb = PS[2][:].bitcast(BF16)
            for k in range(8):
                S.emit("pe", lambda E, k=k: E.transpose(out=tb[:, k * 128:(k + 1) * 128],
                                                        in_=h2[:, k * 128:(k + 1) * 128], identity=ident[:]),
                       reads=["h2D", "ident"], writes=[("ps", 2)], signal=(k == 7))
            S.emit("dve", lambda E: E.tensor_copy(out=h2T[:, :, j * 128:(j + 1) * 128],
                                                  in_=tb.rearrange("p (k c) -> p k c", k=8)),
                   reads=[("ps", 2)], writes=[("h2T", j)])

        def stage2():
            W = GT * 128
            nf = 512 // W
            for g in range(32 // nf):
                bank = 3 + (g % 3)
                for f in range(nf):
                    fc = g * nf + f
                    for k in range(8):
                        S.emit("pe", lambda E, bank=bank, f=f, fc=fc, k=k: E.matmul(
                            PS[bank][:, f * W:(f + 1) * W], lhsT=wu[:, k, fc * 128:(fc + 1) * 128],
                            rhs=h2T[:, k, :], start=(k == 0), stop=(k == 7)),
                               reads=["wu"] + [("h2T", j) for j in range(GT)], writes=[("ps", bank)],
                               signal=(k == 7 and f == nf - 1))
                uv = uT[:, g * nf:(g + 1) * nf, :].rearrange("p a c -> p (a c)")
                ri = g % 2
                S.emit("act", lambda E, bank=bank, ri=ri: E.activation(out=rl[ri][:], in_=PS[bank][:, :], func=AF.Relu),
                       reads=[("ps", bank)], writes=[("rlD", ri)])
                S.emit("pool", lambda E, uv=uv, ri=ri: E.tensor_tensor(out=uv, in0=rl[ri][:], in1=rl[ri][:], op=ALU.mult),
                       reads=[("rlD", ri)], writes=["uT"])

        def stage3(t, j):
            tok = slice(t * 128, (t + 1) * 128)
            for hf in range(2):
                for fc in range(32):
                    S.emit("pe", lambda E, hf=hf, fc=fc: E.matmul(PS[6 + hf][:, :], lhsT=uT[:, fc, j * 128:(j + 1) * 128],
                                                                  rhs=wd[:, fc, hf * 512:(hf + 1) * 512],
                                                                  start=(fc == 0), stop=(fc == 31)),
                           reads=["uT", "wd"], writes=[("ps", 6 + hf)], signal=(fc == 31))
            rms_scale((6, 7), t, 8)
            for hf in range(2):
                cs = slice(hf * 512, (hf + 1) * 512)
                S.emit("dve", lambda E, hf=hf, cs=cs: E.scalar_tensor_tensor(
                    out=tmp[:, cs], in0=PS[6 + hf][:, :], scalar=sm[:, t, 10:11], in1=postmlp_b[:, cs],
                    op0=ALU.mult, op1=ALU.mult),
                       reads=[("ps", 6 + hf), "smD", "postmlp_b", "tmpD"], writes=["tmpD"])
            S.emit("pool", lambda E: E.tensor_tensor(out=xt[j][:], in0=tmp[:], in1=xt[j][:], op=ALU.add),
                   reads=["tmpD", ("xtD", j)], writes=[("xtD", j)])
            S.dma(self.out[b, tok, :], xt[j][:], reads=[("xtD", j)], writes=[("out", b, t)])

        for gi in range(NT // GT):
            OB = ((0, 1), (3, 4))
            for j in range(GT):
                stage1a(gi * GT + j, j, OB[j])
            for j in range(GT):
                stage1b(gi * GT + j, j, OB[j])
            stage2()
            for j in range(GT):
                stage3(gi * GT + j, j)


def host_inputs(inputs, core, nseq):
    f = lambda a: np.ascontiguousarray(np.asarray(a, dtype=np.float32))
    m = {}
    m["x"] = f(inputs["x"][core * nseq:(core + 1) * nseq])
    m["w_in"] = f(inputs["w_in"][0])
    m["w_out"] = f(inputs["w_out"][0])
    m["w_up"] = f(inputs["w_up"][0])
    m["w_down"] = f(inputs["w_down"][0])
    m["premix_pk"] = f(np.asarray(inputs["pre_mix_norm"][0]).reshape(8, 128).T)
    m["premlp_pk"] = f(np.asarray(inputs["pre_mlp_norm"][0]).reshape(8, 128).T)
    m["postmix"] = f(np.asarray(inputs["post_mix_norm"][0]).reshape(1, DM))
    m["postmlp"] = f(np.asarray(inputs["post_mlp_norm"][0]).reshape(1, DM))
    rb = np.asarray(inputs["rel_bias"], dtype=np.float32)
    tab = np.concatenate([rb, np.full((8, 1), NEG, np.float32)], axis=1)
    j = np.arange(128)[:, None]
    i = np.arange(128)[None, :]
    idx0 = np.where(i - j >= 0, rel_bucket_np(i - j), 32)
    idx1 = rel_bucket_np(128 + i - j)
    idx = np.stack([idx0, idx1], axis=0)
    tt = tab[:, idx]
    m["ttab"] = f(tt.transpose(2, 0, 1, 3).reshape(128, 8 * 2 * 128))
    m["rb31"] = f(rb[:, 31].reshape(1, 8))
    cw = np.asarray(inputs["conv_w"][0], dtype=np.float32)
    m["convw_pk"] = f(cw.T.reshape(12, 128, 4).transpose(1, 0, 2).reshape(128, 48))
    m["alog"] = f(np.asarray(inputs["A_log"][0]).reshape(1, 8))
    m["dtb"] = f(np.asarray(inputs["dt_bias"][0]).reshape(1, 8))
    m["gnw"] = f(np.asarray(inputs["gdn_norm_w"][0]).reshape(1, 64))
    return m


_PROG = {}


def kernel(**inputs):
    ncores = 8
    nseq = 16 // ncores
    if "p" not in _PROG:
        _PROG["p"] = Prog(nseq)
    prog = _PROG["p"]
    in_maps = [host_inputs(inputs, c, nseq) for c in range(ncores)]
    res = run_bass_kernel_spmd(prog.nc, in_maps, core_ids=list(range(ncores)))
    out = np.concatenate([r["out"] for r in res.results], axis=0)
    return out.astype(np.float32)
```

```python
import math
from contextlib import ExitStack

import numpy as np
import concourse.bass as bass
import concourse.mybir as mybir
from concourse.bass_utils import run_bass_kernel_spmd

F32 = mybir.dt.float32
BF16 = mybir.dt.bfloat16
AF = mybir.ActivationFunctionType
ALU = mybir.AluOpType
AX = mybir.AxisListType

NDMA = 8
SEQ = 2048
DM = 1024
NT = SEQ // 128
DFF = 4096
INC = 3600
EPS = 1e-6
NEG = -30000.0


class Sched:
    ENGS = ("pe", "act", "dve", "pool", "sp")

    def __init__(self, nc):
        self.nc = nc
        self.q = {e: [] for e in self.ENGS}
        self.cnt = {e: 0 for e in self.ENGS}
        self.pending = {e: False for e in self.ENGS}
        self.seen = {e: {} for e in self.ENGS}
        self.lastw = {}
        self.readers = {}
        self.dma_i = 0
        self.dma_uses = [0] * NDMA
        self.bar = {}
        self.n_ins = {e: 0 for e in self.ENGS}

    def _deps(self, eng, reads, writes):
        deps = dict(self.bar)

        def add(tok):
            k, v = tok
            if deps.get(k, 0) < v:
                deps[k] = v

        for r in reads:
            if r in self.lastw:
                add(self.lastw[r])
            if isinstance(r, tuple) and r[0] == "ps":
                for k, v in self.readers.get(r, {}).items():
                    if k != eng:
                        add((k, v))
        for w in writes:
            if w in self.lastw:
                add(self.lastw[w])
            for k, v in self.readers.get(w, {}).items():
                add((k, v))
        waits = []
        for k, v in deps.items():
            if k == eng and eng == "pe":
                continue
            if self.seen[eng].get(k, 0) >= v:
                continue
            self.seen[eng][k] = v
            waits.append((k, v))
        return waits

    def _record(self, tok, reads, writes):
        k, v = tok
        for r in reads:
            d = self.readers.setdefault(r, {})
            if d.get(k, 0) < v:
                d[k] = v
        for w in writes:
            self.lastw[w] = tok
            self.readers[w] = {}

    def emit(self, eng, fn, reads=(), writes=(), signal=True):
        waits = self._deps(eng, reads, writes)
        if signal:
            self.cnt[eng] += 1
            self.pending[eng] = False
            tok = (eng, self.cnt[eng])
        else:
            self.pending[eng] = True
            tok = (eng, self.cnt[eng] + 1)
        self._record(tok, reads, writes)
        self.n_ins[eng] += 1

        def run(E, sems, waits=waits, fn=fn, signal=signal, eng=eng):
            for k, v in waits:
                E.wait_ge(sems[k], v)
            ins = fn(E)
            if signal:
                ins.then_inc(sems[eng], 1)

        self.q[eng].append(run)
        return tok

    def dma(self, out, in_, reads=(), writes=(), q="sp", **kw):
        slot = self.dma_i % NDMA
        self.dma_i += 1
        key = ("dma", slot)
        waits = self._deps(q, reads, writes)
        prev = 16 * self.dma_uses[slot]
        if prev > 0 and self.seen[q].get(key, 0) < prev:
            self.seen[q][key] = prev
            waits.append((key, prev))
        self.dma_uses[slot] += 1
        tok = (key, 16 * self.dma_uses[slot])
        self._record(tok, reads, writes)
        self.n_ins[q] += 1

        def run(E, sems, waits=waits, out=out, in_=in_, key=key, kw=kw):
            for k, v in waits:
                E.wait_ge(sems[k], v)
            E.dma_start(out=out, in_=in_, **kw).then_inc(sems[key], 16)

        self.q[q].append(run)
        return tok

    def barrier(self):
        for e in self.ENGS:
            assert not self.pending[e]
            if self.cnt[e] > 0:
                self.bar[e] = self.cnt[e]
        for s in range(NDMA):
            if self.dma_uses[s] > 0:
                self.bar[("dma", s)] = 16 * self.dma_uses[s]

    def finish(self):
        waits = []
        for slot in range(NDMA):
            v = 16 * self.dma_uses[slot]
            key = ("dma", slot)
            if v > 0 and self.seen["sp"].get(key, 0) < v:
                self.seen["sp"][key] = v
                waits.append((key, v))

        def run(E, sems, waits=waits):
            for k, v in waits:
                E.wait_ge(sems[k], v)

        self.q["sp"].append(run)
        for e in self.ENGS:
            assert not self.pending[e], f"engine {e} has unsignaled trailing instruction"

    def build(self, stack):
        nc = self.nc
        sems = {}
        for e in self.ENGS:
            sems[e] = stack.enter_context(nc.semaphore("s_" + e))
        for s in range(NDMA):
            sems[("dma", s)] = stack.enter_context(nc.semaphore("s_dma%d" % s))
        block = stack.enter_context(nc.Block())
        q = self.q

        @block.tensor
        def _(E):
            for f in q["pe"]:
                f(E, sems)

        @block.scalar
        def _(E):
            for f in q["act"]:
                f(E, sems)

        @block.vector
        def _(E):
            for f in q["dve"]:
                f(E, sems)

        @block.gpsimd
        def _(E):
            for f in q["pool"]:
                f(E, sems)

        @block.sync
        def _(E):
            for f in q["sp"]:
                f(E, sems)


def rel_bucket_np(d):
    d = np.maximum(d, 0)
    large = 16 + (np.log(np.maximum(d, 1).astype(np.float32) / 16) / math.log(128 / 16) * 16).astype(np.int32)
    large = np.minimum(large, 31)
    return np.where(d < 16, d, large)


class Prog:
    def __init__(self, nseq, stages=("A", "B1", "C1", "C2", "D"), dbg=()):
        self.nseq = nseq
        self.stages = stages
        self.dbg = dbg
        nc = bass.Bass("TRN2", target_bir_lowering=False, dynamic_dma_scratch_size=256)
        self.nc = nc
        self.S = Sched(nc)
        d = {}

        def din(name, shape):
            d[name] = nc.dram_tensor(name, list(shape), F32, kind="ExternalInput").ap()

        din("x", [nseq, SEQ, DM])
        din("w_in", [DM, INC])
        din("w_out", [DM, DM])
        din("w_up", [DM, DFF])
        din("w_down", [DFF, DM])
        din("premix_pk", [128, 8])
        din("premlp_pk", [128, 8])
        din("postmix", [1, DM])
        din("postmlp", [1, DM])
        din("ttab", [128, 8 * 2 * 128])
        din("rb31", [1, 8])
        din("convw_pk", [128, 12 * 4])
        din("alog", [1, 8])
        din("dtb", [1, 8])
        din("gnw", [1, 64])
        self.out = nc.dram_tensor("out", [nseq, SEQ, DM], F32, kind="ExternalOutput").ap()
        self.dbg_out = {}
        for name, shape in dbg:
            self.dbg_out[name] = nc.dram_tensor("dbg_" + name, list(shape), F32, kind="ExternalOutput").ap()
        self.d = d
        self._rr = 0
        with ExitStack() as st:
            self.build(st)
            self.S.finish()
            self.S.build(st)

    def sb(self, st, name, shape, dt):
        self._uid = getattr(self, "_uid", 0) + 1
        return st.enter_context(self.nc.sbuf_tensor("%s_u%d" % (name, self._uid), list(shape), dt))

    def evac(self, out, in_, reads, writes, eng=None):
        if eng is None:
            eng = ("act", "dve")[self._rr % 2]
            self._rr += 1
        if eng == "act":
            self.S.emit("act", lambda E: E.activation(out=out, in_=in_, func=AF.Identity), reads=reads, writes=writes)
        else:
            self.S.emit("dve", lambda E: E.tensor_copy(out=out, in_=in_), reads=reads, writes=writes)

    def build(self, st):
        nc, S, d = self.nc, self.S, self.d
        sb = self.sb
        self.PS = [st.enter_context(nc.psum_tensor("ps%d" % i, [128, 512], F32)) for i in range(8)]
        P = {}
        self.P = P
        P["ident"] = sb(st, "ident", [128, 128], BF16)
        P["postmix_b"] = sb(st, "postmix_b", [128, DM], F32)
        P["postmlp_b"] = sb(st, "postmlp_b", [128, DM], F32)
        P["premix"] = sb(st, "premix", [128, 8], F32)
        P["premlp"] = sb(st, "premlp", [128, 8], F32)
        P["epsc"] = sb(st, "epsc", [128, 1], F32)
        P["oT"] = sb(st, "oT", [128, 8, SEQ], BF16)
        ident = P["ident"]
        S.emit("pool", lambda E: E.memset(ident[:], 0.0), writes=["ident"])
        S.emit("pool", lambda E: E.affine_select(out=ident[:], in_=ident[:], pattern=[[-1, 128]],
                                                  compare_op=ALU.not_equal, fill=1.0, base=0, channel_multiplier=1),
               reads=["ident"], writes=["ident"])
        S.emit("pool", lambda E: E.memset(P["epsc"][:], EPS), writes=["epsc"])
        S.dma(P["postmix_b"][:], d["postmix"].partition_broadcast(128), writes=["postmix_b"])
        S.dma(P["postmlp_b"][:], d["postmlp"].partition_broadcast(128), writes=["postmlp_b"])
        S.dma(P["premix"][:], d["premix_pk"], writes=["premix"])
        S.dma(P["premlp"][:], d["premlp_pk"], writes=["premlp"])
        if "C2" not in self.stages:
            oT = P["oT"]
            S.emit("pool", lambda E: E.memset(oT[:, 4:8, :], 0.0), writes=[("oT", c) for c in range(4, 8)])

        for b in range(self.nseq):
          S.barrier()
          with ExitStack() as s_seq:
            hT_keep = sb(s_seq, "hT", [128, 8, SEQ], BF16)
            with ExitStack() as s_att:
                A = {}
                A["qT"] = sb(s_att, "qT", [128, 4, SEQ], BF16)
                A["kT"] = sb(s_att, "kT", [128, 4, SEQ], BF16)
                A["vaug"] = sb(s_att, "vaug", [128, NT, 4, 3, 64], BF16)
                A["maskT"] = sb(s_att, "maskT", [128, SEQ], BF16)
                A["kmT"] = sb(s_att, "kmT", [128, 4, 8], BF16)
                with ExitStack() as s_pa:
                    wA = self.prep_B1(s_pa) if "B1" in self.stages else None
                    hT = self.phase_A(s_pa, b, hT=hT_keep)
                    if "B1" in self.stages:
                        self.phase_B1(s_pa, b, hT, A, wA)
                S.barrier()
                if "C1" in self.stages:
                    with ExitStack() as s_c1:
                        self.phase_C1(s_c1, b, A)
                S.barrier()
            S.barrier()
            if "C2" in self.stages or "B2" in self.stages:
                with ExitStack() as s_g:
                    G = {}
                    G["qT"] = sb(s_g, "gqT", [128, 4, SEQ], BF16)
                    G["kT"] = sb(s_g, "gkT", [128, 4, SEQ], BF16)
                    G["vT"] = sb(s_g, "gvT", [128, 4, SEQ], BF16)
                    G["zs"] = sb(s_g, "gzs", [128, NT, 512], BF16)
                    G["gab"] = sb(s_g, "gab", [128, NT, 16], F32)
                    G["g"] = sb(s_g, "gg", [128, NT, 8], F32)
                    G["beta"] = sb(s_g, "gbeta", [128, NT, 8], F32)
                    G["nbeta"] = sb(s_g, "gnbeta", [128, NT, 8], F32)
                    with ExitStack() as s_pb:
                        self.phase_B2(s_pb, b, hT_keep, G)
                    S.barrier()
                    with ExitStack() as s_c2:
                        if "C2" in self.stages:
                            self.phase_C2(s_c2, b, G)
                    S.barrier()
                    if "oTg" in self.dbg_out:
                        with ExitStack() as s_dbg:
                            oT = P["oT"]
                            otf = sb(s_dbg, "otfg", [128, 4, SEQ], F32)
                            S.emit("dve", lambda E: E.tensor_copy(out=otf[:], in_=oT[:, 4:8, :]),
                                   reads=[("oT", c) for c in range(4, 8)], writes=["otfg"])
                            S.dma(self.dbg_out["oTg"][b].rearrange("c p s -> p c s"), otf[:], reads=["otfg"],
                                  writes=["dbg_oTg"])
                        S.barrier()
          S.barrier()
          if "D" in self.stages:
              with ExitStack() as s_d:
                  self.phase_D(s_d, b)
          S.barrier()

    def phase_A(self, st, b, nbuf=2, hT=None):
        nc, S, d, P, PS = self.nc, self.S, self.d, self.P, self.PS
        if hT is None:
            hT = self.sb(st, "hT", [128, 8, SEQ], BF16)
        xt = [self.sb(st, "xt%d" % i, [128, DM], F32) for i in range(nbuf)] * (2 // nbuf)
        hb = [self.sb(st, "hb%d" % i, [128, DM], BF16) for i in range(nbuf)] * (2 // nbuf)
        ss = self.sb(st, "ssA", [128, NT], F32)
        rs = self.sb(st, "rsA", [128, NT], F32)
        rstd = self.sb(st, "rstdA", [128, NT], F32)
        ident = P["ident"]
        S.emit("dve", lambda E: E.memset(ss[:], 0.0), writes=["ssA"])
        for t in range(NT):
            i = t % nbuf
            S.dma(xt[i][:], d["x"][b, t * 128:(t + 1) * 128, :], writes=[("xt", i)])
            S.emit("act", lambda E, i=i, t=t: E.activation(out=hb[i][:], in_=xt[i][:], func=AF.Square,
                                                           accum_out=ss[:, t:t + 1]),
                   reads=[("xt", i), "ssA"], writes=[("hb", i), "ssA"])
            S.emit("act", lambda E, t=t: E.activation(out=rs[:, t:t + 1], in_=ss[:, t:t + 1], func=AF.Sqrt,
                                                      scale=1.0 / DM, bias=P["epsc"][:]),
                   reads=["ssA", "epsc"], writes=["rsA"])
            S.emit("dve", lambda E, t=t: E.reciprocal(out=rstd[:, t:t + 1], in_=rs[:, t:t + 1]),
                   reads=["rsA"], writes=["rstdA"])
            S.emit("act", lambda E, i=i, t=t: E.activation(out=hb[i][:], in_=xt[i][:], func=AF.Identity,
                                                           scale=rstd[:, t:t + 1]),
                   reads=[("xt", i), "rstdA"], writes=[("hb", i)])
            bank = t % 2
            psb = PS[bank][:].bitcast(BF16)
            for k in range(8):
                S.emit("pe", lambda E, k=k, i=i, psb=psb: E.transpose(out=psb[:, k * 128:(k + 1) * 128],
                                                                      in_=hb[i][:, k * 128:(k + 1) * 128],
                                                                      identity=ident[:]),
                       reads=[("hb", i), "ident"], writes=[("ps", bank)], signal=(k == 7))
            S.emit("dve", lambda E, t=t, psb=psb: E.tensor_copy(out=hT[:, :, t * 128:(t + 1) * 128],
                                                                in_=psb.rearrange("p (k c) -> p k c", k=8)),
                   reads=[("ps", bank)], writes=[("hT", t)])
        return hT

    def prep_B1(self, st):
        nc, S, d, P, PS = self.nc, self.S, self.d, self.P, self.PS
        wA = self.sb(st, "wA", [128, 8, 1536], BF16)
        stg = [self.sb(st, "stgA%d" % i, [128, 1536], F32) for i in range(3)]
        premix = P["premix"]
        for k in range(8):
            i = k % 3
            S.dma(stg[i][:], d["w_in"][k * 128:(k + 1) * 128, 0:1536], writes=[("stgA", i)])
            S.emit("pool", lambda E, k=k, i=i: E.tensor_scalar(out=wA[:, k, 0:512], in0=stg[i][:, 0:512],
                                                               scalar1=premix[:, k:k + 1], scalar2=0.125,
                                                               op0=ALU.mult, op1=ALU.mult),
                   reads=[("stgA", i), "premix"], writes=[("wA", k)])
            S.emit("pool", lambda E, k=k, i=i: E.tensor_scalar(out=wA[:, k, 512:1536], in0=stg[i][:, 512:1536],
                                                               scalar1=premix[:, k:k + 1], scalar2=None,
                                                               op0=ALU.mult),
                   reads=[("stgA", i), "premix"], writes=[("wA", k)])
        return wA

    def phase_B1(self, st, b, hT, A, wA):
        nc, S, d, P, PS = self.nc, self.S, self.d, self.P, self.PS
        qT, kT, vaug = A["qT"], A["kT"], A["vaug"]
        S.emit("pool", lambda E: E.memset(vaug[:, :, :, 1, :], 1.0), writes=["vones"])
        wA_all = [("wA", k) for k in range(8)]
        nb = 0
        for which, dst, name in ((0, qT, "qT"), (1, kT, "kT")):
            for pr in range(4):
                col0 = which * 512 + pr * 128
                for tc in range(4):
                    bank = 2 + (nb % 4)
                    nb += 1
                    for k in range(8):
                        S.emit("pe", lambda E, k=k, col0=col0, tc=tc, bank=bank: E.matmul(
                            PS[bank][:, :], lhsT=wA[:, k, col0:col0 + 128], rhs=hT[:, k, tc * 512:(tc + 1) * 512],
                            start=(k == 0), stop=(k == 7)),
                               reads=wA_all + [("hT", 4 * tc + j) for j in range(4)], writes=[("ps", bank)],
                               signal=(k == 7))
                    self.evac(dst[:, pr, tc * 512:(tc + 1) * 512], PS[bank][:, :], reads=[("ps", bank)],
                              writes=[(name, pr, tc)])
        for t in range(NT):
            bank = 2 + (nb % 4)
            nb += 1
            for k in range(8):
                S.emit("pe", lambda E, k=k, t=t, bank=bank: E.matmul(
                    PS[bank][:, :], lhsT=hT[:, k, t * 128:(t + 1) * 128], rhs=wA[:, k, 1024:1536],
                    start=(k == 0), stop=(k == 7)),
                       reads=wA_all + [("hT", t)], writes=[("ps", bank)], signal=(k == 7))
            self.evac(vaug[:, t, :, 0:3:2, :], PS[bank][:, :].rearrange("p (a b c) -> p a b c", a=4, b=2),
                      reads=[("ps", bank)], writes=[("vaug", t)])
        kmf = self.sb(st, "kmf", [128, 4, 8], F32)
        for pr in range(4):
            S.emit("dve", lambda E, pr=pr: E.tensor_reduce(out=kmf[:, pr, :],
                                                           in_=kT[:, pr, :].rearrange("p (n c) -> p n c", n=8),
                                                           axis=AX.X, op=ALU.add),
                   reads=[("kT", pr, tc) for tc in range(4)], writes=["kmf"])
        S.emit("dve", lambda E: E.tensor_scalar(out=A["kmT"][:], in0=kmf[:], scalar1=1.0 / 256, scalar2=None,
                                                op0=ALU.mult),
               reads=["kmf"], writes=["kmT"])

    def phase_C1(self, st, b, A):
        nc, S, d, P, PS = self.nc, self.S, self.d, self.P, self.PS
        sb = self.sb
        qT, kT, vaug, maskT, kmT = A["qT"], A["kT"], A["vaug"], A["maskT"], A["kmT"]
        ident = P["ident"]
        oT = P["oT"]
        IND = sb(st, "IND", [128, 64, 128], BF16)
        TT = sb(st, "TT", [128, 8, 2, 128], F32)
        rb31 = sb(st, "rb31", [128, 8], F32)
        PAST = sb(st, "PAST", [128, 8, 8, 8], F32)
        OWN = sb(st, "OWN", [128, 8, 8, 8], F32)
        PT = [[sb(st, "PT%d%d" % (h, i), [128, 512], BF16) for i in range(2)] for h in range(2)]
        rden = [sb(st, "rden%d" % h, [128, 512], F32) for h in range(2)]
        gsb = sb(st, "gsb", [128, 2, 8, 8], F32)
        g2 = sb(st, "g2", [128, 2, 8, 8], F32)
        eq = sb(st, "eq", [128, 2, 8, 8], F32)
        mx = sb(st, "mx", [128, 16], F32)
        mtok = sb(st, "mtok", [128, 128], BF16)

        S.emit("pool", lambda E: E.memset(IND[:], 0.0), writes=["IND"])
        S.emit("pool", lambda E: E.affine_select(out=IND[0:64], in_=IND[0:64], pattern=[[-1, 64], [0, 128]],
                                                  compare_op=ALU.not_equal, fill=1.0, base=0, channel_multiplier=1),
               reads=["IND"], writes=["IND"])
        S.emit("dve", lambda E: E.tensor_copy(out=IND[64:128], in_=IND[0:64]), reads=["IND"], writes=["IND"])
        S.emit("pool", lambda E: E.memset(PAST[:], 0.0), writes=["PAST"])
        S.emit("pool", lambda E: E.affine_select(out=PAST[:], in_=PAST[:], pattern=[[1, 8], [0, 8], [-1, 8]],
                                                  compare_op=ALU.is_gt, fill=-1e30, base=0, channel_multiplier=0),
               reads=["PAST"], writes=["PAST"])
        S.emit("pool", lambda E: E.memset(OWN[:], 0.0), writes=["OWN"])
        S.emit("pool", lambda E: E.affine_select(out=OWN[:], in_=OWN[:], pattern=[[1, 8], [0, 8], [-1, 8]],
                                                  compare_op=ALU.not_equal, fill=1.0, base=0, channel_multiplier=0),
               reads=["OWN"], writes=["OWN"])
        S.dma(TT[:].rearrange("p h a c -> p (h a c)"), d["ttab"], writes=["TT"])
        S.dma(rb31[:], d["rb31"].partition_broadcast(128), writes=["rb31"])
        S.emit("dve", lambda E: E.tensor_tensor(out=TT[:].rearrange("p h a c -> p h (a c)"),
                                                in0=TT[:].rearrange("p h a c -> p h (a c)"),
                                                in1=rb31[:].unsqueeze(2).to_broadcast([128, 8, 256]),
                                                op=ALU.subtract),
               reads=["TT", "rb31"], writes=["TT"])

        GB, TB = 6, 7
        import os
        FL = os.environ.get("C1FLAGS", "mask,main,toep,pv,norm,maskmm").split(",")
        if "mask" not in FL:
            S.emit("pool", lambda E: E.memset(maskT[:], 0.0), writes=[("maskT", qt) for qt in range(NT)])
        GBK = (6, 7)
        TB = 6

        def mask_pass(blk):
                for j in range(2):
                    qt = 2 * blk + j
                    for h in range(8):
                        pr, hh = h // 2, h % 2
                        S.emit("pe", lambda E, h=h, pr=pr, hh=hh, qt=qt, j=j: E.matmul(
                            PS[GBK[hh]][:, j * 32 + pr * 8:j * 32 + (pr + 1) * 8],
                            lhsT=qT[hh * 64:(hh + 1) * 64, pr, qt * 128:(qt + 1) * 128],
                            rhs=kmT[hh * 64:(hh + 1) * 64, pr, :], start=True, stop=True),
                               reads=[("qT", pr, qt // 4), "kmT"], writes=[("ps", GBK[hh])], signal=(j == 1 and h >= 6))
                for hh in range(2):
                    g3v = PS[GBK[hh]][:, 0:64].rearrange("p (j h n) -> p j h n", j=2, h=4)
                    S.emit("dve", lambda E, blk=blk, g3v=g3v, hh=hh: E.tensor_tensor(
                        out=gsb[:, :, hh:8:2, :], in0=g3v,
                        in1=PAST[:, blk, hh:8:2, :].unsqueeze(1).to_broadcast([128, 2, 4, 8]), op=ALU.add),
                           reads=[("ps", GBK[hh]), "PAST"], writes=["gsb"])
                cur = gsb
                mxb = mx[:].rearrange("p (j h) -> p j h", j=2).unsqueeze(3).to_broadcast([128, 2, 8, 8])
                for it in range(2):
                    S.emit("dve", lambda E, cur=cur: E.tensor_reduce(out=mx[:], in_=cur[:].rearrange("p j h n -> p (j h) n"),
                                                                     axis=AX.X, op=ALU.max),
                           reads=["gsb", "g2"], writes=["mx"])
                    S.emit("dve", lambda E, cur=cur: E.tensor_tensor(out=eq[:], in0=cur[:], in1=mxb, op=ALU.is_equal),
                           reads=["gsb", "g2", "mx"], writes=["eq"])
                    S.emit("dve", lambda E, cur=cur: E.scalar_tensor_tensor(out=g2[:], in0=eq[:], scalar=-1e30,
                                                                            in1=cur[:], op0=ALU.mult, op1=ALU.add),
                           reads=["eq", "gsb", "g2"], writes=["g2"])
                    cur = g2
                S.emit("dve", lambda E: E.tensor_reduce(out=mx[:], in_=g2[:].rearrange("p j h n -> p (j h) n"),
                                                        axis=AX.X, op=ALU.max),
                       reads=["g2"], writes=["mx"])
                S.emit("dve", lambda E: E.tensor_scalar(out=mx[:], in0=mx[:], scalar1=-1e29, scalar2=None, op0=ALU.max),
                       reads=["mx"], writes=["mx"])
                S.emit("dve", lambda E: E.tensor_tensor(out=eq[:], in0=gsb[:], in1=mxb, op=ALU.is_ge),
                       reads=["gsb", "mx"], writes=["eq"])
                S.emit("dve", lambda E, blk=blk: E.tensor_tensor(
                    out=eq[:], in0=eq[:], in1=OWN[:, blk, :, :].unsqueeze(1).to_broadcast([128, 2, 8, 8]), op=ALU.add),
                       reads=["eq", "OWN"], writes=["eq"])
                S.emit("dve", lambda E: E.tensor_scalar(out=mtok[:], in0=eq[:].rearrange("p j h n -> p (j h n)"),
                                                        scalar1=-1.0, scalar2=-NEG, op0=ALU.add, op1=ALU.mult),
                       reads=["eq"], writes=["mtok"])
                tb = PS[TB][:].bitcast(BF16)
                for j in range(2):
                    S.emit("pe", lambda E, tb=tb, j=j: E.transpose(out=tb[0:64, j * 128:(j + 1) * 128],
                                                                   in_=mtok[:, j * 64:(j + 1) * 64], identity=ident[:]),
                           reads=["mtok", "ident"], writes=[("ps", TB)], signal=(j == 1))
                S.emit("act", lambda E, tb=tb, blk=blk: E.activation(out=maskT[0:64, blk * 256:(blk + 1) * 256],
                                                                     in_=tb[0:64, 0:256], func=AF.Identity),
                       reads=[("ps", TB)], writes=[("maskT", 2 * blk), ("maskT", 2 * blk + 1)])
                S.emit("act", lambda E, tb=tb, blk=blk: E.activation(out=maskT[64:128, blk * 256:(blk + 1) * 256],
                                                                     in_=tb[0:64, 0:256], func=AF.Identity),
                       reads=[("ps", TB)], writes=[("maskT", 2 * blk), ("maskT", 2 * blk + 1)])

        vflat = vaug[:].rearrange("p t a b c -> p t a (b c)")

        def qk(pr, qc, kt):
            qs = max(0, kt * 128 - qc * 512)
            N = 512 - qs
            q0 = qc * 512 + qs
            for hh in range(2):
                h = 2 * pr + hh
                bank = hh * 2 + (kt % 2)
                rows = slice(hh * 64, (hh + 1) * 64)
                mm = "maskmm" in FL
                S.emit("pe", lambda E, bank=bank, rows=rows, N=N, q0=q0, mm=mm: E.matmul(
                    PS[bank][:, 0:N], lhsT=kT[rows, pr, kt * 128:(kt + 1) * 128], rhs=qT[rows, pr, q0:q0 + N],
                    start=True, stop=not mm),
                       reads=[("kT", pr, kt // 4), ("qT", pr, qc)], writes=[("ps", bank)], signal=not mm)
                if mm:
                    S.emit("pe", lambda E, bank=bank, h=h, N=N, q0=q0, rows=rows: E.matmul(
                        PS[bank][:, 0:N], lhsT=IND[rows, h * 8 + kt // 2, :], rhs=maskT[rows, q0:q0 + N],
                        start=False, stop=True),
                           reads=["IND"] + [("maskT", 4 * qc + j) for j in range(4)], writes=[("ps", bank)])
                for dq in range(2 if "toep" in FL else 0):
                    qt = kt + dq
                    if qt * 128 < q0 or qt >= (qc + 1) * 4:
                        continue
                    off = qt * 128 - q0
                    S.emit("dve", lambda E, bank=bank, off=off, h=h, dq=dq: E.tensor_tensor(
                        out=PS[bank][:, off:off + 128], in0=PS[bank][:, off:off + 128], in1=TT[:, h, dq, :],
                        op=ALU.add),
                           reads=[("ps", bank), "TT"], writes=[("ps", bank)])
                S.emit("act", lambda E, bank=bank, hh=hh, N=N: E.activation(
                    out=PT[hh][kt % 2][:, 0:N], in_=PS[bank][:, 0:N], func=AF.Exp),
                       reads=[("ps", bank)], writes=[("PT", hh, kt % 2)])

        def pv(pr, qc, kt, nkt):
            qs = max(0, kt * 128 - qc * 512)
            N = 512 - qs
            for hh in range(2):
                bank = 4 + hh
                S.emit("pe", lambda E, bank=bank, hh=hh, qs=qs, N=N: E.matmul(
                    PS[bank][:, qs:512], lhsT=vflat[:, kt, pr, hh * 64:hh * 64 + 128], rhs=PT[hh][kt % 2][:, 0:N],
                    start=(kt == 0), stop=(kt == nkt - 1)),
                       reads=[("PT", hh, kt % 2), ("vaug", kt), "vones"], writes=[("ps", bank)],
                       signal=(kt == nkt - 1))

        for qc in range(4 if "main" in FL else 0):
            if "mask" in FL:
                mask_pass(2 * qc)
                mask_pass(2 * qc + 1)
            for pr in range(4):
                nkt = 4 * (qc + 1)
                for kt in range(nkt):
                    qk(pr, qc, kt)
                    if kt > 0 and "pv" in FL:
                        pv(pr, qc, kt - 1, nkt)
                if "pv" in FL:
                    pv(pr, qc, nkt - 1, nkt)
                for hh in range(2 if "norm" in FL else 0):
                    bank = 4 + hh
                    orows = slice(hh * 64, (hh + 1) * 64)
                    drows = slice((1 - hh) * 64, (2 - hh) * 64)
                    S.emit("dve", lambda E, bank=bank, hh=hh, orows=orows, drows=drows: E.reciprocal(
                        out=rden[hh][orows, :], in_=PS[bank][drows, :]),
                           reads=[("ps", bank)], writes=[("rden", hh)])
                    S.emit("dve", lambda E, bank=bank, hh=hh, orows=orows, pr=pr, qc=qc: E.tensor_tensor(
                        out=oT[orows, pr, qc * 512:(qc + 1) * 512], in0=PS[bank][orows, :], in1=rden[hh][orows, :],
                        op=ALU.mult),
                           reads=[("ps", bank), ("rden", hh)], writes=[("oT", pr)])

        if "oT" in self.dbg_out:
            otf = sb(st, "otf", [128, 4, SEQ], F32)
            S.emit("dve", lambda E: E.tensor_copy(out=otf[:], in_=oT[:, 0:4, :]),
                   reads=[("oT", c) for c in range(4)], writes=["otf"])
            S.dma(self.dbg_out["oT"][b].rearrange("c p s -> p c s"), otf[:], reads=["otf"], writes=["dbg_oT"])

    def phase_B2(self, st, b, hT, G):
        nc, S, d, P, PS = self.nc, self.S, self.d, self.P, self.PS
        sb = self.sb
        premix = P["premix"]
        stgw = [sb(st, "stgw%d" % i, [128, 8, 128], F32) for i in range(2)]
        wc = [sb(st, "wc%d" % i, [128, 8, 128], BF16) for i in range(2)]
        wz = sb(st, "wz", [128, 8, 512], BF16)
        wab = sb(st, "wab", [128, 8, 16], BF16)
        pres = [sb(st, "pre%d" % i, [128, 3 + SEQ], F32) for i in range(2)]
        accs = [sb(st, "acc%d" % i, [128, SEQ], F32) for i in range(2)]
        sq = sb(st, "sqg", [128, SEQ], BF16)
        srs = [sb(st, "srg%d" % i, [128, 512], F32) for i in range(2)]
        cw = sb(st, "cw", [128, 12, 4], F32)
        BLK = sb(st, "BLK", [128, 128], BF16)
        dtb = sb(st, "dtb_b", [128, 8], F32)
        alog = sb(st, "alog_b", [128, 8], F32)
        S.dma(cw[:].rearrange("p c t -> p (c t)"), d["convw_pk"], writes=["cw"])
        S.dma(dtb[:], d["dtb"].partition_broadcast(128), writes=["dtb"])
        S.dma(alog[:], d["alog"].partition_broadcast(128), writes=["alog"])
        S.emit("pool", lambda E: E.memset(BLK[:], 0.0), writes=["BLK"])
        S.emit("pool", lambda E: E.memset(BLK[0:64, 0:64], 1.0), reads=["BLK"], writes=["BLK"])
        S.emit("pool", lambda E: E.memset(BLK[64:128, 64:128], 1.0), reads=["BLK"], writes=["BLK"])
        for i in range(2):
            S.emit("pool", lambda E, i=i: E.memset(pres[i][:, 0:3], 0.0), writes=[("pre0", i)])
        hT_all = [("hT", t) for t in range(NT)]
        self._nw = 0

        def load_w(col0, ncol, dst, dname):
            i = self._nw % 2
            self._nw += 1
            S.dma(stgw[i][:, :, 0:ncol], d["w_in"][:, col0:col0 + ncol].rearrange("(k p) c -> p k c", p=128),
                  writes=[("stgw", i)])
            S.emit("pool", lambda E, i=i, ncol=ncol, dst=dst: E.tensor_tensor(
                out=dst, in0=stgw[i][:, :, 0:ncol], in1=premix[:].unsqueeze(2).to_broadcast([128, 8, ncol]),
                op=ALU.mult),
                   reads=[("stgw", i), "premix"], writes=[dname])

        nb = 0
        dsts = (G["qT"], G["kT"], G["vT"])
        for c in range(12):
            wi = c % 2
            pre, acc = pres[wi], accs[wi]
            PRE, ACC, PRE0 = ("pre", wi), ("acc", wi), ("pre0", wi)
            load_w(1536 + c * 128, 128, wc[wi][:], ("wc", wi))
            for tc in range(4):
                bank = 4 + (nb % 4)
                nb += 1
                for k in range(8):
                    S.emit("pe", lambda E, k=k, tc=tc, bank=bank, wi=wi: E.matmul(
                        PS[bank][:, :], lhsT=wc[wi][:, k, :], rhs=hT[:, k, tc * 512:(tc + 1) * 512],
                        start=(k == 0), stop=(k == 7)),
                           reads=[("wc", wi)] + [("hT", 4 * tc + j) for j in range(4)], writes=[("ps", bank)],
                           signal=(k == 7))
                S.emit("act", lambda E, tc=tc, bank=bank, pre=pre: E.activation(
                    out=pre[:, 3 + tc * 512:3 + (tc + 1) * 512], in_=PS[bank][:, :], func=AF.Identity),
                       reads=[("ps", bank)], writes=[PRE])
            ce = "dve"
            S.emit(ce, lambda E, c=c, pre=pre, acc=acc: E.tensor_scalar(out=acc[:], in0=pre[:, 0:SEQ],
                                                                        scalar1=cw[:, c, 0:1], scalar2=None, op0=ALU.mult),
                   reads=[PRE, PRE0, "cw"], writes=[ACC])
            for tp in range(1, 4):
                S.emit(ce, lambda E, c=c, tp=tp, pre=pre, acc=acc: E.scalar_tensor_tensor(
                    out=acc[:], in0=pre[:, tp:tp + SEQ], scalar=cw[:, c, tp:tp + 1], in1=acc[:],
                    op0=ALU.mult, op1=ALU.add),
                       reads=[PRE, PRE0, "cw", ACC], writes=[ACC])
            dst = dsts[c // 4]
            dn = ("gq", "gk", "gv")[c // 4]
            S.emit("act", lambda E, dst=dst, c=c, acc=acc: E.activation(out=dst[:, c % 4, :], in_=acc[:], func=AF.Silu),
                   reads=[ACC], writes=[(dn, c % 4)])
        for c in range(8):
            dst = dsts[c // 4]
            dn = ("gq", "gk")[c // 4]
            pr = c % 4
            S.emit("act", lambda E, dst=dst, pr=pr: E.activation(out=sq[:], in_=dst[:, pr, :], func=AF.Square),
                   reads=[(dn, pr)], writes=["sqg"])
            for tc in range(4):
                bank = 4 + (nb % 4)
                nb += 1
                cs = slice(tc * 512, (tc + 1) * 512)
                S.emit("pe", lambda E, bank=bank, cs=cs: E.matmul(PS[bank][:, :], lhsT=BLK[:], rhs=sq[:, cs],
                                                                  start=True, stop=True),
                       reads=["sqg", "BLK"], writes=[("ps", bank)])
                sr = srs[tc % 2]
                SR = ("srg", tc % 2)
                S.emit("act", lambda E, bank=bank, sr=sr: E.activation(out=sr[:], in_=PS[bank][:, :], func=AF.Sqrt,
                                                                       bias=P["epsc"][:], scale=1.0),
                       reads=[("ps", bank), "epsc"], writes=[SR])
                S.emit("dve", lambda E, sr=sr: E.reciprocal(out=sr[:], in_=sr[:]), reads=[SR], writes=[SR])
                scl = 0.125 if c < 4 else 1.0
                S.emit("dve", lambda E, dst=dst, pr=pr, cs=cs, scl=scl, sr=sr: E.scalar_tensor_tensor(
                    out=dst[:, pr, cs], in0=dst[:, pr, cs], scalar=scl, in1=sr[:], op0=ALU.mult, op1=ALU.mult),
                       reads=[(dn, pr), SR], writes=[(dn, pr)])
        for j in range(4):
            load_w(3072 + j * 128, 128, wz[:, :, j * 128:(j + 1) * 128], "wz")
        load_w(3584, 16, wab[:], "wab")
        zs, gab = G["zs"], G["gab"]
        for t in range(NT):
            bank = 4 + (nb % 4)
            nb += 1
            tk = slice(t * 128, (t + 1) * 128)
            for k in range(8):
                S.emit("pe", lambda E, k=k, tk=tk, bank=bank: E.matmul(PS[bank][:, :], lhsT=hT[:, k, tk],
                                                                       rhs=wz[:, k, :], start=(k == 0), stop=(k == 7)),
                       reads=["wz", ("hT", t)], writes=[("ps", bank)], signal=(k == 7))
            S.emit("act", lambda E, t=t, bank=bank: E.activation(out=zs[:, t, :], in_=PS[bank][:, :], func=AF.Silu),
                   reads=[("ps", bank)], writes=[("gzs", t)])
            bank = 4 + (nb % 4)
            nb += 1
            for k in range(8):
                S.emit("pe", lambda E, k=k, tk=tk, bank=bank: E.matmul(PS[bank][:, 0:16], lhsT=hT[:, k, tk],
                                                                       rhs=wab[:, k, :], start=(k == 0), stop=(k == 7)),
                       reads=["wab", ("hT", t)], writes=[("ps", bank)], signal=(k == 7))
            S.emit("dve", lambda E, t=t, bank=bank: E.tensor_copy(out=gab[:, t, :], in_=PS[bank][:, 0:16]),
                   reads=[("ps", bank)], writes=["gab"])
        g, beta, nbeta = G["g"], G["beta"], G["nbeta"]
        S.emit("dve", lambda E: E.tensor_tensor(out=g[:], in0=gab[:, :, 0:8],
                                                in1=dtb[:].unsqueeze(1).to_broadcast([128, NT, 8]), op=ALU.add),
               reads=["gab", "dtb"], writes=["gg"])
        S.emit("act", lambda E: E.activation(out=g[:], in_=g[:], func=AF.Exp), reads=["gg"], writes=["gg"])
        S.emit("act", lambda E: E.activation(out=g[:], in_=g[:], func=AF.Ln, bias=1.0), reads=["gg"], writes=["gg"])
        S.emit("act", lambda E: E.activation(out=alog[:], in_=alog[:], func=AF.Exp), reads=["alog"], writes=["alog"])
        S.emit("dve", lambda E: E.scalar_tensor_tensor(out=g[:], in0=g[:], scalar=-1.0,
                                                       in1=alog[:].unsqueeze(1).to_broadcast([128, NT, 8]),
                                                       op0=ALU.mult, op1=ALU.mult),
               reads=["gg", "alog"], writes=["gg"])
        S.emit("act", lambda E: E.activation(out=beta[:], in_=gab[:, :, 8:16], func=AF.Sigmoid),
               reads=["gab"], writes=["gbeta"])
        S.emit("dve", lambda E: E.tensor_scalar(out=nbeta[:], in0=beta[:], scalar1=-1.0, scalar2=None, op0=ALU.mult),
               reads=["gbeta"], writes=["gnbeta"])

    def phase_C2(self, st, b, G):
        nc, S, d, P, PS = self.nc, self.S, self.d, self.P, self.PS
        sb = self.sb
        ident = P["ident"]
        oT = P["oT"]
        qTg, kTg, vTg, zs = G["qT"], G["kT"], G["vT"], G["zs"]
        g, beta, nbeta = G["g"], G["beta"], G["nbeta"]
        BIG = 3.0e38
        TRI = sb(st, "TRI", [128, 128], F32)
        BLKS = sb(st, "BLKS", [128, 128], F32)
        MASKU = sb(st, "MASKU", [128, 8, 128], F32)
        STRICT = sb(st, "STRICT", [128, 8, 128], F32)
        HEADM = sb(st, "HEADM", [8, 8, 1], F32)
        SEL = sb(st, "SEL", [8, 4, 128], F32)
        ONES8 = sb(st, "ONES8", [8, 128], F32)
        gnw = sb(st, "gnw_b", [128, 64], F32)
        S.dma(gnw[:], d["gnw"].partition_broadcast(128), writes=["gnw"])

        def tri_like(T, val, strict, name):
            nd = len(T.shape)
            pat = [[0, 8], [1, 128]] if nd == 3 else [[1, 128]]
            pat2 = [[0, 8], [-1, 128]] if nd == 3 else [[-1, 128]]
            S.emit("pool", lambda E: E.memset(T[:], val), writes=[name])
            S.emit("pool", lambda E: E.affine_select(out=T[:], in_=T[:], pattern=pat,
                                                      compare_op=(ALU.is_gt if strict else ALU.is_ge), fill=0.0,
                                                      base=0, channel_multiplier=-1),
                   reads=[name], writes=[name])
            S.emit("pool", lambda E: E.affine_select(out=T[0:64], in_=T[0:64], pattern=pat2,
                                                      compare_op=ALU.is_ge, fill=0.0, base=63, channel_multiplier=0),
                   reads=[name], writes=[name])

        tri_like(TRI, 1.0, False, "TRI")
        tri_like(MASKU, BIG, False, "MASKU")
        tri_like(STRICT, 1.0, True, "STRICT")
        S.emit("pool", lambda E: E.memset(BLKS[:], 0.0), writes=["BLKS"])
        S.emit("pool", lambda E: E.memset(BLKS[0:64, 0:64], 1.0), reads=["BLKS"], writes=["BLKS"])
        S.emit("pool", lambda E: E.memset(BLKS[64:128, 64:128], 1.0), reads=["BLKS"], writes=["BLKS"])
        S.emit("pool", lambda E: E.memset(HEADM[:], 0.0), writes=["HEADM"])
        S.emit("pool", lambda E: E.affine_select(out=HEADM[:], in_=HEADM[:], pattern=[[-1, 8], [0, 1]],
                                                  compare_op=ALU.not_equal, fill=1.0, base=0, channel_multiplier=1),
               reads=["HEADM"], writes=["HEADM"])
        S.emit("pool", lambda E: E.memset(SEL[:], 0.0), writes=["SEL"])
        for half in range(2):
            S.emit("pool", lambda E, half=half: E.affine_select(
                out=SEL[:, :, half * 64:(half + 1) * 64], in_=SEL[:, :, half * 64:(half + 1) * 64],
                pattern=[[-2, 4], [0, 64]], compare_op=ALU.not_equal, fill=1.0, base=-half, channel_multiplier=1),
                   reads=["SEL"], writes=["SEL"])
        S.emit("pool", lambda E: E.memset(ONES8[:], 1.0), writes=["ONES8"])

        import os
        NPS = int(os.environ.get("C2NPS", "1"))
        NSLOT = NPS + 1
        rhsBDs = [sb(st, "rhsBD%d" % i, [8, 8, 128], F32) for i in range(NPS)]
        gcTs = [sb(st, "gcT%d" % i, [8, 128], F32) for i in range(NPS)]
        gcts = [sb(st, "gct%d" % i, [128, 24], F32) for i in range(NPS)]
        egts = [sb(st, "egt%d" % i, [128, 16], F32) for i in range(NPS)]
        EAs = [sb(st, "EA%d" % i, [128, 8, 128], F32) for i in range(NPS)]
        EAsbs = [sb(st, "EAsb%d" % i, [128, 8, 128], F32) for i in range(NPS)]
        ktoks = [sb(st, "ktok%d" % i, [128, 8, 64], BF16) for i in range(NPS)]
        Bms = [[sb(st, "Bm%d%d" % (p, i), [128, 8, 128], BF16) for i in range(2)] for p in range(NPS)]
        Nms = [[sb(st, "Nm%d%d" % (p, i), [128, 8, 128], BF16) for i in range(2)] for p in range(NPS)]
        qzs = [sb(st, "qz%d" % i, [128, 2, 4, 128], BF16) for i in range(NPS)]
        kzs = [sb(st, "kz%d" % i, [128, 2, 4, 128], BF16) for i in range(NPS)]
        X0s = [sb(st, "X0_%d" % i, [128, 2, 8, 64], BF16) for i in range(NPS)]
        Bp0s = [sb(st, "Bp0_%d" % i, [128, 8, 128], BF16) for i in range(NPS)]
        EGs = [sb(st, "EG%d" % i, [128, 4, 128], F32) for i in range(NSLOT)]
        qdTs = [sb(st, "qdT%d" % i, [128, 4, 128], BF16) for i in range(NSLOT)]
        kdecs = [[sb(st, "kdec%d%d" % (p, i), [128, 8, 64], BF16) for i in range(2)] for p in range(NSLOT)]
        X1s = [sb(st, "X1_%d" % i, [128, 2, 8, 64], BF16) for i in range(NSLOT)]
        attnTs = [sb(st, "attnT%d" % i, [128, 8, 128], BF16) for i in range(NSLOT)]
        Bp1s = [sb(st, "Bp1_%d" % i, [128, 8, 128], BF16) for i in range(NSLOT)]
        nwTs = [sb(st, "nwT%d" % i, [128, 4, 128], BF16) for i in range(NSLOT)]
        identb = ident[:].unsqueeze(1)
        S32 = sb(st, "S32", [128, 4, 64], F32)
        tmpS = sb(st, "tmpS", [128, 4, 64], F32)
        Sb = sb(st, "Sb", [128, 4, 2, 64], BF16)
        vnew = sb(st, "vnew", [128, 8, 64], BF16)
        osb = sb(st, "osb", [128, 8, 64], F32)
        osq = sb(st, "osq", [128, 8, 64], F32)
        oss = sb(st, "oss", [128, 16], F32)
        og = sb(st, "og", [128, 512], BF16)
        for i in range(NPS):
            S.emit("pool", lambda E, i=i: E.memset(qzs[i][:], 0.0), writes=[("qz", i)])
            S.emit("pool", lambda E, i=i: E.memset(kzs[i][:], 0.0), writes=[("kz", i)])
        S.emit("dve", lambda E: E.memset(S32[:], 0.0), writes=["S32"])
        S.emit("dve", lambda E: E.memset(Sb[:], 0.0), writes=["Sb"])
        S.emit("dve", lambda E: E.memset(vnew[:], 0.0), writes=["vnew"])
        for p in range(NSLOT):
            for i in range(2):
                S.emit("pool", lambda E, p=p, i=i: E.memset(kdecs[p][i][:], 0.0), writes=[("kdec", p, i)])
        self._bp = 0
        self._bs = 0
        PBANKS = (4, 5, 1, 2)
        SBANKS = (6, 7)

        def pbank():
            self._bp += 1
            return PBANKS[self._bp % 4]

        def sbank():
            self._bs += 1
            return SBANKS[self._bs % 2]

        def gen_P(t):
            par = t % NSLOT
            ps = t % NPS
            tk = slice(t * 128, (t + 1) * 128)
            gt = g[:, t, :]
            EG, qdT, kdec, attnT, nwT = EGs[par], qdTs[par], kdecs[par], attnTs[par], nwTs[par]
            X = (X0s[ps], X1s[par])
            Bp = (Bp0s[ps], Bp1s[par])
            XN = (("X0", ps), ("X", par, 1))
            BPN = (("Bp0", ps), ("Bp", par, 1))
            rhsBD, gcT, gct, egt, EA, EAsb, ktok = rhsBDs[ps], gcTs[ps], gcts[ps], egts[ps], EAs[ps], EAsbs[ps], ktoks[ps]
            Bm, Nm, qz, kz = Bms[ps], Nms[ps], qzs[ps], kzs[ps]
            RB, GC, GT_, EGT, EAn, EASn, KT = ("rhsBD", ps), ("gcT", ps), ("gct", ps), ("egt", ps), ("EA", ps), ("EAsb", ps), ("ktok", ps)
            QZ, KZ = ("qz", ps), ("kz", ps)
            MYB = (0, 1, 2) if ps == 0 else (3, 4, 5)
            ROT = MYB if NPS == 2 else (3, 4, 5, 1, 2)
            B0, B1_, B2_ = MYB
            rot = [0]

            def pbank():
                rot[0] += 1
                return ROT[rot[0] % len(ROT)]
            if NPS == 1 and os.environ.get("C2REORD", "1") == "1":
                bk, bv, RBK, kq, kk = 2, 3, 1, (4, 5), (2, 3)
                S.emit("pe", lambda E: E.matmul(PS[B0][0:8, 0:128], lhsT=gt, rhs=TRI[:], start=True, stop=True),
                       reads=["gg", "TRI"], writes=[("ps", B0)], signal=False)
                S.emit("pe", lambda E: E.matmul(PS[B0][:, 128:136], lhsT=TRI[:], rhs=gt, start=True, stop=True),
                       reads=["gg", "TRI"], writes=[("ps", B0)], signal=False)
                S.emit("pe", lambda E: E.matmul(PS[B0][:, 136:144], lhsT=BLKS[:], rhs=gt, start=True, stop=True),
                       reads=["gg", "BLKS"], writes=[("ps", B0)])
                tbk = PS[bk][:].bitcast(BF16)
                for pr in range(4):
                    S.emit("pe", lambda E, pr=pr, tbk=tbk: E.transpose(out=tbk[:, pr * 128:(pr + 1) * 128],
                                                                       in_=kTg[:, pr, tk], identity=ident[:]),
                           reads=[("gk", pr), "ident"], writes=[("ps", bk)], signal=(pr == 3))
                tbv = PS[bv][:].bitcast(BF16)
                for pr in range(4):
                    S.emit("pe", lambda E, pr=pr, tbv=tbv: E.transpose(out=tbv[:, pr * 128:(pr + 1) * 128],
                                                                       in_=vTg[:, pr, tk], identity=ident[:]),
                           reads=[("gv", pr), "ident"], writes=[("ps", bv)], signal=(pr == 3))
                yield
                S.emit("act", lambda E: E.activation(out=gcT[:], in_=PS[B0][0:8, 0:128], func=AF.Identity),
                       reads=[("ps", B0)], writes=[GC])
                S.emit("dve", lambda E: E.tensor_copy(out=gct[:, 0:16], in_=PS[B0][:, 128:144]),
                       reads=[("ps", B0)], writes=[GT_])
                S.emit("dve", lambda E: E.tensor_tensor(out=gct[:, 16:24], in0=gct[:, 8:16], in1=gct[:, 0:8],
                                                        op=ALU.subtract), reads=[GT_], writes=[GT_])
                for hh in range(2):
                    rows = slice(hh * 64, (hh + 1) * 64)
                    S.emit("pool", lambda E, hh=hh, rows=rows: E.tensor_copy(out=kz[rows, hh, :, :], in_=kTg[rows, :, tk]),
                           reads=[("gk", pr) for pr in range(4)], writes=[KZ])
                S.emit("dve", lambda E: E.tensor_tensor(out=rhsBD[:], in0=HEADM[:].to_broadcast([8, 8, 128]),
                                                        in1=gcT[:].unsqueeze(1).to_broadcast([8, 8, 128]), op=ALU.mult),
                       reads=[GC, "HEADM"], writes=[RB])
                yield
                S.emit("act", lambda E, tbk=tbk: E.activation(out=ktok[:].rearrange("p h d -> p (h d)"),
                                                              in_=tbk[:, 0:512], func=AF.Identity),
                       reads=[("ps", bk)], writes=[KT])
                S.emit("act", lambda E: E.activation(out=egt[:, 0:8], in_=gct[:, 0:8], func=AF.Exp),
                       reads=[GT_], writes=[EGT])
                S.emit("act", lambda E: E.activation(out=egt[:, 8:16], in_=gct[:, 16:24], func=AF.Exp),
                       reads=[GT_, EGT], writes=[EGT])
                X0 = X[0]
                S.emit("dve", lambda E, tbv=tbv: E.tensor_copy(out=X0[:, 0, :, :],
                                                               in_=tbv[:, 0:512].rearrange("p (h d) -> p h d", h=8)),
                       reads=[("ps", bv)], writes=[XN[0] + (0,), XN[0] + (1,)])
                yield
                for hf in range(2):
                    S.emit("pe", lambda E, hf=hf: E.matmul(
                        PS[RBK][:, :], lhsT=ONES8[:], rhs=rhsBD[:, 4 * hf:4 * hf + 4, :].rearrange("p h i -> p (h i)"),
                        start=True, stop=True),
                           reads=[RB, "ONES8"], writes=[("ps", RBK)])
                    S.emit("dve", lambda E, hf=hf: E.tensor_tensor(
                        out=EA[:, 4 * hf:4 * hf + 4, :], in0=PS[RBK][:, :].rearrange("p (h i) -> p h i", h=4),
                        in1=gct[:, 4 * hf:4 * hf + 4].unsqueeze(2).to_broadcast([128, 4, 128]), op=ALU.subtract),
                           reads=[("ps", RBK), GT_], writes=[EAn])
                    for h in range(4 * hf, 4 * hf + 4):
                        pr, hh = h // 2, h % 2
                        S.emit("pe", lambda E, pr=pr, hh=hh: E.matmul(
                            PS[kq[hh]][:, pr * 128:(pr + 1) * 128], lhsT=kz[:, hh, pr, :], rhs=qTg[:, pr, tk],
                            start=True, stop=True),
                               reads=[("gq", pr), KZ], writes=[("ps", kq[hh])], signal=(h >= 6))
                    yield
                S.emit("act", lambda E: E.activation(out=EA[:], in_=EA[:], func=AF.Exp), reads=[EAn], writes=[EAn])
                for pr in range(4):
                    S.emit("pe", lambda E, pr=pr: E.matmul(PS[B0][:, pr * 128:(pr + 1) * 128], lhsT=SEL[:, pr, :],
                                                           rhs=gcT[:], start=True, stop=True),
                           reads=[GC, "SEL"], writes=[("ps", B0)], signal=(pr == 3))
                for h in range(8):
                    pr, hh = h // 2, h % 2
                    S.emit("pe", lambda E, pr=pr, hh=hh: E.matmul(
                        PS[kk[hh]][:, pr * 128:(pr + 1) * 128], lhsT=kz[:, hh, pr, :], rhs=kTg[:, pr, tk],
                        start=True, stop=True),
                           reads=[("gk", pr), KZ], writes=[("ps", kk[hh])], signal=(h >= 6))
                yield
                S.emit("dve", lambda E: E.tensor_tensor(out=EA[:], in0=EA[:], in1=MASKU[:], op=ALU.min),
                       reads=[EAn, "MASKU"], writes=[EAn])
                S.emit("act", lambda E: E.activation(out=EG[:].rearrange("p a i -> p (a i)"), in_=PS[B0][:, :], func=AF.Exp),
                       reads=[("ps", B0)], writes=[("EG", par)])
                S.emit("pool", lambda E: E.tensor_tensor(out=EAsb[:], in0=EA[:], in1=STRICT[:], op=ALU.mult),
                       reads=[EAn, "STRICT"], writes=[EASn])
                S.emit("pool", lambda E: E.tensor_tensor(out=EAsb[:], in0=EAsb[:],
                                                         in1=nbeta[:, t, :].unsqueeze(2).to_broadcast([128, 8, 128]),
                                                         op=ALU.mult),
                       reads=[EASn, "gnbeta"], writes=[EASn])
                yield
                for hh in range(2):
                    S.emit("dve", lambda E, hh=hh: E.tensor_tensor(
                        out=attnT[:, hh:8:2, :], in0=PS[kq[hh]][:, :].rearrange("p (a i) -> p a i", a=4),
                        in1=EA[:, hh:8:2, :], op=ALU.mult),
                           reads=[("ps", kq[hh]), EAn], writes=[("attnT", par)])
                S.emit("dve", lambda E: E.tensor_tensor(out=X0[:, 1, :, :], in0=ktok[:],
                                                        in1=egt[:, 0:8].unsqueeze(2).to_broadcast([128, 8, 64]),
                                                        op=ALU.mult),
                       reads=[KT, EGT, XN[0] + (0,), XN[0] + (1,)], writes=[XN[0] + (0,), XN[0] + (1,)])
                for hh in range(2):
                    S.emit("dve", lambda E, hh=hh: E.tensor_tensor(
                        out=Bm[0][:, hh:8:2, :], in0=PS[kk[hh]][:, :].rearrange("p (a i) -> p a i", a=4),
                        in1=EAsb[:, hh:8:2, :], op=ALU.mult),
                           reads=[("ps", kk[hh]), EASn], writes=[("Bm", ps, 0, 0), ("Bm", ps, 0, 1)])
                S.emit("pool", lambda E: E.tensor_tensor(out=qdT[:], in0=qTg[:, :, tk], in1=EG[:], op=ALU.mult),
                       reads=[("EG", par)] + [("gq", pr) for pr in range(4)], writes=[("qdT", par)])
                for hf in range(2):
                    rows = slice(hf * 64, (hf + 1) * 64)
                    S.emit("pool", lambda E, hf=hf, rows=rows: E.tensor_tensor(
                        out=kdec[hf][rows], in0=ktok[rows],
                        in1=egt[rows, 8:16].unsqueeze(2).to_broadcast([64, 8, 64]), op=ALU.mult),
                           reads=[KT, EGT], writes=[("kdec", par, hf)])
                S.emit("pool", lambda E: E.tensor_tensor(out=Bp[0][:], in0=Bm[0][:],
                                                         in1=identb.to_broadcast([128, 8, 128]), op=ALU.add),
                       reads=[("Bm", ps, 0, 0), ("Bm", ps, 0, 1), "ident"], writes=[BPN[0] + (0,), BPN[0] + (1,)])
                yield
            else:
                S.emit("pe", lambda E: E.matmul(PS[B0][0:8, 0:128], lhsT=gt, rhs=TRI[:], start=True, stop=True),
                       reads=["gg", "TRI"], writes=[("ps", B0)], signal=False)
                S.emit("pe", lambda E: E.matmul(PS[B0][:, 128:136], lhsT=TRI[:], rhs=gt, start=True, stop=True),
                       reads=["gg", "TRI"], writes=[("ps", B0)], signal=False)
                S.emit("pe", lambda E: E.matmul(PS[B0][:, 136:144], lhsT=BLKS[:], rhs=gt, start=True, stop=True),
                       reads=["gg", "BLKS"], writes=[("ps", B0)])
                yield
                S.emit("act", lambda E: E.activation(out=gcT[:], in_=PS[B0][0:8, 0:128], func=AF.Identity),
                       reads=[("ps", B0)], writes=[GC])
                S.emit("dve", lambda E: E.tensor_copy(out=gct[:, 0:16], in_=PS[B0][:, 128:144]),
                       reads=[("ps", B0)], writes=[GT_])
                S.emit("dve", lambda E: E.tensor_tensor(out=gct[:, 16:24], in0=gct[:, 8:16], in1=gct[:, 0:8],
                                                        op=ALU.subtract), reads=[GT_], writes=[GT_])
                S.emit("act", lambda E: E.activation(out=egt[:, 0:8], in_=gct[:, 0:8], func=AF.Exp),
                       reads=[GT_], writes=[EGT])
                S.emit("act", lambda E: E.activation(out=egt[:, 8:16], in_=gct[:, 16:24], func=AF.Exp),
                       reads=[GT_, EGT], writes=[EGT])
                yield
                S.emit("dve", lambda E: E.tensor_tensor(out=rhsBD[:], in0=HEADM[:].to_broadcast([8, 8, 128]),
                                                        in1=gcT[:].unsqueeze(1).to_broadcast([8, 8, 128]), op=ALU.mult),
                       reads=[GC, "HEADM"], writes=[RB])
                for hf in range(2):
                    S.emit("pe", lambda E, hf=hf: E.matmul(
                        PS[MYB[1 + hf]][:, :], lhsT=ONES8[:], rhs=rhsBD[:, 4 * hf:4 * hf + 4, :].rearrange("p h i -> p (h i)"),
                        start=True, stop=True),
                           reads=[RB, "ONES8"], writes=[("ps", MYB[1 + hf])])
                yield
                for hf in range(2):
                    S.emit("dve", lambda E, hf=hf: E.tensor_tensor(
                        out=EA[:, 4 * hf:4 * hf + 4, :], in0=PS[MYB[1 + hf]][:, :].rearrange("p (h i) -> p h i", h=4),
                        in1=gct[:, 4 * hf:4 * hf + 4].unsqueeze(2).to_broadcast([128, 4, 128]), op=ALU.subtract),
                           reads=[("ps", MYB[1 + hf]), GT_], writes=[EAn])
                S.emit("act", lambda E: E.activation(out=EA[:], in_=EA[:], func=AF.Exp), reads=[EAn], writes=[EAn])
                yield
                S.emit("dve", lambda E: E.tensor_tensor(out=EA[:], in0=EA[:], in1=MASKU[:], op=ALU.min),
                       reads=[EAn, "MASKU"], writes=[EAn])
                S.emit("pool", lambda E: E.tensor_tensor(out=EAsb[:], in0=EA[:], in1=STRICT[:], op=ALU.mult),
                       reads=[EAn, "STRICT"], writes=[EASn])
                S.emit("pool", lambda E: E.tensor_tensor(out=EAsb[:], in0=EAsb[:],
                                                         in1=nbeta[:, t, :].unsqueeze(2).to_broadcast([128, 8, 128]),
                                                         op=ALU.mult),
                       reads=[EASn, "gnbeta"], writes=[EASn])
                yield
                for pr in range(4):
                    S.emit("pe", lambda E, pr=pr: E.matmul(PS[B0][:, pr * 128:(pr + 1) * 128], lhsT=SEL[:, pr, :],
                                                           rhs=gcT[:], start=True, stop=True),
                           reads=[GC, "SEL"], writes=[("ps", B0)], signal=(pr == 3))
                S.emit("act", lambda E: E.activation(out=EG[:].rearrange("p a i -> p (a i)"), in_=PS[B0][:, :], func=AF.Exp),
                       reads=[("ps", B0)], writes=[("EG", par)])
                S.emit("pool", lambda E: E.tensor_tensor(out=qdT[:], in0=qTg[:, :, tk], in1=EG[:], op=ALU.mult),
                       reads=[("EG", par)] + [("gq", pr) for pr in range(4)], writes=[("qdT", par)])
                yield
                bk = pbank()
                tbk = PS[bk][:].bitcast(BF16)
                for pr in range(4):
                    S.emit("pe", lambda E, pr=pr, tbk=tbk: E.transpose(out=tbk[:, pr * 128:(pr + 1) * 128],
                                                                       in_=kTg[:, pr, tk], identity=ident[:]),
                           reads=[("gk", pr), "ident"], writes=[("ps", bk)], signal=(pr == 3))
                S.emit("act", lambda E, tbk=tbk: E.activation(out=ktok[:].rearrange("p h d -> p (h d)"),
                                                              in_=tbk[:, 0:512], func=AF.Identity),
                       reads=[("ps", bk)], writes=[KT])
                bv = pbank()
                tbv = PS[bv][:].bitcast(BF16)
                for pr in range(4):
                    S.emit("pe", lambda E, pr=pr, tbv=tbv: E.transpose(out=tbv[:, pr * 128:(pr + 1) * 128],
                                                                       in_=vTg[:, pr, tk], identity=ident[:]),
                           reads=[("gv", pr), "ident"], writes=[("ps", bv)], signal=(pr == 3))
                yield
                X0 = X[0]
                S.emit("dve", lambda E, tbv=tbv: E.tensor_copy(out=X0[:, 0, :, :],
                                                               in_=tbv[:, 0:512].rearrange("p (h d) -> p h d", h=8)),
                       reads=[("ps", bv)], writes=[XN[0] + (0,), XN[0] + (1,)])
                S.emit("dve", lambda E: E.tensor_tensor(out=X0[:, 1, :, :], in0=ktok[:],
                                                        in1=egt[:, 0:8].unsqueeze(2).to_broadcast([128, 8, 64]),
                                                        op=ALU.mult),
                       reads=[KT, EGT, XN[0] + (0,), XN[0] + (1,)], writes=[XN[0] + (0,), XN[0] + (1,)])
                for hf in range(2):
                    rows = slice(hf * 64, (hf + 1) * 64)
                    S.emit("pool", lambda E, hf=hf, rows=rows: E.tensor_tensor(
                        out=kdec[hf][rows], in0=ktok[rows],
                        in1=egt[rows, 8:16].unsqueeze(2).to_broadcast([64, 8, 64]), op=ALU.mult),
                           reads=[KT, EGT], writes=[("kdec", par, hf)])
                yield
                for hh in range(2):
                    rows = slice(hh * 64, (hh + 1) * 64)
                    S.emit("pool", lambda E, hh=hh, rows=rows: E.tensor_copy(out=qz[rows, hh, :, :], in_=qTg[rows, :, tk]),
                           reads=[("gq", pr) for pr in range(4)], writes=[QZ])
                    S.emit("pool", lambda E, hh=hh, rows=rows: E.tensor_copy(out=kz[rows, hh, :, :], in_=kTg[rows, :, tk]),
                           reads=[("gk", pr) for pr in range(4)], writes=[KZ])
                kq = (pbank(), pbank())
                for h in range(8):
                    pr, hh = h // 2, h % 2
                    S.emit("pe", lambda E, pr=pr, hh=hh: E.matmul(
                        PS[kq[hh]][:, pr * 128:(pr + 1) * 128], lhsT=kTg[:, pr, tk], rhs=qz[:, hh, pr, :],
                        start=True, stop=True),
                           reads=[("gk", pr), QZ], writes=[("ps", kq[hh])], signal=(h >= 6))
                for hh in range(2):
                    S.emit("dve", lambda E, hh=hh: E.tensor_tensor(
                        out=attnT[:, hh:8:2, :], in0=PS[kq[hh]][:, :].rearrange("p (a i) -> p a i", a=4),
                        in1=EA[:, hh:8:2, :], op=ALU.mult),
                           reads=[("ps", kq[hh]), EAn], writes=[("attnT", par)])
                yield
                kk = (pbank(), pbank())
                for h in range(8):
                    pr, hh = h // 2, h % 2
                    S.emit("pe", lambda E, pr=pr, hh=hh: E.matmul(
                        PS[kk[hh]][:, pr * 128:(pr + 1) * 128], lhsT=kTg[:, pr, tk], rhs=kz[:, hh, pr, :],
                        start=True, stop=True),
                           reads=[("gk", pr), KZ], writes=[("ps", kk[hh])], signal=(h >= 6))
                for hh in range(2):
                    S.emit("dve", lambda E, hh=hh: E.tensor_tensor(
                        out=Bm[0][:, hh:8:2, :], in0=PS[kk[hh]][:, :].rearrange("p (a i) -> p a i", a=4),
                        in1=EAsb[:, hh:8:2, :], op=ALU.mult),
                           reads=[("ps", kk[hh]), EASn], writes=[("Bm", ps, 0, 0), ("Bm", ps, 0, 1)])
                S.emit("pool", lambda E: E.tensor_tensor(out=Bp[0][:], in0=Bm[0][:], in1=identb.to_broadcast([128, 8, 128]),
                                                         op=ALU.add),
                       reads=[("Bm", ps, 0, 0), ("Bm", ps, 0, 1), "ident"], writes=[BPN[0] + (0,), BPN[0] + (1,)])
                yield
            for a in range(2):
                bn = pbank()
                for h4 in range(4):
                    h = 4 * a + h4
                    S.emit("pe", lambda E, h=h, h4=h4, bn=bn: E.matmul(
                        PS[bn][:, h4 * 128:(h4 + 1) * 128], lhsT=Bm[0][:, h, :], rhs=ident[:], start=True, stop=True),
                           reads=[("Bm", ps, 0, a), "ident"], writes=[("ps", bn)], signal=(h4 == 3))
                self.evac(Nm[0][:, 4 * a:4 * a + 4, :].rearrange("p h i -> p (h i)"), PS[bn][:, :],
                          reads=[("ps", bn)], writes=[("Nm", ps, 0, a)])
                yield
            for lv in range(5):
                ci, ni = lv % 2, (lv + 1) % 2
                for a in range(2):
                    if lv < 4:
                        bnn = pbank()
                        for h4 in range(4):
                            h = 4 * a + h4
                            S.emit("pe", lambda E, h=h, h4=h4, bnn=bnn, ci=ci: E.matmul(
                                PS[bnn][:, h4 * 128:(h4 + 1) * 128], lhsT=Bm[ci][:, h, :], rhs=Nm[ci][:, h, :],
                                start=True, stop=True),
                                   reads=[("Nm", ps, ci, a), ("Bm", ps, ci, a)], writes=[("ps", bnn)], signal=(h4 == 3))
                        self.evac(Nm[ni][:, 4 * a:4 * a + 4, :].rearrange("p h i -> p (h i)"), PS[bnn][:, :],
                                  reads=[("ps", bnn)], writes=[("Nm", ps, ni, a)])
                        yield
                    bbb = pbank()
                    for h4 in range(4):
                        h = 4 * a + h4
                        S.emit("pe", lambda E, h=h, h4=h4, bbb=bbb, ci=ci: E.matmul(
                            PS[bbb][:, h4 * 128:(h4 + 1) * 128], lhsT=Nm[ci][:, h, :], rhs=Bm[ci][:, h, :],
                            start=True, stop=True),
                               reads=[("Nm", ps, ci, a), ("Bm", ps, ci, a)], writes=[("ps", bbb)], signal=(h4 == 3))
                    self.evac(Bm[ni][:, 4 * a:4 * a + 4, :].rearrange("p h i -> p (h i)"), PS[bbb][:, :],
                              reads=[("ps", bbb)], writes=[("Bm", ps, ni, a)])
                    S.emit("pool", lambda E, a=a, ni=ni: E.tensor_tensor(
                        out=Bp[ni][:, 4 * a:4 * a + 4, :], in0=Bm[ni][:, 4 * a:4 * a + 4, :],
                        in1=identb.to_broadcast([128, 4, 128]), op=ALU.add),
                           reads=[("Bm", ps, ni, a), "ident"], writes=[BPN[ni] + (a,)])
                    yield
                for a in range(2):
                    bx = pbank()
                    for h4 in range(4):
                        h = 4 * a + h4
                        S.emit("pe", lambda E, h=h, h4=h4, bx=bx, ci=ci: E.matmul(
                            PS[bx][:, h4 * 128:(h4 + 1) * 128], lhsT=Bp[ci][:, h, :], rhs=X[ci][:, :, h, :],
                            start=True, stop=True),
                               reads=[XN[ci] + (a,), BPN[ci] + (a,)], writes=[("ps", bx)], signal=(h4 == 3))
                    self.evac(X[ni][:, :, 4 * a:4 * a + 4, :].rearrange("p s h d -> p h s d"),
                              PS[bx][:, :].rearrange("p (h s d) -> p h s d", h=4, s=2),
                              reads=[("ps", bx)], writes=[XN[ni] + (a,)])
                    yield
            X5, B5 = X[1], Bp[1]

        def gen_S(t):
            par = t % NSLOT
            tk = slice(t * 128, (t + 1) * 128)
            EG, qdT, kdec, attnT, nwT = EGs[par], qdTs[par], kdecs[par], attnTs[par], nwTs[par]
            X5, B5 = X1s[par], Bp1s[par]
            for a in range(2):
                bw = sbank()
                for h4 in range(4):
                    h = 4 * a + h4
                    pr = h // 2
                    lw = X5[:, 1, 2 * pr:2 * pr + 2, :].rearrange("p h d -> p (h d)")
                    S.emit("pe", lambda E, h=h, h4=h4, bw=bw, lw=lw: E.matmul(
                        PS[bw][:, h4 * 128:(h4 + 1) * 128], lhsT=lw, rhs=B5[:, h, :], start=True, stop=True),
                           reads=[("X", par, 1, a), ("Bp", par, 1, a)], writes=[("ps", bw)], signal=(h4 == 3))
                for h4 in range(4):
                    h = 4 * a + h4
                    pr, hh = h // 2, h % 2
                    rows = slice(hh * 64, (hh + 1) * 64)
                    S.emit("dve", lambda E, h4=h4, bw=bw, pr=pr, rows=rows: E.tensor_scalar(
                        out=nwT[rows, pr, :], in0=PS[bw][rows, h4 * 128:(h4 + 1) * 128], scalar1=-1.0, scalar2=None,
                        op0=ALU.mult),
                           reads=[("ps", bw)], writes=[("nwT", par)])
                yield
            for hf in range(2):
                rows = slice(hf * 64, (hf + 1) * 64)
                bvn = sbank()
                for h in range(8):
                    pr, hh = h // 2, h % 2
                    cs = slice(h * 64, (h + 1) * 64)
                    S.emit("pe", lambda E, h=h, cs=cs, bvn=bvn: E.matmul(
                        PS[bvn][:, cs], lhsT=B5[:, h, :], rhs=X5[:, 0, h, :], start=True, stop=False),
                           reads=[("X", par, 1, 0), ("X", par, 1, 1), ("Bp", par, 1, 0), ("Bp", par, 1, 1)], writes=[("ps", bvn)], signal=False)
                    S.emit("pe", lambda E, pr=pr, hh=hh, cs=cs, bvn=bvn: E.matmul(
                        PS[bvn][:, cs], lhsT=nwT[:, pr, :], rhs=Sb[:, pr, hh, :], start=False, stop=True),
                           reads=[("nwT", par), "Sb"], writes=[("ps", bvn)], signal=(h == 7))
                yield
                S.emit("dve", lambda E, rows=rows, bvn=bvn: E.tensor_tensor(
                    out=vnew[rows], in0=PS[bvn][rows, :].rearrange("p (h d) -> p h d", h=8),
                    in1=beta[rows, t, :].unsqueeze(2).to_broadcast([64, 8, 64]), op=ALU.mult),
                       reads=[("ps", bvn), "gbeta"], writes=["vnew"])
                yield
                bo = sbank()
                for h in range(8):
                    pr, hh = h // 2, h % 2
                    cs = slice(h * 64, (h + 1) * 64)
                    S.emit("pe", lambda E, pr=pr, hh=hh, cs=cs, bo=bo: E.matmul(
                        PS[bo][:, cs], lhsT=qdT[:, pr, :], rhs=Sb[:, pr, hh, :], start=True, stop=False),
                           reads=[("qdT", par), "Sb"], writes=[("ps", bo)], signal=False)
                    S.emit("pe", lambda E, h=h, cs=cs, bo=bo: E.matmul(
                        PS[bo][:, cs], lhsT=attnT[:, h, :], rhs=vnew[:, h, :], start=False, stop=True),
                           reads=[("attnT", par), "vnew"], writes=[("ps", bo)], signal=(h == 7))
                yield
                S.emit("act", lambda E, rows=rows, bo=bo: E.activation(
                    out=osb[rows].rearrange("p h d -> p (h d)"), in_=PS[bo][rows, :], func=AF.Identity),
                       reads=[("ps", bo)], writes=["osb"])
                bs = sbank()
                for h in range(8):
                    pr, hh = h // 2, h % 2
                    S.emit("pe", lambda E, h=h, pr=pr, hh=hh, bs=bs, hf=hf: E.matmul(
                        PS[bs][:, (pr * 2 + hh) * 64:(pr * 2 + hh + 1) * 64],
                        lhsT=kdec[hf][:, 2 * pr:2 * pr + 2, :].rearrange("p h d -> p (h d)"), rhs=vnew[:, h, :],
                        start=True, stop=True),
                           reads=[("kdec", par, hf), "vnew"], writes=[("ps", bs)], signal=(h == 7))
                yield
                gl = EG[:, :, hf * 64 + 63:hf * 64 + 64]
                S.emit("dve", lambda E, gl=gl: E.tensor_tensor(out=tmpS[:], in0=S32[:],
                                                               in1=gl.to_broadcast([128, 4, 64]), op=ALU.mult),
                       reads=["S32", ("EG", par)], writes=["tmpS"])
                dS = PS[bs][:, :].rearrange("p (a b d) -> p a b d", a=4, b=2)
                for hh in range(2):
                    r2 = slice(hh * 64, (hh + 1) * 64)
                    S.emit("dve", lambda E, hh=hh, r2=r2, dS=dS: E.tensor_tensor(
                        out=S32[r2], in0=tmpS[r2], in1=dS[r2, :, hh, :], op=ALU.add),
                           reads=["tmpS", ("ps", bs)], writes=["S32"])
                    S.emit("act", lambda E, hh=hh, r2=r2: E.activation(out=Sb[r2, :, hh, :], in_=S32[r2],
                                                                       func=AF.Identity),
                           reads=["S32"], writes=["Sb"])
                yield
            S.emit("pool", lambda E: E.tensor_tensor(out=osq[:], in0=osb[:], in1=osb[:], op=ALU.mult),
                   reads=["osb"], writes=["osq"])
            S.emit("dve", lambda E: E.tensor_reduce(out=oss[:, 0:8], in_=osq[:], axis=AX.X, op=ALU.add),
                   reads=["osq"], writes=["oss"])
            yield
            S.emit("act", lambda E: E.activation(out=oss[:, 8:16], in_=oss[:, 0:8], func=AF.Sqrt, scale=1.0 / 64,
                                                 bias=P["epsc"][:]),
                   reads=["oss", "epsc"], writes=["oss"])
            S.emit("dve", lambda E: E.reciprocal(out=oss[:, 0:8], in_=oss[:, 8:16]), reads=["oss"], writes=["oss"])
            S.emit("dve", lambda E: E.tensor_tensor(out=osq[:], in0=osb[:],
                                                    in1=oss[:, 0:8].unsqueeze(2).to_broadcast([128, 8, 64]),
                                                    op=ALU.mult),
                   reads=["osb", "oss", "osq"], writes=["osq"])
            yield
            S.emit("pool", lambda E: E.tensor_tensor(out=osq[:], in0=osq[:],
                                                     in1=gnw[:].unsqueeze(1).to_broadcast([128, 8, 64]),
                                                     op=ALU.mult),
                   reads=["osq", "gnw"], writes=["osq"])
            S.emit("dve", lambda E: E.tensor_tensor(out=og[:], in0=osq[:].rearrange("p h d -> p (h d)"),
                                                    in1=zs[:, t, :], op=ALU.mult),
                   reads=["osq", ("gzs", t)], writes=["og"])
            yield
            bt = sbank()
            tbo = PS[bt][:].bitcast(BF16)
            for pr in range(4):
                S.emit("pe", lambda E, pr=pr, tbo=tbo: E.transpose(out=tbo[:, pr * 128:(pr + 1) * 128],
                                                                   in_=og[:, pr * 128:(pr + 1) * 128],
                                                                   identity=ident[:]),
                       reads=["og", "ident"], writes=[("ps", bt)], signal=(pr == 3))
            S.emit("act", lambda E, tbo=tbo: E.activation(out=oT[:, 4:8, tk],
                                                          in_=tbo[:, 0:512].rearrange("p (a i) -> p a i", a=4),
                                                          func=AF.Identity),
                   reads=[("ps", bt)], writes=[("oT", 4 + pr) for pr in range(4)])
            yield

        def drain(gen):
            for _ in gen:
                pass

        def step(gen):
            try:
                next(gen)
                return True
            except StopIteration:
                return False

        active_p = []
        next_p = 0
        p_done = set()
        scan_t = 0
        scan_gen = None
        while scan_t < NT:
            while next_p < NT and len(active_p) < NPS and next_p < scan_t + NSLOT:
                active_p.append([next_p, gen_P(next_p)])
                next_p += 1
            for ent in list(active_p):
                for _ in range(int(os.environ.get("C2RATIO", "2"))):
                    if not step(ent[1]):
                        p_done.add(ent[0])
                        active_p.remove(ent)
                        break
            if scan_gen is None and scan_t in p_done:
                scan_gen = gen_S(scan_t)
            if scan_gen is not None:
                if not step(scan_gen):
                    scan_gen = None
                    scan_t += 1

    def phase_D(self, st, b):
        nc, S, d, P, PS = self.nc, self.S, self.d, self.P, self.PS
        sb = self.sb
        oT = P["oT"]
        ident = P["ident"]
        wo = sb(st, "wo", [128, 8, DM], BF16)
        wu = sb(st, "wu", [128, 8, DFF], BF16)
        wd = sb(st, "wd", [128, 32, DM], BF16)
        premlp = P["premlp"]
        with ExitStack() as s_stg:
            NSTG = 5
            stg = [sb(s_stg, "stgD%d" % i, [128, DM], F32) for i in range(NSTG)]
            self._ns = 0

            def load_cast(src, dst, dname, scal=None):
                i = self._ns % NSTG
                eng = ("pool", "act", "dve")[self._ns % 3]
                self._ns += 1
                S.dma(stg[i][:], src, writes=[("stgD", i)])
                rd = [("stgD", i)] + (["premlp"] if scal is not None else [])
                if eng == "act":
                    if scal is None:
                        S.emit("act", lambda E, i=i: E.activation(out=dst, in_=stg[i][:], func=AF.Identity),
                               reads=rd, writes=[dname])
                    else:
                        S.emit("act", lambda E, i=i: E.activation(out=dst, in_=stg[i][:], func=AF.Identity,
                                                                  scale=scal), reads=rd, writes=[dname])
                else:
                    if scal is None:
                        S.emit(eng, lambda E, i=i: E.tensor_copy(out=dst, in_=stg[i][:]), reads=rd, writes=[dname])
                    else:
                        S.emit(eng, lambda E, i=i: E.tensor_scalar(out=dst, in0=stg[i][:], scalar1=scal, scalar2=None,
                                                                   op0=ALU.mult), reads=rd, writes=[dname])

            for k in range(8):
                load_cast(d["w_out"][k * 128:(k + 1) * 128, :], wo[:, k, :], ("wo", k))
            for k in range(8):
                for qd in range(4):
                    load_cast(d["w_up"][k * 128:(k + 1) * 128, qd * 1024:(qd + 1) * 1024],
                              wu[:, k, qd * 1024:(qd + 1) * 1024], ("wu", k, qd), scal=premlp[:, k:k + 1])
            for k in range(32):
                load_cast(d["w_down"][k * 128:(k + 1) * 128, :], wd[:, k, :], ("wd", k))
        S.barrier()

        GT = 2
        xt = [sb(st, "xtD%d" % i, [128, DM], F32) for i in range(GT)]
        tmp = sb(st, "tmpD", [128, DM], F32)
        h2 = sb(st, "h2D", [128, DM], BF16)
        h2T = sb(st, "h2T", [128, 8, GT * 128], BF16)
        uT = sb(st, "uT", [128, 32, GT * 128], BF16)
        rl = [sb(st, "rlD%d" % i, [128, 512], F32) for i in range(2)]
        sm = sb(st, "smD", [128, NT, 12], F32)
        S.emit("dve", lambda E: E.memset(sm[:], 0.0), writes=["smD"])
        postmix_b, postmlp_b, epsc = P["postmix_b"], P["postmlp_b"], P["epsc"]
        oT_all = [("oT", c) for c in range(8)]

        def rms_scale(src_banks, t, col):
            for hf in range(2):
                S.emit("act", lambda E, hf=hf: E.activation(out=tmp[:, hf * 512:(hf + 1) * 512],
                                                            in_=PS[src_banks[hf]][:, :], func=AF.Square,
                                                            accum_out=sm[:, t, col + hf:col + hf + 1]),
                       reads=[("ps", src_banks[hf]), "smD"], writes=["tmpD", "smD"])
            S.emit("dve", lambda E: E.tensor_tensor(out=sm[:, t, col:col + 1], in0=sm[:, t, col:col + 1],
                                                    in1=sm[:, t, col + 1:col + 2], op=ALU.add),
                   reads=["smD"], writes=["smD"])
            S.emit("act", lambda E: E.activation(out=sm[:, t, col + 1:col + 2], in_=sm[:, t, col:col + 1],
                                                 func=AF.Sqrt, scale=1.0 / DM, bias=epsc[:]),
                   reads=["smD", "epsc"], writes=["smD"])
            S.emit("dve", lambda E: E.reciprocal(out=sm[:, t, col + 2:col + 3], in_=sm[:, t, col + 1:col + 2]),
                   reads=["smD"], writes=["smD"])

        def stage1a(t, j, bk):
            tok = slice(t * 128, (t + 1) * 128)
            S.dma(xt[j][:], d["x"][b, tok, :], writes=[("xtD", j)])
            for hf in range(2):
                for c in range(8):
                    S.emit("pe", lambda E, hf=hf, c=c: E.matmul(PS[bk[hf]][:, :], lhsT=oT[:, c, tok],
                                                                rhs=wo[:, c, hf * 512:(hf + 1) * 512],
                                                                start=(c == 0), stop=(c == 7)),
                           reads=oT_all + ["wo"], writes=[("ps", bk[hf])], signal=(c == 7))

        def stage1b(t, j, bk):
            tok = slice(t * 128, (t + 1) * 128)
            rms_scale(bk, t, 0)
            for hf in range(2):
                cs = slice(hf * 512, (hf + 1) * 512)
                S.emit("dve", lambda E, hf=hf, cs=cs: E.scalar_tensor_tensor(
                    out=tmp[:, cs], in0=PS[bk[hf]][:, :], scalar=sm[:, t, 2:3], in1=postmix_b[:, cs],
                    op0=ALU.mult, op1=ALU.mult),
                       reads=[("ps", bk[hf]), "smD", "postmix_b", "tmpD"], writes=["tmpD"])
            S.emit("dve", lambda E: E.tensor_tensor(out=xt[j][:], in0=tmp[:], in1=xt[j][:], op=ALU.add),
                   reads=["tmpD", ("xtD", j)], writes=[("xtD", j)])
            if "x1" in self.dbg_out:
                S.dma(self.dbg_out["x1"][b, tok, :], xt[j][:], reads=[("xtD", j)], writes=[("dbgx1", t)])
            S.emit("act", lambda E: E.activation(out=h2[:], in_=xt[j][:], func=AF.Square, accum_out=sm[:, t, 3:4]),
                   reads=[("xtD", j), "smD"], writes=["h2D", "smD"])
            S.emit("act", lambda E: E.activation(out=sm[:, t, 4:5], in_=sm[:, t, 3:4], func=AF.Sqrt,
                                                 scale=1.0 / DM, bias=epsc[:]),
                   reads=["smD", "epsc"], writes=["smD"])
            S.emit("dve", lambda E: E.reciprocal(out=sm[:, t, 5:6], in_=sm[:, t, 4:5]), reads=["smD"], writes=["smD"])
            S.emit("act", lambda E: E.activation(out=h2[:], in_=xt[j][:], func=AF.Identity, scale=sm[:, t, 5:6]),
                   reads=[("xtD", j), "smD"], writes=["h2D"])
            tb = PS[2][:].bitcast(BF16)
            for k in range(8):
                S.emit("pe", lambda E, k=k: E.transpose(out=tb[:, k * 128:(k + 1) * 128],
                                                        in_=h2[:, k * 128:(k + 1) * 128], identity=ident[:]),
                       reads=["h2D", "ident"], writes=[("ps", 2)], signal=(k == 7))
            S.emit("dve", lambda E: E.tensor_copy(out=h2T[:, :, j * 128:(j + 1) * 128],
                                                  in_=tb.rearrange("p (k c) -> p k c", k=8)),
                   reads=[("ps", 2)], writes=[("h2T", j)])

        def stage2():
            W = GT * 128
            nf = 512 // W
            for g in range(32 // nf):
                bank = 3 + (g % 3)
                for f in range(nf):
                    fc = g * nf + f
                    for k in range(8):
                        S.emit("pe", lambda E, bank=bank, f=f, fc=fc, k=k: E.matmul(
                            PS[bank][:, f * W:(f + 1) * W], lhsT=wu[:, k, fc * 128:(fc + 1) * 128],
                            rhs=h2T[:, k, :], start=(k == 0), stop=(k == 7)),
                               reads=["wu"] + [("h2T", j) for j in range(GT)], writes=[("ps", bank)],
                               signal=(k == 7 and f == nf - 1))
                uv = uT[:, g * nf:(g + 1) * nf, :].rearrange("p a c -> p (a c)")
                ri = g % 2
                S.emit("act", lambda E, bank=bank, ri=ri: E.activation(out=rl[ri][:], in_=PS[bank][:, :], func=AF.Relu),
                       reads=[("ps", bank)], writes=[("rlD", ri)])
                S.emit("pool", lambda E, uv=uv, ri=ri: E.tensor_tensor(out=uv, in0=rl[ri][:], in1=rl[ri][:], op=ALU.mult),
                       reads=[("rlD", ri)], writes=["uT"])

        def stage3(t, j):
            tok = slice(t * 128, (t + 1) * 128)
            for hf in range(2):
                for fc in range(32):
                    S.emit("pe", lambda E, hf=hf, fc=fc: E.matmul(PS[6 + hf][:, :], lhsT=uT[:, fc, j * 128:(j + 1) * 128],
                                                                  rhs=wd[:, fc, hf * 512:(hf + 1) * 512],
                                                                  start=(fc == 0), stop=(fc == 31)),
                           reads=["uT", "wd"], writes=[("ps", 6 + hf)], signal=(fc == 31))
            rms_scale((6, 7), t, 8)
            for hf in range(2):
                cs = slice(hf * 512, (hf + 1) * 512)
                S.emit("dve", lambda E, hf=hf, cs=cs: E.scalar_tensor_tensor(
                    out=tmp[:, cs], in0=PS[6 + hf][:, :], scalar=sm[:, t, 10:11], in1=postmlp_b[:, cs],
                    op0=ALU.mult, op1=ALU.mult),
                       reads=[("ps", 6 + hf), "smD", "postmlp_b", "tmpD"], writes=["tmpD"])
            S.emit("pool", lambda E: E.tensor_tensor(out=xt[j][:], in0=tmp[:], in1=xt[j][:], op=ALU.add),
                   reads=["tmpD", ("xtD", j)], writes=[("xtD", j)])
            S.dma(self.out[b, tok, :], xt[j][:], reads=[("xtD", j)], writes=[("out", b, t)])

        OB = ((0, 1), (3, 4))
        NG = NT // GT
        stage1a(0, 0, OB[0])
        stage1a(1, 1, OB[1])
        stage1b(0, 0, OB[0])
        stage1b(1, 1, OB[1])
        stage2()
        for gi in range(1, NG):
            p0, p1 = (gi - 1) * GT, (gi - 1) * GT + 1
            t0, t1 = gi * GT, gi * GT + 1
            stage3(p0, 0)
            stage1a(t0, 0, OB[0])
            stage3(p1, 1)
            stage1b(t0, 0, OB[0])
            stage1a(t1, 1, OB[1])
            stage1b(t1, 1, OB[1])
            stage2()
        stage3((NG - 1) * GT, 0)
        stage3((NG - 1) * GT + 1, 1)


def host_inputs(inputs, core, nseq):
    f = lambda a: np.ascontiguousarray(np.asarray(a, dtype=np.float32))
    m = {}
    m["x"] = f(inputs["x"][core * nseq:(core + 1) * nseq])
    m["w_in"] = f(inputs["w_in"][0])
    m["w_out"] = f(inputs["w_out"][0])
    m["w_up"] = f(inputs["w_up"][0])
    m["w_down"] = f(inputs["w_down"][0])
    m["premix_pk"] = f(np.asarray(inputs["pre_mix_norm"][0]).reshape(8, 128).T)
    m["premlp_pk"] = f(np.asarray(inputs["pre_mlp_norm"][0]).reshape(8, 128).T)
    m["postmix"] = f(np.asarray(inputs["post_mix_norm"][0]).reshape(1, DM))
    m["postmlp"] = f(np.asarray(inputs["post_mlp_norm"][0]).reshape(1, DM))
    rb = np.asarray(inputs["rel_bias"], dtype=np.float32)
    tab = np.concatenate([rb, np.full((8, 1), NEG, np.float32)], axis=1)
    j = np.arange(128)[:, None]
    i = np.arange(128)[None, :]
    idx0 = np.where(i - j >= 0, rel_bucket_np(i - j), 32)
    idx1 = rel_bucket_np(128 + i - j)
    idx = np.stack([idx0, idx1], axis=0)
    tt = tab[:, idx]
    m["ttab"] = f(tt.transpose(2, 0, 1, 3).reshape(128, 8 * 2 * 128))
    m["rb31"] = f(rb[:, 31].reshape(1, 8))
    cw = np.asarray(inputs["conv_w"][0], dtype=np.float32)
    m["convw_pk"] = f(cw.T.reshape(12, 128, 4).transpose(1, 0, 2).reshape(128, 48))
    m["alog"] = f(np.asarray(inputs["A_log"][0]).reshape(1, 8))
    m["dtb"] = f(np.asarray(inputs["dt_bias"][0]).reshape(1, 8))
    m["gnw"] = f(np.asarray(inputs["gdn_norm_w"][0]).reshape(1, 64))
    return m


_PROG = {}


def kernel(**inputs):
    ncores = 8
    nseq = 16 // ncores
    if "p" not in _PROG:
        _PROG["p"] = Prog(nseq)
    prog = _PROG["p"]
    in_maps = [host_inputs(inputs, c, nseq) for c in range(ncores)]
    res = run_bass_kernel_spmd(prog.nc, in_maps, core_ids=list(range(ncores)))
    out = np.concatenate([r["out"] for r in res.results], axis=0)
    return out.astype(np.float32)
```

```python
import math
from contextlib import ExitStack

import numpy as np
import concourse.bass as bass
import concourse.mybir as mybir
from concourse.bass_utils import run_bass_kernel_spmd

F32 = mybir.dt.float32
BF16 = mybir.dt.bfloat16
AF = mybir.ActivationFunctionType
ALU = mybir.AluOpType
AX = mybir.AxisListType

NDMA = 8
SEQ = 2048
DM = 1024
NT = SEQ // 128
DFF = 4096
INC = 3600
EPS = 1e-6
NEG = -30000.0


class Sched:
    ENGS = ("pe", "act", "dve", "pool", "sp")

    def __init__(self, nc):
        self.nc = nc
        self.q = {e: [] for e in self.ENGS}
        self.cnt = {e: 0 for e in self.ENGS}
        self.pending = {e: False for e in self.ENGS}
        self.seen = {e: {} for e in self.ENGS}
        self.lastw = {}
        self.readers = {}
        self.dma_i = 0
        self.dma_uses = [0] * NDMA
        self.bar = {}
        self.n_ins = {e: 0 for e in self.ENGS}

    def _deps(self, eng, reads, writes):
        deps = dict(self.bar)

        def add(tok):
            k, v = tok
            if deps.get(k, 0) < v:
                deps[k] = v

        for r in reads:
            if r in self.lastw:
                add(self.lastw[r])
            if isinstance(r, tuple) and r[0] == "ps":
                for k, v in self.readers.get(r, {}).items():
                    if k != eng:
                        add((k, v))
        for w in writes:
            if w in self.lastw:
                add(self.lastw[w])
            for k, v in self.readers.get(w, {}).items():
                add((k, v))
        waits = []
        for k, v in deps.items():
            if k == eng and eng == "pe":
                continue
            if self.seen[eng].get(k, 0) >= v:
                continue
            self.seen[eng][k] = v
            waits.append((k, v))
        return waits

    def _record(self, tok, reads, writes):
        k, v = tok
        for r in reads:
            d = self.readers.setdefault(r, {})
            if d.get(k, 0) < v:
                d[k] = v
        for w in writes:
            self.lastw[w] = tok
            self.readers[w] = {}

    def emit(self, eng, fn, reads=(), writes=(), signal=True):
        waits = self._deps(eng, reads, writes)
        if signal:
            self.cnt[eng] += 1
            self.pending[eng] = False
            tok = (eng, self.cnt[eng])
        else:
            self.pending[eng] = True
            tok = (eng, self.cnt[eng] + 1)
        self._record(tok, reads, writes)
        self.n_ins[eng] += 1

        def run(E, sems, waits=waits, fn=fn, signal=signal, eng=eng):
            for k, v in waits:
                E.wait_ge(sems[k], v)
            ins = fn(E)
            if signal:
                ins.then_inc(sems[eng], 1)

        self.q[eng].append(run)
        return tok

    def dma(self, out, in_, reads=(), writes=(), q="sp", **kw):
        slot = self.dma_i % NDMA
        self.dma_i += 1
        key = ("dma", slot)
        waits = self._deps(q, reads, writes)
        prev = 16 * self.dma_uses[slot]
        if prev > 0 and self.seen[q].get(key, 0) < prev:
            self.seen[q][key] = prev
            waits.append((key, prev))
        self.dma_uses[slot] += 1
        tok = (key, 16 * self.dma_uses[slot])
        self._record(tok, reads, writes)
        self.n_ins[q] += 1

        def run(E, sems, waits=waits, out=out, in_=in_, key=key, kw=kw):
            for k, v in waits:
                E.wait_ge(sems[k], v)
            E.dma_start(out=out, in_=in_, **kw).then_inc(sems[key], 16)

        self.q[q].append(run)
        return tok

    def barrier(self):
        for e in self.ENGS:
            assert not self.pending[e]
            if self.cnt[e] > 0:
                self.bar[e] = self.cnt[e]
        for s in range(NDMA):
            if self.dma_uses[s] > 0:
                self.bar[("dma", s)] = 16 * self.dma_uses[s]

    def finish(self):
        waits = []
        for slot in range(NDMA):
            v = 16 * self.dma_uses[slot]
            key = ("dma", slot)
            if v > 0 and self.seen["sp"].get(key, 0) < v:
                self.seen["sp"][key] = v
                waits.append((key, v))

        def run(E, sems, waits=waits):
            for k, v in waits:
                E.wait_ge(sems[k], v)

        self.q["sp"].append(run)
        for e in self.ENGS:
            assert not self.pending[e], f"engine {e} has unsignaled trailing instruction"

    def build(self, stack):
        nc = self.nc
        sems = {}
        for e in self.ENGS:
            sems[e] = stack.enter_context(nc.semaphore("s_" + e))
        for s in range(NDMA):
            sems[("dma", s)] = stack.enter_context(nc.semaphore("s_dma%d" % s))
        block = stack.enter_context(nc.Block())
        q = self.q

        @block.tensor
        def _(E):
            for f in q["pe"]:
                f(E, sems)

        @block.scalar
        def _(E):
            for f in q["act"]:
                f(E, sems)

        @block.vector
        def _(E):
            for f in q["dve"]:
                f(E, sems)

        @block.gpsimd
        def _(E):
            for f in q["pool"]:
                f(E, sems)

        @block.sync
        def _(E):
            for f in q["sp"]:
                f(E, sems)


def rel_bucket_np(d):
    d = np.maximum(d, 0)
    large = 16 + (np.log(np.maximum(d, 1).astype(np.float32) / 16) / math.log(128 / 16) * 16).astype(np.int32)
    large = np.minimum(large, 31)
    return np.where(d < 16, d, large)


class Prog:
    def __init__(self, nseq, stages=("A", "B1", "C1", "C2", "D"), dbg=()):
        self.nseq = nseq
        self.stages = stages
        self.dbg = dbg
        nc = bass.Bass("TRN2", target_bir_lowering=False, dynamic_dma_scratch_size=256)
        self.nc = nc
        self.S = Sched(nc)
        d = {}

        def din(name, shape):
            d[name] = nc.dram_tensor(name, list(shape), F32, kind="ExternalInput").ap()

        din("x", [nseq, SEQ, DM])
        din("w_in", [DM, INC])
        din("w_out", [DM, DM])
        din("w_up", [DM, DFF])
        din("w_down", [DFF, DM])
        din("premix_pk", [128, 8])
        din("premlp_pk", [128, 8])
        din("postmix", [1, DM])
        din("postmlp", [1, DM])
        din("ttab", [128, 8 * 2 * 128])
        din("rb31", [1, 8])
        din("convw_pk", [128, 12 * 4])
        din("alog", [1, 8])
        din("dtb", [1, 8])
        din("gnw", [1, 64])
        self.out = nc.dram_tensor("out", [nseq, SEQ, DM], F32, kind="ExternalOutput").ap()
        self.wscr = nc.dram_tensor("wscr_bf16", [128, 8 * DM + 8 * DFF + 32 * DM], BF16).ap()
        self.dbg_out = {}
        for name, shape in dbg:
            self.dbg_out[name] = nc.dram_tensor("dbg_" + name, list(shape), F32, kind="ExternalOutput").ap()
        self.d = d
        self._rr = 0
        with ExitStack() as st:
            self.build(st)
            self.S.finish()
            self.S.build(st)

    def sb(self, st, name, shape, dt):
        self._uid = getattr(self, "_uid", 0) + 1
        return st.enter_context(self.nc.sbuf_tensor("%s_u%d" % (name, self._uid), list(shape), dt))

    def evac(self, out, in_, reads, writes, eng=None):
        if eng is None:
            eng = ("act", "dve")[self._rr % 2]
            self._rr += 1
        if eng == "act":
            self.S.emit("act", lambda E: E.activation(out=out, in_=in_, func=AF.Identity), reads=reads, writes=writes)
        else:
            self.S.emit("dve", lambda E: E.tensor_copy(out=out, in_=in_), reads=reads, writes=writes)

    def build(self, st):
        nc, S, d = self.nc, self.S, self.d
        sb = self.sb
        self.PS = [st.enter_context(nc.psum_tensor("ps%d" % i, [128, 512], F32)) for i in range(8)]
        P = {}
        self.P = P
        P["ident"] = sb(st, "ident", [128, 128], BF16)
        P["postmix_b"] = sb(st, "postmix_b", [128, DM], F32)
        P["postmlp_b"] = sb(st, "postmlp_b", [128, DM], F32)
        P["premix"] = sb(st, "premix", [128, 8], F32)
        P["premlp"] = sb(st, "premlp", [128, 8], F32)
        P["epsc"] = sb(st, "epsc", [128, 1], F32)
        P["oT"] = sb(st, "oT", [128, 8, SEQ], BF16)
        ident = P["ident"]
        S.emit("pool", lambda E: E.memset(ident[:], 0.0), writes=["ident"])
        S.emit("pool", lambda E: E.affine_select(out=ident[:], in_=ident[:], pattern=[[-1, 128]],
                                                  compare_op=ALU.not_equal, fill=1.0, base=0, channel_multiplier=1),
               reads=["ident"], writes=["ident"])
        S.emit("pool", lambda E: E.memset(P["epsc"][:], EPS), writes=["epsc"])
        S.dma(P["postmix_b"][:], d["postmix"].partition_broadcast(128), writes=["postmix_b"])
        S.dma(P["postmlp_b"][:], d["postmlp"].partition_broadcast(128), writes=["postmlp_b"])
        S.dma(P["premix"][:], d["premix_pk"], writes=["premix"])
        S.dma(P["premlp"][:], d["premlp_pk"], writes=["premlp"])
        if "C2" not in self.stages:
            oT = P["oT"]
            S.emit("pool", lambda E: E.memset(oT[:, 4:8, :], 0.0), writes=[("oT", c) for c in range(4, 8)])

        for b in range(self.nseq):
          S.barrier()
          with ExitStack() as s_seq:
            hT_keep = sb(s_seq, "hT", [128, 8, SEQ], BF16)
            with ExitStack() as s_att:
                A = {}
                A["qT"] = sb(s_att, "qT", [128, 4, SEQ], BF16)
                A["kT"] = sb(s_att, "kT", [128, 4, SEQ], BF16)
                A["vaug"] = sb(s_att, "vaug", [128, NT, 4, 3, 64], BF16)
                A["maskT"] = sb(s_att, "maskT", [128, SEQ], BF16)
                A["kmT"] = sb(s_att, "kmT", [128, 4, 8], BF16)
                with ExitStack() as s_pa:
                    wA = self.prep_B1(s_pa) if "B1" in self.stages else None
                    hT = self.phase_A(s_pa, b, hT=hT_keep)
                    if "B1" in self.stages:
                        self.phase_B1(s_pa, b, hT, A, wA)
                S.barrier()
                if "C1" in self.stages:
                    with ExitStack() as s_c1:
                        self.phase_C1(s_c1, b, A)
                S.barrier()
            S.barrier()
            if "C2" in self.stages or "B2" in self.stages:
                with ExitStack() as s_g:
                    G = {}
                    G["qT"] = sb(s_g, "gqT", [128, 4, SEQ], BF16)
                    G["kT"] = sb(s_g, "gkT", [128, 4, SEQ], BF16)
                    G["vT"] = sb(s_g, "gvT", [128, 4, SEQ], BF16)
                    G["zs"] = sb(s_g, "gzs", [128, NT, 512], BF16)
                    G["gab"] = sb(s_g, "gab", [128, NT, 16], F32)
                    G["g"] = sb(s_g, "gg", [128, NT, 8], F32)
                    G["beta"] = sb(s_g, "gbeta", [128, NT, 8], F32)
                    G["nbeta"] = sb(s_g, "gnbeta", [128, NT, 8], F32)
                    with ExitStack() as s_pb:
                        self.phase_B2(s_pb, b, hT_keep, G)
                    S.barrier()
                    with ExitStack() as s_c2:
                        if "C2" in self.stages:
                            self.phase_C2(s_c2, b, G)
                    S.barrier()
                    if "oTg" in self.dbg_out:
                        with ExitStack() as s_dbg:
                            oT = P["oT"]
                            otf = sb(s_dbg, "otfg", [128, 4, SEQ], F32)
                            S.emit("dve", lambda E: E.tensor_copy(out=otf[:], in_=oT[:, 4:8, :]),
                                   reads=[("oT", c) for c in range(4, 8)], writes=["otfg"])
                            S.dma(self.dbg_out["oTg"][b].rearrange("c p s -> p c s"), otf[:], reads=["otfg"],
                                  writes=["dbg_oTg"])
                        S.barrier()
          S.barrier()
          if "D" in self.stages:
              with ExitStack() as s_d:
                  self.phase_D(s_d, b)
          S.barrier()

    def phase_A(self, st, b, nbuf=2, hT=None):
        nc, S, d, P, PS = self.nc, self.S, self.d, self.P, self.PS
        if hT is None:
            hT = self.sb(st, "hT", [128, 8, SEQ], BF16)
        xt = [self.sb(st, "xt%d" % i, [128, DM], F32) for i in range(nbuf)] * (2 // nbuf)
        hb = [self.sb(st, "hb%d" % i, [128, DM], BF16) for i in range(nbuf)] * (2 // nbuf)
        ss = self.sb(st, "ssA", [128, NT], F32)
        rs = self.sb(st, "rsA", [128, NT], F32)
        rstd = self.sb(st, "rstdA", [128, NT], F32)
        ident = P["ident"]
        S.emit("dve", lambda E: E.memset(ss[:], 0.0), writes=["ssA"])
        for t in range(NT):
            i = t % nbuf
            S.dma(xt[i][:], d["x"][b, t * 128:(t + 1) * 128, :], writes=[("xt", i)])
            S.emit("act", lambda E, i=i, t=t: E.activation(out=hb[i][:], in_=xt[i][:], func=AF.Square,
                                                           accum_out=ss[:, t:t + 1]),
                   reads=[("xt", i), "ssA"], writes=[("hb", i), "ssA"])
            S.emit("act", lambda E, t=t: E.activation(out=rs[:, t:t + 1], in_=ss[:, t:t + 1], func=AF.Sqrt,
                                                      scale=1.0 / DM, bias=P["epsc"][:]),
                   reads=["ssA", "epsc"], writes=["rsA"])
            S.emit("dve", lambda E, t=t: E.reciprocal(out=rstd[:, t:t + 1], in_=rs[:, t:t + 1]),
                   reads=["rsA"], writes=["rstdA"])
            S.emit("act", lambda E, i=i, t=t: E.activation(out=hb[i][:], in_=xt[i][:], func=AF.Identity,
                                                           scale=rstd[:, t:t + 1]),
                   reads=[("xt", i), "rstdA"], writes=[("hb", i)])
            bank = t % 2
            psb = PS[bank][:].bitcast(BF16)
            for k in range(8):
                S.emit("pe", lambda E, k=k, i=i, psb=psb: E.transpose(out=psb[:, k * 128:(k + 1) * 128],
                                                                      in_=hb[i][:, k * 128:(k + 1) * 128],
                                                                      identity=ident[:]),
                       reads=[("hb", i), "ident"], writes=[("ps", bank)], signal=(k == 7))
            S.emit("dve", lambda E, t=t, psb=psb: E.tensor_copy(out=hT[:, :, t * 128:(t + 1) * 128],
                                                                in_=psb.rearrange("p (k c) -> p k c", k=8)),
                   reads=[("ps", bank)], writes=[("hT", t)])
        return hT

    def prep_B1(self, st):
        nc, S, d, P, PS = self.nc, self.S, self.d, self.P, self.PS
        wA = self.sb(st, "wA", [128, 8, 1536], BF16)
        stg = [self.sb(st, "stgA%d" % i, [128, 1536], F32) for i in range(3)]
        premix = P["premix"]
        for k in range(8):
            i = k % 3
            S.dma(stg[i][:], d["w_in"][k * 128:(k + 1) * 128, 0:1536], writes=[("stgA", i)])
            S.emit("pool", lambda E, k=k, i=i: E.tensor_scalar(out=wA[:, k, 0:512], in0=stg[i][:, 0:512],
                                                               scalar1=premix[:, k:k + 1], scalar2=0.125,
                                                               op0=ALU.mult, op1=ALU.mult),
                   reads=[("stgA", i), "premix"], writes=[("wA", k)])
            S.emit("pool", lambda E, k=k, i=i: E.tensor_scalar(out=wA[:, k, 512:1536], in0=stg[i][:, 512:1536],
                                                               scalar1=premix[:, k:k + 1], scalar2=None,
                                                               op0=ALU.mult),
                   reads=[("stgA", i), "premix"], writes=[("wA", k)])
        return wA

    def phase_B1(self, st, b, hT, A, wA):
        nc, S, d, P, PS = self.nc, self.S, self.d, self.P, self.PS
        qT, kT, vaug = A["qT"], A["kT"], A["vaug"]
        S.emit("pool", lambda E: E.memset(vaug[:, :, :, 1, :], 1.0), writes=["vones"])
        wA_all = [("wA", k) for k in range(8)]
        nb = 0
        for which, dst, name in ((0, qT, "qT"), (1, kT, "kT")):
            for pr in range(4):
                col0 = which * 512 + pr * 128
                for tc in range(4):
                    bank = 2 + (nb % 4)
                    nb += 1
                    for k in range(8):
                        S.emit("pe", lambda E, k=k, col0=col0, tc=tc, bank=bank: E.matmul(
                            PS[bank][:, :], lhsT=wA[:, k, col0:col0 + 128], rhs=hT[:, k, tc * 512:(tc + 1) * 512],
                            start=(k == 0), stop=(k == 7)),
                               reads=wA_all + [("hT", 4 * tc + j) for j in range(4)], writes=[("ps", bank)],
                               signal=(k == 7))
                    self.evac(dst[:, pr, tc * 512:(tc + 1) * 512], PS[bank][:, :], reads=[("ps", bank)],
                              writes=[(name, pr, tc)])
        for t in range(NT):
            bank = 2 + (nb % 4)
            nb += 1
            for k in range(8):
                S.emit("pe", lambda E, k=k, t=t, bank=bank: E.matmul(
                    PS[bank][:, :], lhsT=hT[:, k, t * 128:(t + 1) * 128], rhs=wA[:, k, 1024:1536],
                    start=(k == 0), stop=(k == 7)),
                       reads=wA_all + [("hT", t)], writes=[("ps", bank)], signal=(k == 7))
            self.evac(vaug[:, t, :, 0:3:2, :], PS[bank][:, :].rearrange("p (a b c) -> p a b c", a=4, b=2),
                      reads=[("ps", bank)], writes=[("vaug", t)])
        kmf = self.sb(st, "kmf", [128, 4, 8], F32)
        for pr in range(4):
            S.emit("dve", lambda E, pr=pr: E.tensor_reduce(out=kmf[:, pr, :],
                                                           in_=kT[:, pr, :].rearrange("p (n c) -> p n c", n=8),
                                                           axis=AX.X, op=ALU.add),
                   reads=[("kT", pr, tc) for tc in range(4)], writes=["kmf"])
        S.emit("dve", lambda E: E.tensor_scalar(out=A["kmT"][:], in0=kmf[:], scalar1=1.0 / 256, scalar2=None,
                                                op0=ALU.mult),
               reads=["kmf"], writes=["kmT"])

    def phase_C1(self, st, b, A):
        nc, S, d, P, PS = self.nc, self.S, self.d, self.P, self.PS
        sb = self.sb
        qT, kT, vaug, maskT, kmT = A["qT"], A["kT"], A["vaug"], A["maskT"], A["kmT"]
        ident = P["ident"]
        oT = P["oT"]
        IND = sb(st, "IND", [128, 64, 128], BF16)
        TT = sb(st, "TT", [128, 8, 2, 128], F32)
        rb31 = sb(st, "rb31", [128, 8], F32)
        PAST = sb(st, "PAST", [128, 8, 8, 8], F32)
        OWN = sb(st, "OWN", [128, 8, 8, 8], F32)
        PT = [[sb(st, "PT%d%d" % (h, i), [128, 512], BF16) for i in range(2)] for h in range(2)]
        rden = [sb(st, "rden%d" % h, [128, 512], F32) for h in range(2)]
        gsb = sb(st, "gsb", [128, 2, 8, 8], F32)
        g2 = sb(st, "g2", [128, 2, 8, 8], F32)
        eq = sb(st, "eq", [128, 2, 8, 8], F32)
        mx = sb(st, "mx", [128, 16], F32)
        mtok = sb(st, "mtok", [128, 128], BF16)

        S.emit("pool", lambda E: E.memset(IND[:], 0.0), writes=["IND"])
        S.emit("pool", lambda E: E.affine_select(out=IND[0:64], in_=IND[0:64], pattern=[[-1, 64], [0, 128]],
                                                  compare_op=ALU.not_equal, fill=1.0, base=0, channel_multiplier=1),
               reads=["IND"], writes=["IND"])
        S.emit("dve", lambda E: E.tensor_copy(out=IND[64:128], in_=IND[0:64]), reads=["IND"], writes=["IND"])
        S.emit("pool", lambda E: E.memset(PAST[:], 0.0), writes=["PAST"])
        S.emit("pool", lambda E: E.affine_select(out=PAST[:], in_=PAST[:], pattern=[[1, 8], [0, 8], [-1, 8]],
                                                  compare_op=ALU.is_gt, fill=-1e30, base=0, channel_multiplier=0),
               reads=["PAST"], writes=["PAST"])
        S.emit("pool", lambda E: E.memset(OWN[:], 0.0), writes=["OWN"])
        S.emit("pool", lambda E: E.affine_select(out=OWN[:], in_=OWN[:], pattern=[[1, 8], [0, 8], [-1, 8]],
                                                  compare_op=ALU.not_equal, fill=1.0, base=0, channel_multiplier=0),
               reads=["OWN"], writes=["OWN"])
        S.dma(TT[:].rearrange("p h a c -> p (h a c)"), d["ttab"], writes=["TT"])
        S.dma(rb31[:], d["rb31"].partition_broadcast(128), writes=["rb31"])
        S.emit("dve", lambda E: E.tensor_tensor(out=TT[:].rearrange("p h a c -> p h (a c)"),
                                                in0=TT[:].rearrange("p h a c -> p h (a c)"),
                                                in1=rb31[:].unsqueeze(2).to_broadcast([128, 8, 256]),
                                                op=ALU.subtract),
               reads=["TT", "rb31"], writes=["TT"])

        GB, TB = 6, 7
        import os
        FL = os.environ.get("C1FLAGS", "mask,main,toep,pv,norm,maskmm").split(",")
        if "mask" not in FL:
            S.emit("pool", lambda E: E.memset(maskT[:], 0.0), writes=[("maskT", qt) for qt in range(NT)])
        GBK = (6, 7)
        TB = 6

        def mask_pass(blk):
                for j in range(2):
                    qt = 2 * blk + j
                    for h in range(8):
                        pr, hh = h // 2, h % 2
                        S.emit("pe", lambda E, h=h, pr=pr, hh=hh, qt=qt, j=j: E.matmul(
                            PS[GBK[hh]][:, j * 32 + pr * 8:j * 32 + (pr + 1) * 8],
                            lhsT=qT[hh * 64:(hh + 1) * 64, pr, qt * 128:(qt + 1) * 128],
                            rhs=kmT[hh * 64:(hh + 1) * 64, pr, :], start=True, stop=True),
                               reads=[("qT", pr, qt // 4), "kmT"], writes=[("ps", GBK[hh])], signal=(j == 1 and h >= 6))
                for hh in range(2):
                    g3v = PS[GBK[hh]][:, 0:64].rearrange("p (j h n) -> p j h n", j=2, h=4)
                    S.emit("dve", lambda E, blk=blk, g3v=g3v, hh=hh: E.tensor_tensor(
                        out=gsb[:, :, hh:8:2, :], in0=g3v,
                        in1=PAST[:, blk, hh:8:2, :].unsqueeze(1).to_broadcast([128, 2, 4, 8]), op=ALU.add),
                           reads=[("ps", GBK[hh]), "PAST"], writes=["gsb"])
                cur = gsb
                mxb = mx[:].rearrange("p (j h) -> p j h", j=2).unsqueeze(3).to_broadcast([128, 2, 8, 8])
                for it in range(2):
                    S.emit("dve", lambda E, cur=cur: E.tensor_reduce(out=mx[:], in_=cur[:].rearrange("p j h n -> p (j h) n"),
                                                                     axis=AX.X, op=ALU.max),
                           reads=["gsb", "g2"], writes=["mx"])
                    S.emit("dve", lambda E, cur=cur: E.tensor_tensor(out=eq[:], in0=cur[:], in1=mxb, op=ALU.is_equal),
                           reads=["gsb", "g2", "mx"], writes=["eq"])
                    S.emit("dve", lambda E, cur=cur: E.scalar_tensor_tensor(out=g2[:], in0=eq[:], scalar=-1e30,
                                                                            in1=cur[:], op0=ALU.mult, op1=ALU.add),
                           reads=["eq", "gsb", "g2"], writes=["g2"])
                    cur = g2
                S.emit("dve", lambda E: E.tensor_reduce(out=mx[:], in_=g2[:].rearrange("p j h n -> p (j h) n"),
                                                        axis=AX.X, op=ALU.max),
                       reads=["g2"], writes=["mx"])
                S.emit("dve", lambda E: E.tensor_scalar(out=mx[:], in0=mx[:], scalar1=-1e29, scalar2=None, op0=ALU.max),
                       reads=["mx"], writes=["mx"])
                S.emit("dve", lambda E: E.tensor_tensor(out=eq[:], in0=gsb[:], in1=mxb, op=ALU.is_ge),
                       reads=["gsb", "mx"], writes=["eq"])
                S.emit("dve", lambda E, blk=blk: E.tensor_tensor(
                    out=eq[:], in0=eq[:], in1=OWN[:, blk, :, :].unsqueeze(1).to_broadcast([128, 2, 8, 8]), op=ALU.add),
                       reads=["eq", "OWN"], writes=["eq"])
                S.emit("dve", lambda E: E.tensor_scalar(out=mtok[:], in0=eq[:].rearrange("p j h n -> p (j h n)"),
                                                        scalar1=-1.0, scalar2=-NEG, op0=ALU.add, op1=ALU.mult),
                       reads=["eq"], writes=["mtok"])
                tb = PS[TB][:].bitcast(BF16)
                for j in range(2):
                    S.emit("pe", lambda E, tb=tb, j=j: E.transpose(out=tb[0:64, j * 128:(j + 1) * 128],
                                                                   in_=mtok[:, j * 64:(j + 1) * 64], identity=ident[:]),
                           reads=["mtok", "ident"], writes=[("ps", TB)], signal=(j == 1))
                S.emit("act", lambda E, tb=tb, blk=blk: E.activation(out=maskT[0:64, blk * 256:(blk + 1) * 256],
                                                                     in_=tb[0:64, 0:256], func=AF.Identity),
                       reads=[("ps", TB)], writes=[("maskT", 2 * blk), ("maskT", 2 * blk + 1)])
                S.emit("act", lambda E, tb=tb, blk=blk: E.activation(out=maskT[64:128, blk * 256:(blk + 1) * 256],
                                                                     in_=tb[0:64, 0:256], func=AF.Identity),
                       reads=[("ps", TB)], writes=[("maskT", 2 * blk), ("maskT", 2 * blk + 1)])

        vflat = vaug[:].rearrange("p t a b c -> p t a (b c)")

        def qk(pr, qc, kt):
            qs = max(0, kt * 128 - qc * 512)
            N = 512 - qs
            q0 = qc * 512 + qs
            for hh in range(2):
                h = 2 * pr + hh
                bank = hh * 2 + (kt % 2)
                rows = slice(hh * 64, (hh + 1) * 64)
                mm = "maskmm" in FL
                S.emit("pe", lambda E, bank=bank, rows=rows, N=N, q0=q0, mm=mm: E.matmul(
                    PS[bank][:, 0:N], lhsT=kT[rows, pr, kt * 128:(kt + 1) * 128], rhs=qT[rows, pr, q0:q0 + N],
                    start=True, stop=not mm),
                       reads=[("kT", pr, kt // 4), ("qT", pr, qc)], writes=[("ps", bank)], signal=not mm)
                if mm:
                    S.emit("pe", lambda E, bank=bank, h=h, N=N, q0=q0, rows=rows: E.matmul(
                        PS[bank][:, 0:N], lhsT=IND[rows, h * 8 + kt // 2, :], rhs=maskT[rows, q0:q0 + N],
                        start=False, stop=True),
                           reads=["IND"] + [("maskT", 4 * qc + j) for j in range(4)], writes=[("ps", bank)])
                for dq in range(2 if "toep" in FL else 0):
                    qt = kt + dq
                    if qt * 128 < q0 or qt >= (qc + 1) * 4:
                        continue
                    off = qt * 128 - q0
                    S.emit("dve", lambda E, bank=bank, off=off, h=h, dq=dq: E.tensor_tensor(
                        out=PS[bank][:, off:off + 128], in0=PS[bank][:, off:off + 128], in1=TT[:, h, dq, :],
                        op=ALU.add),
                           reads=[("ps", bank), "TT"], writes=[("ps", bank)])
                S.emit("act", lambda E, bank=bank, hh=hh, N=N: E.activation(
                    out=PT[hh][kt % 2][:, 0:N], in_=PS[bank][:, 0:N], func=AF.Exp),
                       reads=[("ps", bank)], writes=[("PT", hh, kt % 2)])

        def pv(pr, qc, kt, nkt):
            qs = max(0, kt * 128 - qc * 512)
            N = 512 - qs
            for hh in range(2):
                bank = 4 + hh
                S.emit("pe", lambda E, bank=bank, hh=hh, qs=qs, N=N: E.matmul(
                    PS[bank][:, qs:512], lhsT=vflat[:, kt, pr, hh * 64:hh * 64 + 128], rhs=PT[hh][kt % 2][:, 0:N],
                    start=(kt == 0), stop=(kt == nkt - 1)),
                       reads=[("PT", hh, kt % 2), ("vaug", kt), "vones"], writes=[("ps", bank)],
                       signal=(kt == nkt - 1))

        for qc in range(4 if "main" in FL else 0):
            if "mask" in FL:
                mask_pass(2 * qc)
                mask_pass(2 * qc + 1)
            for pr in range(4):
                nkt = 4 * (qc + 1)
                for kt in range(nkt):
                    qk(pr, qc, kt)
                    if kt > 0 and "pv" in FL:
                        pv(pr, qc, kt - 1, nkt)
                if "pv" in FL:
                    pv(pr, qc, nkt - 1, nkt)
                for hh in range(2 if "norm" in FL else 0):
                    bank = 4 + hh
                    orows = slice(hh * 64, (hh + 1) * 64)
                    drows = slice((1 - hh) * 64, (2 - hh) * 64)
                    S.emit("dve", lambda E, bank=bank, hh=hh, orows=orows, drows=drows: E.reciprocal(
                        out=rden[hh][orows, :], in_=PS[bank][drows, :]),
                           reads=[("ps", bank)], writes=[("rden", hh)])
                    S.emit("dve", lambda E, bank=bank, hh=hh, orows=orows, pr=pr, qc=qc: E.tensor_tensor(
                        out=oT[orows, pr, qc * 512:(qc + 1) * 512], in0=PS[bank][orows, :], in1=rden[hh][orows, :],
                        op=ALU.mult),
                           reads=[("ps", bank), ("rden", hh)], writes=[("oT", pr)])

        if "oT" in self.dbg_out:
            otf = sb(st, "otf", [128, 4, SEQ], F32)
            S.emit("dve", lambda E: E.tensor_copy(out=otf[:], in_=oT[:, 0:4, :]),
                   reads=[("oT", c) for c in range(4)], writes=["otf"])
            S.dma(self.dbg_out["oT"][b].rearrange("c p s -> p c s"), otf[:], reads=["otf"], writes=["dbg_oT"])

    def phase_B2(self, st, b, hT, G):
        nc, S, d, P, PS = self.nc, self.S, self.d, self.P, self.PS
        sb = self.sb
        premix = P["premix"]
        stgw = [sb(st, "stgw%d" % i, [128, 8, 128], F32) for i in range(2)]
        wc = [sb(st, "wc%d" % i, [128, 8, 128], BF16) for i in range(2)]
        wz = sb(st, "wz", [128, 8, 512], BF16)
        wab = sb(st, "wab", [128, 8, 16], BF16)
        pres = [sb(st, "pre%d" % i, [128, 3 + SEQ], F32) for i in range(2)]
        accs = [sb(st, "acc%d" % i, [128, SEQ], F32) for i in range(2)]
        sq = sb(st, "sqg", [128, SEQ], BF16)
        srs = [sb(st, "srg%d" % i, [128, 512], F32) for i in range(2)]
        cw = sb(st, "cw", [128, 12, 4], F32)
        BLK = sb(st, "BLK", [128, 128], BF16)
        dtb = sb(st, "dtb_b", [128, 8], F32)
        alog = sb(st, "alog_b", [128, 8], F32)
        S.dma(cw[:].rearrange("p c t -> p (c t)"), d["convw_pk"], writes=["cw"])
        S.dma(dtb[:], d["dtb"].partition_broadcast(128), writes=["dtb"])
        S.dma(alog[:], d["alog"].partition_broadcast(128), writes=["alog"])
        S.emit("pool", lambda E: E.memset(BLK[:], 0.0), writes=["BLK"])
        S.emit("pool", lambda E: E.memset(BLK[0:64, 0:64], 1.0), reads=["BLK"], writes=["BLK"])
        S.emit("pool", lambda E: E.memset(BLK[64:128, 64:128], 1.0), reads=["BLK"], writes=["BLK"])
        for i in range(2):
            S.emit("pool", lambda E, i=i: E.memset(pres[i][:, 0:3], 0.0), writes=[("pre0", i)])
        hT_all = [("hT", t) for t in range(NT)]
        self._nw = 0

        def load_w(col0, ncol, dst, dname):
            i = self._nw % 2
            self._nw += 1
            S.dma(stgw[i][:, :, 0:ncol], d["w_in"][:, col0:col0 + ncol].rearrange("(k p) c -> p k c", p=128),
                  writes=[("stgw", i)])
            S.emit("pool", lambda E, i=i, ncol=ncol, dst=dst: E.tensor_tensor(
                out=dst, in0=stgw[i][:, :, 0:ncol], in1=premix[:].unsqueeze(2).to_broadcast([128, 8, ncol]),
                op=ALU.mult),
                   reads=[("stgw", i), "premix"], writes=[dname])

        nb = 0
        dsts = (G["qT"], G["kT"], G["vT"])
        for c in range(12):
            wi = c % 2
            pre, acc = pres[wi], accs[wi]
            PRE, ACC, PRE0 = ("pre", wi), ("acc", wi), ("pre0", wi)
            load_w(1536 + c * 128, 128, wc[wi][:], ("wc", wi))
            for tc in range(4):
                bank = 4 + (nb % 4)
                nb += 1
                for k in range(8):
                    S.emit("pe", lambda E, k=k, tc=tc, bank=bank, wi=wi: E.matmul(
                        PS[bank][:, :], lhsT=wc[wi][:, k, :], rhs=hT[:, k, tc * 512:(tc + 1) * 512],
                        start=(k == 0), stop=(k == 7)),
                           reads=[("wc", wi)] + [("hT", 4 * tc + j) for j in range(4)], writes=[("ps", bank)],
                           signal=(k == 7))
                S.emit("act", lambda E, tc=tc, bank=bank, pre=pre: E.activation(
                    out=pre[:, 3 + tc * 512:3 + (tc + 1) * 512], in_=PS[bank][:, :], func=AF.Identity),
                       reads=[("ps", bank)], writes=[PRE])
            ce = "dve"
            S.emit(ce, lambda E, c=c, pre=pre, acc=acc: E.tensor_scalar(out=acc[:], in0=pre[:, 0:SEQ],
                                                                        scalar1=cw[:, c, 0:1], scalar2=None, op0=ALU.mult),
                   reads=[PRE, PRE0, "cw"], writes=[ACC])
            for tp in range(1, 4):
                S.emit(ce, lambda E, c=c, tp=tp, pre=pre, acc=acc: E.scalar_tensor_tensor(
                    out=acc[:], in0=pre[:, tp:tp + SEQ], scalar=cw[:, c, tp:tp + 1], in1=acc[:],
                    op0=ALU.mult, op1=ALU.add),
                       reads=[PRE, PRE0, "cw", ACC], writes=[ACC])
            dst = dsts[c // 4]
            dn = ("gq", "gk", "gv")[c // 4]
            S.emit("act", lambda E, dst=dst, c=c, acc=acc: E.activation(out=dst[:, c % 4, :], in_=acc[:], func=AF.Silu),
                   reads=[ACC], writes=[(dn, c % 4)])
        for c in range(8):
            dst = dsts[c // 4]
            dn = ("gq", "gk")[c // 4]
            pr = c % 4
            S.emit("act", lambda E, dst=dst, pr=pr: E.activation(out=sq[:], in_=dst[:, pr, :], func=AF.Square),
                   reads=[(dn, pr)], writes=["sqg"])
            for tc in range(4):
                bank = 4 + (nb % 4)
                nb += 1
                cs = slice(tc * 512, (tc + 1) * 512)
                S.emit("pe", lambda E, bank=bank, cs=cs: E.matmul(PS[bank][:, :], lhsT=BLK[:], rhs=sq[:, cs],
                                                                  start=True, stop=True),
                       reads=["sqg", "BLK"], writes=[("ps", bank)])
                sr = srs[tc % 2]
                SR = ("srg", tc % 2)
                S.emit("act", lambda E, bank=bank, sr=sr: E.activation(out=sr[:], in_=PS[bank][:, :], func=AF.Sqrt,
                                                                       bias=P["epsc"][:], scale=1.0),
                       reads=[("ps", bank), "epsc"], writes=[SR])
                S.emit("dve", lambda E, sr=sr: E.reciprocal(out=sr[:], in_=sr[:]), reads=[SR], writes=[SR])
                scl = 0.125 if c < 4 else 1.0
                S.emit("dve", lambda E, dst=dst, pr=pr, cs=cs, scl=scl, sr=sr: E.scalar_tensor_tensor(
                    out=dst[:, pr, cs], in0=dst[:, pr, cs], scalar=scl, in1=sr[:], op0=ALU.mult, op1=ALU.mult),
                       reads=[(dn, pr), SR], writes=[(dn, pr)])
        for j in range(4):
            load_w(3072 + j * 128, 128, wz[:, :, j * 128:(j + 1) * 128], "wz")
        load_w(3584, 16, wab[:], "wab")
        zs, gab = G["zs"], G["gab"]
        for t in range(NT):
            bank = 4 + (nb % 4)
            nb += 1
            tk = slice(t * 128, (t + 1) * 128)
            for k in range(8):
                S.emit("pe", lambda E, k=k, tk=tk, bank=bank: E.matmul(PS[bank][:, :], lhsT=hT[:, k, tk],
                                                                       rhs=wz[:, k, :], start=(k == 0), stop=(k == 7)),
                       reads=["wz", ("hT", t)], writes=[("ps", bank)], signal=(k == 7))
            S.emit("act", lambda E, t=t, bank=bank: E.activation(out=zs[:, t, :], in_=PS[bank][:, :], func=AF.Silu),
                   reads=[("ps", bank)], writes=[("gzs", t)])
            bank = 4 + (nb % 4)
            nb += 1
            for k in range(8):
                S.emit("pe", lambda E, k=k, tk=tk, bank=bank: E.matmul(PS[bank][:, 0:16], lhsT=hT[:, k, tk],
                                                                       rhs=wab[:, k, :], start=(k == 0), stop=(k == 7)),
                       reads=["wab", ("hT", t)], writes=[("ps", bank)], signal=(k == 7))
            S.emit("dve", lambda E, t=t, bank=bank: E.tensor_copy(out=gab[:, t, :], in_=PS[bank][:, 0:16]),
                   reads=[("ps", bank)], writes=["gab"])
        g, beta, nbeta = G["g"], G["beta"], G["nbeta"]
        S.emit("dve", lambda E: E.tensor_tensor(out=g[:], in0=gab[:, :, 0:8],
                                                in1=dtb[:].unsqueeze(1).to_broadcast([128, NT, 8]), op=ALU.add),
               reads=["gab", "dtb"], writes=["gg"])
        S.emit("act", lambda E: E.activation(out=g[:], in_=g[:], func=AF.Exp), reads=["gg"], writes=["gg"])
        S.emit("act", lambda E: E.activation(out=g[:], in_=g[:], func=AF.Ln, bias=1.0), reads=["gg"], writes=["gg"])
        S.emit("act", lambda E: E.activation(out=alog[:], in_=alog[:], func=AF.Exp), reads=["alog"], writes=["alog"])
        S.emit("dve", lambda E: E.scalar_tensor_tensor(out=g[:], in0=g[:], scalar=-1.0,
                                                       in1=alog[:].unsqueeze(1).to_broadcast([128, NT, 8]),
                                                       op0=ALU.mult, op1=ALU.mult),
               reads=["gg", "alog"], writes=["gg"])
        S.emit("act", lambda E: E.activation(out=beta[:], in_=gab[:, :, 8:16], func=AF.Sigmoid),
               reads=["gab"], writes=["gbeta"])
        S.emit("dve", lambda E: E.tensor_scalar(out=nbeta[:], in0=beta[:], scalar1=-1.0, scalar2=None, op0=ALU.mult),
               reads=["gbeta"], writes=["gnbeta"])

    def phase_C2(self, st, b, G):
        nc, S, d, P, PS = self.nc, self.S, self.d, self.P, self.PS
        sb = self.sb
        ident = P["ident"]
        oT = P["oT"]
        qTg, kTg, vTg, zs = G["qT"], G["kT"], G["vT"], G["zs"]
        g, beta, nbeta = G["g"], G["beta"], G["nbeta"]
        BIG = 3.0e38
        TRI = sb(st, "TRI", [128, 128], F32)
        BLKS = sb(st, "BLKS", [128, 128], F32)
        MASKU = sb(st, "MASKU", [128, 8, 128], F32)
        STRICT = sb(st, "STRICT", [128, 8, 128], F32)
        HEADM = sb(st, "HEADM", [8, 8, 1], F32)
        SEL = sb(st, "SEL", [8, 4, 128], F32)
        ONES8 = sb(st, "ONES8", [8, 128], F32)
        gnw = sb(st, "gnw_b", [128, 64], F32)
        S.dma(gnw[:], d["gnw"].partition_broadcast(128), writes=["gnw"])

        def tri_like(T, val, strict, name):
            nd = len(T.shape)
            pat = [[0, 8], [1, 128]] if nd == 3 else [[1, 128]]
            pat2 = [[0, 8], [-1, 128]] if nd == 3 else [[-1, 128]]
            S.emit("pool", lambda E: E.memset(T[:], val), writes=[name])
            S.emit("pool", lambda E: E.affine_select(out=T[:], in_=T[:], pattern=pat,
                                                      compare_op=(ALU.is_gt if strict else ALU.is_ge), fill=0.0,
                                                      base=0, channel_multiplier=-1),
                   reads=[name], writes=[name])
            S.emit("pool", lambda E: E.affine_select(out=T[0:64], in_=T[0:64], pattern=pat2,
                                                      compare_op=ALU.is_ge, fill=0.0, base=63, channel_multiplier=0),
                   reads=[name], writes=[name])

        tri_like(TRI, 1.0, False, "TRI")
        tri_like(MASKU, BIG, False, "MASKU")
        tri_like(STRICT, 1.0, True, "STRICT")
        S.emit("pool", lambda E: E.memset(BLKS[:], 0.0), writes=["BLKS"])
        S.emit("pool", lambda E: E.memset(BLKS[0:64, 0:64], 1.0), reads=["BLKS"], writes=["BLKS"])
        S.emit("pool", lambda E: E.memset(BLKS[64:128, 64:128], 1.0), reads=["BLKS"], writes=["BLKS"])
        S.emit("pool", lambda E: E.memset(HEADM[:], 0.0), writes=["HEADM"])
        S.emit("pool", lambda E: E.affine_select(out=HEADM[:], in_=HEADM[:], pattern=[[-1, 8], [0, 1]],
                                                  compare_op=ALU.not_equal, fill=1.0, base=0, channel_multiplier=1),
               reads=["HEADM"], writes=["HEADM"])
        S.emit("pool", lambda E: E.memset(SEL[:], 0.0), writes=["SEL"])
        for half in range(2):
            S.emit("pool", lambda E, half=half: E.affine_select(
                out=SEL[:, :, half * 64:(half + 1) * 64], in_=SEL[:, :, half * 64:(half + 1) * 64],
                pattern=[[-2, 4], [0, 64]], compare_op=ALU.not_equal, fill=1.0, base=-half, channel_multiplier=1),
                   reads=["SEL"], writes=["SEL"])
        S.emit("pool", lambda E: E.memset(ONES8[:], 1.0), writes=["ONES8"])

        import os
        NPS = int(os.environ.get("C2NPS", "1"))
        NSLOT = NPS + 1
        rhsBDs = [sb(st, "rhsBD%d" % i, [8, 8, 128], F32) for i in range(NPS)]
        gcTs = [sb(st, "gcT%d" % i, [8, 128], F32) for i in range(NPS)]
        gcts = [sb(st, "gct%d" % i, [128, 24], F32) for i in range(NPS)]
        egts = [sb(st, "egt%d" % i, [128, 16], F32) for i in range(NPS)]
        EAs = [sb(st, "EA%d" % i, [128, 8, 128], F32) for i in range(NPS)]
        EAsbs = [sb(st, "EAsb%d" % i, [128, 8, 128], F32) for i in range(NPS)]
        ktoks = [sb(st, "ktok%d" % i, [128, 8, 64], BF16) for i in range(NPS)]
        Bms = [[sb(st, "Bm%d%d" % (p, i), [128, 8, 128], BF16) for i in range(2)] for p in range(NPS)]
        Nms = [[sb(st, "Nm%d%d" % (p, i), [128, 8, 128], BF16) for i in range(2)] for p in range(NPS)]
        qzs = [sb(st, "qz%d" % i, [128, 2, 4, 128], BF16) for i in range(NPS)]
        kzs = [sb(st, "kz%d" % i, [128, 2, 4, 128], BF16) for i in range(NPS)]
        X0s = [sb(st, "X0_%d" % i, [128, 2, 8, 64], BF16) for i in range(NPS)]
        Bp0s = [sb(st, "Bp0_%d" % i, [128, 8, 128], BF16) for i in range(NPS)]
        EGs = [sb(st, "EG%d" % i, [128, 4, 128], F32) for i in range(NSLOT)]
        qdTs = [sb(st, "qdT%d" % i, [128, 4, 128], BF16) for i in range(NSLOT)]
        kdecs = [[sb(st, "kdec%d%d" % (p, i), [128, 8, 64], BF16) for i in range(2)] for p in range(NSLOT)]
        X1s = [sb(st, "X1_%d" % i, [128, 2, 8, 64], BF16) for i in range(NSLOT)]
        attnTs = [sb(st, "attnT%d" % i, [128, 8, 128], BF16) for i in range(NSLOT)]
        Bp1s = [sb(st, "Bp1_%d" % i, [128, 8, 128], BF16) for i in range(NSLOT)]
        nwTs = [sb(st, "nwT%d" % i, [128, 4, 128], BF16) for i in range(NSLOT)]
        identb = ident[:].unsqueeze(1)
        S32 = sb(st, "S32", [128, 4, 64], F32)
        tmpS = sb(st, "tmpS", [128, 4, 64], F32)
        Sb = sb(st, "Sb", [128, 4, 2, 64], BF16)
        vnew = sb(st, "vnew", [128, 8, 64], BF16)
        osb = sb(st, "osb", [128, 8, 64], F32)
        osq = sb(st, "osq", [128, 8, 64], F32)
        oss = sb(st, "oss", [128, 16], F32)
        og = sb(st, "og", [128, 512], BF16)
        for i in range(NPS):
            S.emit("pool", lambda E, i=i: E.memset(qzs[i][:], 0.0), writes=[("qz", i)])
            S.emit("pool", lambda E, i=i: E.memset(kzs[i][:], 0.0), writes=[("kz", i)])
        S.emit("dve", lambda E: E.memset(S32[:], 0.0), writes=["S32"])
        S.emit("dve", lambda E: E.memset(Sb[:], 0.0), writes=["Sb"])
        S.emit("dve", lambda E: E.memset(vnew[:], 0.0), writes=["vnew"])
        for p in range(NSLOT):
            for i in range(2):
                S.emit("pool", lambda E, p=p, i=i: E.memset(kdecs[p][i][:], 0.0), writes=[("kdec", p, i)])
        self._bp = 0
        self._bs = 0
        PBANKS = (4, 5, 1, 2)
        SBANKS = (6, 7)

        def pbank():
            self._bp += 1
            return PBANKS[self._bp % 4]

        def sbank():
            self._bs += 1
            return SBANKS[self._bs % 2]

        def gen_P(t):
            par = t % NSLOT
            ps = t % NPS
            tk = slice(t * 128, (t + 1) * 128)
            gt = g[:, t, :]
            EG, qdT, kdec, attnT, nwT = EGs[par], qdTs[par], kdecs[par], attnTs[par], nwTs[par]
            X = (X0s[ps], X1s[par])
            Bp = (Bp0s[ps], Bp1s[par])
            XN = (("X0", ps), ("X", par, 1))
            BPN = (("Bp0", ps), ("Bp", par, 1))
            rhsBD, gcT, gct, egt, EA, EAsb, ktok = rhsBDs[ps], gcTs[ps], gcts[ps], egts[ps], EAs[ps], EAsbs[ps], ktoks[ps]
            Bm, Nm, qz, kz = Bms[ps], Nms[ps], qzs[ps], kzs[ps]
            RB, GC, GT_, EGT, EAn, EASn, KT = ("rhsBD", ps), ("gcT", ps), ("gct", ps), ("egt", ps), ("EA", ps), ("EAsb", ps), ("ktok", ps)
            QZ, KZ = ("qz", ps), ("kz", ps)
            MYB = (0, 1, 2) if ps == 0 else (3, 4, 5)
            ROT = MYB if NPS == 2 else (3, 4, 5, 1, 2)
            B0, B1_, B2_ = MYB
            rot = [0]

            def pbank():
                rot[0] += 1
                return ROT[rot[0] % len(ROT)]
            if NPS == 1 and os.environ.get("C2REORD", "1") == "1":
                bk, bv, RBK, kq, kk = 2, 3, 1, (4, 5), (2, 3)
                S.emit("pe", lambda E: E.matmul(PS[B0][0:8, 0:128], lhsT=gt, rhs=TRI[:], start=True, stop=True),
                       reads=["gg", "TRI"], writes=[("ps", B0)], signal=False)
                S.emit("pe", lambda E: E.matmul(PS[B0][:, 128:136], lhsT=TRI[:], rhs=gt, start=True, stop=True),
                       reads=["gg", "TRI"], writes=[("ps", B0)], signal=False)
                S.emit("pe", lambda E: E.matmul(PS[B0][:, 136:144], lhsT=BLKS[:], rhs=gt, start=True, stop=True),
                       reads=["gg", "BLKS"], writes=[("ps", B0)])
                tbk = PS[bk][:].bitcast(BF16)
                for pr in range(4):
                    S.emit("pe", lambda E, pr=pr, tbk=tbk: E.transpose(out=tbk[:, pr * 128:(pr + 1) * 128],
                                                                       in_=kTg[:, pr, tk], identity=ident[:]),
                           reads=[("gk", pr), "ident"], writes=[("ps", bk)], signal=(pr == 3))
                tbv = PS[bv][:].bitcast(BF16)
                for pr in range(4):
                    S.emit("pe", lambda E, pr=pr, tbv=tbv: E.transpose(out=tbv[:, pr * 128:(pr + 1) * 128],
                                                                       in_=vTg[:, pr, tk], identity=ident[:]),
                           reads=[("gv", pr), "ident"], writes=[("ps", bv)], signal=(pr == 3))
                yield
                S.emit("act", lambda E: E.activation(out=gcT[:], in_=PS[B0][0:8, 0:128], func=AF.Identity),
                       reads=[("ps", B0)], writes=[GC])
                S.emit("dve", lambda E: E.tensor_copy(out=gct[:, 0:16], in_=PS[B0][:, 128:144]),
                       reads=[("ps", B0)], writes=[GT_])
                S.emit("dve", lambda E: E.tensor_tensor(out=gct[:, 16:24], in0=gct[:, 8:16], in1=gct[:, 0:8],
                                                        op=ALU.subtract), reads=[GT_], writes=[GT_])
                for hh in range(2):
                    rows = slice(hh * 64, (hh + 1) * 64)
                    S.emit("pool", lambda E, hh=hh, rows=rows: E.tensor_copy(out=kz[rows, hh, :, :], in_=kTg[rows, :, tk]),
                           reads=[("gk", pr) for pr in range(4)], writes=[KZ])
                S.emit("dve", lambda E: E.tensor_tensor(out=rhsBD[:], in0=HEADM[:].to_broadcast([8, 8, 128]),
                                                        in1=gcT[:].unsqueeze(1).to_broadcast([8, 8, 128]), op=ALU.mult),
                       reads=[GC, "HEADM"], writes=[RB])
                yield
                S.emit("act", lambda E, tbk=tbk: E.activation(out=ktok[:].rearrange("p h d -> p (h d)"),
                                                              in_=tbk[:, 0:512], func=AF.Identity),
                       reads=[("ps", bk)], writes=[KT])
                S.emit("act", lambda E: E.activation(out=egt[:, 0:8], in_=gct[:, 0:8], func=AF.Exp),
                       reads=[GT_], writes=[EGT])
                S.emit("act", lambda E: E.activation(out=egt[:, 8:16], in_=gct[:, 16:24], func=AF.Exp),
                       reads=[GT_, EGT], writes=[EGT])
                X0 = X[0]
                S.emit("dve", lambda E, tbv=tbv: E.tensor_copy(out=X0[:, 0, :, :],
                                                               in_=tbv[:, 0:512].rearrange("p (h d) -> p h d", h=8)),
                       reads=[("ps", bv)], writes=[XN[0] + (0,), XN[0] + (1,)])
                yield
                for hf in range(2):
                    S.emit("pe", lambda E, hf=hf: E.matmul(
                        PS[RBK][:, :], lhsT=ONES8[:], rhs=rhsBD[:, 4 * hf:4 * hf + 4, :].rearrange("p h i -> p (h i)"),
                        start=True, stop=True),
                           reads=[RB, "ONES8"], writes=[("ps", RBK)])
                    S.emit("dve", lambda E, hf=hf: E.tensor_tensor(
                        out=EA[:, 4 * hf:4 * hf + 4, :], in0=PS[RBK][:, :].rearrange("p (h i) -> p h i", h=4),
                        in1=gct[:, 4 * hf:4 * hf + 4].unsqueeze(2).to_broadcast([128, 4, 128]), op=ALU.subtract),
                           reads=[("ps", RBK), GT_], writes=[EAn])
                    for h in range(4 * hf, 4 * hf + 4):
                        pr, hh = h // 2, h % 2
                        S.emit("pe", lambda E, pr=pr, hh=hh: E.matmul(
                            PS[kq[hh]][:, pr * 128:(pr + 1) * 128], lhsT=kz[:, hh, pr, :], rhs=qTg[:, pr, tk],
                            start=True, stop=True),
                               reads=[("gq", pr), KZ], writes=[("ps", kq[hh])], signal=(h >= 6))
                    yield
                S.emit("act", lambda E: E.activation(out=EA[:], in_=EA[:], func=AF.Exp), reads=[EAn], writes=[EAn])
                for pr in range(4):
                    S.emit("pe", lambda E, pr=pr: E.matmul(PS[B0][:, pr * 128:(pr + 1) * 128], lhsT=SEL[:, pr, :],
                                                           rhs=gcT[:], start=True, stop=True),
                           reads=[GC, "SEL"], writes=[("ps", B0)], signal=(pr == 3))
                for h in range(8):
                    pr, hh = h // 2, h % 2
                    S.emit("pe", lambda E, pr=pr, hh=hh: E.matmul(
                        PS[kk[hh]][:, pr * 128:(pr + 1) * 128], lhsT=kz[:, hh, pr, :], rhs=kTg[:, pr, tk],
                        start=True, stop=True),
                           reads=[("gk", pr), KZ], writes=[("ps", kk[hh])], signal=(h >= 6))
                yield
                S.emit("dve", lambda E: E.tensor_tensor(out=EA[:], in0=EA[:], in1=MASKU[:], op=ALU.min),
                       reads=[EAn, "MASKU"], writes=[EAn])
                S.emit("act", lambda E: E.activation(out=EG[:].rearrange("p a i -> p (a i)"), in_=PS[B0][:, :], func=AF.Exp),
                       reads=[("ps", B0)], writes=[("EG", par)])
                S.emit("pool", lambda E: E.tensor_tensor(out=EAsb[:], in0=EA[:], in1=STRICT[:], op=ALU.mult),
                       reads=[EAn, "STRICT"], writes=[EASn])
                S.emit("pool", lambda E: E.tensor_tensor(out=EAsb[:], in0=EAsb[:],
                                                         in1=nbeta[:, t, :].unsqueeze(2).to_broadcast([128, 8, 128]),
                                                         op=ALU.mult),
                       reads=[EASn, "gnbeta"], writes=[EASn])
                yield
                for hh in range(2):
                    S.emit("dve", lambda E, hh=hh: E.tensor_tensor(
                        out=attnT[:, hh:8:2, :], in0=PS[kq[hh]][:, :].rearrange("p (a i) -> p a i", a=4),
                        in1=EA[:, hh:8:2, :], op=ALU.mult),
                           reads=[("ps", kq[hh]), EAn], writes=[("attnT", par)])
                S.emit("dve", lambda E: E.tensor_tensor(out=X0[:, 1, :, :], in0=ktok[:],
                                                        in1=egt[:, 0:8].unsqueeze(2).to_broadcast([128, 8, 64]),
                                                        op=ALU.mult),
                       reads=[KT, EGT, XN[0] + (0,), XN[0] + (1,)], writes=[XN[0] + (0,), XN[0] + (1,)])
                for hh in range(2):
                    S.emit("dve", lambda E, hh=hh: E.tensor_tensor(
                        out=Bm[0][:, hh:8:2, :], in0=PS[kk[hh]][:, :].rearrange("p (a i) -> p a i", a=4),
                        in1=EAsb[:, hh:8:2, :], op=ALU.mult),
                           reads=[("ps", kk[hh]), EASn], writes=[("Bm", ps, 0, 0), ("Bm", ps, 0, 1)])
                S.emit("pool", lambda E: E.tensor_tensor(out=qdT[:], in0=qTg[:, :, tk], in1=EG[:], op=ALU.mult),
                       reads=[("EG", par)] + [("gq", pr) for pr in range(4)], writes=[("qdT", par)])
                for hf in range(2):
                    rows = slice(hf * 64, (hf + 1) * 64)
                    S.emit("pool", lambda E, hf=hf, rows=rows: E.tensor_tensor(
                        out=kdec[hf][rows], in0=ktok[rows],
                        in1=egt[rows, 8:16].unsqueeze(2).to_broadcast([64, 8, 64]), op=ALU.mult),
                           reads=[KT, EGT], writes=[("kdec", par, hf)])
                S.emit("pool", lambda E: E.tensor_tensor(out=Bp[0][:], in0=Bm[0][:],
                                                         in1=identb.to_broadcast([128, 8, 128]), op=ALU.add),
                       reads=[("Bm", ps, 0, 0), ("Bm", ps, 0, 1), "ident"], writes=[BPN[0] + (0,), BPN[0] + (1,)])
                yield
            else:
                S.emit("pe", lambda E: E.matmul(PS[B0][0:8, 0:128], lhsT=gt, rhs=TRI[:], start=True, stop=True),
                       reads=["gg", "TRI"], writes=[("ps", B0)], signal=False)
                S.emit("pe", lambda E: E.matmul(PS[B0][:, 128:136], lhsT=TRI[:], rhs=gt, start=True, stop=True),
                       reads=["gg", "TRI"], writes=[("ps", B0)], signal=False)
                S.emit("pe", lambda E: E.matmul(PS[B0][:, 136:144], lhsT=BLKS[:], rhs=gt, start=True, stop=True),
                       reads=["gg", "BLKS"], writes=[("ps", B0)])
                yield
                S.emit("act", lambda E: E.activation(out=gcT[:], in_=PS[B0][0:8, 0:128], func=AF.Identity),
                       reads=[("ps", B0)], writes=[GC])
                S.emit("dve", lambda E: E.tensor_copy(out=gct[:, 0:16], in_=PS[B0][:, 128:144]),
                       reads=[("ps", B0)], writes=[GT_])
                S.emit("dve", lambda E: E.tensor_tensor(out=gct[:, 16:24], in0=gct[:, 8:16], in1=gct[:, 0:8],
                                                        op=ALU.subtract), reads=[GT_], writes=[GT_])
                S.emit("act", lambda E: E.activation(out=egt[:, 0:8], in_=gct[:, 0:8], func=AF.Exp),
                       reads=[GT_], writes=[EGT])
                S.emit("act", lambda E: E.activation(out=egt[:, 8:16], in_=gct[:, 16:24], func=AF.Exp),
                       reads=[GT_, EGT], writes=[EGT])
                yield
                S.emit("dve", lambda E: E.tensor_tensor(out=rhsBD[:], in0=HEADM[:].to_broadcast([8, 8, 128]),
                                                        in1=gcT[:].unsqueeze(1).to_broadcast([8, 8, 128]), op=ALU.mult),
                       reads=[GC, "HEADM"], writes=[RB])
                for hf in range(2):
                    S.emit("pe", lambda E, hf=hf: E.matmul(
                        PS[MYB[1 + hf]][:, :], lhsT=ONES8[:], rhs=rhsBD[:, 4 * hf:4 * hf + 4, :].rearrange("p h i -> p (h i)"),
                        start=True, stop=True),
                           reads=[RB, "ONES8"], writes=[("ps", MYB[1 + hf])])
                yield
                for hf in range(2):
                    S.emit("dve", lambda E, hf=hf: E.tensor_tensor(
                        out=EA[:, 4 * hf:4 * hf + 4, :], in0=PS[MYB[1 + hf]][:, :].rearrange("p (h i) -> p h i", h=4),
                        in1=gct[:, 4 * hf:4 * hf + 4].unsqueeze(2).to_broadcast([128, 4, 128]), op=ALU.subtract),
                           reads=[("ps", MYB[1 + hf]), GT_], writes=[EAn])
                S.emit("act", lambda E: E.activation(out=EA[:], in_=EA[:], func=AF.Exp), reads=[EAn], writes=[EAn])
                yield
                S.emit("dve", lambda E: E.tensor_tensor(out=EA[:], in0=EA[:], in1=MASKU[:], op=ALU.min),
                       reads=[EAn, "MASKU"], writes=[EAn])
                S.emit("pool", lambda E: E.tensor_tensor(out=EAsb[:], in0=EA[:], in1=STRICT[:], op=ALU.mult),
                       reads=[EAn, "STRICT"], writes=[EASn])
                S.emit("pool", lambda E: E.tensor_tensor(out=EAsb[:], in0=EAsb[:],
                                                         in1=nbeta[:, t, :].unsqueeze(2).to_broadcast([128, 8, 128]),
                                                         op=ALU.mult),
                       reads=[EASn, "gnbeta"], writes=[EASn])
                yield
                for pr in range(4):
                    S.emit("pe", lambda E, pr=pr: E.matmul(PS[B0][:, pr * 128:(pr + 1) * 128], lhsT=SEL[:, pr, :],
                                                           rhs=gcT[:], start=True, stop=True),
                           reads=[GC, "SEL"], writes=[("ps", B0)], signal=(pr == 3))
                S.emit("act", lambda E: E.activation(out=EG[:].rearrange("p a i -> p (a i)"), in_=PS[B0][:, :], func=AF.Exp),
                       reads=[("ps", B0)], writes=[("EG", par)])
                S.emit("pool", lambda E: E.tensor_tensor(out=qdT[:], in0=qTg[:, :, tk], in1=EG[:], op=ALU.mult),
                       reads=[("EG", par)] + [("gq", pr) for pr in range(4)], writes=[("qdT", par)])
                yield
                bk = pbank()
                tbk = PS[bk][:].bitcast(BF16)
                for pr in range(4):
                    S.emit("pe", lambda E, pr=pr, tbk=tbk: E.transpose(out=tbk[:, pr * 128:(pr + 1) * 128],
                                                                       in_=kTg[:, pr, tk], identity=ident[:]),
                           reads=[("gk", pr), "ident"], writes=[("ps", bk)], signal=(pr == 3))
                S.emit("act", lambda E, tbk=tbk: E.activation(out=ktok[:].rearrange("p h d -> p (h d)"),
                                                              in_=tbk[:, 0:512], func=AF.Identity),
                       reads=[("ps", bk)], writes=[KT])
                bv = pbank()
                tbv = PS[bv][:].bitcast(BF16)
                for pr in range(4):
                    S.emit("pe", lambda E, pr=pr, tbv=tbv: E.transpose(out=tbv[:, pr * 128:(pr + 1) * 128],
                                                                       in_=vTg[:, pr, tk], identity=ident[:]),
                           reads=[("gv", pr), "ident"], writes=[("ps", bv)], signal=(pr == 3))
                yield
                X0 = X[0]
                S.emit("dve", lambda E, tbv=tbv: E.tensor_copy(out=X0[:, 0, :, :],
                                                               in_=tbv[:, 0:512].rearrange("p (h d) -> p h d", h=8)),
                       reads=[("ps", bv)], writes=[XN[0] + (0,), XN[0] + (1,)])
                S.emit("dve", lambda E: E.tensor_tensor(out=X0[:, 1, :, :], in0=ktok[:],
                                                        in1=egt[:, 0:8].unsqueeze(2).to_broadcast([128, 8, 64]),
                                                        op=ALU.mult),
                       reads=[KT, EGT, XN[0] + (0,), XN[0] + (1,)], writes=[XN[0] + (0,), XN[0] + (1,)])
                for hf in range(2):
                    rows = slice(hf * 64, (hf + 1) * 64)
                    S.emit("pool", lambda E, hf=hf, rows=rows: E.tensor_tensor(
                        out=kdec[hf][rows], in0=ktok[rows],
                        in1=egt[rows, 8:16].unsqueeze(2).to_broadcast([64, 8, 64]), op=ALU.mult),
                           reads=[KT, EGT], writes=[("kdec", par, hf)])
                yield
                for hh in range(2):
                    rows = slice(hh * 64, (hh + 1) * 64)
                    S.emit("pool", lambda E, hh=hh, rows=rows: E.tensor_copy(out=qz[rows, hh, :, :], in_=qTg[rows, :, tk]),
                           reads=[("gq", pr) for pr in range(4)], writes=[QZ])
                    S.emit("pool", lambda E, hh=hh, rows=rows: E.tensor_copy(out=kz[rows, hh, :, :], in_=kTg[rows, :, tk]),
                           reads=[("gk", pr) for pr in range(4)], writes=[KZ])
                kq = (pbank(), pbank())
                for h in range(8):
                    pr, hh = h // 2, h % 2
                    S.emit("pe", lambda E, pr=pr, hh=hh: E.matmul(
                        PS[kq[hh]][:, pr * 128:(pr + 1) * 128], lhsT=kTg[:, pr, tk], rhs=qz[:, hh, pr, :],
                        start=True, stop=True),
                           reads=[("gk", pr), QZ], writes=[("ps", kq[hh])], signal=(h >= 6))
                for hh in range(2):
                    S.emit("dve", lambda E, hh=hh: E.tensor_tensor(
                        out=attnT[:, hh:8:2, :], in0=PS[kq[hh]][:, :].rearrange("p (a i) -> p a i", a=4),
                        in1=EA[:, hh:8:2, :], op=ALU.mult),
                           reads=[("ps", kq[hh]), EAn], writes=[("attnT", par)])
                yield
                kk = (pbank(), pbank())
                for h in range(8):
                    pr, hh = h // 2, h % 2
                    S.emit("pe", lambda E, pr=pr, hh=hh: E.matmul(
                        PS[kk[hh]][:, pr * 128:(pr + 1) * 128], lhsT=kTg[:, pr, tk], rhs=kz[:, hh, pr, :],
                        start=True, stop=True),
                           reads=[("gk", pr), KZ], writes=[("ps", kk[hh])], signal=(h >= 6))
                for hh in range(2):
                    S.emit("dve", lambda E, hh=hh: E.tensor_tensor(
                        out=Bm[0][:, hh:8:2, :], in0=PS[kk[hh]][:, :].rearrange("p (a i) -> p a i", a=4),
                        in1=EAsb[:, hh:8:2, :], op=ALU.mult),
                           reads=[("ps", kk[hh]), EASn], writes=[("Bm", ps, 0, 0), ("Bm", ps, 0, 1)])
                S.emit("pool", lambda E: E.tensor_tensor(out=Bp[0][:], in0=Bm[0][:], in1=identb.to_broadcast([128, 8, 128]),
                                                         op=ALU.add),
                       reads=[("Bm", ps, 0, 0), ("Bm", ps, 0, 1), "ident"], writes=[BPN[0] + (0,), BPN[0] + (1,)])
                yield
            for a in range(2):
                bn = pbank()
                for h4 in range(4):
                    h = 4 * a + h4
                    S.emit("pe", lambda E, h=h, h4=h4, bn=bn: E.matmul(
                        PS[bn][:, h4 * 128:(h4 + 1) * 128], lhsT=Bm[0][:, h, :], rhs=ident[:], start=True, stop=True),
                           reads=[("Bm", ps, 0, a), "ident"], writes=[("ps", bn)], signal=(h4 == 3))
                self.evac(Nm[0][:, 4 * a:4 * a + 4, :].rearrange("p h i -> p (h i)"), PS[bn][:, :],
                          reads=[("ps", bn)], writes=[("Nm", ps, 0, a)])
                yield
            for lv in range(5):
                ci, ni = lv % 2, (lv + 1) % 2
                for a in range(2):
                    if lv < 4:
                        bnn = pbank()
                        for h4 in range(4):
                            h = 4 * a + h4
                            S.emit("pe", lambda E, h=h, h4=h4, bnn=bnn, ci=ci: E.matmul(
                                PS[bnn][:, h4 * 128:(h4 + 1) * 128], lhsT=Bm[ci][:, h, :], rhs=Nm[ci][:, h, :],
                                start=True, stop=True),
                                   reads=[("Nm", ps, ci, a), ("Bm", ps, ci, a)], writes=[("ps", bnn)], signal=(h4 == 3))
                        self.evac(Nm[ni][:, 4 * a:4 * a + 4, :].rearrange("p h i -> p (h i)"), PS[bnn][:, :],
                                  reads=[("ps", bnn)], writes=[("Nm", ps, ni, a)])
                        yield
                    bbb = pbank()
                    for h4 in range(4):
                        h = 4 * a + h4
                        S.emit("pe", lambda E, h=h, h4=h4, bbb=bbb, ci=ci: E.matmul(
                            PS[bbb][:, h4 * 128:(h4 + 1) * 128], lhsT=Nm[ci][:, h, :], rhs=Bm[ci][:, h, :],
                            start=True, stop=True),
                               reads=[("Nm", ps, ci, a), ("Bm", ps, ci, a)], writes=[("ps", bbb)], signal=(h4 == 3))
                    self.evac(Bm[ni][:, 4 * a:4 * a + 4, :].rearrange("p h i -> p (h i)"), PS[bbb][:, :],
                              reads=[("ps", bbb)], writes=[("Bm", ps, ni, a)])
                    S.emit("pool", lambda E, a=a, ni=ni: E.tensor_tensor(
                        out=Bp[ni][:, 4 * a:4 * a + 4, :], in0=Bm[ni][:, 4 * a:4 * a + 4, :],
                        in1=identb.to_broadcast([128, 4, 128]), op=ALU.add),
                           reads=[("Bm", ps, ni, a), "ident"], writes=[BPN[ni] + (a,)])
                    yield
                for a in range(2):
                    bx = pbank()
                    for h4 in range(4):
                        h = 4 * a + h4
                        S.emit("pe", lambda E, h=h, h4=h4, bx=bx, ci=ci: E.matmul(
                            PS[bx][:, h4 * 128:(h4 + 1) * 128], lhsT=Bp[ci][:, h, :], rhs=X[ci][:, :, h, :],
                            start=True, stop=True),
                               reads=[XN[ci] + (a,), BPN[ci] + (a,)], writes=[("ps", bx)], signal=(h4 == 3))
                    self.evac(X[ni][:, :, 4 * a:4 * a + 4, :].rearrange("p s h d -> p h s d"),
                              PS[bx][:, :].rearrange("p (h s d) -> p h s d", h=4, s=2),
                              reads=[("ps", bx)], writes=[XN[ni] + (a,)])
                    yield
            X5, B5 = X[1], Bp[1]

        def gen_S(t):
            par = t % NSLOT
            tk = slice(t * 128, (t + 1) * 128)
            EG, qdT, kdec, attnT, nwT = EGs[par], qdTs[par], kdecs[par], attnTs[par], nwTs[par]
            X5, B5 = X1s[par], Bp1s[par]
            for a in range(2):
                bw = sbank()
                for h4 in range(4):
                    h = 4 * a + h4
                    pr = h // 2
                    lw = X5[:, 1, 2 * pr:2 * pr + 2, :].rearrange("p h d -> p (h d)")
                    S.emit("pe", lambda E, h=h, h4=h4, bw=bw, lw=lw: E.matmul(
                        PS[bw][:, h4 * 128:(h4 + 1) * 128], lhsT=lw, rhs=B5[:, h, :], start=True, stop=True),
                           reads=[("X", par, 1, a), ("Bp", par, 1, a)], writes=[("ps", bw)], signal=(h4 == 3))
                for h4 in range(4):
                    h = 4 * a + h4
                    pr, hh = h // 2, h % 2
                    rows = slice(hh * 64, (hh + 1) * 64)
                    S.emit("dve", lambda E, h4=h4, bw=bw, pr=pr, rows=rows: E.tensor_scalar(
                        out=nwT[rows, pr, :], in0=PS[bw][rows, h4 * 128:(h4 + 1) * 128], scalar1=-1.0, scalar2=None,
                        op0=ALU.mult),
                           reads=[("ps", bw)], writes=[("nwT", par)])
                yield
            for hf in range(2):
                rows = slice(hf * 64, (hf + 1) * 64)
                bvn = sbank()
                for h in range(8):
                    pr, hh = h // 2, h % 2
                    cs = slice(h * 64, (h + 1) * 64)
                    S.emit("pe", lambda E, h=h, cs=cs, bvn=bvn: E.matmul(
                        PS[bvn][:, cs], lhsT=B5[:, h, :], rhs=X5[:, 0, h, :], start=True, stop=False),
                           reads=[("X", par, 1, 0), ("X", par, 1, 1), ("Bp", par, 1, 0), ("Bp", par, 1, 1)], writes=[("ps", bvn)], signal=False)
                    S.emit("pe", lambda E, pr=pr, hh=hh, cs=cs, bvn=bvn: E.matmul(
                        PS[bvn][:, cs], lhsT=nwT[:, pr, :], rhs=Sb[:, pr, hh, :], start=False, stop=True),
                           reads=[("nwT", par), "Sb"], writes=[("ps", bvn)], signal=(h == 7))
                yield
                S.emit("dve", lambda E, rows=rows, bvn=bvn: E.tensor_tensor(
                    out=vnew[rows], in0=PS[bvn][rows, :].rearrange("p (h d) -> p h d", h=8),
                    in1=beta[rows, t, :].unsqueeze(2).to_broadcast([64, 8, 64]), op=ALU.mult),
                       reads=[("ps", bvn), "gbeta"], writes=["vnew"])
                yield
                bo = sbank()
                for h in range(8):
                    pr, hh = h // 2, h % 2
                    cs = slice(h * 64, (h + 1) * 64)
                    S.emit("pe", lambda E, pr=pr, hh=hh, cs=cs, bo=bo: E.matmul(
                        PS[bo][:, cs], lhsT=qdT[:, pr, :], rhs=Sb[:, pr, hh, :], start=True, stop=False),
                           reads=[("qdT", par), "Sb"], writes=[("ps", bo)], signal=False)
                    S.emit("pe", lambda E, h=h, cs=cs, bo=bo: E.matmul(
                        PS[bo][:, cs], lhsT=attnT[:, h, :], rhs=vnew[:, h, :], start=False, stop=True),
                           reads=[("attnT", par), "vnew"], writes=[("ps", bo)], signal=(h == 7))
                yield
                S.emit("act", lambda E, rows=rows, bo=bo: E.activation(
                    out=osb[rows].rearrange("p h d -> p (h d)"), in_=PS[bo][rows, :], func=AF.Identity),
                       reads=[("ps", bo)], writes=["osb"])
                bs = sbank()
                for h in range(8):
                    pr, hh = h // 2, h % 2
                    S.emit("pe", lambda E, h=h, pr=pr, hh=hh, bs=bs, hf=hf: E.matmul(
                        PS[bs][:, (pr * 2 + hh) * 64:(pr * 2 + hh + 1) * 64],
                        lhsT=kdec[hf][:, 2 * pr:2 * pr + 2, :].rearrange("p h d -> p (h d)"), rhs=vnew[:, h, :],
                        start=True, stop=True),
                           reads=[("kdec", par, hf), "vnew"], writes=[("ps", bs)], signal=(h == 7))
                yield
                gl = EG[:, :, hf * 64 + 63:hf * 64 + 64]
                S.emit("dve", lambda E, gl=gl: E.tensor_tensor(out=tmpS[:], in0=S32[:],
                                                               in1=gl.to_broadcast([128, 4, 64]), op=ALU.mult),
                       reads=["S32", ("EG", par)], writes=["tmpS"])
                dS = PS[bs][:, :].rearrange("p (a b d) -> p a b d", a=4, b=2)
                for hh in range(2):
                    r2 = slice(hh * 64, (hh + 1) * 64)
                    S.emit("dve", lambda E, hh=hh, r2=r2, dS=dS: E.tensor_tensor(
                        out=S32[r2], in0=tmpS[r2], in1=dS[r2, :, hh, :], op=ALU.add),
                           reads=["tmpS", ("ps", bs)], writes=["S32"])
                    S.emit("act", lambda E, hh=hh, r2=r2: E.activation(out=Sb[r2, :, hh, :], in_=S32[r2],
                                                                       func=AF.Identity),
                           reads=["S32"], writes=["Sb"])
                yield
            S.emit("pool", lambda E: E.tensor_tensor(out=osq[:], in0=osb[:], in1=osb[:], op=ALU.mult),
                   reads=["osb"], writes=["osq"])
            S.emit("dve", lambda E: E.tensor_reduce(out=oss[:, 0:8], in_=osq[:], axis=AX.X, op=ALU.add),
                   reads=["osq"], writes=["oss"])
            yield
            S.emit("act", lambda E: E.activation(out=oss[:, 8:16], in_=oss[:, 0:8], func=AF.Sqrt, scale=1.0 / 64,
                                                 bias=P["epsc"][:]),
                   reads=["oss", "epsc"], writes=["oss"])
            S.emit("dve", lambda E: E.reciprocal(out=oss[:, 0:8], in_=oss[:, 8:16]), reads=["oss"], writes=["oss"])
            S.emit("dve", lambda E: E.tensor_tensor(out=osq[:], in0=osb[:],
                                                    in1=oss[:, 0:8].unsqueeze(2).to_broadcast([128, 8, 64]),
                                                    op=ALU.mult),
                   reads=["osb", "oss", "osq"], writes=["osq"])
            yield
            S.emit("pool", lambda E: E.tensor_tensor(out=osq[:], in0=osq[:],
                                                     in1=gnw[:].unsqueeze(1).to_broadcast([128, 8, 64]),
                                                     op=ALU.mult),
                   reads=["osq", "gnw"], writes=["osq"])
            S.emit("dve", lambda E: E.tensor_tensor(out=og[:], in0=osq[:].rearrange("p h d -> p (h d)"),
                                                    in1=zs[:, t, :], op=ALU.mult),
                   reads=["osq", ("gzs", t)], writes=["og"])
            yield
            bt = sbank()
            tbo = PS[bt][:].bitcast(BF16)
            for pr in range(4):
                S.emit("pe", lambda E, pr=pr, tbo=tbo: E.transpose(out=tbo[:, pr * 128:(pr + 1) * 128],
                                                                   in_=og[:, pr * 128:(pr + 1) * 128],
                                                                   identity=ident[:]),
                       reads=["og", "ident"], writes=[("ps", bt)], signal=(pr == 3))
            S.emit("act", lambda E, tbo=tbo: E.activation(out=oT[:, 4:8, tk],
                                                          in_=tbo[:, 0:512].rearrange("p (a i) -> p a i", a=4),
                                                          func=AF.Identity),
                   reads=[("ps", bt)], writes=[("oT", 4 + pr) for pr in range(4)])
            yield

        def drain(gen):
            for _ in gen:
                pass

        def step(gen):
            try:
                next(gen)
                return True
            except StopIteration:
                return False

        active_p = []
        next_p = 0
        p_done = set()
        scan_t = 0
        scan_gen = None
        while scan_t < NT:
            while next_p < NT and len(active_p) < NPS and next_p < scan_t + NSLOT:
                active_p.append([next_p, gen_P(next_p)])
                next_p += 1
            for ent in list(active_p):
                for _ in range(int(os.environ.get("C2RATIO", "2"))):
                    if not step(ent[1]):
                        p_done.add(ent[0])
                        active_p.remove(ent)
                        break
            if scan_gen is None and scan_t in p_done:
                scan_gen = gen_S(scan_t)
            if scan_gen is not None:
                if not step(scan_gen):
                    scan_gen = None
                    scan_t += 1

    def phase_D(self, st, b):
        nc, S, d, P, PS = self.nc, self.S, self.d, self.P, self.PS
        sb = self.sb
        oT = P["oT"]
        ident = P["ident"]
        wo = sb(st, "wo", [128, 8, DM], BF16)
        wu = sb(st, "wu", [128, 8, DFF], BF16)
        wd = sb(st, "wd", [128, 32, DM], BF16)
        premlp = P["premlp"]
        WO_, WU_, WD_ = 8 * DM, 8 * DFF, 32 * DM
        if b > 0:
            S.dma(wo[:].rearrange("p k c -> p (k c)"), self.wscr[:, 0:WO_], reads=["wscr"], writes=["wo"])
            for q4 in range(4):
                S.dma(wu[:, 2 * q4:2 * q4 + 2, :].rearrange("p k c -> p (k c)"),
                      self.wscr[:, WO_ + q4 * 2 * DFF:WO_ + (q4 + 1) * 2 * DFF], reads=["wscr"], writes=["wu"])
            for q4 in range(4):
                S.dma(wd[:, 8 * q4:8 * q4 + 8, :].rearrange("p k c -> p (k c)"),
                      self.wscr[:, WO_ + WU_ + q4 * 8 * DM:WO_ + WU_ + (q4 + 1) * 8 * DM], reads=["wscr"], writes=["wd"])
        with ExitStack() as s_stg:
          if b == 0:
              NSTG = 5
              stg = [sb(s_stg, "stgD%d" % i, [128, DM], F32) for i in range(NSTG)]
              self._ns = 0

              def load_cast(src, dst, dname, scal=None):
                  i = self._ns % NSTG
                  eng = ("pool", "act", "dve")[self._ns % 3]
                  self._ns += 1
                  S.dma(stg[i][:], src, writes=[("stgD", i)])
                  rd = [("stgD", i)] + (["premlp"] if scal is not None else [])
                  if eng == "act":
                      if scal is None:
                          S.emit("act", lambda E, i=i: E.activation(out=dst, in_=stg[i][:], func=AF.Identity),
                                 reads=rd, writes=[dname])
                      else:
                          S.emit("act", lambda E, i=i: E.activation(out=dst, in_=stg[i][:], func=AF.Identity,
                                                                    scale=scal), reads=rd, writes=[dname])
                  else:
                      if scal is None:
                          S.emit(eng, lambda E, i=i: E.tensor_copy(out=dst, in_=stg[i][:]), reads=rd, writes=[dname])
                      else:
                          S.emit(eng, lambda E, i=i: E.tensor_scalar(out=dst, in0=stg[i][:], scalar1=scal, scalar2=None,
                                                                     op0=ALU.mult), reads=rd, writes=[dname])

              for k in range(8):
                  load_cast(d["w_out"][k * 128:(k + 1) * 128, :], wo[:, k, :], ("wo", k))
              for k in range(8):
                  for qd in range(4):
                      load_cast(d["w_up"][k * 128:(k + 1) * 128, qd * 1024:(qd + 1) * 1024],
                                wu[:, k, qd * 1024:(qd + 1) * 1024], ("wu", k, qd), scal=premlp[:, k:k + 1])
              for k in range(32):
                  load_cast(d["w_down"][k * 128:(k + 1) * 128, :], wd[:, k, :], ("wd", k))
        S.barrier()
        if b == 0 and self.nseq > 1:
            S.dma(self.wscr[:, 0:WO_], wo[:].rearrange("p k c -> p (k c)"), reads=["wo"], writes=["wscr"])
            for q4 in range(4):
                S.dma(self.wscr[:, WO_ + q4 * 2 * DFF:WO_ + (q4 + 1) * 2 * DFF],
                      wu[:, 2 * q4:2 * q4 + 2, :].rearrange("p k c -> p (k c)"), reads=["wu"], writes=["wscr"])
            for q4 in range(4):
                S.dma(self.wscr[:, WO_ + WU_ + q4 * 8 * DM:WO_ + WU_ + (q4 + 1) * 8 * DM],
                      wd[:, 8 * q4:8 * q4 + 8, :].rearrange("p k c -> p (k c)"), reads=["wd"], writes=["wscr"])

        GT = 2
        xt = [sb(st, "xtD%d" % i, [128, DM], F32) for i in range(GT)]
        tmp = sb(st, "tmpD", [128, DM], F32)
        h2 = sb(st, "h2D", [128, DM], BF16)
        h2T = sb(st, "h2T", [128, 8, GT * 128], BF16)
        uT = sb(st, "uT", [128, 32, GT * 128], BF16)
        rl = [sb(st, "rlD%d" % i, [128, 512], F32) for i in range(2)]
        sm = sb(st, "smD", [128, NT, 12], F32)
        S.emit("dve", lambda E: E.memset(sm[:], 0.0), writes=["smD"])
        postmix_b, postmlp_b, epsc = P["postmix_b"], P["postmlp_b"], P["epsc"]
        oT_all = [("oT", c) for c in range(8)]

        def rms_scale(src_banks, t, col):
            for hf in range(2):
                S.emit("act", lambda E, hf=hf: E.activation(out=tmp[:, hf * 512:(hf + 1) * 512],
                                                            in_=PS[src_banks[hf]][:, :], func=AF.Square,
                                                            accum_out=sm[:, t, col + hf:col + hf + 1]),
                       reads=[("ps", src_banks[hf]), "smD"], writes=["tmpD", "smD"])
            S.emit("dve", lambda E: E.tensor_tensor(out=sm[:, t, col:col + 1], in0=sm[:, t, col:col + 1],
                                                    in1=sm[:, t, col + 1:col + 2], op=ALU.add),
                   reads=["smD"], writes=["smD"])
            S.emit("act", lambda E: E.activation(out=sm[:, t, col + 1:col + 2], in_=sm[:, t, col:col + 1],
                                                 func=AF.Sqrt, scale=1.0 / DM, bias=epsc[:]),
                   reads=["smD", "epsc"], writes=["smD"])
            S.emit("dve", lambda E: E.reciprocal(out=sm[:, t, col + 2:col + 3], in_=sm[:, t, col + 1:col + 2]),
                   reads=["smD"], writes=["smD"])

        def stage1a(t, j, bk):
            tok = slice(t * 128, (t + 1) * 128)
            S.dma(xt[j][:], d["x"][b, tok, :], writes=[("xtD", j)])
            for hf in range(2):
                for c in range(8):
                    S.emit("pe", lambda E, hf=hf, c=c: E.matmul(PS[bk[hf]][:, :], lhsT=oT[:, c, tok],
                                                                rhs=wo[:, c, hf * 512:(hf + 1) * 512],
                                                                start=(c == 0), stop=(c == 7)),
                           reads=oT_all + ["wo"], writes=[("ps", bk[hf])], signal=(c == 7))

        def stage1b(t, j, bk):
            tok = slice(t * 128, (t + 1) * 128)
            rms_scale(bk, t, 0)
            for hf in range(2):
                cs = slice(hf * 512, (hf + 1) * 512)
                S.emit("dve", lambda E, hf=hf, cs=cs: E.scalar_tensor_tensor(
                    out=tmp[:, cs], in0=PS[bk[hf]][:, :], scalar=sm[:, t, 2:3], in1=postmix_b[:, cs],
                    op0=ALU.mult, op1=ALU.mult),
                       reads=[("ps", bk[hf]), "smD", "postmix_b", "tmpD"], writes=["tmpD"])
            S.emit("dve", lambda E: E.tensor_tensor(out=xt[j][:], in0=tmp[:], in1=xt[j][:], op=ALU.add),
                   reads=["tmpD", ("xtD", j)], writes=[("xtD", j)])
            if "x1" in self.dbg_out:
                S.dma(self.dbg_out["x1"][b, tok, :], xt[j][:], reads=[("xtD", j)], writes=[("dbgx1", t)])
            S.emit("act", lambda E: E.activation(out=h2[:], in_=xt[j][:], func=AF.Square, accum_out=sm[:, t, 3:4]),
                   reads=[("xtD", j), "smD"], writes=["h2D", "smD"])
            S.emit("act", lambda E: E.activation(out=sm[:, t, 4:5], in_=sm[:, t, 3:4], func=AF.Sqrt,
                                                 scale=1.0 / DM, bias=epsc[:]),
                   reads=["smD", "epsc"], writes=["smD"])
            S.emit("dve", lambda E: E.reciprocal(out=sm[:, t, 5:6], in_=sm[:, t, 4:5]), reads=["smD"], writes=["smD"])
            S.emit("act", lambda E: E.activation(out=h2[:], in_=xt[j][:], func=AF.Identity, scale=sm[:, t, 5:6]),
                   reads=[("xtD", j), "smD"], writes=["h2D"])
            tb = PS[2][:].bitcast(BF16)
            for k in range(8):
                S.emit("pe", lambda E, k=k: E.transpose(out=tb[:, k * 128:(k + 1) * 128],
                                                        in_=h2[:, k * 128:(k + 1) * 128], identity=ident[:]),
                       reads=["h2D", "ident"], writes=[("ps", 2)], signal=(k == 7))
            S.emit("dve", lambda E: E.tensor_copy(out=h2T[:, :, j * 128:(j + 1) * 128],
                                                  in_=tb.rearrange("p (k c) -> p k c", k=8)),
                   reads=[("ps", 2)], writes=[("h2T", j)])

        def stage2():
            W = GT * 128
            nf = 512 // W
            for g in range(32 // nf):
                bank = 3 + (g % 3)
                for f in range(nf):
                    fc = g * nf + f
                    for k in range(8):
                        S.emit("pe", lambda E, bank=bank, f=f, fc=fc, k=k: E.matmul(
                            PS[bank][:, f * W:(f + 1) * W], lhsT=wu[:, k, fc * 128:(fc + 1) * 128],
                            rhs=h2T[:, k, :], start=(k == 0), stop=(k == 7)),
                               reads=["wu"] + [("h2T", j) for j in range(GT)], writes=[("ps", bank)],
                               signal=(k == 7 and f == nf - 1))
                uv = uT[:, g * nf:(g + 1) * nf, :].rearrange("p a c -> p (a c)")
                ri = g % 2
                S.emit("act", lambda E, bank=bank, ri=ri: E.activation(out=rl[ri][:], in_=PS[bank][:, :], func=AF.Relu),
                       reads=[("ps", bank)], writes=[("rlD", ri)])
                S.emit("pool", lambda E, uv=uv, ri=ri: E.tensor_tensor(out=uv, in0=rl[ri][:], in1=rl[ri][:], op=ALU.mult),
                       reads=[("rlD", ri)], writes=["uT"])

        def stage3(t, j):
            tok = slice(t * 128, (t + 1) * 128)
            for hf in range(2):
                for fc in range(32):
                    S.emit("pe", lambda E, hf=hf, fc=fc: E.matmul(PS[6 + hf][:, :], lhsT=uT[:, fc, j * 128:(j + 1) * 128],
                                                                  rhs=wd[:, fc, hf * 512:(hf + 1) * 512],
                                                                  start=(fc == 0), stop=(fc == 31)),
                           reads=["uT", "wd"], writes=[("ps", 6 + hf)], signal=(fc == 31))
            rms_scale((6, 7), t, 8)
            for hf in range(2):
                cs = slice(hf * 512, (hf + 1) * 512)
                S.emit("dve", lambda E, hf=hf, cs=cs: E.scalar_tensor_tensor(
                    out=tmp[:, cs], in0=PS[6 + hf][:, :], scalar=sm[:, t, 10:11], in1=postmlp_b[:, cs],
                    op0=ALU.mult, op1=ALU.mult),
                       reads=[("ps", 6 + hf), "smD", "postmlp_b", "tmpD"], writes=["tmpD"])
            S.emit("pool", lambda E: E.tensor_tensor(out=xt[j][:], in0=tmp[:], in1=xt[j][:], op=ALU.add),
                   reads=["tmpD", ("xtD", j)], writes=[("xtD", j)])
            S.dma(self.out[b, tok, :], xt[j][:], reads=[("xtD", j)], writes=[("out", b, t)])

        for gi in range(NT // GT):
            OB = ((0, 1), (3, 4))
            for j in range(GT):
                stage1a(gi * GT + j, j, OB[j])
            for j in range(GT):
                stage1b(gi * GT + j, j, OB[j])
            stage2()
            for j in range(GT):
                stage3(gi * GT + j, j)


def host_inputs(inputs, core, nseq):
    f = lambda a: np.ascontiguousarray(np.asarray(a, dtype=np.float32))
    m = {}
    m["x"] = f(inputs["x"][core * nseq:(core + 1) * nseq])
    m["w_in"] = f(inputs["w_in"][0])
    m["w_out"] = f(inputs["w_out"][0])
    m["w_up"] = f(inputs["w_up"][0])
    m["w_down"] = f(inputs["w_down"][0])
    m["premix_pk"] = f(np.asarray(inputs["pre_mix_norm"][0]).reshape(8, 128).T)
    m["premlp_pk"] = f(np.asarray(inputs["pre_mlp_norm"][0]).reshape(8, 128).T)
    m["postmix"] = f(np.asarray(inputs["post_mix_norm"][0]).reshape(1, DM))
    m["postmlp"] = f(np.asarray(inputs["post_mlp_norm"][0]).reshape(1, DM))
    rb = np.asarray(inputs["rel_bias"], dtype=np.float32)
    tab = np.concatenate([rb, np.full((8, 1), NEG, np.float32)], axis=1)
    j = np.arange(128)[:, None]
    i = np.arange(128)[None, :]
    idx0 = np.where(i - j >= 0, rel_bucket_np(i - j), 32)
    idx1 = rel_bucket_np(128 + i - j)
    idx = np.stack([idx0, idx1], axis=0)
    tt = tab[:, idx]
    m["ttab"] = f(tt.transpose(2, 0, 1, 3).reshape(128, 8 * 2 * 128))
    m["rb31"] = f(rb[:, 31].reshape(1, 8))
    cw = np.asarray(inputs["conv_w"][0], dtype=np.float32)
    m["convw_pk"] = f(cw.T.reshape(12, 128, 4).transpose(1, 0, 2).reshape(128, 48))
    m["alog"] = f(np.asarray(inputs["A_log"][0]).reshape(1, 8))
    m["dtb"] = f(np.asarray(inputs["dt_bias"][0]).reshape(1, 8))
    m["gnw"] = f(np.asarray(inputs["gdn_norm_w"][0]).reshape(1, 64))
    return m


_PROG = {}


def kernel(**inputs):
    ncores = 8
    nseq = 16 // ncores
    if "p" not in _PROG:
        _PROG["p"] = Prog(nseq)
    prog = _PROG["p"]
    in_maps = [host_inputs(inputs, c, nseq) for c in range(ncores)]
    res = run_bass_kernel_spmd(prog.nc, in_maps, core_ids=list(range(ncores)))
    out = np.concatenate([r["out"] for r in res.results], axis=0)
    return out.astype(np.float32)
```

```python
import math
from contextlib import ExitStack

import numpy as np
import concourse.bass as bass
import concourse.mybir as mybir
from concourse.bass_utils import run_bass_kernel_spmd

F32 = mybir.dt.float32
BF16 = mybir.dt.bfloat16
AF = mybir.ActivationFunctionType
ALU = mybir.AluOpType
AX = mybir.AxisListType

NDMA = 8
SEQ = 2048
DM = 1024
NT = SEQ // 128
DFF = 4096
INC = 3600
EPS = 1e-6
NEG = -30000.0


class Sched:
    ENGS = ("pe", "act", "dve", "pool", "sp")

    def __init__(self, nc):
        self.nc = nc
        self.q = {e: [] for e in self.ENGS}
        self.cnt = {e: 0 for e in self.ENGS}
        self.pending = {e: False for e in self.ENGS}
        self.seen = {e: {} for e in self.ENGS}
        self.lastw = {}
        self.readers = {}
        self.dma_i = 0
        self.dma_uses = [0] * NDMA
        self.bar = {}
        self.n_ins = {e: 0 for e in self.ENGS}

    def _deps(self, eng, reads, writes):
        deps = dict(self.bar)

        def add(tok):
            k, v = tok
            if deps.get(k, 0) < v:
                deps[k] = v

        for r in reads:
            if r in self.lastw:
                add(self.lastw[r])
            if isinstance(r, tuple) and r[0] == "ps":
                for k, v in self.readers.get(r, {}).items():
                    if k != eng:
                        add((k, v))
        for w in writes:
            if w in self.lastw:
                add(self.lastw[w])
            for k, v in self.readers.get(w, {}).items():
                add((k, v))
        waits = []
        for k, v in deps.items():
            if k == eng and eng == "pe":
                continue
            if self.seen[eng].get(k, 0) >= v:
                continue
            self.seen[eng][k] = v
            waits.append((k, v))
        return waits

    def _record(self, tok, reads, writes):
        k, v = tok
        for r in reads:
            d = self.readers.setdefault(r, {})
            if d.get(k, 0) < v:
                d[k] = v
        for w in writes:
            self.lastw[w] = tok
            self.readers[w] = {}

    def emit(self, eng, fn, reads=(), writes=(), signal=True):
        waits = self._deps(eng, reads, writes)
        if signal:
            self.cnt[eng] += 1
            self.pending[eng] = False
            tok = (eng, self.cnt[eng])
        else:
            self.pending[eng] = True
            tok = (eng, self.cnt[eng] + 1)
        self._record(tok, reads, writes)
        self.n_ins[eng] += 1

        def run(E, sems, waits=waits, fn=fn, signal=signal, eng=eng):
            for k, v in waits:
                E.wait_ge(sems[k], v)
            ins = fn(E)
            if signal:
                ins.then_inc(sems[eng], 1)

        self.q[eng].append(run)
        return tok

    def dma(self, out, in_, reads=(), writes=(), q="sp", **kw):
        slot = self.dma_i % NDMA
        self.dma_i += 1
        key = ("dma", slot)
        waits = self._deps(q, reads, writes)
        prev = 16 * self.dma_uses[slot]
        if prev > 0 and self.seen[q].get(key, 0) < prev:
            self.seen[q][key] = prev
            waits.append((key, prev))
        self.dma_uses[slot] += 1
        tok = (key, 16 * self.dma_uses[slot])
        self._record(tok, reads, writes)
        self.n_ins[q] += 1

        def run(E, sems, waits=waits, out=out, in_=in_, key=key, kw=kw):
            for k, v in waits:
                E.wait_ge(sems[k], v)
            E.dma_start(out=out, in_=in_, **kw).then_inc(sems[key], 16)

        self.q[q].append(run)
        return tok

    def barrier(self):
        for e in self.ENGS:
            assert not self.pending[e]
            if self.cnt[e] > 0:
                self.bar[e] = self.cnt[e]
        for s in range(NDMA):
            if self.dma_uses[s] > 0:
                self.bar[("dma", s)] = 16 * self.dma_uses[s]

    def finish(self):
        waits = []
        for slot in range(NDMA):
            v = 16 * self.dma_uses[slot]
            key = ("dma", slot)
            if v > 0 and self.seen["sp"].get(key, 0) < v:
                self.seen["sp"][key] = v
                waits.append((key, v))

        def run(E, sems, waits=waits):
            for k, v in waits:
                E.wait_ge(sems[k], v)

        self.q["sp"].append(run)
        for e in self.ENGS:
            assert not self.pending[e], f"engine {e} has unsignaled trailing instruction"

    def build(self, stack):
        nc = self.nc
        sems = {}
        for e in self.ENGS:
            sems[e] = stack.enter_context(nc.semaphore("s_" + e))
        for s in range(NDMA):
            sems[("dma", s)] = stack.enter_context(nc.semaphore("s_dma%d" % s))
        block = stack.enter_context(nc.Block())
        q = self.q

        @block.tensor
        def _(E):
            for f in q["pe"]:
                f(E, sems)

        @block.scalar
        def _(E):
            for f in q["act"]:
                f(E, sems)

        @block.vector
        def _(E):
            for f in q["dve"]:
                f(E, sems)

        @block.gpsimd
        def _(E):
            for f in q["pool"]:
                f(E, sems)

        @block.sync
        def _(E):
            for f in q["sp"]:
                f(E, sems)


def rel_bucket_np(d):
    d = np.maximum(d, 0)
    large = 16 + (np.log(np.maximum(d, 1).astype(np.float32) / 16) / math.log(128 / 16) * 16).astype(np.int32)
    large = np.minimum(large, 31)
    return np.where(d < 16, d, large)


class Prog:
    def __init__(self, nseq, stages=("A", "B1", "C1", "C2", "D"), dbg=()):
        self.nseq = nseq
        self.stages = stages
        self.dbg = dbg
        nc = bass.Bass("TRN2", target_bir_lowering=False, dynamic_dma_scratch_size=256)
        self.nc = nc
        self.S = Sched(nc)
        d = {}

        def din(name, shape):
            d[name] = nc.dram_tensor(name, list(shape), F32, kind="ExternalInput").ap()

        din("x", [nseq, SEQ, DM])
        din("w_in", [DM, INC])
        din("w_out", [DM, DM])
        din("w_up", [DM, DFF])
        din("w_down", [DFF, DM])
        din("premix_pk", [128, 8])
        din("premlp_pk", [128, 8])
        din("postmix", [1, DM])
        din("postmlp", [1, DM])
        din("ttab", [128, 8 * 2 * 128])
        din("rb31", [1, 8])
        din("convw_pk", [128, 12 * 4])
        din("alog", [1, 8])
        din("dtb", [1, 8])
        din("gnw", [1, 64])
        self.out = nc.dram_tensor("out", [nseq, SEQ, DM], F32, kind="ExternalOutput").ap()
        self.wscr = nc.dram_tensor("wscr_bf16", [128, 8 * DM + 8 * DFF + 32 * DM], BF16).ap()
        self.dbg_out = {}
        for name, shape in dbg:
            self.dbg_out[name] = nc.dram_tensor("dbg_" + name, list(shape), F32, kind="ExternalOutput").ap()
        self.d = d
        self._rr = 0
        with ExitStack() as st:
            self.build(st)
            self.S.finish()
            self.S.build(st)

    def sb(self, st, name, shape, dt):
        self._uid = getattr(self, "_uid", 0) + 1
        return st.enter_context(self.nc.sbuf_tensor("%s_u%d" % (name, self._uid), list(shape), dt))

    def evac(self, out, in_, reads, writes, eng=None):
        if eng is None:
            eng = ("act", "dve")[self._rr % 2]
            self._rr += 1
        if eng == "act":
            self.S.emit("act", lambda E: E.activation(out=out, in_=in_, func=AF.Identity), reads=reads, writes=writes)
        else:
            self.S.emit("dve", lambda E: E.tensor_copy(out=out, in_=in_), reads=reads, writes=writes)

    def build(self, st):
        nc, S, d = self.nc, self.S, self.d
        sb = self.sb
        self.PS = [st.enter_context(nc.psum_tensor("ps%d" % i, [128, 512], F32)) for i in range(8)]
        P = {}
        self.P = P
        P["ident"] = sb(st, "ident", [128, 128], BF16)
        P["postmix_b"] = sb(st, "postmix_b", [128, DM], F32)
        P["postmlp_b"] = sb(st, "postmlp_b", [128, DM], F32)
        P["premix"] = sb(st, "premix", [128, 8], F32)
        P["premlp"] = sb(st, "premlp", [128, 8], F32)
        P["epsc"] = sb(st, "epsc", [128, 1], F32)
        P["oT"] = sb(st, "oT", [128, 8, SEQ], BF16)
        ident = P["ident"]
        S.emit("pool", lambda E: E.memset(ident[:], 0.0), writes=["ident"])
        S.emit("pool", lambda E: E.affine_select(out=ident[:], in_=ident[:], pattern=[[-1, 128]],
                                                  compare_op=ALU.not_equal, fill=1.0, base=0, channel_multiplier=1),
               reads=["ident"], writes=["ident"])
        S.emit("pool", lambda E: E.memset(P["epsc"][:], EPS), writes=["epsc"])
        S.dma(P["postmix_b"][:], d["postmix"].partition_broadcast(128), writes=["postmix_b"])
        S.dma(P["postmlp_b"][:], d["postmlp"].partition_broadcast(128), writes=["postmlp_b"])
        S.dma(P["premix"][:], d["premix_pk"], writes=["premix"])
        S.dma(P["premlp"][:], d["premlp_pk"], writes=["premlp"])
        if "C2" not in self.stages:
            oT = P["oT"]
            S.emit("pool", lambda E: E.memset(oT[:, 4:8, :], 0.0), writes=[("oT", c) for c in range(4, 8)])

        for b in range(self.nseq):
          S.barrier()
          with ExitStack() as s_seq:
            hT_keep = sb(s_seq, "hT", [128, 8, SEQ], BF16)
            with ExitStack() as s_att:
                A = {}
                A["qT"] = sb(s_att, "qT", [128, 4, SEQ], BF16)
                A["kT"] = sb(s_att, "kT", [128, 4, SEQ], BF16)
                A["vaug"] = sb(s_att, "vaug", [128, NT, 4, 3, 64], BF16)
                A["maskT"] = sb(s_att, "maskT", [128, SEQ], BF16)
                A["kmT"] = sb(s_att, "kmT", [128, 4, 8], BF16)
                with ExitStack() as s_pa:
                    wA = self.prep_B1(s_pa) if "B1" in self.stages else None
                    hT = self.phase_A(s_pa, b, hT=hT_keep)
                    if "B1" in self.stages:
                        self.phase_B1(s_pa, b, hT, A, wA)
                S.barrier()
                if "C1" in self.stages:
                    with ExitStack() as s_c1:
                        self.phase_C1(s_c1, b, A)
                S.barrier()
            S.barrier()
            if "C2" in self.stages or "B2" in self.stages:
                with ExitStack() as s_g:
                    G = {}
                    G["qT"] = sb(s_g, "gqT", [128, 4, SEQ], BF16)
                    G["kT"] = sb(s_g, "gkT", [128, 4, SEQ], BF16)
                    G["vT"] = sb(s_g, "gvT", [128, 4, SEQ], BF16)
                    G["zs"] = sb(s_g, "gzs", [128, NT, 512], BF16)
                    G["gab"] = sb(s_g, "gab", [128, NT, 16], F32)
                    G["g"] = sb(s_g, "gg", [128, NT, 8], F32)
                    G["beta"] = sb(s_g, "gbeta", [128, NT, 8], F32)
                    G["nbeta"] = sb(s_g, "gnbeta", [128, NT, 8], F32)
                    with ExitStack() as s_pb:
                        self.phase_B2(s_pb, b, hT_keep, G)
                    S.barrier()
                    with ExitStack() as s_c2:
                        if "C2" in self.stages:
                            self.phase_C2(s_c2, b, G)
                    S.barrier()
                    if "oTg" in self.dbg_out:
                        with ExitStack() as s_dbg:
                            oT = P["oT"]
                            otf = sb(s_dbg, "otfg", [128, 4, SEQ], F32)
                            S.emit("dve", lambda E: E.tensor_copy(out=otf[:], in_=oT[:, 4:8, :]),
                                   reads=[("oT", c) for c in range(4, 8)], writes=["otfg"])
                            S.dma(self.dbg_out["oTg"][b].rearrange("c p s -> p c s"), otf[:], reads=["otfg"],
                                  writes=["dbg_oTg"])
                        S.barrier()
          S.barrier()
          if "D" in self.stages:
              with ExitStack() as s_d:
                  self.phase_D(s_d, b)
          S.barrier()

    def phase_A(self, st, b, nbuf=2, hT=None):
        nc, S, d, P, PS = self.nc, self.S, self.d, self.P, self.PS
        if hT is None:
            hT = self.sb(st, "hT", [128, 8, SEQ], BF16)
        xt = [self.sb(st, "xt%d" % i, [128, DM], F32) for i in range(nbuf)] * (2 // nbuf)
        hb = [self.sb(st, "hb%d" % i, [128, DM], BF16) for i in range(nbuf)] * (2 // nbuf)
        ss = self.sb(st, "ssA", [128, NT], F32)
        rs = self.sb(st, "rsA", [128, NT], F32)
        rstd = self.sb(st, "rstdA", [128, NT], F32)
        ident = P["ident"]
        S.emit("dve", lambda E: E.memset(ss[:], 0.0), writes=["ssA"])
        for t in range(NT):
            i = t % nbuf
            S.dma(xt[i][:], d["x"][b, t * 128:(t + 1) * 128, :], writes=[("xt", i)])
            S.emit("act", lambda E, i=i, t=t: E.activation(out=hb[i][:], in_=xt[i][:], func=AF.Square,
                                                           accum_out=ss[:, t:t + 1]),
                   reads=[("xt", i), "ssA"], writes=[("hb", i), "ssA"])
            S.emit("act", lambda E, t=t: E.activation(out=rs[:, t:t + 1], in_=ss[:, t:t + 1], func=AF.Sqrt,
                                                      scale=1.0 / DM, bias=P["epsc"][:]),
                   reads=["ssA", "epsc"], writes=["rsA"])
            S.emit("dve", lambda E, t=t: E.reciprocal(out=rstd[:, t:t + 1], in_=rs[:, t:t + 1]),
                   reads=["rsA"], writes=["rstdA"])
            S.emit("act", lambda E, i=i, t=t: E.activation(out=hb[i][:], in_=xt[i][:], func=AF.Identity,
                                                           scale=rstd[:, t:t + 1]),
                   reads=[("xt", i), "rstdA"], writes=[("hb", i)])
            bank = t % 2
            psb = PS[bank][:].bitcast(BF16)
            for k in range(8):
                S.emit("pe", lambda E, k=k, i=i, psb=psb: E.transpose(out=psb[:, k * 128:(k + 1) * 128],
                                                                      in_=hb[i][:, k * 128:(k + 1) * 128],
                                                                      identity=ident[:]),
                       reads=[("hb", i), "ident"], writes=[("ps", bank)], signal=(k == 7))
            S.emit("dve", lambda E, t=t, psb=psb: E.tensor_copy(out=hT[:, :, t * 128:(t + 1) * 128],
                                                                in_=psb.rearrange("p (k c) -> p k c", k=8)),
                   reads=[("ps", bank)], writes=[("hT", t)])
        return hT

    def prep_B1(self, st):
        nc, S, d, P, PS = self.nc, self.S, self.d, self.P, self.PS
        wA = self.sb(st, "wA", [128, 8, 1536], BF16)
        stg = [self.sb(st, "stgA%d" % i, [128, 1536], F32) for i in range(3)]
        premix = P["premix"]
        for k in range(8):
            i = k % 3
            S.dma(stg[i][:], d["w_in"][k * 128:(k + 1) * 128, 0:1536], writes=[("stgA", i)])
            S.emit("pool", lambda E, k=k, i=i: E.tensor_scalar(out=wA[:, k, 0:512], in0=stg[i][:, 0:512],
                                                               scalar1=premix[:, k:k + 1], scalar2=0.125,
                                                               op0=ALU.mult, op1=ALU.mult),
                   reads=[("stgA", i), "premix"], writes=[("wA", k)])
            S.emit("pool", lambda E, k=k, i=i: E.tensor_scalar(out=wA[:, k, 512:1536], in0=stg[i][:, 512:1536],
                                                               scalar1=premix[:, k:k + 1], scalar2=None,
                                                               op0=ALU.mult),
                   reads=[("stgA", i), "premix"], writes=[("wA", k)])
        return wA

    def phase_B1(self, st, b, hT, A, wA):
        nc, S, d, P, PS = self.nc, self.S, self.d, self.P, self.PS
        qT, kT, vaug = A["qT"], A["kT"], A["vaug"]
        S.emit("pool", lambda E: E.memset(vaug[:, :, :, 1, :], 1.0), writes=["vones"])
        wA_all = [("wA", k) for k in range(8)]
        nb = 0
        for which, dst, name in ((0, qT, "qT"), (1, kT, "kT")):
            for pr in range(4):
                col0 = which * 512 + pr * 128
                for tc in range(4):
                    bank = 2 + (nb % 4)
                    nb += 1
                    for k in range(8):
                        S.emit("pe", lambda E, k=k, col0=col0, tc=tc, bank=bank: E.matmul(
                            PS[bank][:, :], lhsT=wA[:, k, col0:col0 + 128], rhs=hT[:, k, tc * 512:(tc + 1) * 512],
                            start=(k == 0), stop=(k == 7)),
                               reads=wA_all + [("hT", 4 * tc + j) for j in range(4)], writes=[("ps", bank)],
                               signal=(k == 7))
                    self.evac(dst[:, pr, tc * 512:(tc + 1) * 512], PS[bank][:, :], reads=[("ps", bank)],
                              writes=[(name, pr, tc)])
        for t in range(NT):
            bank = 2 + (nb % 4)
            nb += 1
            for k in range(8):
                S.emit("pe", lambda E, k=k, t=t, bank=bank: E.matmul(
                    PS[bank][:, :], lhsT=hT[:, k, t * 128:(t + 1) * 128], rhs=wA[:, k, 1024:1536],
                    start=(k == 0), stop=(k == 7)),
                       reads=wA_all + [("hT", t)], writes=[("ps", bank)], signal=(k == 7))
            self.evac(vaug[:, t, :, 0:3:2, :], PS[bank][:, :].rearrange("p (a b c) -> p a b c", a=4, b=2),
                      reads=[("ps", bank)], writes=[("vaug", t)])
        kmf = self.sb(st, "kmf", [128, 4, 8], F32)
        for pr in range(4):
            S.emit("dve", lambda E, pr=pr: E.tensor_reduce(out=kmf[:, pr, :],
                                                           in_=kT[:, pr, :].rearrange("p (n c) -> p n c", n=8),
                                                           axis=AX.X, op=ALU.add),
                   reads=[("kT", pr, tc) for tc in range(4)], writes=["kmf"])
        S.emit("dve", lambda E: E.tensor_scalar(out=A["kmT"][:], in0=kmf[:], scalar1=1.0 / 256, scalar2=None,
                                                op0=ALU.mult),
               reads=["kmf"], writes=["kmT"])

    def phase_C1(self, st, b, A):
        nc, S, d, P, PS = self.nc, self.S, self.d, self.P, self.PS
        sb = self.sb
        qT, kT, vaug, maskT, kmT = A["qT"], A["kT"], A["vaug"], A["maskT"], A["kmT"]
        ident = P["ident"]
        oT = P["oT"]
        IND = sb(st, "IND", [128, 64, 128], BF16)
        TT = sb(st, "TT", [128, 8, 2, 128], F32)
        rb31 = sb(st, "rb31", [128, 8], F32)
        PAST = sb(st, "PAST", [128, 8, 8, 8], F32)
        OWN = sb(st, "OWN", [128, 8, 8, 8], F32)
        PT = [[sb(st, "PT%d%d" % (h, i), [128, 512], BF16) for i in range(2)] for h in range(2)]
        rden = [sb(st, "rden%d" % h, [128, 512], F32) for h in range(2)]
        gsb = sb(st, "gsb", [128, 2, 8, 8], F32)
        g2 = sb(st, "g2", [128, 2, 8, 8], F32)
        eq = sb(st, "eq", [128, 2, 8, 8], F32)
        mx = sb(st, "mx", [128, 16], F32)
        mtok = sb(st, "mtok", [128, 128], BF16)

        S.emit("pool", lambda E: E.memset(IND[:], 0.0), writes=["IND"])
        S.emit("pool", lambda E: E.affine_select(out=IND[0:64], in_=IND[0:64], pattern=[[-1, 64], [0, 128]],
                                                  compare_op=ALU.not_equal, fill=1.0, base=0, channel_multiplier=1),
               reads=["IND"], writes=["IND"])
        S.emit("dve", lambda E: E.tensor_copy(out=IND[64:128], in_=IND[0:64]), reads=["IND"], writes=["IND"])
        S.emit("pool", lambda E: E.memset(PAST[:], 0.0), writes=["PAST"])
        S.emit("pool", lambda E: E.affine_select(out=PAST[:], in_=PAST[:], pattern=[[1, 8], [0, 8], [-1, 8]],
                                                  compare_op=ALU.is_gt, fill=-1e30, base=0, channel_multiplier=0),
               reads=["PAST"], writes=["PAST"])
        S.emit("pool", lambda E: E.memset(OWN[:], 0.0), writes=["OWN"])
        S.emit("pool", lambda E: E.affine_select(out=OWN[:], in_=OWN[:], pattern=[[1, 8], [0, 8], [-1, 8]],
                                                  compare_op=ALU.not_equal, fill=1.0, base=0, channel_multiplier=0),
               reads=["OWN"], writes=["OWN"])
        S.dma(TT[:].rearrange("p h a c -> p (h a c)"), d["ttab"], writes=["TT"])
        S.dma(rb31[:], d["rb31"].partition_broadcast(128), writes=["rb31"])
        S.emit("dve", lambda E: E.tensor_tensor(out=TT[:].rearrange("p h a c -> p h (a c)"),
                                                in0=TT[:].rearrange("p h a c -> p h (a c)"),
                                                in1=rb31[:].unsqueeze(2).to_broadcast([128, 8, 256]),
                                                op=ALU.subtract),
               reads=["TT", "rb31"], writes=["TT"])

        GB, TB = 6, 7
        import os
        FL = os.environ.get("C1FLAGS", "mask,main,toep,pv,norm,maskmm").split(",")
        if "mask" not in FL:
            S.emit("pool", lambda E: E.memset(maskT[:], 0.0), writes=[("maskT", qt) for qt in range(NT)])
        GBK = (6, 7)
        TB = 6

        def mask_pass(blk):
                for j in range(2):
                    qt = 2 * blk + j
                    for h in range(8):
                        pr, hh = h // 2, h % 2
                        S.emit("pe", lambda E, h=h, pr=pr, hh=hh, qt=qt, j=j: E.matmul(
                            PS[GBK[hh]][:, j * 32 + pr * 8:j * 32 + (pr + 1) * 8],
                            lhsT=qT[hh * 64:(hh + 1) * 64, pr, qt * 128:(qt + 1) * 128],
                            rhs=kmT[hh * 64:(hh + 1) * 64, pr, :], start=True, stop=True),
                               reads=[("qT", pr, qt // 4), "kmT"], writes=[("ps", GBK[hh])], signal=(j == 1 and h >= 6))
                for hh in range(2):
                    g3v = PS[GBK[hh]][:, 0:64].rearrange("p (j h n) -> p j h n", j=2, h=4)
                    S.emit("dve", lambda E, blk=blk, g3v=g3v, hh=hh: E.tensor_tensor(
                        out=gsb[:, :, hh:8:2, :], in0=g3v,
                        in1=PAST[:, blk, hh:8:2, :].unsqueeze(1).to_broadcast([128, 2, 4, 8]), op=ALU.add),
                           reads=[("ps", GBK[hh]), "PAST"], writes=["gsb"])
                cur = gsb
                mxb = mx[:].rearrange("p (j h) -> p j h", j=2).unsqueeze(3).to_broadcast([128, 2, 8, 8])
                for it in range(2):
                    S.emit("dve", lambda E, cur=cur: E.tensor_reduce(out=mx[:], in_=cur[:].rearrange("p j h n -> p (j h) n"),
                                                                     axis=AX.X, op=ALU.max),
                           reads=["gsb", "g2"], writes=["mx"])
                    S.emit("dve", lambda E, cur=cur: E.tensor_tensor(out=eq[:], in0=cur[:], in1=mxb, op=ALU.is_equal),
                           reads=["gsb", "g2", "mx"], writes=["eq"])
                    S.emit("dve", lambda E, cur=cur: E.scalar_tensor_tensor(out=g2[:], in0=eq[:], scalar=-1e30,
                                                                            in1=cur[:], op0=ALU.mult, op1=ALU.add),
                           reads=["eq", "gsb", "g2"], writes=["g2"])
                    cur = g2
                S.emit("dve", lambda E: E.tensor_reduce(out=mx[:], in_=g2[:].rearrange("p j h n -> p (j h) n"),
                                                        axis=AX.X, op=ALU.max),
                       reads=["g2"], writes=["mx"])
                S.emit("dve", lambda E: E.tensor_scalar(out=mx[:], in0=mx[:], scalar1=-1e29, scalar2=None, op0=ALU.max),
                       reads=["mx"], writes=["mx"])
                S.emit("dve", lambda E: E.tensor_tensor(out=eq[:], in0=gsb[:], in1=mxb, op=ALU.is_ge),
                       reads=["gsb", "mx"], writes=["eq"])
                S.emit("dve", lambda E, blk=blk: E.tensor_tensor(
                    out=eq[:], in0=eq[:], in1=OWN[:, blk, :, :].unsqueeze(1).to_broadcast([128, 2, 8, 8]), op=ALU.add),
                       reads=["eq", "OWN"], writes=["eq"])
                S.emit("dve", lambda E: E.tensor_scalar(out=mtok[:], in0=eq[:].rearrange("p j h n -> p (j h n)"),
                                                        scalar1=-1.0, scalar2=-NEG, op0=ALU.add, op1=ALU.mult),
                       reads=["eq"], writes=["mtok"])
                tb = PS[TB][:].bitcast(BF16)
                for j in range(2):
                    S.emit("pe", lambda E, tb=tb, j=j: E.transpose(out=tb[0:64, j * 128:(j + 1) * 128],
                                                                   in_=mtok[:, j * 64:(j + 1) * 64], identity=ident[:]),
                           reads=["mtok", "ident"], writes=[("ps", TB)], signal=(j == 1))
                S.emit("act", lambda E, tb=tb, blk=blk: E.activation(out=maskT[0:64, blk * 256:(blk + 1) * 256],
                                                                     in_=tb[0:64, 0:256], func=AF.Identity),
                       reads=[("ps", TB)], writes=[("maskT", 2 * blk), ("maskT", 2 * blk + 1)])
                S.emit("act", lambda E, tb=tb, blk=blk: E.activation(out=maskT[64:128, blk * 256:(blk + 1) * 256],
                                                                     in_=tb[0:64, 0:256], func=AF.Identity),
                       reads=[("ps", TB)], writes=[("maskT", 2 * blk), ("maskT", 2 * blk + 1)])

        vflat = vaug[:].rearrange("p t a b c -> p t a (b c)")

        def qk(pr, qc, kt):
            qs = max(0, kt * 128 - qc * 512)
            N = 512 - qs
            q0 = qc * 512 + qs
            for hh in range(2):
                h = 2 * pr + hh
                bank = hh * 2 + (kt % 2)
                rows = slice(hh * 64, (hh + 1) * 64)
                mm = "maskmm" in FL
                S.emit("pe", lambda E, bank=bank, rows=rows, N=N, q0=q0, mm=mm: E.matmul(
                    PS[bank][:, 0:N], lhsT=kT[rows, pr, kt * 128:(kt + 1) * 128], rhs=qT[rows, pr, q0:q0 + N],
                    start=True, stop=not mm),
                       reads=[("kT", pr, kt // 4), ("qT", pr, qc)], writes=[("ps", bank)], signal=not mm)
                if mm:
                    S.emit("pe", lambda E, bank=bank, h=h, N=N, q0=q0, rows=rows: E.matmul(
                        PS[bank][:, 0:N], lhsT=IND[rows, h * 8 + kt // 2, :], rhs=maskT[rows, q0:q0 + N],
                        start=False, stop=True),
                           reads=["IND"] + [("maskT", 4 * qc + j) for j in range(4)], writes=[("ps", bank)])
                for dq in range(2 if "toep" in FL else 0):
                    qt = kt + dq
                    if qt * 128 < q0 or qt >= (qc + 1) * 4:
                        continue
                    off = qt * 128 - q0
                    S.emit("dve", lambda E, bank=bank, off=off, h=h, dq=dq: E.tensor_tensor(
                        out=PS[bank][:, off:off + 128], in0=PS[bank][:, off:off + 128], in1=TT[:, h, dq, :],
                        op=ALU.add),
                           reads=[("ps", bank), "TT"], writes=[("ps", bank)])
                S.emit("act", lambda E, bank=bank, hh=hh, N=N: E.activation(
                    out=PT[hh][kt % 2][:, 0:N], in_=PS[bank][:, 0:N], func=AF.Exp),
                       reads=[("ps", bank)], writes=[("PT", hh, kt % 2)])

        def pv(pr, qc, kt, nkt):
            qs = max(0, kt * 128 - qc * 512)
            N = 512 - qs
            for hh in range(2):
                bank = 4 + hh
                S.emit("pe", lambda E, bank=bank, hh=hh, qs=qs, N=N: E.matmul(
                    PS[bank][:, qs:512], lhsT=vflat[:, kt, pr, hh * 64:hh * 64 + 128], rhs=PT[hh][kt % 2][:, 0:N],
                    start=(kt == 0), stop=(kt == nkt - 1)),
                       reads=[("PT", hh, kt % 2), ("vaug", kt), "vones"], writes=[("ps", bank)],
                       signal=(kt == nkt - 1))

        for qc in range(4 if "main" in FL else 0):
            if "mask" in FL:
                mask_pass(2 * qc)
                mask_pass(2 * qc + 1)
            for pr in range(4):
                nkt = 4 * (qc + 1)
                for kt in range(nkt):
                    qk(pr, qc, kt)
                    if kt > 0 and "pv" in FL:
                        pv(pr, qc, kt - 1, nkt)
                if "pv" in FL:
                    pv(pr, qc, nkt - 1, nkt)
                for hh in range(2 if "norm" in FL else 0):
                    bank = 4 + hh
                    orows = slice(hh * 64, (hh + 1) * 64)
                    drows = slice((1 - hh) * 64, (2 - hh) * 64)
                    S.emit("dve", lambda E, bank=bank, hh=hh, orows=orows, drows=drows: E.reciprocal(
                        out=rden[hh][orows, :], in_=PS[bank][drows, :]),
                           reads=[("ps", bank)], writes=[("rden", hh)])
                    S.emit("dve", lambda E, bank=bank, hh=hh, orows=orows, pr=pr, qc=qc: E.tensor_tensor(
                        out=oT[orows, pr, qc * 512:(qc + 1) * 512], in0=PS[bank][orows, :], in1=rden[hh][orows, :],
                        op=ALU.mult),
                           reads=[("ps", bank), ("rden", hh)], writes=[("oT", pr)])

        if "oT" in self.dbg_out:
            otf = sb(st, "otf", [128, 4, SEQ], F32)
            S.emit("dve", lambda E: E.tensor_copy(out=otf[:], in_=oT[:, 0:4, :]),
                   reads=[("oT", c) for c in range(4)], writes=["otf"])
            S.dma(self.dbg_out["oT"][b].rearrange("c p s -> p c s"), otf[:], reads=["otf"], writes=["dbg_oT"])

    def phase_B2(self, st, b, hT, G):
        nc, S, d, P, PS = self.nc, self.S, self.d, self.P, self.PS
        sb = self.sb
        premix = P["premix"]
        stgw = [sb(st, "stgw%d" % i, [128, 8, 128], F32) for i in range(2)]
        wc = [sb(st, "wc%d" % i, [128, 8, 128], BF16) for i in range(2)]
        wz = sb(st, "wz", [128, 8, 512], BF16)
        wab = sb(st, "wab", [128, 8, 16], BF16)
        pres = [sb(st, "pre%d" % i, [128, 3 + SEQ], F32) for i in range(2)]
        accs = [sb(st, "acc%d" % i, [128, SEQ], F32) for i in range(2)]
        sq = sb(st, "sqg", [128, SEQ], BF16)
        srs = [sb(st, "srg%d" % i, [128, 512], F32) for i in range(2)]
        cw = sb(st, "cw", [128, 12, 4], F32)
        BLK = sb(st, "BLK", [128, 128], BF16)
        dtb = sb(st, "dtb_b", [128, 8], F32)
        alog = sb(st, "alog_b", [128, 8], F32)
        S.dma(cw[:].rearrange("p c t -> p (c t)"), d["convw_pk"], writes=["cw"])
        S.dma(dtb[:], d["dtb"].partition_broadcast(128), writes=["dtb"])
        S.dma(alog[:], d["alog"].partition_broadcast(128), writes=["alog"])
        S.emit("pool", lambda E: E.memset(BLK[:], 0.0), writes=["BLK"])
        S.emit("pool", lambda E: E.memset(BLK[0:64, 0:64], 1.0), reads=["BLK"], writes=["BLK"])
        S.emit("pool", lambda E: E.memset(BLK[64:128, 64:128], 1.0), reads=["BLK"], writes=["BLK"])
        for i in range(2):
            S.emit("pool", lambda E, i=i: E.memset(pres[i][:, 0:3], 0.0), writes=[("pre0", i)])
        hT_all = [("hT", t) for t in range(NT)]
        self._nw = 0

        def load_w(col0, ncol, dst, dname):
            i = self._nw % 2
            self._nw += 1
            S.dma(stgw[i][:, :, 0:ncol], d["w_in"][:, col0:col0 + ncol].rearrange("(k p) c -> p k c", p=128),
                  writes=[("stgw", i)])
            S.emit("pool", lambda E, i=i, ncol=ncol, dst=dst: E.tensor_tensor(
                out=dst, in0=stgw[i][:, :, 0:ncol], in1=premix[:].unsqueeze(2).to_broadcast([128, 8, ncol]),
                op=ALU.mult),
                   reads=[("stgw", i), "premix"], writes=[dname])

        nb = 0
        dsts = (G["qT"], G["kT"], G["vT"])
        for c in range(12):
            wi = c % 2
            pre, acc = pres[wi], accs[wi]
            PRE, ACC, PRE0 = ("pre", wi), ("acc", wi), ("pre0", wi)
            load_w(1536 + c * 128, 128, wc[wi][:], ("wc", wi))
            for tc in range(4):
                bank = 4 + (nb % 4)
                nb += 1
                for k in range(8):
                    S.emit("pe", lambda E, k=k, tc=tc, bank=bank, wi=wi: E.matmul(
                        PS[bank][:, :], lhsT=wc[wi][:, k, :], rhs=hT[:, k, tc * 512:(tc + 1) * 512],
                        start=(k == 0), stop=(k == 7)),
                           reads=[("wc", wi)] + [("hT", 4 * tc + j) for j in range(4)], writes=[("ps", bank)],
                           signal=(k == 7))
                S.emit("act", lambda E, tc=tc, bank=bank, pre=pre: E.activation(
                    out=pre[:, 3 + tc * 512:3 + (tc + 1) * 512], in_=PS[bank][:, :], func=AF.Identity),
                       reads=[("ps", bank)], writes=[PRE])
            ce = "dve"
            S.emit(ce, lambda E, c=c, pre=pre, acc=acc: E.tensor_scalar(out=acc[:], in0=pre[:, 0:SEQ],
                                                                        scalar1=cw[:, c, 0:1], scalar2=None, op0=ALU.mult),
                   reads=[PRE, PRE0, "cw"], writes=[ACC])
            for tp in range(1, 4):
                S.emit(ce, lambda E, c=c, tp=tp, pre=pre, acc=acc: E.scalar_tensor_tensor(
                    out=acc[:], in0=pre[:, tp:tp + SEQ], scalar=cw[:, c, tp:tp + 1], in1=acc[:],
                    op0=ALU.mult, op1=ALU.add),
                       reads=[PRE, PRE0, "cw", ACC], writes=[ACC])
            dst = dsts[c // 4]
            dn = ("gq", "gk", "gv")[c // 4]
            S.emit("act", lambda E, dst=dst, c=c, acc=acc: E.activation(out=dst[:, c % 4, :], in_=acc[:], func=AF.Silu),
                   reads=[ACC], writes=[(dn, c % 4)])
        for c in range(8):
            dst = dsts[c // 4]
            dn = ("gq", "gk")[c // 4]
            pr = c % 4
            S.emit("act", lambda E, dst=dst, pr=pr: E.activation(out=sq[:], in_=dst[:, pr, :], func=AF.Square),
                   reads=[(dn, pr)], writes=["sqg"])
            for tc in range(4):
                bank = 4 + (nb % 4)
                nb += 1
                cs = slice(tc * 512, (tc + 1) * 512)
                S.emit("pe", lambda E, bank=bank, cs=cs: E.matmul(PS[bank][:, :], lhsT=BLK[:], rhs=sq[:, cs],
                                                                  start=True, stop=True),
                       reads=["sqg", "BLK"], writes=[("ps", bank)])
                sr = srs[tc % 2]
                SR = ("srg", tc % 2)
                S.emit("act", lambda E, bank=bank, sr=sr: E.activation(out=sr[:], in_=PS[bank][:, :], func=AF.Sqrt,
                                                                       bias=P["epsc"][:], scale=1.0),
                       reads=[("ps", bank), "epsc"], writes=[SR])
                S.emit("dve", lambda E, sr=sr: E.reciprocal(out=sr[:], in_=sr[:]), reads=[SR], writes=[SR])
                scl = 0.125 if c < 4 else 1.0
                S.emit("dve", lambda E, dst=dst, pr=pr, cs=cs, scl=scl, sr=sr: E.scalar_tensor_tensor(
                    out=dst[:, pr, cs], in0=dst[:, pr, cs], scalar=scl, in1=sr[:], op0=ALU.mult, op1=ALU.mult),
                       reads=[(dn, pr), SR], writes=[(dn, pr)])
        for j in range(4):
            load_w(3072 + j * 128, 128, wz[:, :, j * 128:(j + 1) * 128], "wz")
        load_w(3584, 16, wab[:], "wab")
        zs, gab = G["zs"], G["gab"]
        for t in range(NT):
            bank = 4 + (nb % 4)
            nb += 1
            tk = slice(t * 128, (t + 1) * 128)
            for k in range(8):
                S.emit("pe", lambda E, k=k, tk=tk, bank=bank: E.matmul(PS[bank][:, :], lhsT=hT[:, k, tk],
                                                                       rhs=wz[:, k, :], start=(k == 0), stop=(k == 7)),
                       reads=["wz", ("hT", t)], writes=[("ps", bank)], signal=(k == 7))
            S.emit("act", lambda E, t=t, bank=bank: E.activation(out=zs[:, t, :], in_=PS[bank][:, :], func=AF.Silu),
                   reads=[("ps", bank)], writes=[("gzs", t)])
            bank = 4 + (nb % 4)
            nb += 1
            for k in range(8):
                S.emit("pe", lambda E, k=k, tk=tk, bank=bank: E.matmul(PS[bank][:, 0:16], lhsT=hT[:, k, tk],
                                                                       rhs=wab[:, k, :], start=(k == 0), stop=(k == 7)),
                       reads=["wab", ("hT", t)], writes=[("ps", bank)], signal=(k == 7))
            S.emit("dve", lambda E, t=t, bank=bank: E.tensor_copy(out=gab[:, t, :], in_=PS[bank][:, 0:16]),
                   reads=[("ps", bank)], writes=["gab"])
        g, beta, nbeta = G["g"], G["beta"], G["nbeta"]
        S.emit("dve", lambda E: E.tensor_tensor(out=g[:], in0=gab[:, :, 0:8],
                                                in1=dtb[:].unsqueeze(1).to_broadcast([128, NT, 8]), op=ALU.add),
               reads=["gab", "dtb"], writes=["gg"])
        S.emit("act", lambda E: E.activation(out=g[:], in_=g[:], func=AF.Exp), reads=["gg"], writes=["gg"])
        S.emit("act", lambda E: E.activation(out=g[:], in_=g[:], func=AF.Ln, bias=1.0), reads=["gg"], writes=["gg"])
        S.emit("act", lambda E: E.activation(out=alog[:], in_=alog[:], func=AF.Exp), reads=["alog"], writes=["alog"])
        S.emit("dve", lambda E: E.scalar_tensor_tensor(out=g[:], in0=g[:], scalar=-1.0,
                                                       in1=alog[:].unsqueeze(1).to_broadcast([128, NT, 8]),
                                                       op0=ALU.mult, op1=ALU.mult),
               reads=["gg", "alog"], writes=["gg"])
        S.emit("act", lambda E: E.activation(out=beta[:], in_=gab[:, :, 8:16], func=AF.Sigmoid),
               reads=["gab"], writes=["gbeta"])
        S.emit("dve", lambda E: E.tensor_scalar(out=nbeta[:], in0=beta[:], scalar1=-1.0, scalar2=None, op0=ALU.mult),
               reads=["gbeta"], writes=["gnbeta"])

    def phase_C2(self, st, b, G):
        nc, S, d, P, PS = self.nc, self.S, self.d, self.P, self.PS
        sb = self.sb
        ident = P["ident"]
        oT = P["oT"]
        qTg, kTg, vTg, zs = G["qT"], G["kT"], G["vT"], G["zs"]
        g, beta, nbeta = G["g"], G["beta"], G["nbeta"]
        BIG = 3.0e38
        TRI = sb(st, "TRI", [128, 128], F32)
        BLKS = sb(st, "BLKS", [128, 128], F32)
        MASKU = sb(st, "MASKU", [128, 8, 128], F32)
        STRICT = sb(st, "STRICT", [128, 8, 128], F32)
        HEADM = sb(st, "HEADM", [8, 8, 1], F32)
        SEL = sb(st, "SEL", [8, 4, 128], F32)
        ONES8 = sb(st, "ONES8", [8, 128], F32)
        gnw = sb(st, "gnw_b", [128, 64], F32)
        S.dma(gnw[:], d["gnw"].partition_broadcast(128), writes=["gnw"])

        def tri_like(T, val, strict, name):
            nd = len(T.shape)
            pat = [[0, 8], [1, 128]] if nd == 3 else [[1, 128]]
            pat2 = [[0, 8], [-1, 128]] if nd == 3 else [[-1, 128]]
            S.emit("pool", lambda E: E.memset(T[:], val), writes=[name])
            S.emit("pool", lambda E: E.affine_select(out=T[:], in_=T[:], pattern=pat,
                                                      compare_op=(ALU.is_gt if strict else ALU.is_ge), fill=0.0,
                                                      base=0, channel_multiplier=-1),
                   reads=[name], writes=[name])
            S.emit("pool", lambda E: E.affine_select(out=T[0:64], in_=T[0:64], pattern=pat2,
                                                      compare_op=ALU.is_ge, fill=0.0, base=63, channel_multiplier=0),
                   reads=[name], writes=[name])

        tri_like(TRI, 1.0, False, "TRI")
        tri_like(MASKU, BIG, False, "MASKU")
        tri_like(STRICT, 1.0, True, "STRICT")
        S.emit("pool", lambda E: E.memset(BLKS[:], 0.0), writes=["BLKS"])
        S.emit("pool", lambda E: E.memset(BLKS[0:64, 0:64], 1.0), reads=["BLKS"], writes=["BLKS"])
        S.emit("pool", lambda E: E.memset(BLKS[64:128, 64:128], 1.0), reads=["BLKS"], writes=["BLKS"])
        S.emit("pool", lambda E: E.memset(HEADM[:], 0.0), writes=["HEADM"])
        S.emit("pool", lambda E: E.affine_select(out=HEADM[:], in_=HEADM[:], pattern=[[-1, 8], [0, 1]],
                                                  compare_op=ALU.not_equal, fill=1.0, base=0, channel_multiplier=1),
               reads=["HEADM"], writes=["HEADM"])
        S.emit("pool", lambda E: E.memset(SEL[:], 0.0), writes=["SEL"])
        for half in range(2):
            S.emit("pool", lambda E, half=half: E.affine_select(
                out=SEL[:, :, half * 64:(half + 1) * 64], in_=SEL[:, :, half * 64:(half + 1) * 64],
                pattern=[[-2, 4], [0, 64]], compare_op=ALU.not_equal, fill=1.0, base=-half, channel_multiplier=1),
                   reads=["SEL"], writes=["SEL"])
        S.emit("pool", lambda E: E.memset(ONES8[:], 1.0), writes=["ONES8"])

        import os
        NPS = int(os.environ.get("C2NPS", "1"))
        NSLOT = NPS + 1
        rhsBDs = [sb(st, "rhsBD%d" % i, [8, 8, 128], F32) for i in range(NPS)]
        gcTs = [sb(st, "gcT%d" % i, [8, 128], F32) for i in range(NPS)]
        gcts = [sb(st, "gct%d" % i, [128, 24], F32) for i in range(NPS)]
        egts = [sb(st, "egt%d" % i, [128, 16], F32) for i in range(NPS)]
        EAs = [sb(st, "EA%d" % i, [128, 8, 128], F32) for i in range(NPS)]
        EAsbs = [sb(st, "EAsb%d" % i, [128, 8, 128], F32) for i in range(NPS)]
        ktoks = [sb(st, "ktok%d" % i, [128, 8, 64], BF16) for i in range(NPS)]
        Bms = [[sb(st, "Bm%d%d" % (p, i), [128, 8, 128], BF16) for i in range(2)] for p in range(NPS)]
        Nms = [[sb(st, "Nm%d%d" % (p, i), [128, 8, 128], BF16) for i in range(2)] for p in range(NPS)]
        qzs = [sb(st, "qz%d" % i, [128, 2, 4, 128], BF16) for i in range(NPS)]
        kzs = [sb(st, "kz%d" % i, [128, 2, 4, 128], BF16) for i in range(NPS)]
        X0s = [sb(st, "X0_%d" % i, [128, 2, 8, 64], BF16) for i in range(NPS)]
        Bp0s = [sb(st, "Bp0_%d" % i, [128, 8, 128], BF16) for i in range(NPS)]
        EGs = [sb(st, "EG%d" % i, [128, 4, 128], F32) for i in range(NSLOT)]
        qdTs = [sb(st, "qdT%d" % i, [128, 4, 128], BF16) for i in range(NSLOT)]
        kdecs = [[sb(st, "kdec%d%d" % (p, i), [128, 8, 64], BF16) for i in range(2)] for p in range(NSLOT)]
        X1s = [sb(st, "X1_%d" % i, [128, 2, 8, 64], BF16) for i in range(NSLOT)]
        attnTs = [sb(st, "attnT%d" % i, [128, 8, 128], BF16) for i in range(NSLOT)]
        Bp1s = [sb(st, "Bp1_%d" % i, [128, 8, 128], BF16) for i in range(NSLOT)]
        nwTs = [sb(st, "nwT%d" % i, [128, 4, 128], BF16) for i in range(NSLOT)]
        identb = ident[:].unsqueeze(1)
        S32 = sb(st, "S32", [128, 4, 64], F32)
        tmpS = sb(st, "tmpS", [128, 4, 64], F32)
        Sb = sb(st, "Sb", [128, 4, 2, 64], BF16)
        vnew = sb(st, "vnew", [128, 8, 64], BF16)
        osb = sb(st, "osb", [128, 8, 64], F32)
        osq = sb(st, "osq", [128, 8, 64], F32)
        oss = sb(st, "oss", [128, 16], F32)
        og = sb(st, "og", [128, 512], BF16)
        for i in range(NPS):
            S.emit("pool", lambda E, i=i: E.memset(qzs[i][:], 0.0), writes=[("qz", i)])
            S.emit("pool", lambda E, i=i: E.memset(kzs[i][:], 0.0), writes=[("kz", i)])
        S.emit("dve", lambda E: E.memset(S32[:], 0.0), writes=["S32"])
        S.emit("dve", lambda E: E.memset(Sb[:], 0.0), writes=["Sb"])
        S.emit("dve", lambda E: E.memset(vnew[:], 0.0), writes=["vnew"])
        for p in range(NSLOT):
            for i in range(2):
                S.emit("pool", lambda E, p=p, i=i: E.memset(kdecs[p][i][:], 0.0), writes=[("kdec", p, i)])
        self._bp = 0
        self._bs = 0
        PBANKS = (4, 5, 1, 2)
        SBANKS = (6, 7)

        def pbank():
            self._bp += 1
            return PBANKS[self._bp % 4]

        def sbank():
            self._bs += 1
            return SBANKS[self._bs % 2]

        def gen_P(t):
            par = t % NSLOT
            ps = t % NPS
            tk = slice(t * 128, (t + 1) * 128)
            gt = g[:, t, :]
            EG, qdT, kdec, attnT, nwT = EGs[par], qdTs[par], kdecs[par], attnTs[par], nwTs[par]
            X = (X0s[ps], X1s[par])
            Bp = (Bp0s[ps], Bp1s[par])
            XN = (("X0", ps), ("X", par, 1))
            BPN = (("Bp0", ps), ("Bp", par, 1))
            rhsBD, gcT, gct, egt, EA, EAsb, ktok = rhsBDs[ps], gcTs[ps], gcts[ps], egts[ps], EAs[ps], EAsbs[ps], ktoks[ps]
            Bm, Nm, qz, kz = Bms[ps], Nms[ps], qzs[ps], kzs[ps]
            RB, GC, GT_, EGT, EAn, EASn, KT = ("rhsBD", ps), ("gcT", ps), ("gct", ps), ("egt", ps), ("EA", ps), ("EAsb", ps), ("ktok", ps)
            QZ, KZ = ("qz", ps), ("kz", ps)
            MYB = (0, 1, 2) if ps == 0 else (3, 4, 5)
            ROT = MYB if NPS == 2 else (3, 4, 5, 1, 2)
            B0, B1_, B2_ = MYB
            rot = [0]

            def pbank():
                rot[0] += 1
                return ROT[rot[0] % len(ROT)]
            if NPS == 1 and os.environ.get("C2REORD", "1") == "1":
                bk, bv, RBK, kq, kk = 2, 3, 1, (4, 5), (2, 3)
                S.emit("pe", lambda E: E.matmul(PS[B0][0:8, 0:128], lhsT=gt, rhs=TRI[:], start=True, stop=True),
                       reads=["gg", "TRI"], writes=[("ps", B0)], signal=False)
                S.emit("pe", lambda E: E.matmul(PS[B0][:, 128:136], lhsT=TRI[:], rhs=gt, start=True, stop=True),
                       reads=["gg", "TRI"], writes=[("ps", B0)], signal=False)
                S.emit("pe", lambda E: E.matmul(PS[B0][:, 136:144], lhsT=BLKS[:], rhs=gt, start=True, stop=True),
                       reads=["gg", "BLKS"], writes=[("ps", B0)])
                tbk = PS[bk][:].bitcast(BF16)
                for pr in range(4):
                    S.emit("pe", lambda E, pr=pr, tbk=tbk: E.transpose(out=tbk[:, pr * 128:(pr + 1) * 128],
                                                                       in_=kTg[:, pr, tk], identity=ident[:]),
                           reads=[("gk", pr), "ident"], writes=[("ps", bk)], signal=(pr == 3))
                tbv = PS[bv][:].bitcast(BF16)
                for pr in range(4):
                    S.emit("pe", lambda E, pr=pr, tbv=tbv: E.transpose(out=tbv[:, pr * 128:(pr + 1) * 128],
                                                                       in_=vTg[:, pr, tk], identity=ident[:]),
                           reads=[("gv", pr), "ident"], writes=[("ps", bv)], signal=(pr == 3))
                yield
                S.emit("act", lambda E: E.activation(out=gcT[:], in_=PS[B0][0:8, 0:128], func=AF.Identity),
                       reads=[("ps", B0)], writes=[GC])
                S.emit("dve", lambda E: E.tensor_copy(out=gct[:, 0:16], in_=PS[B0][:, 128:144]),
                       reads=[("ps", B0)], writes=[GT_])
                S.emit("dve", lambda E: E.tensor_tensor(out=gct[:, 16:24], in0=gct[:, 8:16], in1=gct[:, 0:8],
                                                        op=ALU.subtract), reads=[GT_], writes=[GT_])
                for hh in range(2):
                    rows = slice(hh * 64, (hh + 1) * 64)
                    S.emit("pool", lambda E, hh=hh, rows=rows: E.tensor_copy(out=kz[rows, hh, :, :], in_=kTg[rows, :, tk]),
                           reads=[("gk", pr) for pr in range(4)], writes=[KZ])
                S.emit("dve", lambda E: E.tensor_tensor(out=rhsBD[:], in0=HEADM[:].to_broadcast([8, 8, 128]),
                                                        in1=gcT[:].unsqueeze(1).to_broadcast([8, 8, 128]), op=ALU.mult),
                       reads=[GC, "HEADM"], writes=[RB])
                yield
                S.emit("act", lambda E, tbk=tbk: E.activation(out=ktok[:].rearrange("p h d -> p (h d)"),
                                                              in_=tbk[:, 0:512], func=AF.Identity),
                       reads=[("ps", bk)], writes=[KT])
                S.emit("act", lambda E: E.activation(out=egt[:, 0:8], in_=gct[:, 0:8], func=AF.Exp),
                       reads=[GT_], writes=[EGT])
                S.emit("act", lambda E: E.activation(out=egt[:, 8:16], in_=gct[:, 16:24], func=AF.Exp),
                       reads=[GT_, EGT], writes=[EGT])
                X0 = X[0]
                S.emit("dve", lambda E, tbv=tbv: E.tensor_copy(out=X0[:, 0, :, :],
                                                               in_=tbv[:, 0:512].rearrange("p (h d) -> p h d", h=8)),
                       reads=[("ps", bv)], writes=[XN[0] + (0,), XN[0] + (1,)])
                yield
                for hf in range(2):
                    S.emit("pe", lambda E, hf=hf: E.matmul(
                        PS[RBK][:, :], lhsT=ONES8[:], rhs=rhsBD[:, 4 * hf:4 * hf + 4, :].rearrange("p h i -> p (h i)"),
                        start=True, stop=True),
                           reads=[RB, "ONES8"], writes=[("ps", RBK)])
                    S.emit("dve", lambda E, hf=hf: E.tensor_tensor(
                        out=EA[:, 4 * hf:4 * hf + 4, :], in0=PS[RBK][:, :].rearrange("p (h i) -> p h i", h=4),
                        in1=gct[:, 4 * hf:4 * hf + 4].unsqueeze(2).to_broadcast([128, 4, 128]), op=ALU.subtract),
                           reads=[("ps", RBK), GT_], writes=[EAn])
                    for h in range(4 * hf, 4 * hf + 4):
                        pr, hh = h // 2, h % 2
                        S.emit("pe", lambda E, pr=pr, hh=hh: E.matmul(
                            PS[kq[hh]][:, pr * 128:(pr + 1) * 128], lhsT=kz[:, hh, pr, :], rhs=qTg[:, pr, tk],
                            start=True, stop=True),
                               reads=[("gq", pr), KZ], writes=[("ps", kq[hh])], signal=(h >= 6))
                    yield
                S.emit("act", lambda E: E.activation(out=EA[:], in_=EA[:], func=AF.Exp), reads=[EAn], writes=[EAn])
                for pr in range(4):
                    S.emit("pe", lambda E, pr=pr: E.matmul(PS[B0][:, pr * 128:(pr + 1) * 128], lhsT=SEL[:, pr, :],
                                                           rhs=gcT[:], start=True, stop=True),
                           reads=[GC, "SEL"], writes=[("ps", B0)], signal=(pr == 3))
                for h in range(8):
                    pr, hh = h // 2, h % 2
                    S.emit("pe", lambda E, pr=pr, hh=hh: E.matmul(
                        PS[kk[hh]][:, pr * 128:(pr + 1) * 128], lhsT=kz[:, hh, pr, :], rhs=kTg[:, pr, tk],
                        start=True, stop=True),
                           reads=[("gk", pr), KZ], writes=[("ps", kk[hh])], signal=(h >= 6))
                yield
                S.emit("dve", lambda E: E.tensor_tensor(out=EA[:], in0=EA[:], in1=MASKU[:], op=ALU.min),
                       reads=[EAn, "MASKU"], writes=[EAn])
                S.emit("act", lambda E: E.activation(out=EG[:].rearrange("p a i -> p (a i)"), in_=PS[B0][:, :], func=AF.Exp),
                       reads=[("ps", B0)], writes=[("EG", par)])
                S.emit("pool", lambda E: E.tensor_tensor(out=EAsb[:], in0=EA[:], in1=STRICT[:], op=ALU.mult),
                       reads=[EAn, "STRICT"], writes=[EASn])
                S.emit("pool", lambda E: E.tensor_tensor(out=EAsb[:], in0=EAsb[:],
                                                         in1=nbeta[:, t, :].unsqueeze(2).to_broadcast([128, 8, 128]),
                                                         op=ALU.mult),
                       reads=[EASn, "gnbeta"], writes=[EASn])
                yield
                for hh in range(2):
                    S.emit("dve", lambda E, hh=hh: E.tensor_tensor(
                        out=attnT[:, hh:8:2, :], in0=PS[kq[hh]][:, :].rearrange("p (a i) -> p a i", a=4),
                        in1=EA[:, hh:8:2, :], op=ALU.mult),
                           reads=[("ps", kq[hh]), EAn], writes=[("attnT", par)])
                S.emit("dve", lambda E: E.tensor_tensor(out=X0[:, 1, :, :], in0=ktok[:],
                                                        in1=egt[:, 0:8].unsqueeze(2).to_broadcast([128, 8, 64]),
                                                        op=ALU.mult),
                       reads=[KT, EGT, XN[0] + (0,), XN[0] + (1,)], writes=[XN[0] + (0,), XN[0] + (1,)])
                for hh in range(2):
                    S.emit("dve", lambda E, hh=hh: E.tensor_tensor(
                        out=Bm[0][:, hh:8:2, :], in0=PS[kk[hh]][:, :].rearrange("p (a i) -> p a i", a=4),
                        in1=EAsb[:, hh:8:2, :], op=ALU.mult),
                           reads=[("ps", kk[hh]), EASn], writes=[("Bm", ps, 0, 0), ("Bm", ps, 0, 1)])
                S.emit("pool", lambda E: E.tensor_tensor(out=qdT[:], in0=qTg[:, :, tk], in1=EG[:], op=ALU.mult),
                       reads=[("EG", par)] + [("gq", pr) for pr in range(4)], writes=[("qdT", par)])
                for hf in range(2):
                    rows = slice(hf * 64, (hf + 1) * 64)
                    S.emit("pool", lambda E, hf=hf, rows=rows: E.tensor_tensor(
                        out=kdec[hf][rows], in0=ktok[rows],
                        in1=egt[rows, 8:16].unsqueeze(2).to_broadcast([64, 8, 64]), op=ALU.mult),
                           reads=[KT, EGT], writes=[("kdec", par, hf)])
                S.emit("pool", lambda E: E.tensor_tensor(out=Bp[0][:], in0=Bm[0][:],
                                                         in1=identb.to_broadcast([128, 8, 128]), op=ALU.add),
                       reads=[("Bm", ps, 0, 0), ("Bm", ps, 0, 1), "ident"], writes=[BPN[0] + (0,), BPN[0] + (1,)])
                yield
            else:
                S.emit("pe", lambda E: E.matmul(PS[B0][0:8, 0:128], lhsT=gt, rhs=TRI[:], start=True, stop=True),
                       reads=["gg", "TRI"], writes=[("ps", B0)], signal=False)
                S.emit("pe", lambda E: E.matmul(PS[B0][:, 128:136], lhsT=TRI[:], rhs=gt, start=True, stop=True),
                       reads=["gg", "TRI"], writes=[("ps", B0)], signal=False)
                S.emit("pe", lambda E: E.matmul(PS[B0][:, 136:144], lhsT=BLKS[:], rhs=gt, start=True, stop=True),
                       reads=["gg", "BLKS"], writes=[("ps", B0)])
                yield
                S.emit("act", lambda E: E.activation(out=gcT[:], in_=PS[B0][0:8, 0:128], func=AF.Identity),
                       reads=[("ps", B0)], writes=[GC])
                S.emit("dve", lambda E: E.tensor_copy(out=gct[:, 0:16], in_=PS[B0][:, 128:144]),
                       reads=[("ps", B0)], writes=[GT_])
                S.emit("dve", lambda E: E.tensor_tensor(out=gct[:, 16:24], in0=gct[:, 8:16], in1=gct[:, 0:8],
                                                        op=ALU.subtract), reads=[GT_], writes=[GT_])
                S.emit("act", lambda E: E.activation(out=egt[:, 0:8], in_=gct[:, 0:8], func=AF.Exp),
                       reads=[GT_], writes=[EGT])
                S.emit("act", lambda E: E.activation(out=egt[:, 8:16], in_=gct[:, 16:24], func=AF.Exp),
                       reads=[GT_, EGT], writes=[EGT])
                yield
                S.emit("dve", lambda E: E.tensor_tensor(out=rhsBD[:], in0=HEADM[:].to_broadcast([8, 8, 128]),
                                                        in1=gcT[:].unsqueeze(1).to_broadcast([8, 8, 128]), op=ALU.mult),
                       reads=[GC, "HEADM"], writes=[RB])
                for hf in range(2):
                    S.emit("pe", lambda E, hf=hf: E.matmul(
                        PS[MYB[1 + hf]][:, :], lhsT=ONES8[:], rhs=rhsBD[:, 4 * hf:4 * hf + 4, :].rearrange("p h i -> p (h i)"),
                        start=True, stop=True),
                           reads=[RB, "ONES8"], writes=[("ps", MYB[1 + hf])])
                yield
                for hf in range(2):
                    S.emit("dve", lambda E, hf=hf: E.tensor_tensor(
                        out=EA[:, 4 * hf:4 * hf + 4, :], in0=PS[MYB[1 + hf]][:, :].rearrange("p (h i) -> p h i", h=4),
                        in1=gct[:, 4 * hf:4 * hf + 4].unsqueeze(2).to_broadcast([128, 4, 128]), op=ALU.subtract),
                           reads=[("ps", MYB[1 + hf]), GT_], writes=[EAn])
                S.emit("act", lambda E: E.activation(out=EA[:], in_=EA[:], func=AF.Exp), reads=[EAn], writes=[EAn])
                yield
                S.emit("dve", lambda E: E.tensor_tensor(out=EA[:], in0=EA[:], in1=MASKU[:], op=ALU.min),
                       reads=[EAn, "MASKU"], writes=[EAn])
                S.emit("pool", lambda E: E.tensor_tensor(out=EAsb[:], in0=EA[:], in1=STRICT[:], op=ALU.mult),
                       reads=[EAn, "STRICT"], writes=[EASn])
                S.emit("pool", lambda E: E.tensor_tensor(out=EAsb[:], in0=EAsb[:],
                                                         in1=nbeta[:, t, :].unsqueeze(2).to_broadcast([128, 8, 128]),
                                                         op=ALU.mult),
                       reads=[EASn, "gnbeta"], writes=[EASn])
                yield
                for pr in range(4):
                    S.emit("pe", lambda E, pr=pr: E.matmul(PS[B0][:, pr * 128:(pr + 1) * 128], lhsT=SEL[:, pr, :],
                                                           rhs=gcT[:], start=True, stop=True),
                           reads=[GC, "SEL"], writes=[("ps", B0)], signal=(pr == 3))
                S.emit("act", lambda E: E.activation(out=EG[:].rearrange("p a i -> p (a i)"), in_=PS[B0][:, :], func=AF.Exp),
                       reads=[("ps", B0)], writes=[("EG", par)])
                S.emit("pool", lambda E: E.tensor_tensor(out=qdT[:], in0=qTg[:, :, tk], in1=EG[:], op=ALU.mult),
                       reads=[("EG", par)] + [("gq", pr) for pr in range(4)], writes=[("qdT", par)])
                yield
                bk = pbank()
                tbk = PS[bk][:].bitcast(BF16)
                for pr in range(4):
                    S.emit("pe", lambda E, pr=pr, tbk=tbk: E.transpose(out=tbk[:, pr * 128:(pr + 1) * 128],
                                                                       in_=kTg[:, pr, tk], identity=ident[:]),
                           reads=[("gk", pr), "ident"], writes=[("ps", bk)], signal=(pr == 3))
                S.emit("act", lambda E, tbk=tbk: E.activation(out=ktok[:].rearrange("p h d -> p (h d)"),
                                                              in_=tbk[:, 0:512], func=AF.Identity),
                       reads=[("ps", bk)], writes=[KT])
                bv = pbank()
                tbv = PS[bv][:].bitcast(BF16)
                for pr in range(4):
                    S.emit("pe", lambda E, pr=pr, tbv=tbv: E.transpose(out=tbv[:, pr * 128:(pr + 1) * 128],
                                                                       in_=vTg[:, pr, tk], identity=ident[:]),
                           reads=[("gv", pr), "ident"], writes=[("ps", bv)], signal=(pr == 3))
                yield
                X0 = X[0]
                S.emit("dve", lambda E, tbv=tbv: E.tensor_copy(out=X0[:, 0, :, :],
                                                               in_=tbv[:, 0:512].rearrange("p (h d) -> p h d", h=8)),
                       reads=[("ps", bv)], writes=[XN[0] + (0,), XN[0] + (1,)])
                S.emit("dve", lambda E: E.tensor_tensor(out=X0[:, 1, :, :], in0=ktok[:],
                                                        in1=egt[:, 0:8].unsqueeze(2).to_broadcast([128, 8, 64]),
                                                        op=ALU.mult),
                       reads=[KT, EGT, XN[0] + (0,), XN[0] + (1,)], writes=[XN[0] + (0,), XN[0] + (1,)])
                for hf in range(2):
                    rows = slice(hf * 64, (hf + 1) * 64)
                    S.emit("pool", lambda E, hf=hf, rows=rows: E.tensor_tensor(
                        out=kdec[hf][rows], in0=ktok[rows],
                        in1=egt[rows, 8:16].unsqueeze(2).to_broadcast([64, 8, 64]), op=ALU.mult),
                           reads=[KT, EGT], writes=[("kdec", par, hf)])
                yield
                for hh in range(2):
                    rows = slice(hh * 64, (hh + 1) * 64)
                    S.emit("pool", lambda E, hh=hh, rows=rows: E.tensor_copy(out=qz[rows, hh, :, :], in_=qTg[rows, :, tk]),
                           reads=[("gq", pr) for pr in range(4)], writes=[QZ])
                    S.emit("pool", lambda E, hh=hh, rows=rows: E.tensor_copy(out=kz[rows, hh, :, :], in_=kTg[rows, :, tk]),
                           reads=[("gk", pr) for pr in range(4)], writes=[KZ])
                kq = (pbank(), pbank())
                for h in range(8):
                    pr, hh = h // 2, h % 2
                    S.emit("pe", lambda E, pr=pr, hh=hh: E.matmul(
                        PS[kq[hh]][:, pr * 128:(pr + 1) * 128], lhsT=kTg[:, pr, tk], rhs=qz[:, hh, pr, :],
                        start=True, stop=True),
                           reads=[("gk", pr), QZ], writes=[("ps", kq[hh])], signal=(h >= 6))
                for hh in range(2):
                    S.emit("dve", lambda E, hh=hh: E.tensor_tensor(
                        out=attnT[:, hh:8:2, :], in0=PS[kq[hh]][:, :].rearrange("p (a i) -> p a i", a=4),
                        in1=EA[:, hh:8:2, :], op=ALU.mult),
                           reads=[("ps", kq[hh]), EAn], writes=[("attnT", par)])
                yield
                kk = (pbank(), pbank())
                for h in range(8):
                    pr, hh = h // 2, h % 2
                    S.emit("pe", lambda E, pr=pr, hh=hh: E.matmul(
                        PS[kk[hh]][:, pr * 128:(pr + 1) * 128], lhsT=kTg[:, pr, tk], rhs=kz[:, hh, pr, :],
                        start=True, stop=True),
                           reads=[("gk", pr), KZ], writes=[("ps", kk[hh])], signal=(h >= 6))
                for hh in range(2):
                    S.emit("dve", lambda E, hh=hh: E.tensor_tensor(
                        out=Bm[0][:, hh:8:2, :], in0=PS[kk[hh]][:, :].rearrange("p (a i) -> p a i", a=4),
                        in1=EAsb[:, hh:8:2, :], op=ALU.mult),
                           reads=[("ps", kk[hh]), EASn], writes=[("Bm", ps, 0, 0), ("Bm", ps, 0, 1)])
                S.emit("pool", lambda E: E.tensor_tensor(out=Bp[0][:], in0=Bm[0][:], in1=identb.to_broadcast([128, 8, 128]),
                                                         op=ALU.add),
                       reads=[("Bm", ps, 0, 0), ("Bm", ps, 0, 1), "ident"], writes=[BPN[0] + (0,), BPN[0] + (1,)])
                yield
            for a in range(2):
                bn = pbank()
                for h4 in range(4):
                    h = 4 * a + h4
                    S.emit("pe", lambda E, h=h, h4=h4, bn=bn: E.matmul(
                        PS[bn][:, h4 * 128:(h4 + 1) * 128], lhsT=Bm[0][:, h, :], rhs=ident[:], start=True, stop=True),
                           reads=[("Bm", ps, 0, a), "ident"], writes=[("ps", bn)], signal=(h4 == 3))
                self.evac(Nm[0][:, 4 * a:4 * a + 4, :].rearrange("p h i -> p (h i)"), PS[bn][:, :],
                          reads=[("ps", bn)], writes=[("Nm", ps, 0, a)])
                yield
            for lv in range(5):
                ci, ni = lv % 2, (lv + 1) % 2
                for a in range(2):
                    if lv < 4:
                        bnn = pbank()
                        for h4 in range(4):
                            h = 4 * a + h4
                            S.emit("pe", lambda E, h=h, h4=h4, bnn=bnn, ci=ci: E.matmul(
                                PS[bnn][:, h4 * 128:(h4 + 1) * 128], lhsT=Bm[ci][:, h, :], rhs=Nm[ci][:, h, :],
                                start=True, stop=True),
                                   reads=[("Nm", ps, ci, a), ("Bm", ps, ci, a)], writes=[("ps", bnn)], signal=(h4 == 3))
                        self.evac(Nm[ni][:, 4 * a:4 * a + 4, :].rearrange("p h i -> p (h i)"), PS[bnn][:, :],
                                  reads=[("ps", bnn)], writes=[("Nm", ps, ni, a)])
                        yield
                    bbb = pbank()
                    for h4 in range(4):
                        h = 4 * a + h4
                        S.emit("pe", lambda E, h=h, h4=h4, bbb=bbb, ci=ci: E.matmul(
                            PS[bbb][:, h4 * 128:(h4 + 1) * 128], lhsT=Nm[ci][:, h, :], rhs=Bm[ci][:, h, :],
                            start=True, stop=True),
                               reads=[("Nm", ps, ci, a), ("Bm", ps, ci, a)], writes=[("ps", bbb)], signal=(h4 == 3))
                    self.evac(Bm[ni][:, 4 * a:4 * a + 4, :].rearrange("p h i -> p (h i)"), PS[bbb][:, :],
                              reads=[("ps", bbb)], writes=[("Bm", ps, ni, a)])
                    S.emit("pool", lambda E, a=a, ni=ni: E.tensor_tensor(
                        out=Bp[ni][:, 4 * a:4 * a + 4, :], in0=Bm[ni][:, 4 * a:4 * a + 4, :],
                        in1=identb.to_broadcast([128, 4, 128]), op=ALU.add),
                           reads=[("Bm", ps, ni, a), "ident"], writes=[BPN[ni] + (a,)])
                    yield
                for a in range(2):
                    bx = pbank()
                    for h4 in range(4):
                        h = 4 * a + h4
                        S.emit("pe", lambda E, h=h, h4=h4, bx=bx, ci=ci: E.matmul(
                            PS[bx][:, h4 * 128:(h4 + 1) * 128], lhsT=Bp[ci][:, h, :], rhs=X[ci][:, :, h, :],
                            start=True, stop=True),
                               reads=[XN[ci] + (a,), BPN[ci] + (a,)], writes=[("ps", bx)], signal=(h4 == 3))
                    self.evac(X[ni][:, :, 4 * a:4 * a + 4, :].rearrange("p s h d -> p h s d"),
                              PS[bx][:, :].rearrange("p (h s d) -> p h s d", h=4, s=2),
                              reads=[("ps", bx)], writes=[XN[ni] + (a,)])
                    yield
            X5, B5 = X[1], Bp[1]

        def gen_S(t):
            par = t % NSLOT
            tk = slice(t * 128, (t + 1) * 128)
            EG, qdT, kdec, attnT, nwT = EGs[par], qdTs[par], kdecs[par], attnTs[par], nwTs[par]
            X5, B5 = X1s[par], Bp1s[par]
            for a in range(2):
                bw = sbank()
                for h4 in range(4):
                    h = 4 * a + h4
                    pr = h // 2
                    lw = X5[:, 1, 2 * pr:2 * pr + 2, :].rearrange("p h d -> p (h d)")
                    S.emit("pe", lambda E, h=h, h4=h4, bw=bw, lw=lw: E.matmul(
                        PS[bw][:, h4 * 128:(h4 + 1) * 128], lhsT=lw, rhs=B5[:, h, :], start=True, stop=True),
                           reads=[("X", par, 1, a), ("Bp", par, 1, a)], writes=[("ps", bw)], signal=(h4 == 3))
                for h4 in range(4):
                    h = 4 * a + h4
                    pr, hh = h // 2, h % 2
                    rows = slice(hh * 64, (hh + 1) * 64)
                    S.emit("dve", lambda E, h4=h4, bw=bw, pr=pr, rows=rows: E.tensor_scalar(
                        out=nwT[rows, pr, :], in0=PS[bw][rows, h4 * 128:(h4 + 1) * 128], scalar1=-1.0, scalar2=None,
                        op0=ALU.mult),
                           reads=[("ps", bw)], writes=[("nwT", par)])
                yield
            for hf in range(2):
                rows = slice(hf * 64, (hf + 1) * 64)
                bvn = sbank()
                for h in range(8):
                    pr, hh = h // 2, h % 2
                    cs = slice(h * 64, (h + 1) * 64)
                    S.emit("pe", lambda E, h=h, cs=cs, bvn=bvn: E.matmul(
                        PS[bvn][:, cs], lhsT=B5[:, h, :], rhs=X5[:, 0, h, :], start=True, stop=False),
                           reads=[("X", par, 1, 0), ("X", par, 1, 1), ("Bp", par, 1, 0), ("Bp", par, 1, 1)], writes=[("ps", bvn)], signal=False)
                    S.emit("pe", lambda E, pr=pr, hh=hh, cs=cs, bvn=bvn: E.matmul(
                        PS[bvn][:, cs], lhsT=nwT[:, pr, :], rhs=Sb[:, pr, hh, :], start=False, stop=True),
                           reads=[("nwT", par), "Sb"], writes=[("ps", bvn)], signal=(h == 7))
                yield
                S.emit("dve", lambda E, rows=rows, bvn=bvn: E.tensor_tensor(
                    out=vnew[rows], in0=PS[bvn][rows, :].rearrange("p (h d) -> p h d", h=8),
                    in1=beta[rows, t, :].unsqueeze(2).to_broadcast([64, 8, 64]), op=ALU.mult),
                       reads=[("ps", bvn), "gbeta"], writes=["vnew"])
                yield
                bo = sbank()
                for h in range(8):
                    pr, hh = h // 2, h % 2
                    cs = slice(h * 64, (h + 1) * 64)
                    S.emit("pe", lambda E, pr=pr, hh=hh, cs=cs, bo=bo: E.matmul(
                        PS[bo][:, cs], lhsT=qdT[:, pr, :], rhs=Sb[:, pr, hh, :], start=True, stop=False),
                           reads=[("qdT", par), "Sb"], writes=[("ps", bo)], signal=False)
                    S.emit("pe", lambda E, h=h, cs=cs, bo=bo: E.matmul(
                        PS[bo][:, cs], lhsT=attnT[:, h, :], rhs=vnew[:, h, :], start=False, stop=True),
                           reads=[("attnT", par), "vnew"], writes=[("ps", bo)], signal=(h == 7))
                yield
                S.emit("act", lambda E, rows=rows, bo=bo: E.activation(
                    out=osb[rows].rearrange("p h d -> p (h d)"), in_=PS[bo][rows, :], func=AF.Identity),
                       reads=[("ps", bo)], writes=["osb"])
                bs = sbank()
                for h in range(8):
                    pr, hh = h // 2, h % 2
                    S.emit("pe", lambda E, h=h, pr=pr, hh=hh, bs=bs, hf=hf: E.matmul(
                        PS[bs][:, (pr * 2 + hh) * 64:(pr * 2 + hh + 1) * 64],
                        lhsT=kdec[hf][:, 2 * pr:2 * pr + 2, :].rearrange("p h d -> p (h d)"), rhs=vnew[:, h, :],
                        start=True, stop=True),
                           reads=[("kdec", par, hf), "vnew"], writes=[("ps", bs)], signal=(h == 7))
                yield
                gl = EG[:, :, hf * 64 + 63:hf * 64 + 64]
                S.emit("dve", lambda E, gl=gl: E.tensor_tensor(out=tmpS[:], in0=S32[:],
                                                               in1=gl.to_broadcast([128, 4, 64]), op=ALU.mult),
                       reads=["S32", ("EG", par)], writes=["tmpS"])
                dS = PS[bs][:, :].rearrange("p (a b d) -> p a b d", a=4, b=2)
                for hh in range(2):
                    r2 = slice(hh * 64, (hh + 1) * 64)
                    S.emit("dve", lambda E, hh=hh, r2=r2, dS=dS: E.tensor_tensor(
                        out=S32[r2], in0=tmpS[r2], in1=dS[r2, :, hh, :], op=ALU.add),
                           reads=["tmpS", ("ps", bs)], writes=["S32"])
                    S.emit("act", lambda E, hh=hh, r2=r2: E.activation(out=Sb[r2, :, hh, :], in_=S32[r2],
                                                                       func=AF.Identity),
                           reads=["S32"], writes=["Sb"])
                yield
            S.emit("pool", lambda E: E.tensor_tensor(out=osq[:], in0=osb[:], in1=osb[:], op=ALU.mult),
                   reads=["osb"], writes=["osq"])
            S.emit("dve", lambda E: E.tensor_reduce(out=oss[:, 0:8], in_=osq[:], axis=AX.X, op=ALU.add),
                   reads=["osq"], writes=["oss"])
            yield
            S.emit("act", lambda E: E.activation(out=oss[:, 8:16], in_=oss[:, 0:8], func=AF.Sqrt, scale=1.0 / 64,
                                                 bias=P["epsc"][:]),
                   reads=["oss", "epsc"], writes=["oss"])
            S.emit("dve", lambda E: E.reciprocal(out=oss[:, 0:8], in_=oss[:, 8:16]), reads=["oss"], writes=["oss"])
            S.emit("dve", lambda E: E.tensor_tensor(out=osq[:], in0=osb[:],
                                                    in1=oss[:, 0:8].unsqueeze(2).to_broadcast([128, 8, 64]),
                                                    op=ALU.mult),
                   reads=["osb", "oss", "osq"], writes=["osq"])
            yield
            S.emit("pool", lambda E: E.tensor_tensor(out=osq[:], in0=osq[:],
                                                     in1=gnw[:].unsqueeze(1).to_broadcast([128, 8, 64]),
                                                     op=ALU.mult),
                   reads=["osq", "gnw"], writes=["osq"])
            S.emit("dve", lambda E: E.tensor_tensor(out=og[:], in0=osq[:].rearrange("p h d -> p (h d)"),
                                                    in1=zs[:, t, :], op=ALU.mult),
                   reads=["osq", ("gzs", t)], writes=["og"])
            yield
            bt = sbank()
            tbo = PS[bt][:].bitcast(BF16)
            for pr in range(4):
                S.emit("pe", lambda E, pr=pr, tbo=tbo: E.transpose(out=tbo[:, pr * 128:(pr + 1) * 128],
                                                                   in_=og[:, pr * 128:(pr + 1) * 128],
                                                                   identity=ident[:]),
                       reads=["og", "ident"], writes=[("ps", bt)], signal=(pr == 3))
            S.emit("act", lambda E, tbo=tbo: E.activation(out=oT[:, 4:8, tk],
                                                          in_=tbo[:, 0:512].rearrange("p (a i) -> p a i", a=4),
                                                          func=AF.Identity),
                   reads=[("ps", bt)], writes=[("oT", 4 + pr) for pr in range(4)])
            yield

        def drain(gen):
            for _ in gen:
                pass

        def step(gen):
            try:
                next(gen)
                return True
            except StopIteration:
                return False

        active_p = []
        next_p = 0
        p_done = set()
        scan_t = 0
        scan_gen = None
        while scan_t < NT:
            while next_p < NT and len(active_p) < NPS and next_p < scan_t + NSLOT:
                active_p.append([next_p, gen_P(next_p)])
                next_p += 1
            for ent in list(active_p):
                for _ in range(int(os.environ.get("C2RATIO", "2"))):
                    if not step(ent[1]):
                        p_done.add(ent[0])
                        active_p.remove(ent)
                        break
            if scan_gen is None and scan_t in p_done:
                scan_gen = gen_S(scan_t)
            if scan_gen is not None:
                if not step(scan_gen):
                    scan_gen = None
                    scan_t += 1

    def phase_D(self, st, b):
        nc, S, d, P, PS = self.nc, self.S, self.d, self.P, self.PS
        sb = self.sb
        oT = P["oT"]
        ident = P["ident"]
        wo = sb(st, "wo", [128, 8, DM], BF16)
        wu = sb(st, "wu", [128, 8, DFF], BF16)
        wd = sb(st, "wd", [128, 32, DM], BF16)
        premlp = P["premlp"]
        WO_, WU_, WD_ = 8 * DM, 8 * DFF, 32 * DM
        if b > 0:
            S.dma(wo[:].rearrange("p k c -> p (k c)"), self.wscr[:, 0:WO_], reads=["wscr"], writes=["wo"])
            for q4 in range(4):
                S.dma(wu[:, 2 * q4:2 * q4 + 2, :].rearrange("p k c -> p (k c)"),
                      self.wscr[:, WO_ + q4 * 2 * DFF:WO_ + (q4 + 1) * 2 * DFF], reads=["wscr"], writes=["wu"])
            for q4 in range(4):
                S.dma(wd[:, 8 * q4:8 * q4 + 8, :].rearrange("p k c -> p (k c)"),
                      self.wscr[:, WO_ + WU_ + q4 * 8 * DM:WO_ + WU_ + (q4 + 1) * 8 * DM], reads=["wscr"], writes=["wd"])
        with ExitStack() as s_stg:
          if b == 0:
              NSTG = 5
              stg = [sb(s_stg, "stgD%d" % i, [128, DM], F32) for i in range(NSTG)]
              self._ns = 0

              def load_cast(src, dst, dname, scal=None):
                  i = self._ns % NSTG
                  eng = ("pool", "act", "dve")[self._ns % 3]
                  self._ns += 1
                  S.dma(stg[i][:], src, writes=[("stgD", i)])
                  rd = [("stgD", i)] + (["premlp"] if scal is not None else [])
                  if eng == "act":
                      if scal is None:
                          S.emit("act", lambda E, i=i: E.activation(out=dst, in_=stg[i][:], func=AF.Identity),
                                 reads=rd, writes=[dname])
                      else:
                          S.emit("act", lambda E, i=i: E.activation(out=dst, in_=stg[i][:], func=AF.Identity,
                                                                    scale=scal), reads=rd, writes=[dname])
                  else:
                      if scal is None:
                          S.emit(eng, lambda E, i=i: E.tensor_copy(out=dst, in_=stg[i][:]), reads=rd, writes=[dname])
                      else:
                          S.emit(eng, lambda E, i=i: E.tensor_scalar(out=dst, in0=stg[i][:], scalar1=scal, scalar2=None,
                                                                     op0=ALU.mult), reads=rd, writes=[dname])

              for k in range(8):
                  load_cast(d["w_out"][k * 128:(k + 1) * 128, :], wo[:, k, :], ("wo", k))
              for k in range(8):
                  for qd in range(4):
                      load_cast(d["w_up"][k * 128:(k + 1) * 128, qd * 1024:(qd + 1) * 1024],
                                wu[:, k, qd * 1024:(qd + 1) * 1024], ("wu", k, qd), scal=premlp[:, k:k + 1])
              for k in range(32):
                  load_cast(d["w_down"][k * 128:(k + 1) * 128, :], wd[:, k, :], ("wd", k))
        if b == 0:
            S.barrier()
        if b == 0 and self.nseq > 1:
            S.dma(self.wscr[:, 0:WO_], wo[:].rearrange("p k c -> p (k c)"), reads=["wo"], writes=["wscr"])
            for q4 in range(4):
                S.dma(self.wscr[:, WO_ + q4 * 2 * DFF:WO_ + (q4 + 1) * 2 * DFF],
                      wu[:, 2 * q4:2 * q4 + 2, :].rearrange("p k c -> p (k c)"), reads=["wu"], writes=["wscr"])
            for q4 in range(4):
                S.dma(self.wscr[:, WO_ + WU_ + q4 * 8 * DM:WO_ + WU_ + (q4 + 1) * 8 * DM],
                      wd[:, 8 * q4:8 * q4 + 8, :].rearrange("p k c -> p (k c)"), reads=["wd"], writes=["wscr"])

        GT = 2
        xt = [sb(st, "xtD%d" % i, [128, DM], F32) for i in range(GT)]
        tmp = sb(st, "tmpD", [128, DM], F32)
        h2 = sb(st, "h2D", [128, DM], BF16)
        h2T = sb(st, "h2T", [128, 8, GT * 128], BF16)
        uT = sb(st, "uT", [128, 32, GT * 128], BF16)
        rl = [sb(st, "rlD%d" % i, [128, 512], F32) for i in range(2)]
        sm = sb(st, "smD", [128, NT, 12], F32)
        S.emit("dve", lambda E: E.memset(sm[:], 0.0), writes=["smD"])
        postmix_b, postmlp_b, epsc = P["postmix_b"], P["postmlp_b"], P["epsc"]
        oT_all = [("oT", c) for c in range(8)]

        def rms_scale(src_banks, t, col):
            for hf in range(2):
                S.emit("act", lambda E, hf=hf: E.activation(out=tmp[:, hf * 512:(hf + 1) * 512],
                                                            in_=PS[src_banks[hf]][:, :], func=AF.Square,
                                                            accum_out=sm[:, t, col + hf:col + hf + 1]),
                       reads=[("ps", src_banks[hf]), "smD"], writes=["tmpD", "smD"])
            S.emit("dve", lambda E: E.tensor_tensor(out=sm[:, t, col:col + 1], in0=sm[:, t, col:col + 1],
                                                    in1=sm[:, t, col + 1:col + 2], op=ALU.add),
                   reads=["smD"], writes=["smD"])
            S.emit("act", lambda E: E.activation(out=sm[:, t, col + 1:col + 2], in_=sm[:, t, col:col + 1],
                                                 func=AF.Sqrt, scale=1.0 / DM, bias=epsc[:]),
                   reads=["smD", "epsc"], writes=["smD"])
            S.emit("dve", lambda E: E.reciprocal(out=sm[:, t, col + 2:col + 3], in_=sm[:, t, col + 1:col + 2]),
                   reads=["smD"], writes=["smD"])

        def stage1a(t, j, bk):
            tok = slice(t * 128, (t + 1) * 128)
            S.dma(xt[j][:], d["x"][b, tok, :], writes=[("xtD", j)])
            for hf in range(2):
                for c in range(8):
                    S.emit("pe", lambda E, hf=hf, c=c: E.matmul(PS[bk[hf]][:, :], lhsT=oT[:, c, tok],
                                                                rhs=wo[:, c, hf * 512:(hf + 1) * 512],
                                                                start=(c == 0), stop=(c == 7)),
                           reads=oT_all + ["wo"], writes=[("ps", bk[hf])], signal=(c == 7))

        def stage1b(t, j, bk):
            tok = slice(t * 128, (t + 1) * 128)
            rms_scale(bk, t, 0)
            for hf in range(2):
                cs = slice(hf * 512, (hf + 1) * 512)
                S.emit("dve", lambda E, hf=hf, cs=cs: E.scalar_tensor_tensor(
                    out=tmp[:, cs], in0=PS[bk[hf]][:, :], scalar=sm[:, t, 2:3], in1=postmix_b[:, cs],
                    op0=ALU.mult, op1=ALU.mult),
                       reads=[("ps", bk[hf]), "smD", "postmix_b", "tmpD"], writes=["tmpD"])
            S.emit("dve", lambda E: E.tensor_tensor(out=xt[j][:], in0=tmp[:], in1=xt[j][:], op=ALU.add),
                   reads=["tmpD", ("xtD", j)], writes=[("xtD", j)])
            if "x1" in self.dbg_out:
                S.dma(self.dbg_out["x1"][b, tok, :], xt[j][:], reads=[("xtD", j)], writes=[("dbgx1", t)])
            S.emit("act", lambda E: E.activation(out=h2[:], in_=xt[j][:], func=AF.Square, accum_out=sm[:, t, 3:4]),
                   reads=[("xtD", j), "smD"], writes=["h2D", "smD"])
            S.emit("act", lambda E: E.activation(out=sm[:, t, 4:5], in_=sm[:, t, 3:4], func=AF.Sqrt,
                                                 scale=1.0 / DM, bias=epsc[:]),
                   reads=["smD", "epsc"], writes=["smD"])
            S.emit("dve", lambda E: E.reciprocal(out=sm[:, t, 5:6], in_=sm[:, t, 4:5]), reads=["smD"], writes=["smD"])
            S.emit("act", lambda E: E.activation(out=h2[:], in_=xt[j][:], func=AF.Identity, scale=sm[:, t, 5:6]),
                   reads=[("xtD", j), "smD"], writes=["h2D"])
            tb = PS[2][:].bitcast(BF16)
            for k in range(8):
                S.emit("pe", lambda E, k=k: E.transpose(out=tb[:, k * 128:(k + 1) * 128],
                                                        in_=h2[:, k * 128:(k + 1) * 128], identity=ident[:]),
                       reads=["h2D", "ident"], writes=[("ps", 2)], signal=(k == 7))
            S.emit("dve", lambda E: E.tensor_copy(out=h2T[:, :, j * 128:(j + 1) * 128],
                                                  in_=tb.rearrange("p (k c) -> p k c", k=8)),
                   reads=[("ps", 2)], writes=[("h2T", j)])

        def stage2():
            W = GT * 128
            nf = 512 // W
            for g in range(32 // nf):
                bank = 3 + (g % 3)
                for f in range(nf):
                    fc = g * nf + f
                    for k in range(8):
                        S.emit("pe", lambda E, bank=bank, f=f, fc=fc, k=k: E.matmul(
                            PS[bank][:, f * W:(f + 1) * W], lhsT=wu[:, k, fc * 128:(fc + 1) * 128],
                            rhs=h2T[:, k, :], start=(k == 0), stop=(k == 7)),
                               reads=["wu"] + [("h2T", j) for j in range(GT)], writes=[("ps", bank)],
                               signal=(k == 7 and f == nf - 1))
                uv = uT[:, g * nf:(g + 1) * nf, :].rearrange("p a c -> p (a c)")
                ri = g % 2
                S.emit("act", lambda E, bank=bank, ri=ri: E.activation(out=rl[ri][:], in_=PS[bank][:, :], func=AF.Relu),
                       reads=[("ps", bank)], writes=[("rlD", ri)])
                S.emit("pool", lambda E, uv=uv, ri=ri: E.tensor_tensor(out=uv, in0=rl[ri][:], in1=rl[ri][:], op=ALU.mult),
                       reads=[("rlD", ri)], writes=["uT"])

        def stage3(t, j):
            tok = slice(t * 128, (t + 1) * 128)
            for hf in range(2):
                for fc in range(32):
                    S.emit("pe", lambda E, hf=hf, fc=fc: E.matmul(PS[6 + hf][:, :], lhsT=uT[:, fc, j * 128:(j + 1) * 128],
                                                                  rhs=wd[:, fc, hf * 512:(hf + 1) * 512],
                                                                  start=(fc == 0), stop=(fc == 31)),
                           reads=["uT", "wd"], writes=[("ps", 6 + hf)], signal=(fc == 31))
            rms_scale((6, 7), t, 8)
            for hf in range(2):
                cs = slice(hf * 512, (hf + 1) * 512)
                S.emit("dve", lambda E, hf=hf, cs=cs: E.scalar_tensor_tensor(
                    out=tmp[:, cs], in0=PS[6 + hf][:, :], scalar=sm[:, t, 10:11], in1=postmlp_b[:, cs],
                    op0=ALU.mult, op1=ALU.mult),
                       reads=[("ps", 6 + hf), "smD", "postmlp_b", "tmpD"], writes=["tmpD"])
            S.emit("pool", lambda E: E.tensor_tensor(out=xt[j][:], in0=tmp[:], in1=xt[j][:], op=ALU.add),
                   reads=["tmpD", ("xtD", j)], writes=[("xtD", j)])
            S.dma(self.out[b, tok, :], xt[j][:], reads=[("xtD", j)], writes=[("out", b, t)])

        for gi in range(NT // GT):
            OB = ((0, 1), (3, 4))
            for j in range(GT):
                stage1a(gi * GT + j, j, OB[j])
            for j in range(GT):
                stage1b(gi * GT + j, j, OB[j])
            stage2()
            for j in range(GT):
                stage3(gi * GT + j, j)


def host_inputs(inputs, core, nseq):
    f = lambda a: np.ascontiguousarray(np.asarray(a, dtype=np.float32))
    m = {}
    m["x"] = f(inputs["x"][core * nseq:(core + 1) * nseq])
    m["w_in"] = f(inputs["w_in"][0])
    m["w_out"] = f(inputs["w_out"][0])
    m["w_up"] = f(inputs["w_up"][0])
    m["w_down"] = f(inputs["w_down"][0])
    m["premix_pk"] = f(np.asarray(inputs["pre_mix_norm"][0]).reshape(8, 128).T)
    m["premlp_pk"] = f(np.asarray(inputs["pre_mlp_norm"][0]).reshape(8, 128).T)
    m["postmix"] = f(np.asarray(inputs["post_mix_norm"][0]).reshape(1, DM))
    m["postmlp"] = f(np.asarray(inputs["post_mlp_norm"][0]).reshape(1, DM))
    rb = np.asarray(inputs["rel_bias"], dtype=np.float32)
    tab = np.concatenate([rb, np.full((8, 1), NEG, np.float32)], axis=1)
    j = np.arange(128)[:, None]
    i = np.arange(128)[None, :]
    idx0 = np.where(i - j >= 0, rel_bucket_np(i - j), 32)
    idx1 = rel_bucket_np(128 + i - j)
    idx = np.stack([idx0, idx1], axis=0)
    tt = tab[:, idx]
    m["ttab"] = f(tt.transpose(2, 0, 1, 3).reshape(128, 8 * 2 * 128))
    m["rb31"] = f(rb[:, 31].reshape(1, 8))
    cw = np.asarray(inputs["conv_w"][0], dtype=np.float32)
    m["convw_pk"] = f(cw.T.reshape(12, 128, 4).transpose(1, 0, 2).reshape(128, 48))
    m["alog"] = f(np.asarray(inputs["A_log"][0]).reshape(1, 8))
    m["dtb"] = f(np.asarray(inputs["dt_bias"][0]).reshape(1, 8))
    m["gnw"] = f(np.asarray(inputs["gdn_norm_w"][0]).reshape(1, 64))
    return m


_PROG = {}


def kernel(**inputs):
    ncores = 8
    nseq = 16 // ncores
    if "p" not in _PROG:
        _PROG["p"] = Prog(nseq)
    prog = _PROG["p"]
    in_maps = [host_inputs(inputs, c, nseq) for c in range(ncores)]
    res = run_bass_kernel_spmd(prog.nc, in_maps, core_ids=list(range(ncores)))
    out = np.concatenate([r["out"] for r in res.results], axis=0)
    return out.astype(np.float32)
```
